# Optimizing a Trainium2 kernel written in Bass

```python
import math
import jax, jax.numpy as jnp
from jax import lax
import numpy as np

D_MODEL = 1024
BATCH = 8
SEQ = 2048
DEPTH = 2

N_MIXERS = 2
HEAD_DIM = 64
ATTN_HEADS = D_MODEL // HEAD_DIM
ATTN_WIDTH = ATTN_HEADS * HEAD_DIM
DILATED_GROUPS = ((128, 1), (512, 4), (2048, 16))
N_GROUPS_A = len(DILATED_GROUPS)
ATTN_IN_COLS = 3 * N_GROUPS_A * ATTN_WIDTH + ATTN_WIDTH
REL_BUCKETS = 32
REL_MAX_DIST = 2048
SSM_WIDTH = D_MODEL
SSM_GROUP = 16
SSM_N_GROUPS = SSM_WIDTH // SSM_GROUP
SSM_STATE = 64
N_ATTN_LAYERS = (DEPTH + 1) // 2
N_SSM_LAYERS = DEPTH // 2
EPS = 1e-6
NEG = -1e30

kernel_name = "hybrid_dilated_attn_s5_interleaved"


def rms_norm(x, g):
    xf = x.astype(jnp.float32)
    y = xf * lax.rsqrt(jnp.mean(xf * xf, axis=-1, keepdims=True) + EPS) * g.astype(jnp.float32)
    return y.astype(x.dtype)


def t5_bucket(dist):
    max_exact = REL_BUCKETS // 2
    n = jnp.maximum(dist, 1).astype(jnp.float32)
    large = max_exact + (jnp.log(n / max_exact) / math.log(REL_MAX_DIST / max_exact)
                         * (REL_BUCKETS - max_exact)).astype(jnp.int32)
    large = jnp.minimum(large, REL_BUCKETS - 1)
    return jnp.where(dist < max_exact, dist, large)


def dilated_group_attention(q, k, v, rel_bias, window, dilation):
    B, S, H, E = q.shape
    steps = window // dilation
    blk = steps
    L = S // dilation
    nb = -(-L // blk)
    Lp = nb * blk

    def to_sub(t):
        t = t.reshape(B, L, dilation, H, E)
        return jnp.pad(t, ((0, 0), (0, Lp - L), (0, 0), (0, 0), (0, 0)))

    qs = to_sub(q).reshape(B, nb, blk, dilation, H, E)
    front = ((0, 0), (blk, 0), (0, 0), (0, 0), (0, 0))
    ks = jnp.pad(to_sub(k), front).reshape(B, nb + 1, blk, dilation, H, E)
    vs = jnp.pad(to_sub(v), front).reshape(B, nb + 1, blk, dilation, H, E)
    kw = jnp.concatenate([ks[:, :-1], ks[:, 1:]], axis=2)
    vw = jnp.concatenate([vs[:, :-1], vs[:, 1:]], axis=2)

    i = jnp.arange(blk)[:, None]
    j = jnp.arange(2 * blk)[None, :]
    back = blk + i - j
    band = (back >= 0) & (back <= steps)
    valid = band[None] & ((jnp.arange(nb)[:, None, None] > 0) | (j[None] >= blk))
    bias = rel_bias[t5_bucket(jnp.maximum(back, 0) * dilation)]
    bias = jnp.transpose(bias, (2, 0, 1)).astype(jnp.float32)

    scale = HEAD_DIM ** -0.5
    logits = jnp.einsum('bnqrhe,bnkrhe->bnrhqk', qs, kw,
                        preferred_element_type=jnp.float32) * scale + bias
    logits = jnp.where(valid[None, :, None, None], logits, NEG)
    lse = jax.nn.logsumexp(logits, axis=-1)
    p = jnp.exp(logits - lse[..., None])
    o = jnp.einsum('bnrhqk,bnkrhe->bnqrhe', p.astype(v.dtype), vw)
    o = o.reshape(B, Lp, dilation, H, E)[:, :L].reshape(B, S, H, E)
    lse = jnp.transpose(lse, (0, 1, 4, 2, 3)).reshape(B, Lp, dilation, H)[:, :L].reshape(B, S, H)
    return o, lse


def dilated_attention_mixer(h, w_in, w_out, rel_bias):
    B, S, _ = h.shape
    proj = h @ w_in
    n_qkv = 3 * N_GROUPS_A * ATTN_WIDTH
    qkv = proj[..., :n_qkv].reshape(B, S, 3, N_GROUPS_A, ATTN_HEADS, HEAD_DIM)
    z = proj[..., n_qkv:]
    outs, lses = [], []
    for g, (window, dilation) in enumerate(DILATED_GROUPS):
        o, l = dilated_group_attention(qkv[:, :, 0, g], qkv[:, :, 1, g], qkv[:, :, 2, g],
                                       rel_bias, window, dilation)
        outs.append(o)
        lses.append(l)
    wts = jax.nn.softmax(jnp.stack(lses), axis=0)
    o = jnp.sum(wts[..., None] * jnp.stack(outs).astype(jnp.float32), axis=0)
    o = o.reshape(B, S, ATTN_WIDTH).astype(h.dtype)
    return (o * jax.nn.silu(z)) @ w_out


def s5_scan(u, a_re, a_im, log_dt, b_re, b_im, c_re, c_im, d_skip):
    B, S, _ = u.shape
    f = jnp.float32
    uf = u.astype(f).reshape(B, S, SSM_N_GROUPS, SSM_GROUP)
    A = lax.complex(a_re.astype(f), a_im.astype(f))
    dt = jnp.exp(log_dt.astype(f))[:, None]
    a_bar = jnp.exp(A * dt)
    Bm = lax.complex(b_re.astype(f), b_im.astype(f))
    b_bar = ((a_bar - 1.0) / A)[..., None] * Bm
    bu = jnp.einsum('bsgc,gpc->bsgp', uf.astype(jnp.complex64), b_bar)
    a_elems = jnp.broadcast_to(a_bar, bu.shape)

    def combine(left, right):
        return (right[0] * left[0], right[0] * left[1] + right[1])

    _, states = lax.associative_scan(combine, (a_elems, bu), axis=1)
    Cm = lax.complex(c_re.astype(f), c_im.astype(f))
    y = jnp.einsum('bsgp,gcp->bsgc', states, Cm).real
    y = y + d_skip.astype(f).reshape(SSM_N_GROUPS, SSM_GROUP) * uf
    return y.reshape(B, S, SSM_WIDTH).astype(u.dtype)


def s5_mixer(h, w_in, a_re, a_im, log_dt, b_re, b_im, c_re, c_im, d_skip, w_glu, b_glu, w_out):
    proj = h @ w_in
    u, z = proj[..., :SSM_WIDTH], proj[..., SSM_WIDTH:]
    y = s5_scan(u, a_re, a_im, log_dt, b_re, b_im, c_re, c_im, d_skip)
    g = jax.nn.gelu(y)
    y = g * jax.nn.sigmoid(g @ w_glu + b_glu)
    return (y * jax.nn.silu(z)) @ w_out


def setup_inputs(seed: int = 0) -> dict:
    key = jax.random.key(seed)
    ks = jax.random.split(key, 24)
    nrm = jax.random.normal
    f = jnp.float32
    NA, NS, D = N_ATTN_LAYERS, N_SSM_LAYERS, D_MODEL
    G, P, C, E = SSM_N_GROUPS, SSM_STATE, SSM_GROUP, SSM_WIDTH
    x = nrm(ks[0], (BATCH, SEQ, D), f)
    rel_bias = 0.5 * nrm(ks[1], (REL_BUCKETS, ATTN_HEADS), f)
    attn_pre_norm = 1.0 + 0.05 * nrm(ks[2], (NA, D), f)
    attn_w_in = nrm(ks[3], (NA, D, ATTN_IN_COLS), f) * D ** -0.5
    attn_w_out = nrm(ks[4], (NA, ATTN_WIDTH, D), f) * ATTN_WIDTH ** -0.5
    attn_post_norm = 1.0 + 0.05 * nrm(ks[5], (NA, D), f)
    ssm_pre_norm = 1.0 + 0.05 * nrm(ks[6], (NS, D), f)
    ssm_w_in = nrm(ks[7], (NS, D, 2 * E), f) * D ** -0.5
    ssm_a_re = -0.5 + 0.01 * nrm(ks[8], (NS, G, P), f)
    ssm_a_im = math.pi * jnp.arange(P, dtype=f)[None, None, :] + 0.01 * nrm(ks[9], (NS, G, P), f)
    ssm_log_dt = jax.random.uniform(ks[10], (NS, G), f, math.log(1e-3), math.log(1e-1))
    ssm_b_re = nrm(ks[11], (NS, G, P, C), f) * (2 * C) ** -0.5
    ssm_b_im = nrm(ks[12], (NS, G, P, C), f) * (2 * C) ** -0.5
    ssm_c_re = nrm(ks[13], (NS, G, C, P), f) * (2 * P) ** -0.5
    ssm_c_im = nrm(ks[14], (NS, G, C, P), f) * (2 * P) ** -0.5
    ssm_d = nrm(ks[15], (NS, E), f)
    ssm_w_glu = nrm(ks[16], (NS, E, E), f) * E ** -0.5
    ssm_b_glu = 0.02 * nrm(ks[17], (NS, E), f)
    ssm_w_out = nrm(ks[18], (NS, E, D), f) * E ** -0.5
    ssm_post_norm = 1.0 + 0.05 * nrm(ks[19], (NS, D), f)
    return {"x": x, "rel_bias": rel_bias,
            "attn_pre_norm": attn_pre_norm, "attn_w_in": attn_w_in,
            "attn_w_out": attn_w_out, "attn_post_norm": attn_post_norm,
            "ssm_pre_norm": ssm_pre_norm, "ssm_w_in": ssm_w_in,
            "ssm_a_re": ssm_a_re, "ssm_a_im": ssm_a_im, "ssm_log_dt": ssm_log_dt,
            "ssm_b_re": ssm_b_re, "ssm_b_im": ssm_b_im,
            "ssm_c_re": ssm_c_re, "ssm_c_im": ssm_c_im, "ssm_d": ssm_d,
            "ssm_w_glu": ssm_w_glu, "ssm_b_glu": ssm_b_glu,
            "ssm_w_out": ssm_w_out, "ssm_post_norm": ssm_post_norm}


def reference(x, rel_bias, attn_pre_norm, attn_w_in, attn_w_out, attn_post_norm,
              ssm_pre_norm, ssm_w_in, ssm_a_re, ssm_a_im, ssm_log_dt,
              ssm_b_re, ssm_b_im, ssm_c_re, ssm_c_im, ssm_d,
              ssm_w_glu, ssm_b_glu, ssm_w_out, ssm_post_norm):
    for i in range(DEPTH):
        j = i // N_MIXERS
        if i % N_MIXERS == 0:
            h = rms_norm(x, attn_pre_norm[j])
            h = dilated_attention_mixer(h, attn_w_in[j], attn_w_out[j], rel_bias)
            x = x + rms_norm(h, attn_post_norm[j])
        else:
            h = rms_norm(x, ssm_pre_norm[j])
            h = s5_mixer(h, ssm_w_in[j], ssm_a_re[j], ssm_a_im[j], ssm_log_dt[j],
                         ssm_b_re[j], ssm_b_im[j], ssm_c_re[j], ssm_c_im[j], ssm_d[j],
                         ssm_w_glu[j], ssm_b_glu[j], ssm_w_out[j])
            x = x + rms_norm(h, ssm_post_norm[j])
    return x
```

```python
import math
import numpy as np
import ml_dtypes
import concourse.bass as bass
import concourse.mybir as mybir
from concourse.bass_utils import run_bass_kernel_spmd

F32 = mybir.dt.float32
BF16 = mybir.dt.bfloat16
AF = mybir.ActivationFunctionType
ALU = mybir.AluOpType

D = 1024
S = 2048
NT = 16
EPS = 1e-6
GROUPS = ((128, 1), (512, 4), (2048, 16))
NEGM = -30000.0
SSM_G = 64
SSM_P = 64
SSM_C = 16


class Res:
    __slots__ = ("name", "w", "rs", "rd")

    def __init__(self, name):
        self.name = name
        self.w = None
        self.rs = {}
        self.rd = []


class DmaSem:
    def __init__(self, sem):
        self.sem = sem
        self.count = 0


class Op:
    __slots__ = ("eng", "fn", "deps", "need", "seq", "dsem", "dval", "idx", "dreq", "group", "gidx")


class Sched:
    ENGS = ("pe", "act", "dve", "pool", "sp")

    def __init__(self, nc):
        self.nc = nc
        self.ops = {e: [] for e in self.ENGS}
        self.allops = []
        self.cur_group = None
        self.ngroups = 0

    def begin_group(self):
        self.ngroups += 1
        self.cur_group = self.ngroups

    def end_group(self):
        self.cur_group = None

    def _mk(self, eng, fn, reads, writes, isdma=False):
        o = Op()
        o.eng = eng
        o.fn = fn
        o.need = False
        o.seq = 0
        o.dsem = None
        o.dval = 0
        deps = []
        for r in reads:
            if r.w is not None:
                deps.append(r.w)
        for r in writes:
            if r.w is not None:
                deps.append(r.w)
            deps.extend(r.rs.values())
            deps.extend(r.rd)
        o.deps = deps
        o.dreq = {}
        for p in deps:
            if p.dsem is not None:
                o.dreq[id(p.dsem)] = (p.dsem.sem, p.dsem.count)
        for r in reads:
            if isdma:
                r.rd.append(o)
            else:
                r.rs[eng] = o
        for r in writes:
            r.w = o
            r.rs = {}
            r.rd = []
        o.idx = len(self.ops[eng])
        o.group = self.cur_group
        o.gidx = len(self.allops)
        self.ops[eng].append(o)
        self.allops.append(o)
        return o

    def op(self, eng, fn, reads=(), writes=()):
        return self._mk(eng, fn, list(reads), list(writes))

    def dma(self, eng, out, in_, dsem, reads=(), writes=()):
        def fn(e):
            return e.dma_start(out=out, in_=in_)
        o = self._mk(eng, fn, list(reads), list(writes), isdma=True)
        dsem.count += 16
        o.dsem = dsem
        o.dval = dsem.count
        return o


def _t5_bucket_np(dist):
    max_exact = 16
    n = np.maximum(dist, 1).astype(np.float32)
    large = max_exact + (np.log(n / np.float32(max_exact)) / np.float32(math.log(2048 / max_exact))
                         * np.float32(32 - max_exact)).astype(np.int32)
    large = np.minimum(large, 31)
    return np.where(dist < max_exact, dist, large)


def _bias_tiles(rel_bias):
    j = np.arange(128)[:, None]
    q = np.arange(256)[None, :]
    delta = q - j
    valid = (delta >= 0) & (delta <= 128)
    out = np.empty((8, 128, 6, 256), np.float32)
    for g, (_, dil) in enumerate(GROUPS):
        bucket = _t5_bucket_np(np.maximum(delta, 0) * dil)
        for h in range(16):
            t = rel_bias[bucket, h]
            t = np.where(valid, t, np.float32(NEGM)).astype(np.float32)
            out[h // 2, :, g * 2 + (h % 2), :] = t
    return out


def _perm_w_in(w):
    out = np.empty((8, D, 1280), np.float32)
    for hp in range(8):
        cs = slice(hp * 128, (hp + 1) * 128)
        blk = []
        for g in range(3):
            blk.append(w[:, (0 * 3 + g) * 1024:(0 * 3 + g + 1) * 1024][:, cs])
            blk.append(w[:, (1 * 3 + g) * 1024:(1 * 3 + g + 1) * 1024][:, cs])
        for g in range(3):
            blk.append(w[:, (2 * 3 + g) * 1024:(2 * 3 + g + 1) * 1024][:, cs])
        blk.append(w[:, 9216:10240][:, cs])
        out[hp] = np.concatenate(blk, axis=1)
    return out


class Builder:
    def __init__(self, nc):
        self.nc = nc
        self.S = Sched(nc)
        self.dsems = []
        self.allres = []
        self.last_barrier = None
        self.scratch = None

    def R(self, name):
        r = Res(name)
        r.w = self.last_barrier
        self.allres.append(r)
        return r

    def barrier(self):
        scr = self.scratch

        def fn(e):
            return e.memset(scr, 0.0)
        o = self.S.op("pool", fn, (), list(self.allres))
        self.last_barrier = o
        return o

    def mm(self, out, lhsT, rhs, start, stop, reads, writes):
        def fn(e):
            return e.matmul(out, lhsT, rhs, start=start, stop=stop)
        return self.S.op("pe", fn, reads, writes)

    def tr(self, out, in_, ident, reads, writes):
        def fn(e):
            return e.transpose(out, in_, ident)
        return self.S.op("pe", fn, reads, writes)

    def act(self, out, in_, func, reads, writes, scale=None, bias=None, accum_out=None):
        kw = {}
        if scale is not None:
            kw["scale"] = scale
        if bias is not None:
            kw["bias"] = bias
        if accum_out is not None:
            kw["accum_out"] = accum_out

        def fn(e):
            return e.activation(out, in_, func, **kw)
        return self.S.op("act", fn, reads, writes)

    def tt(self, eng, out, in0, in1, op, reads, writes):
        def fn(e):
            return e.tensor_tensor(out, in0, in1, op)
        return self.S.op(eng, fn, reads, writes)

    def ts(self, eng, out, in0, s1, op0, reads, writes, s2=None, op1=None):
        def fn(e):
            if op1 is None:
                return e.tensor_scalar(out, in0, s1, None, op0)
            return e.tensor_scalar(out, in0, s1, s2, op0, op1)
        return self.S.op(eng, fn, reads, writes)

    def stt(self, out, in0, scalar, in1, op0, op1, reads, writes):
        def fn(e):
            return e.scalar_tensor_tensor(out, in0, scalar, in1, op0, op1)
        return self.S.op("dve", fn, reads, writes)

    def cp(self, eng, out, in_, reads, writes):
        def fn(e):
            return e.tensor_copy(out, in_)
        return self.S.op(eng, fn, reads, writes)

    def memset(self, eng, ap, val, writes):
        def fn(e):
            return e.memset(ap, val)
        return self.S.op(eng, fn, (), writes)

    def recip(self, out, in_, reads, writes):
        def fn(e):
            return e.reciprocal(out, in_)
        return self.S.op("dve", fn, reads, writes)


def rms_rstd(B, ss, rs, r_ss, r_rs):
    B.ts("dve", rs, ss, 1.0 / D, ALU.mult, [r_ss], [r_rs], s2=EPS, op1=ALU.add)
    B.act(rs, rs, AF.Sqrt, [r_rs], [r_rs])
    B.recip(rs, rs, [r_rs], [r_rs])


def build_program(mode="fused", debug=False, stop=None):
    nc = bass.Bass("TRN2", target_bir_lowering=False)
    B = Builder(nc)
    dt = nc.dram_tensor
    dr = {}
    do_l0 = mode in ("l0", "fused")
    do_l1 = mode in ("l1", "fused")
    if do_l0:
        dr["x"] = dt("x", [S, D], F32, kind="ExternalInput").ap()
        dr["w0"] = dt("w0", [8, D, 1280], F32, kind="ExternalInput").ap()
        dr["wo0"] = dt("wo0", [D, D], F32, kind="ExternalInput").ap()
        dr["gpre0"] = dt("gpre0", [128, 8], F32, kind="ExternalInput").ap()
        dr["gpost0"] = dt("gpost0", [128, D], F32, kind="ExternalInput").ap()
        dr["bias0"] = dt("bias0", [8, 128, 6, 256], F32, kind="ExternalInput").ap()
    dr["ident"] = dt("ident", [128, 128], F32, kind="ExternalInput").ap()
    if mode == "l0":
        dr["x1"] = dt("x1", [S, D], F32, kind="ExternalOutput").ap()
    elif mode == "l1":
        dr["x1"] = dt("x1", [S, D], F32, kind="ExternalInput").ap()
    else:
        dr["x1"] = dt("x1", [S, D], F32, kind="Internal").ap()
    if do_l1:
        build_l1_decl(dt, dr)
        dr["out"] = dt("out", [S, D], F32, kind="ExternalOutput").ap()

    from contextlib import ExitStack
    with ExitStack() as es:
        E = es.enter_context
        sems = {e: E(nc.semaphore("c_" + e)) for e in ("pe", "act", "dve", "pool", "sp")}

        def new_dsem(name):
            return DmaSem(E(nc.semaphore(name)))

        psum = [E(nc.psum_tensor("ps%d" % i, [128, 512], F32)) for i in range(8)]
        rps = [B.R("psum%d" % i) for i in range(8)]
        hT = E(nc.sbuf_tensor("hT", [128, 8, S], BF16))
        GT = E(nc.sbuf_tensor("GT", [128, 8, S], BF16))
        ident = E(nc.sbuf_tensor("ident_sb", [128, 128], F32))
        ones = None
        small = E(nc.sbuf_tensor("small", [128, 64], F32))
        bscr = E(nc.sbuf_tensor("bscr", [128, 8], F32))
        B.scratch = bscr[:, 0:1]
        r_hT = [B.R("hT%d" % i) for i in range(NT)]
        r_GT = B.R("GT")
        r_ident = B.R("ident")
        r_ones = B.R("ones")
        d_const = new_dsem("d_const")
        B.S.dma("sp", ident[:], dr["ident"], d_const, (), [r_ident])
        dumps = {}
        d_dump = new_dsem("d_dump")

        def dump(name, ap, reads):
            if not debug or name in dumps:
                return
            t = dt("dbg_" + name, list(ap.shape), ap.dtype, kind="ExternalOutput").ap()
            o = B.S.dma("pool", t, ap, d_dump, reads, [])
            fr = B.R("dfin_" + name)
            fr.w = o
            dumps[name] = fr
        ctx = dict(nc=nc, B=B, dr=dr, E=E, new_dsem=new_dsem, psum=psum, rps=rps, hT=hT, GT=GT, dump=dump,
                   stop=stop, ident=ident, ones=ones, small=small, r_hT=r_hT, r_GT=r_GT,
                   r_ident=r_ident, r_ones=r_ones, sems=sems)
        if mode == "fused":
            ctx["r_x1"] = [B.R("x1_%d" % i) for i in range(NT)]
        if do_l1:
            l1_prep_stage1(ctx, E)
        else:
            ctx["pump"] = lambda n=None: None
        with nc.Block() as block:
            final = []
            if do_l0:
                final = build_l0(ctx)
            if do_l1:
                final = build_l1(ctx)
            B.S.op("sp", None, list(final) + list(dumps.values()), ())
            emit_all(B, block, sems)
    return nc


def emit_all(B, block, sems):
    S_ = B.S
    nc = B.nc
    for o in S_.allops:
        for p in o.deps:
            if p.dsem is None and not (p.eng == "pe" and o.eng == "pe"):
                p.need = True
    for e in S_.ENGS:
        c = 0
        for o in S_.ops[e]:
            if o.need:
                c += 1
                o.seq = c

    def reqs_of(o, e, before=None):
        req = {}
        for p in o.deps:
            if before is not None and p.gidx >= before:
                continue
            if p.dsem is not None:
                key = ("d", id(p.dsem))
                v = o.dreq[id(p.dsem)]
                if req.get(key, (None, 0))[1] < v[1]:
                    req[key] = v
            else:
                if p.eng == "pe" and e == "pe":
                    continue
                key = ("e", p.eng)
                if req.get(key, (None, 0))[1] < p.seq:
                    req[key] = (sems[p.eng], p.seq)
        return req

    def run(e, eo):
        waited = {}
        ops = S_.ops[e]
        seen_groups = set()
        for k, o in enumerate(ops):
            req = reqs_of(o, e)
            if o.group is not None and o.group not in seen_groups:
                seen_groups.add(o.group)
                j = k + 1
                while j < len(ops) and ops[j].group == o.group:
                    for key, v in reqs_of(ops[j], e, before=o.gidx).items():
                        if req.get(key, (None, 0))[1] < v[1]:
                            req[key] = v
                    j += 1
            for key, (sem, val) in req.items():
                if waited.get(key, 0) >= val:
                    continue
                waited[key] = val
                eo.wait_ge(sem, val)
            if o.fn is None:
                continue
            ins = o.fn(eo)
            if o.dsem is not None:
                ins.then_inc(o.dsem.sem, 16)
            elif o.need:
                ins.then_inc(sems[e], 1)

    @block.tensor
    def _(eo):
        run("pe", eo)

    @block.scalar
    def _(eo):
        run("act", eo)

    @block.vector
    def _(eo):
        run("dve", eo)

    @block.gpsimd
    def _(eo):
        run("pool", eo)

    @block.sync
    def _(eo):
        run("sp", eo)


def make_hT(ctx, src, scope, gp, r_gp, perm8=False, src_res=None, name="a"):
    nc, B, psum, rps, hT, ident = ctx["nc"], ctx["B"], ctx["psum"], ctx["rps"], ctx["hT"], ctx["ident"]
    E = scope
    NS = 3
    xt = E(nc.sbuf_tensor("xt_" + name, [128, NS, D], F32))
    xs = E(nc.sbuf_tensor("xs_" + name, [128, NS, D], F32))
    jk = E(nc.sbuf_tensor("jk_" + name, [128, D], F32))
    st = E(nc.sbuf_tensor("st_" + name, [128, NS, 4], F32))
    r_xt = [B.R("xt%d" % i) for i in range(NS)]
    r_xs = [B.R("xs%d" % i) for i in range(NS)]
    r_st = [B.R("st%d" % i) for i in range(NS)]
    r_jk = B.R("jk")
    d_x = [ctx["new_dsem"]("d_x%s%d" % (name, i)) for i in range(NS)]

    def stage1(tt):
        sl = tt % NS
        B.S.dma("sp", xt[:, sl, :], src[tt * 128:(tt + 1) * 128, :], d_x[sl],
                [src_res[tt]] if src_res else (), [r_xt[sl]])
        B.act(jk[:, :], xt[:, sl, :], AF.Square, [r_xt[sl]], [r_jk, r_st[sl]], accum_out=st[:, sl, 0:1])
        B.ts("dve", st[:, sl, 1:2], st[:, sl, 0:1], 1.0 / D, ALU.mult, [r_st[sl]], [r_st[sl]],
             s2=EPS, op1=ALU.add)

    def stage2(tt):
        sl = tt % NS
        B.act(st[:, sl, 1:2], st[:, sl, 1:2], AF.Sqrt, [r_st[sl]], [r_st[sl]])
        B.recip(st[:, sl, 1:2], st[:, sl, 1:2], [r_st[sl]], [r_st[sl]])
        B.ts("dve", xs[:, sl, :], xt[:, sl, :], st[:, sl, 1:2], ALU.mult,
             [r_xt[sl], r_st[sl]], [r_xs[sl]])

    def stage3(tt):
        sl = tt % NS
        for half in range(2):
            bk = half + 2 * (tt % 2)
            for j in range(4):
                c = half * 4 + j
                B.tr(psum[bk][:, j * 128:(j + 1) * 128], xs[:, sl, c * 128:(c + 1) * 128], ident[:],
                     [r_xs[sl], ctx["r_ident"]], [rps[bk]])
            for j in range(4):
                c = half * 4 + j
                if perm8:
                    dst = hT[:, c, :].rearrange("p (j n) -> p j n", j=8)[:, :, 16 * tt:16 * tt + 16]
                    srcp = psum[bk][:, j * 128:(j + 1) * 128].rearrange("p (n j) -> p j n", j=8)
                else:
                    dst = hT[:, c, tt * 128:(tt + 1) * 128]
                    srcp = psum[bk][:, j * 128:(j + 1) * 128]
                wr = ctx["r_hT"] if perm8 else [ctx["r_hT"][tt]]
                B.ts("dve", dst, srcp, gp[:, c:c + 1], ALU.mult, [r_gp], [rps[bk]] + wr)

    for t in range(NT + 2):
        if t < NT:
            stage1(t)
        if 0 <= t - 1 < NT:
            stage2(t - 1)
        if 0 <= t - 2 < NT:
            stage3(t - 2)


class WStream:
    def __init__(self, ctx, scope, nslot=2, name="ws", cols=256):
        nc, B = ctx["nc"], ctx["B"]
        self.ctx = ctx
        self.n = nslot
        self.wst = scope(nc.sbuf_tensor(name + "_st", [128, nslot, 8, cols], F32))
        self.wbf = scope(nc.sbuf_tensor(name + "_bf", [128, nslot, 8, cols], BF16))
        self.r_st = [B.R(name + "st%d" % i) for i in range(nslot)]
        self.r_bf = [B.R(name + "bf%d" % i) for i in range(nslot)]
        self.ds = [ctx["new_dsem"]("d_" + name + str(i)) for i in range(nslot)]
        self.k = 0

    def dma(self, src):
        B = self.ctx["B"]
        sl = self.k % self.n
        self.k += 1
        B.S.dma("sp", self.wst[:, sl], src.rearrange("(c p) n -> p c n", p=128), self.ds[sl],
                (), [self.r_st[sl]])
        return sl

    def cast(self, sl):
        B = self.ctx["B"]
        B.act(self.wbf[:, sl], self.wst[:, sl], AF.Copy, [self.r_st[sl]], [self.r_bf[sl]])
        return self.wbf[:, sl], self.r_bf[sl]

    def load(self, src, dst=None, r_dst=None):
        B = self.ctx["B"]
        sl = self.k % self.n
        self.k += 1
        B.S.dma("sp", self.wst[:, sl], src.rearrange("(c p) n -> p c n", p=128), self.ds[sl],
                (), [self.r_st[sl]])
        if dst is None:
            dst, r_dst = self.wbf[:, sl], self.r_bf[sl]
        B.act(dst, self.wst[:, sl], AF.Copy, [self.r_st[sl]], [r_dst])
        return dst, r_dst


def grp_tokens(g, bi):
    d = GROUPS[g][1]
    nbr = 16 // d
    r = bi // nbr
    m0 = (bi % nbr) * 128
    return d * m0 + r, d


def build_l0(ctx):
    nc, B, dr, psum, rps = ctx["nc"], ctx["B"], ctx["dr"], ctx["psum"], ctx["rps"]
    hT, GT, ones, small = ctx["hT"], ctx["GT"], ctx["ones"], ctx["small"]
    r_hT, r_GT = ctx["r_hT"], ctx["r_GT"]
    from contextlib import ExitStack
    d_par = ctx["new_dsem"]("d_par0")
    r_gp = B.R("gpre0")
    B.S.dma("sp", small[:, 0:8], dr["gpre0"], d_par, (), [r_gp])
    with ExitStack() as es:
        make_hT(ctx, dr["x"], es.enter_context, small[:, 0:8], r_gp)
        B.barrier()
    allhT = list(r_hT)
    ctx["dump"]("hT", hT[:], allhT)
    with ExitStack() as es:
        E = es.enter_context
        ws = WStream(ctx, E, 2, "w0s", cols=128)
        QK = E(nc.sbuf_tensor("QK", [128, 6, S], BF16))
        VA0 = E(nc.sbuf_tensor("VA0", [128, 48, 65], BF16))
        VA1 = E(nc.sbuf_tensor("VA1", [128, 48, 128], BF16))
        SZ = E(nc.sbuf_tensor("SZ", [128, S], F32))
        BT = E(nc.sbuf_tensor("BT", [128, 2, 6, 256], F32))
        OA = E(nc.sbuf_tensor("OA", [128, 2, S], F32))
        LT = E(nc.sbuf_tensor("LT", [128, 2, 2, 256], F32))
        PT = E(nc.sbuf_tensor("PT", [128, 5, 2, 256], BF16))
        VT = E(nc.sbuf_tensor("VT", [128, 3, S], BF16))
        RHL = E(nc.sbuf_tensor("RHL", [128, 2, S], BF16))
        identb = E(nc.sbuf_tensor("identb", [128, 128], BF16))
        onesb = E(nc.sbuf_tensor("onesb", [128, 128], BF16))
        r_VT = [B.R("VT%d" % i) for i in range(3)]
        r_RHL = B.R("RHL")
        r_cb = B.R("constb")
        B.cp("pool", identb[:], ctx["ident"][:], [ctx["r_ident"]], [r_cb])
        B.memset("pool", onesb[:], 1.0, [r_cb])
        r_QK = [B.R("QK%d" % i) for i in range(6)]
        r_VA = [B.R("VA%d" % i) for i in range(3)]
        r_SZ = B.R("SZ")
        r_BT = [B.R("BT0"), B.R("BT1")]
        r_OA = [B.R("OA0"), B.R("OA1")]
        r_LT = [B.R("LT%d" % i) for i in range(2)]
        r_PT = [B.R("PT%d" % i) for i in range(5)]
        d_bt = [ctx["new_dsem"]("d_bt0"), ctx["new_dsem"]("d_bt1")]
        B.memset("pool", VA0[:], 1.0, r_VA)
        B.memset("pool", VA1[:], 0.0, r_VA)
        B.memset("pool", VA1[:, :, 0:1], 1.0, r_VA)
        pb = [0]

        def proj_bank():
            pb[0] ^= 1
            return pb[0]

        stb = [0]
        fb = [0]
        dq = []
        pre = {}
        wdma = {}

        def wsrc(h, wb):
            return dr["w0"][h][:, wb * 128:(wb + 1) * 128]
        for hp in range(8):
            bsl = hp % 2
            B.S.dma("sp", BT[:, bsl], dr["bias0"][hp], d_bt[bsl], (), [r_BT[bsl]])
            for wb in range(10):
                key = (hp, wb)
                if key not in pre:
                    if key not in wdma:
                        wdma[key] = ws.dma(wsrc(*key))
                    pre[key] = ws.cast(wdma.pop(key))
                wbf, r_w = pre.pop(key)
                nkey = (hp, wb + 1) if wb < 9 else ((hp + 1, 0) if hp + 1 < 8 else None)
                if nkey is not None and nkey not in wdma and nkey not in pre:
                    wdma[nkey] = ws.dma(wsrc(*nkey))

                def at_s2(nkey=nkey):
                    if nkey is not None and nkey in wdma:
                        pre[nkey] = ws.cast(wdma.pop(nkey))
                for half_ in range(1):
                    colo = 0
                    kind = ("q", "k", "q", "k", "q", "k", "v0", "v1", "v2", "z")[wb]
                    half = wb % 2
                    if kind in ("q", "k", "z"):
                        g = wb // 2 if wb < 6 else 0
                        d = GROUPS[g][1] if kind != "z" else 1
                        for s in range(4):
                            if s == 2:
                                at_s2()
                            bk = proj_bank()
                            for c in range(8):
                                B.mm(psum[bk][:, :], wbf[:, c, colo:colo + 128],
                                     hT[:, c, s * 512:(s + 1) * 512], c == 0, c == 7,
                                     [r_w] + allhT, [rps[bk]])
                            if kind == "z":
                                B.act(SZ[:, s * 512:(s + 1) * 512], psum[bk][:, :], AF.Silu,
                                      [], [rps[bk], r_SZ])
                            else:
                                idx = 2 * g + half
                                L = S // d
                                dst = QK[:, idx, :].rearrange("p (r m) -> p r m", r=d)[
                                    :, :, (512 // d) * s:(512 // d) * (s + 1)]
                                srcp = psum[bk][:, :].rearrange("p (m r) -> p r m", r=d)
                                B.act(dst, srcp, AF.Copy, [], [rps[bk], r_QK[idx]],
                                      scale=(0.125 if kind == "q" else 1.0))
                    else:
                        g = int(kind[1])
                        for s in range(4):
                            if s == 2:
                                at_s2()
                            bk = proj_bank()
                            for c in range(8):
                                B.mm(psum[bk][:, :], wbf[:, c, colo:colo + 128],
                                     hT[:, c, s * 512:(s + 1) * 512], c == 0, c == 7,
                                     [r_w] + allhT, [rps[bk]])
                            B.act(VT[:, g, s * 512:(s + 1) * 512], psum[bk][:, :], AF.Copy, [],
                                  [rps[bk], r_VT[g]])
                        for quad in range(4):
                            bk = proj_bank()
                            pb16 = psum[bk][:, :].bitcast(BF16)
                            for j in range(4):
                                bi = quad * 4 + j
                                o, st = grp_tokens(g, bi)
                                B.tr(pb16[:, j * 128:(j + 1) * 128], VT[:, g, o:o + 127 * st + 1:st], identb[:],
                                     [r_VT[g], r_cb], [rps[bk]])
                            pv = pb16[:, 0:512].rearrange("p (j c) -> p j c", j=4)
                            b0 = g * 16 + quad * 4
                            B.cp("dve", VA0[:, b0:b0 + 4, 0:64], pv[:, :, 0:64], [], [rps[bk], r_VA[g]])
                            B.cp("dve", VA1[:, b0:b0 + 4, 64:128], pv[:, :, 64:128], [], [rps[bk], r_VA[g]])
                ctx["pump"](4)
                if wb < 7:
                    while dq:
                        f = dq.pop(0)
                        if f is not None:
                            f()
                            break
                if wb == 8:
                    while dq:
                        f = dq.pop(0)
                        if f is not None:
                            f()
            its = [(hh, g, m) for hh in range(2) for g in range(3) for m in range(8)]
            st_info = {}

            def nq_of(g, kb):
                nbr = 16 // GROUPS[g][1]
                return 256 if (kb + 1) % nbr != 0 else 128

            def issue_st(i):
                hh, g, m = its[i]
                bk = (0, 1, 2, 7)[stb[0] % 4]
                stb[0] += 1
                rows = slice(hh * 64, (hh + 1) * 64)
                for t in range(2):
                    kb = 2 * m + t
                    nq = nq_of(g, kb)
                    B.mm(psum[bk][:, t * 256:t * 256 + nq], QK[rows, 2 * g + 1, kb * 128:(kb + 1) * 128],
                         QK[rows, 2 * g, kb * 128:kb * 128 + nq], True, True,
                         [r_QK[2 * g], r_QK[2 * g + 1]], [rps[bk]])
                ls = i % 2
                ps_ = i % 5
                bias2 = BT[:, bsl, g * 2 + hh, :].unsqueeze(1).to_broadcast([128, 2, 256])
                B.tt("dve", LT[:, ls], psum[bk][:, :].rearrange("p (t q) -> p t q", t=2), bias2, ALU.add,
                     [r_BT[bsl]], [rps[bk], r_LT[ls]])
                B.act(PT[:, ps_], LT[:, ls], AF.Exp, [r_LT[ls]], [r_PT[ps_]])
                st_info[i] = ps_

            def issue_pv(i):
                hh, g, m = its[i]
                d = GROUPS[g][1]
                nbr = 16 // d
                M = 65 if hh == 0 else 128
                VA = VA0 if hh == 0 else VA1
                ps_ = st_info[i]
                for t in range(2):
                    kb = 2 * m + t
                    has_prev = kb % nbr != 0
                    if has_prev:
                        pslot, pt = (ps_, 0) if t == 1 else (st_info[i - 1], 1)
                    if g == 0:
                        outs = [(3 + kb // 4, slice((kb % 4) * 128, (kb % 4) * 128 + 128), slice(0, 128))]
                    elif g == 1:
                        r, nn = kb // 4, kb % 4
                        outs = [(3 + nn, slice(r, 512, 4), slice(0, 128))]
                    else:
                        outs = [(3 + sub, slice(kb, 512, 16), slice(32 * sub, 32 * sub + 32)) for sub in range(4)]
                    for (bo, ocols, qcols) in outs:
                        outp = psum[bo][0:M, ocols]
                        first = (g == 0 and kb % 4 == 0)
                        if has_prev:
                            pq = slice(128 + qcols.start, 128 + qcols.stop)

                            def f1(e, outp=outp, a=VA[:, g * 16 + kb - 1, 0:M], b=PT[:, pslot, pt, pq], first=first):
                                return e.matmul(outp, a, b, start=first, stop=False, skip_group_check=True)
                            B.S.op("pe", f1, [r_VA[g], r_PT[pslot]], [rps[bo]])
                            first = False

                        last = (g == 2 and kb == 15)

                        def f2(e, outp=outp, a=VA[:, g * 16 + kb, 0:M], b=PT[:, ps_, t, qcols], first=first, last=last):
                            return e.matmul(outp, a, b, start=first, stop=last, skip_group_check=True)
                        B.S.op("pe", f2, [r_VA[g], r_PT[ps_]], [rps[bo]])
                if g == 2 and m == 7:
                    for span in range(4):
                        if span % 2 == 0:
                            B.cp("dve", OA[0:M, hh, span * 512:(span + 1) * 512], psum[3 + span][0:M, :],
                                 [], [rps[3 + span], r_OA[hh]])
                        else:
                            B.act(OA[0:M, hh, span * 512:(span + 1) * 512], psum[3 + span][0:M, :], AF.Copy,
                                  [], [rps[3 + span], r_OA[hh]])
                    finalize(hh)

            def finalize(hh, hp=hp):
                drow = 64 if hh == 0 else 0
                R = slice(hh * 64, (hh + 1) * 64)

                def prepA(s):
                    cs = slice(s * 512, (s + 1) * 512)
                    den = OA[drow:drow + 1, hh, cs]
                    B.act(den, den, AF.Ln, [], [r_OA[hh]])
                    B.act(den, den, AF.Exp, [], [r_OA[hh]], scale=-1.0)

                def prepB(s):
                    cs = slice(s * 512, (s + 1) * 512)
                    den = OA[drow:drow + 1, hh, cs]
                    B.cp("dve", RHL[drow:drow + 1, 0, cs], den, [r_OA[hh]], [r_RHL])
                    B.tt("dve", RHL[drow:drow + 1, 1, cs], den, RHL[drow:drow + 1, 0, cs], ALU.subtract,
                         [r_OA[hh]], [r_RHL])

                def span(s):
                    bk = (0, 1, 2, 7)[stb[0] % 4]
                    stb[0] += 1
                    cs = slice(s * 512, (s + 1) * 512)
                    B.mm(psum[bk][:, :], onesb[drow:drow + 1, :], RHL[drow:drow + 1, 0, cs], True, False,
                         [r_cb, r_RHL], [rps[bk]])
                    B.mm(psum[bk][:, :], onesb[drow:drow + 1, :], RHL[drow:drow + 1, 1, cs], False, True,
                         [r_cb, r_RHL], [rps[bk]])
                    B.tt("dve", OA[R, hh, cs], OA[R, hh, cs], psum[bk][R, :], ALU.mult,
                         [], [rps[bk], r_OA[hh]])
                    B.tt("pool", GT[R, hp, cs], OA[R, hh, cs], SZ[R, cs], ALU.mult,
                         [r_OA[hh], r_SZ], [r_GT])
                for s in range(4):
                    dq.append(lambda s=s: prepA(s))
                    dq.append(None)
                    dq.append(lambda s=s: prepB(s))
                for _ in range(4):
                    dq.append(None)
                for s in range(4):
                    dq.append(lambda s=s: span(s))
                    if s < 3:
                        dq.append(None)

            if hp == 0:
                ctx["dump"]("QK", QK[:], r_QK)
                ctx["dump"]("VA0", VA0[:], r_VA)
                ctx["dump"]("VA1", VA1[:], r_VA)
                ctx["dump"]("SZ", SZ[:], [r_SZ])
                ctx["dump"]("BT", BT[:, 0], [r_BT[0]])
            LOOK = 3
            n = len(its)
            for i in range(min(LOOK, n)):
                issue_st(i)
            for i in range(n):
                B.S.begin_group()
                issue_pv(i)
                if i + LOOK < n:
                    issue_st(i + LOOK)
                B.S.end_group()
                if dq:
                    f = dq.pop(0)
                    if f is not None:
                        f()
            if hp == 7:
                while dq:
                    f = dq.pop(0)
                    if f is not None:
                        f()
        B.barrier()
    ctx["dump"]("GT", GT[:], [r_GT])
    finals = []
    with ExitStack() as es:
        E = es.enter_context
        ws = WStream(ctx, E, 2, "wo0s")
        WO = E(nc.sbuf_tensor("WO0", [128, 8, D], BF16))
        gpo = E(nc.sbuf_tensor("gpo0", [128, D], F32))
        r_WO = [B.R("WO%d" % i) for i in range(4)]
        r_gpo = B.R("gpo0")
        B.S.dma("sp", gpo[:], dr["gpost0"], d_par, (), [r_gpo])
        for wb in range(4):
            ws.load(dr["wo0"][:, wb * 256:(wb + 1) * 256], WO[:, :, wb * 256:(wb + 1) * 256], r_WO[wb])
        finals = out_proj_tail(ctx, E, WO, r_WO, gpo, r_gpo, dr["x"], dr["x1"], GT, r_GT, None, "t0",
                               dst_res=ctx.get("r_x1"))
        B.barrier()
    return finals


def out_proj_tail(ctx, E, WO, r_WO, gpo, r_gpo, xsrc, dst, GT, r_GT, tokap, name, src_res=None, dst_res=None):
    nc, B, psum, rps = ctx["nc"], ctx["B"], ctx["psum"], ctx["rps"]
    xt = E(nc.sbuf_tensor("xt_" + name, [128, 2, D], F32))
    yt = E(nc.sbuf_tensor("yt_" + name, [128, 2, D], F32))
    sq = E(nc.sbuf_tensor("sq_" + name, [128, 512], F32))
    st = E(nc.sbuf_tensor("st_" + name, [128, 2, 4], F32))
    r_xt = [B.R("xt0"), B.R("xt1")]
    r_yt = [B.R("yt0"), B.R("yt1")]
    r_sq = B.R("sq")
    r_st = [B.R("st0"), B.R("st1")]
    d_x = [ctx["new_dsem"]("d_x%s0" % name), ctx["new_dsem"]("d_x%s1" % name)]
    d_o = [ctx["new_dsem"]("d_o%s0" % name), ctx["new_dsem"]("d_o%s1" % name)]
    finals = []
    for tt in range(NT):
        sl = tt % 2
        if tokap is None:
            xrows, orows = xsrc[tt * 128:(tt + 1) * 128, :], dst[tt * 128:(tt + 1) * 128, :]
        else:
            jj, nb = tt // 2, tt % 2
            xrows = xsrc.rearrange("(n j) d -> j n d", j=8)[jj, nb * 128:(nb + 1) * 128, :]
            orows = dst.rearrange("(n j) d -> j n d", j=8)[jj, nb * 128:(nb + 1) * 128, :]
        B.S.dma("sp", xt[:, sl, :], xrows, d_x[sl],
                [src_res[tt]] if src_res else (), [r_xt[sl]])
        banks = (0, 1) if sl == 0 else (2, 3)
        for half in range(2):
            bk = banks[half]
            for c in range(8):
                lhs = GT[:, c, tt * 128:(tt + 1) * 128]
                B.mm(psum[bk][:, :], lhs, WO[:, c, half * 512:(half + 1) * 512], c == 0, c == 7,
                     [r_GT] + r_WO[2 * half:2 * half + 2], [rps[bk]])
            B.act(sq[:, :], psum[bk][:, :], AF.Square, [], [rps[bk], r_sq, r_st[sl]],
                  accum_out=st[:, sl, half:half + 1])
        B.tt("dve", st[:, sl, 2:3], st[:, sl, 0:1], st[:, sl, 1:2], ALU.add, [r_st[sl]], [r_st[sl]])
        rms_rstd(B, st[:, sl, 2:3], st[:, sl, 3:4], r_st[sl], r_st[sl])
        for half in range(2):
            bk = banks[half]
            hs = slice(half * 512, (half + 1) * 512)
            B.stt(yt[:, sl, hs], psum[bk][:, :], st[:, sl, 3:4], gpo[:, hs], ALU.mult, ALU.mult,
                  [r_st[sl], r_gpo], [rps[bk], r_yt[sl]])
        B.tt("pool", yt[:, sl, :], yt[:, sl, :], xt[:, sl, :], ALU.add, [r_xt[sl]], [r_yt[sl]])
        o = B.S.dma("pool", orows, yt[:, sl, :], d_o[sl], [r_yt[sl]],
                    [dst_res[tt]] if dst_res else [])
        fr = B.R("fin%d" % tt)
        fr.w = o
        finals.append(fr)
    return finals


def build_l1_decl(dt, dr):
    EI = "ExternalInput"
    dr["w1"] = dt("w1", [D, 2048], F32, kind=EI).ap()
    dr["wg1"] = dt("wg1", [D, D], F32, kind=EI).ap()
    dr["wo1"] = dt("wo1", [D, D], F32, kind=EI).ap()
    dr["gpre1"] = dt("gpre1", [128, 8], F32, kind=EI).ap()
    dr["gpost1"] = dt("gpost1", [128, D], F32, kind=EI).ap()
    dr["bglu"] = dt("bglu", [128, 8], F32, kind=EI).ap()
    dr["pl_a"] = dt("pl_a", [128, 3, 32], F32, kind=EI).ap()
    dr["pl_bc"] = dt("pl_bc", [128, 4, 32, 16], F32, kind=EI).ap()
    dr["dcol"] = dt("dcol", [128, 64], F32, kind=EI).ap()
    dr["mask8"] = dt("mask8", [128, 128], F32, kind=EI).ap()
    dr["esel"] = dt("esel", [128, 64, 128], BF16, kind=EI).ap()


def cmul(B, outr, outi, ar, ai, br, bi, t1, t2, R, neg_i=False, eng="dve", emit=None, Rw=None):
    if emit is None:
        emit = lambda f: f()
    Rw = R if Rw is None else Rw
    rd = [R] if Rw is R else [R, Rw]
    emit(lambda: B.tt(eng, t1, ar, br, ALU.mult, rd, [Rw]))
    emit(lambda: B.tt(eng, t2, ai, bi, ALU.mult, rd, [Rw]))
    emit(lambda: B.tt(eng, outr, t1, t2, ALU.subtract, rd, [Rw]))
    emit(lambda: B.tt(eng, t1, ar, bi, ALU.mult, rd, [Rw]))
    emit(lambda: B.tt(eng, t2, ai, br, ALU.mult, rd, [Rw]))
    if neg_i:
        emit(lambda: B.tt(eng, t1, t1, t2, ALU.add, rd, [Rw]))
        emit(lambda: B.ts(eng, outi, t1, -1.0, ALU.mult, rd, [Rw]))
    else:
        emit(lambda: B.tt(eng, outi, t1, t2, ALU.add, rd, [Rw]))


def l1_prep_stage1(ctx, E):
    nc, B, dr = ctx["nc"], ctx["B"], ctx["dr"]
    PEN = "dve"
    pa = E(nc.sbuf_tensor("pa", [128, 3, 32], F32))
    pbc = E(nc.sbuf_tensor("pbc", [128, 4, 32, 16], F32))
    W = E(nc.sbuf_tensor("Wk", [128, 16, 32], F32))
    PW = E(nc.sbuf_tensor("PW", [128, 9, 2, 32], F32))
    QW = E(nc.sbuf_tensor("QW", [128, 8, 2, 32], F32))
    bb = E(nc.sbuf_tensor("bbar", [128, 2, 32, 16], F32))
    T1 = E(nc.sbuf_tensor("T1", [128, 32, 16], F32))
    T2 = E(nc.sbuf_tensor("T2", [128, 32, 16], F32))
    RP = B.R("prep")
    d_p = ctx["new_dsem"]("d_prep")
    B.S.dma("sp", pa[:], dr["pl_a"], d_p, (), [RP])
    B.S.dma("sp", pbc[:], dr["pl_bc"], d_p, (), [RP])
    q = []
    emit = q.append
    RK = E(nc.sbuf_tensor("RK", [128, 16], F32))
    for k in range(1, 16):
        emit(lambda k=k: B.memset(PEN, RK[:, k:k + 1], float(k), [RP]))
    emit(lambda: B.memset(PEN, RK[:, 0:1], 1.0, [RP]))
    emit(lambda: B.recip(RK[:], RK[:], [RP], [RP]))
    are, aim, ldt = pa[:, 0, :], pa[:, 1, :], pa[:, 2, :]
    w = lambda k: W[:, k, :]
    t1s, t2s = T1[:, :, 0], T2[:, :, 0]
    tt = lambda o, a, b, op: emit(lambda: B.tt(PEN, o, a, b, op, [RP], [RP]))
    ts = lambda o, a, s1, op0, s2=None, op1=None: emit(lambda: B.ts(PEN, o, a, s1, op0, [RP], [RP], s2=s2, op1=op1))
    ms = lambda o, v: emit(lambda: B.memset(PEN, o, v, [RP]))
    cpy = lambda o, a: emit(lambda: B.cp(PEN, o, a, [RP], [RP]))
    cm = lambda *a, **k: cmul(B, *a, eng=PEN, emit=emit, **k)
    ts(w(1), ldt, 0.125, ALU.mult)
    ms(w(0), 1.0)
    for k in range(10, 0, -1):
        tt(w(0), w(0), w(1), ALU.mult)
        ts(w(0), w(0), RK[:, k:k + 1], ALU.mult, 1.0, ALU.add)
    for _ in range(3):
        tt(w(0), w(0), w(0), ALU.mult)
    ts(w(8), w(0), 1.0 / 16, ALU.mult)
    tt(w(1), are, w(8), ALU.mult)
    tt(w(2), aim, w(8), ALU.mult)
    ms(w(3), 1.0)
    ms(w(4), 0.0)
    for k in range(14, 1, -1):
        cm(w(5), w(6), w(3), w(4), w(1), w(2), t1s, t2s, RP)
        ts(w(3), w(5), RK[:, k:k + 1], ALU.mult, 1.0, ALU.add)
        ts(w(4), w(6), RK[:, k:k + 1], ALU.mult)
    cm(w(5), w(6), w(3), w(4), w(1), w(2), t1s, t2s, RP)
    for _ in range(4):
        ts(w(7), w(5), 2.0, ALU.add)
        cm(w(3), w(4), w(5), w(6), w(7), w(6), t1s, t2s, RP)
        cpy(w(5), w(3))
        cpy(w(6), w(4))
    ms(PW[:, 0, 0, :], 1.0)
    ms(PW[:, 0, 1, :], 0.0)
    ms(QW[:, 0, 0, :], 1.0)
    ms(QW[:, 0, 1, :], 0.0)
    abr, abi = PW[:, 1, 0, :], PW[:, 1, 1, :]
    ts(abr, w(5), 1.0, ALU.add)
    cpy(abi, w(6))
    for k in range(2, 9):
        cm(PW[:, k, 0, :], PW[:, k, 1, :], PW[:, k - 1, 0, :], PW[:, k - 1, 1, :], abr, abi, t1s, t2s, RP)
    tt(w(8), abr, abr, ALU.mult)
    tt(w(9), abi, abi, ALU.mult)
    tt(w(8), w(8), w(9), ALU.add)
    emit(lambda: B.recip(w(9), w(8), [RP], [RP]))
    tt(QW[:, 1, 0, :], abr, w(9), ALU.mult)
    tt(w(8), abi, w(9), ALU.mult)
    ts(QW[:, 1, 1, :], w(8), -1.0, ALU.mult)
    for k in range(2, 8):
        cm(QW[:, k, 0, :], QW[:, k, 1, :], QW[:, k - 1, 0, :], QW[:, k - 1, 1, :],
           QW[:, 1, 0, :], QW[:, 1, 1, :], t1s, t2s, RP)
    cpy(w(10), w(5))
    tt(w(11), are, are, ALU.mult)
    tt(w(12), aim, aim, ALU.mult)
    tt(w(11), w(11), w(12), ALU.add)
    emit(lambda: B.recip(w(12), w(11), [RP], [RP]))
    tt(w(13), w(10), are, ALU.mult)
    tt(w(14), abi, aim, ALU.mult)
    tt(w(13), w(13), w(14), ALU.add)
    tt(w(13), w(13), w(12), ALU.mult)
    tt(w(14), abi, are, ALU.mult)
    tt(w(15), w(10), aim, ALU.mult)
    tt(w(14), w(14), w(15), ALU.subtract)
    tt(w(14), w(14), w(12), ALU.mult)
    bc = lambda ap: ap.unsqueeze(2).to_broadcast([128, 32, 16])
    cm(bb[:, 0], bb[:, 1], bc(w(13)), bc(w(14)), pbc[:, 0], pbc[:, 1], T1[:], T2[:], RP)

    def pump(n=None):
        k = len(q) if n is None else min(n, len(q))
        for _ in range(k):
            q.pop(0)()
    ctx["pump"] = pump
    ctx["prep1"] = dict(pbc=pbc, PW=PW, QW=QW, bb=bb, T1=T1, T2=T2, RP=RP, bc=bc)


def build_l1(ctx):
    from contextlib import ExitStack
    nc, B, dr, psum, rps = ctx["nc"], ctx["B"], ctx["dr"], ctx["psum"], ctx["rps"]
    hT, GT, ident, small = ctx["hT"], ctx["GT"], ctx["ident"], ctx["small"]
    r_ident = ctx["r_ident"]
    PI = math.pi
    d_par = ctx["new_dsem"]("d_par1")
    r_gp = B.R("gpre1")
    r_bg = B.R("bglu")
    B.S.dma("sp", small[:, 8:16], dr["gpre1"], d_par, (), [r_gp])
    B.S.dma("sp", small[:, 16:24], dr["bglu"], d_par, (), [r_bg])
    r_hTall = B.R("hTall")
    r_Gb = B.R("Gb")
    Gb = GT[:, :, :].rearrange("p c (a n) -> p (c a) n", n=256)
    with ExitStack() as es1:
        E1 = es1.enter_context
        T0b = E1(nc.sbuf_tensor("T0b", [128, 64, 128], BF16))
        BmT = E1(nc.sbuf_tensor("BmT", [128, 64, 2, 64], BF16))
        Cmb = E1(nc.sbuf_tensor("Cmb", [128, 32, 2, 128], BF16))
        A8 = E1(nc.sbuf_tensor("A8", [128, 2, 2, 32], F32))
        r_T0b, r_BmT, r_Cmb, r_A8 = B.R("T0b"), B.R("BmT"), B.R("Cmb"), B.R("A8")
        with ExitStack() as es:
            E = es.enter_context
            ctx["pump"]()
            p1 = ctx["prep1"]
            pbc, PW, QW, bb, T1, T2, RP, bc = (p1[k] for k in ("pbc", "PW", "QW", "bb", "T1", "T2", "RP", "bc"))
            cre, cim = pbc[:, 2], pbc[:, 3]
            dcol = E(nc.sbuf_tensor("dcol_sb", [128, 64], F32))
            mask8 = E(nc.sbuf_tensor("mask8_sb", [128, 128], F32))
            Lh = E(nc.sbuf_tensor("Lh", [128, 2, 32, 128], BF16))
            Rh = E(nc.sbuf_tensor("Rh", [128, 2, 32, 128], BF16))
            identb1 = E(nc.sbuf_tensor("identb1", [128, 128], BF16))
            r_ib1 = B.R("identb1")
            B.cp("dve", identb1[:], ident[:], [r_ident], [r_ib1])
            TM = E(nc.sbuf_tensor("TM", [128, 2, 128], F32))
            r_dm, r_TM = B.R("dcolmask"), [B.R("TM0"), B.R("TM1")]
            B.S.dma("sp", dcol[:], dr["dcol"], d_par, (), [r_dm])
            B.S.dma("sp", mask8[:], dr["mask8"], d_par, (), [r_dm])
            T3 = E(nc.sbuf_tensor("T3", [128, 32, 16], F32))
            T4 = E(nc.sbuf_tensor("T4", [128, 32, 16], F32))
            R_L, R_R = B.R("prepL"), B.R("prepR")
            for j in range(8):
                js = slice(j * 16, (j + 1) * 16)
                cmul(B, Rh[:, 0, :, js], Rh[:, 1, :, js], bc(QW[:, 7 - j, 0, :]), bc(QW[:, 7 - j, 1, :]),
                     cre, cim, T3[:], T4[:], RP, neg_i=True, eng="dve", Rw=R_R)
            for j in range(8):
                js = slice(j * 16, (j + 1) * 16)
                cmul(B, Lh[:, 0, :, js], Lh[:, 1, :, js], bc(PW[:, 7 - j, 0, :]), bc(PW[:, 7 - j, 1, :]),
                     bb[:, 0], bb[:, 1], T1[:], T2[:], RP, Rw=R_L)
                cmul(B, Cmb[:, :, 0, js], Cmb[:, :, 1, js], bc(PW[:, j + 1, 0, :]), bc(PW[:, j + 1, 1, :]),
                     cre, cim, T1[:], T2[:], RP, neg_i=True, Rw=R_L)
            B.cp("dve", Cmb[:, 0, 0, 0:1], Cmb[:, 0, 0, 0:1], [R_L], [r_Cmb])
            B.cp("dve", A8[:, 0, 0, :], PW[:, 8, 0, :], [RP], [r_A8])
            B.cp("dve", A8[:, 0, 1, :], PW[:, 8, 0, :], [RP], [r_A8])
            B.ts("dve", A8[:, 1, 0, :], PW[:, 8, 1, :], -1.0, ALU.mult, [RP], [r_A8])
            B.cp("dve", A8[:, 1, 1, :], PW[:, 8, 1, :], [RP], [r_A8])
            r_T0g = [B.R("T0g%d" % i) for i in range(4)]
            for g in range(64):
                gh, pr = g // 32, g % 32
                Rr = slice(gh * 64, (gh + 1) * 64)
                bk = g % 2
                B.mm(psum[bk][:, 0:128], Lh[Rr, 0, pr, :], Rh[Rr, 0, pr, :], True, False, [R_L, R_R], [rps[bk]])
                B.mm(psum[bk][:, 0:128], Lh[Rr, 1, pr, :], Rh[Rr, 1, pr, :], False, True, [R_L, R_R], [rps[bk]])
                tm = g % 2
                B.tt("dve", TM[:, tm, :], psum[bk][:, 0:128], mask8[:], ALU.mult, [r_dm], [rps[bk], r_TM[tm]])
                B.stt(T0b[:, g, :], ident[:], dcol[:, g:g + 1], TM[:, tm, :], ALU.mult, ALU.add,
                      [r_dm, r_TM[tm], r_ident], [r_T0g[g % 4]])
            for g4 in range(16):
                bk = 2 + g4 % 2
                for gg in range(4):
                    g = g4 * 4 + gg
                    gh, pr = g // 32, g % 32
                    Rr = slice(gh * 64, (gh + 1) * 64)
                    for part in range(2):
                        cs = (gg * 2 + part) * 64
                        B.tr(psum[bk][:, :].bitcast(BF16)[:, cs:cs + 64], Lh[Rr, part, pr, :], identb1[Rr, Rr],
                             [R_L, r_ib1], [rps[bk]])
                B.cp("dve", BmT[:, g4 * 4:g4 * 4 + 4].rearrange("p g a c -> p (g a c)"),
                     psum[bk][:, :].bitcast(BF16)[:, 0:512], [], [rps[bk], r_BmT])
            ctx["dump"]("T0b", T0b[:], [r_T0b])
            ctx["dump"]("BmT", BmT[:], [r_BmT])
            ctx["dump"]("Cmb", Cmb[:], [r_Cmb])
            ctx["dump"]("A8", A8[:], [r_A8])
            ctx["dump"]("PW", PW[:], [RP])
            ctx["dump"]("QW", QW[:], [RP])
            ctx["dump"]("Lh", Lh[:], [R_L])
            ctx["dump"]("Rh", Rh[:], [R_R])
            B.barrier()
        if ctx.get("stop") == "prep":
            return []
        V = E1(nc.sbuf_tensor("Vssm", [128, 64, 256], BF16))
        r_V = B.R("V")
        with ExitStack() as es:
            ctx2 = dict(ctx)
            ctx2["r_hT"] = [r_hTall]
            make_hT(ctx2, dr["x1"], es.enter_context, small[:, 8:16], r_gp, perm8=True,
                    src_res=ctx.get("r_x1"), name="b")
            B.barrier()
        ctx["dump"]("hT1", hT[:], [r_hTall])
        if ctx.get("stop") == "A":
            return []
        with ExitStack() as es:
            E = es.enter_context
            ws = WStream(ctx, E, 2, "w1s", cols=128)
            uTb = E(nc.sbuf_tensor("uTb", [128, 2, S], BF16))
            Es = E(nc.sbuf_tensor("Esel", [128, 64, 128], BF16))
            r_uT = [B.R("uT0"), B.R("uT1")]
            r_Es = B.R("Esel")
            d_es = ctx["new_dsem"]("d_es")
            B.S.dma("sp", Es[:], dr["esel"], d_es, (), [r_Es])
            pbk = 0
            for fc in range(8):
                if fc == 0:
                    nxt_b = ws.dma(dr["w1"][:, 0:128])
                wbf, r_w = ws.cast(nxt_b)
                if fc + 1 < 8:
                    nxt_b = ws.dma(dr["w1"][:, (fc + 1) * 128:(fc + 1) * 128 + 128])
                colo = 0
                us = fc % 2
                for s in range(4):
                    bk = pbk
                    pbk ^= 1
                    for c in range(8):
                        B.mm(psum[bk][:, :], wbf[:, c, colo:colo + 128], hT[:, c, s * 512:(s + 1) * 512],
                             c == 0, c == 7, [r_w, r_hTall], [rps[bk]])
                    B.act(uTb[:, us, s * 512:(s + 1) * 512], psum[bk][:, :], AF.Copy, [],
                          [rps[bk], r_uT[us]])
                for g8 in range(8):
                    g = fc * 8 + g8
                    bk = 2 + g % 4
                    for j in range(8):
                        B.mm(psum[bk][:, 0:256], Es[:, g8 * 8 + j, :], uTb[:, us, j * 256:(j + 1) * 256],
                             j == 0, j == 7, [r_Es, r_uT[us]], [rps[bk]])
                    B.cp("dve", V[:, g, :], psum[bk][:, 0:256], [], [rps[bk], r_V])
            ctx["dump"]("V", V[:], [r_V])
            B.barrier()
        if ctx.get("stop") == "B":
            return []
        with ExitStack() as es:
            E = es.enter_context
            Zb = E(nc.sbuf_tensor("Zb", [128, 65, 2, 32], F32))
            Sb = E(nc.sbuf_tensor("Sb", [128, 64, 2, 32], BF16))
            t1 = E(nc.sbuf_tensor("zt1", [128, 2, 32], F32))
            t2 = E(nc.sbuf_tensor("zt2", [128, 2, 32], F32))
            ga = E(nc.sbuf_tensor("ga", [128, 2, 512], F32))
            gb = E(nc.sbuf_tensor("gb", [128, 2, 512], F32))
            r_Z, r_Sb = B.R("Zb"), B.R("Sb")
            r_ga = [B.R("ga0"), B.R("ga1")]
            B.memset("dve", Zb[:, 0], 0.0, [r_Z])
            KG = 2.0 * math.sqrt(2.0 / math.pi)
            for tb in range(4):
                ns = slice(tb * 64, (tb + 1) * 64)
                for q in range(8):
                    bk = q % 2
                    for pp in range(4):
                        pr = q * 4 + pp
                        for gh in range(2):
                            g = gh * 32 + pr
                            for part in range(2):
                                cs = (pp * 2 + part) * 64
                                B.mm(psum[bk][gh * 64:(gh + 1) * 64, cs:cs + 64], BmT[:, g, part, :],
                                     V[:, g, ns], True, True, [r_BmT, r_V], [rps[bk]])
                    dst = Zb[:, 1:65, :, q * 4:q * 4 + 4].rearrange("p n a r -> p r a n")
                    srcp = psum[bk][:, :].rearrange("p (r a n) -> p r a n", r=4, a=2)
                    B.cp("dve", dst, srcp, [], [rps[bk], r_Z])
                for s in range(64):
                    zs = Zb[:, s]
                    zn = Zb[:, s + 1]
                    B.tt("dve", t1[:], A8[:, 0], zs, ALU.mult, [r_A8, r_Z], [r_Z])
                    B.tt("dve", t2[:], A8[:, 1], Zb[:, s, ::-1, :], ALU.mult, [r_A8, r_Z], [r_Z])
                    B.tt("dve", zn, zn, t1[:], ALU.add, [r_Z], [r_Z])
                    B.tt("dve", zn, zn, t2[:], ALU.add, [r_Z], [r_Z])
                B.cp("dve", Sb[:], Zb[:, 0:64], [r_Z], [r_Sb])
                B.cp("dve", Zb[:, 0], Zb[:, 64], [r_Z], [r_Z])
                for fc in range(8):
                    bk = 2 + fc % 4
                    for g8 in range(8):
                        g = fc * 8 + g8
                        gh, pr = g // 32, g % 32
                        Rr = slice(gh * 64, (gh + 1) * 64)
                        outp = psum[bk][:, g8 * 64:(g8 + 1) * 64]
                        B.mm(outp, T0b[:, g, :], V[:, g, ns], True, False, [r_T0b, r_V], [rps[bk]])
                        B.mm(outp, Cmb[Rr, pr, 0, :], Sb[Rr, :, 0, pr], False, False, [r_Cmb, r_Sb], [rps[bk]])
                        B.mm(outp, Cmb[Rr, pr, 1, :], Sb[Rr, :, 1, pr], False, True, [r_Cmb, r_Sb], [rps[bk]])
                    sl = fc % 2
                    yp = psum[bk][:, :]
                    B.act(ga[:, sl, :], yp, AF.Square, [], [rps[bk], r_ga[sl]])
                    B.ts("dve", ga[:, sl, :], ga[:, sl, :], 0.044715, ALU.mult, [r_ga[sl]], [r_ga[sl]],
                         s2=1.0, op1=ALU.add)
                    B.tt("dve", ga[:, sl, :], ga[:, sl, :], yp, ALU.mult, [], [rps[bk], r_ga[sl]])
                    B.act(gb[:, sl, :], ga[:, sl, :], AF.Sigmoid, [r_ga[sl]], [r_ga[sl]], scale=KG)
                    dst = Gb[:, fc * 8:(fc + 1) * 8, ns]
                    B.tt("dve", dst, gb[:, sl, :].rearrange("p (g n) -> p g n", g=8),
                         yp.rearrange("p (g n) -> p g n", g=8), ALU.mult, [r_ga[sl]], [rps[bk], r_Gb])
            ctx["dump"]("Gb", GT[:], [r_Gb])
            B.barrier()
        B.barrier()
    if ctx.get("stop") == "C":
        return []
    finals = []
    with ExitStack() as es2:
        E2 = es2.enter_context
        gTb = E2(nc.sbuf_tensor("gTb", [128, 8, S], BF16))
        r_gT = B.R("gTb")
        r_G2 = B.R("G2T")
        with ExitStack() as es:
            E = es.enter_context
            Es = E(nc.sbuf_tensor("Esel2", [128, 64, 128], BF16))
            r_Es = B.R("Esel2")
            d_es = ctx["new_dsem"]("d_es2")
            B.S.dma("sp", Es[:], dr["esel"], d_es, (), [r_Es])
            for fc in range(8):
                for ip in range(4):
                    bk = (fc * 4 + ip) % 4
                    for ii in range(2):
                        i0 = ip * 2 + ii
                        for g8 in range(8):
                            B.mm(psum[bk][:, ii * 256:(ii + 1) * 256], Es[:, i0 * 8 + g8, :],
                                 Gb[:, fc * 8 + g8, :], g8 == 0, g8 == 7, [r_Es, r_Gb], [rps[bk]])
                    B.cp("dve" if ip % 2 == 0 else "act_copy", gTb[:, fc, ip * 512:(ip + 1) * 512],
                         psum[bk][:, :], [], [rps[bk], r_gT]) if False else \
                        B.cp("dve", gTb[:, fc, ip * 512:(ip + 1) * 512], psum[bk][:, :], [], [rps[bk], r_gT])
            ctx["dump"]("gTb", gTb[:], [r_gT])
            B.barrier()
        if ctx.get("stop") == "D1":
            return []
        with ExitStack() as es:
            E = es.enter_context
            wsg = WStream(ctx, E, 2, "wg1s")
            wsz = WStream(ctx, E, 2, "wz1s")
            sg = E(nc.sbuf_tensor("sg", [128, 2, 512], F32))
            sz = E(nc.sbuf_tensor("sz", [128, 2, 512], F32))
            r_sg = [B.R("sg0"), B.R("sg1")]
            r_sz = [B.R("sz0"), B.R("sz1")]
            k = 0
            for fo in range(8):
                if fo == 0:
                    nxt_g = wsg.dma(dr["wg1"][:, 0:256])
                    nxt_z = wsz.dma(dr["w1"][:, 1024:1024 + 256])
                if fo % 2 == 0:
                    wg, r_wg = wsg.cast(nxt_g)
                    wz, r_wz = wsz.cast(nxt_z)
                    if fo + 2 < 8:
                        nxt_g = wsg.dma(dr["wg1"][:, (fo + 2) * 128:(fo + 2) * 128 + 256])
                        nxt_z = wsz.dma(dr["w1"][:, 1024 + (fo + 2) * 128:1024 + (fo + 2) * 128 + 256])
                colo = (fo % 2) * 128
                for s in range(4):
                    sl = k % 2
                    k += 1
                    ba, bb_ = (0, 1) if sl == 0 else (2, 3)
                    cs = slice(s * 512, (s + 1) * 512)
                    for c in range(8):
                        B.mm(psum[ba][:, :], wg[:, c, colo:colo + 128], gTb[:, c, cs], c == 0, c == 7,
                             [r_wg, r_gT], [rps[ba]])
                    for c in range(8):
                        B.mm(psum[bb_][:, :], wz[:, c, colo:colo + 128], hT[:, c, cs], c == 0, c == 7,
                             [r_wz, r_hTall], [rps[bb_]])
                    B.act(sg[:, sl, :], psum[ba][:, :], AF.Sigmoid, [r_bg], [rps[ba], r_sg[sl]],
                          bias=small[:, 16 + fo:17 + fo])
                    B.act(sz[:, sl, :], psum[bb_][:, :], AF.Sigmoid, [], [rps[bb_], r_sz[sl]])
                    B.tt("pool", sg[:, sl, :], sg[:, sl, :], sz[:, sl, :], ALU.mult, [r_sz[sl]], [r_sg[sl]])
                    B.tt("dve", sg[:, sl, :], sg[:, sl, :], psum[bb_][:, :], ALU.mult, [], [rps[bb_], r_sg[sl]])
                    dstn = GT[:, fo, :].rearrange("p (n j) -> p j n", j=8)[:, 2 * s:2 * s + 2, :]
                    B.tt("dve", dstn, sg[:, sl, :].rearrange("p (j n) -> p j n", j=2),
                         gTb[:, fo, cs].rearrange("p (j n) -> p j n", j=2), ALU.mult,
                         [r_sg[sl], r_gT], [r_G2])
            ctx["dump"]("G2T", GT[:], [r_G2])
            B.barrier()
        if ctx.get("stop") == "D2":
            return []
        with ExitStack() as es:
            E = es.enter_context
            ws = WStream(ctx, E, 2, "wo1s")
            WO = E(nc.sbuf_tensor("WO1", [128, 8, D], BF16))
            gpo = E(nc.sbuf_tensor("gpo1", [128, D], F32))
            r_WO = [B.R("WO1_%d" % i) for i in range(4)]
            r_gpo = B.R("gpo1")
            B.S.dma("sp", gpo[:], dr["gpost1"], d_par, (), [r_gpo])
            for wb in range(4):
                ws.load(dr["wo1"][:, wb * 256:(wb + 1) * 256], WO[:, :, wb * 256:(wb + 1) * 256], r_WO[wb])

            tokap = None
            finals = out_proj_tail(ctx, E, WO, r_WO, gpo, r_gpo, dr["x1"], dr["out"], GT, r_G2, tokap, "t1",
                                   src_res=ctx.get("r_x1"))
            B.barrier()
    return finals


_PROG = {}


def _prog(mode):
    if mode not in _PROG:
        _PROG[mode] = build_program(mode)
    return _PROG[mode]


def _l0_inputs(inp, b):
    f = np.float32
    return {
        "x": np.ascontiguousarray(inp["x"][b], dtype=f),
        "w0": inp["_w0"], "wo0": inp["_wo0"], "gpre0": inp["_gpre0"], "gpost0": inp["_gpost0"],
        "bias0": inp["_bias0"], "ident": inp["_ident"],
    }


def _prep_l0(inputs):
    f = np.float32
    p = {}
    p["_w0"] = _perm_w_in(np.asarray(inputs["attn_w_in"][0], f))
    p["_wo0"] = np.ascontiguousarray(np.asarray(inputs["attn_w_out"][0], f))
    p["_gpre0"] = np.ascontiguousarray(np.asarray(inputs["attn_pre_norm"][0], f).reshape(8, 128).T)
    p["_gpost0"] = np.ascontiguousarray(np.broadcast_to(np.asarray(inputs["attn_post_norm"][0], f), (128, D)))
    p["_bias0"] = _bias_tiles(np.asarray(inputs["rel_bias"], f))
    p["_ident"] = np.eye(128, dtype=f)
    return p


def run_l0(inputs, debug=False):
    inp = dict(inputs)
    inp.update(_prep_l0(inputs))
    inp["x"] = np.asarray(inputs["x"], np.float32)
    nc = build_program("l0", debug=debug)
    in_maps = [_l0_inputs(inp, b) for b in range(8)]
    res = run_bass_kernel_spmd(nc, in_maps, core_ids=list(range(8)))
    if debug:
        return np.stack([r["x1"] for r in res.results], axis=0), res.results[0]
    return np.stack([r["x1"] for r in res.results], axis=0)


def _prep_l1(inputs):
    f = np.float32
    p = {}
    p["w1"] = np.ascontiguousarray(np.asarray(inputs["ssm_w_in"][0], f))
    p["wg1"] = np.ascontiguousarray(np.asarray(inputs["ssm_w_glu"][0], f))
    p["wo1"] = np.ascontiguousarray(np.asarray(inputs["ssm_w_out"][0], f))
    p["gpre1"] = np.ascontiguousarray(np.asarray(inputs["ssm_pre_norm"][0], f).reshape(8, 128).T)
    p["gpost1"] = np.ascontiguousarray(np.broadcast_to(np.asarray(inputs["ssm_post_norm"][0], f), (128, D)))
    p["bglu"] = np.ascontiguousarray(np.asarray(inputs["ssm_b_glu"][0], f).reshape(8, 128).T)

    def pl(a):
        a = np.asarray(a, f)
        a = a.reshape((2, 32, 64) + a.shape[2:])
        a = np.moveaxis(a, 2, 1)
        return np.ascontiguousarray(a.reshape((128, 32) + a.shape[3:]))
    are = pl(inputs["ssm_a_re"][0])
    aim = pl(inputs["ssm_a_im"][0])
    ldt = pl(np.broadcast_to(np.asarray(inputs["ssm_log_dt"][0], f)[:, None], (64, 64)))
    p["pl_a"] = np.ascontiguousarray(np.stack([are, aim, ldt], axis=1))
    bre = pl(inputs["ssm_b_re"][0])
    bim = pl(inputs["ssm_b_im"][0])
    cre = pl(np.swapaxes(np.asarray(inputs["ssm_c_re"][0], f), 1, 2))
    cim = pl(np.swapaxes(np.asarray(inputs["ssm_c_im"][0], f), 1, 2))
    p["pl_bc"] = np.ascontiguousarray(np.stack([bre, bim, cre, cim], axis=1))
    dvec = np.asarray(inputs["ssm_d"][0], f).reshape(64, 16)
    p["dcol"] = np.ascontiguousarray(np.tile(dvec.T, (8, 1)))
    jj = np.arange(128) // 16
    p["mask8"] = (jj[:, None] <= jj[None, :]).astype(f)
    k = np.arange(128)
    es = np.zeros((128, 64, 128), f)
    for a in range(8):
        for b in range(8):
            es[:, a * 8 + b, :] = ((k[:, None] // 16 == a) & (k[None, :] // 16 == b)
                                   & (k[:, None] % 16 == k[None, :] % 16))
    p["esel"] = es.astype(ml_dtypes.bfloat16)
    p["ident"] = np.eye(128, dtype=f)
    return p


def run_l1(inputs, x1, debug=False, stop=None):
    p = _prep_l1(inputs)
    nc = build_program("l1", debug=debug, stop=stop)
    in_maps = []
    for b in range(8):
        m = dict(p)
        m["x1"] = np.ascontiguousarray(x1[b], dtype=np.float32)
        in_maps.append(m)
    res = run_bass_kernel_spmd(nc, in_maps, core_ids=list(range(8)))
    out = np.stack([r["out"] for r in res.results], axis=0)
    if debug:
        return out, res.results[0]
    return out


FUSED = True


def kernel(**inputs):
    p0 = _prep_l0(inputs)
    p1 = _prep_l1(inputs)
    x = np.asarray(inputs["x"], np.float32)
    shared0 = {"w0": p0["_w0"], "wo0": p0["_wo0"], "gpre0": p0["_gpre0"], "gpost0": p0["_gpost0"],
               "bias0": p0["_bias0"], "ident": p0["_ident"]}
    if FUSED:
        nc = _prog("fused")
        shared = dict(shared0)
        shared.update(p1)
        in_maps = []
        for b in range(8):
            m = dict(shared)
            m["x"] = np.ascontiguousarray(x[b])
            in_maps.append(m)
        res = run_bass_kernel_spmd(nc, in_maps, core_ids=list(range(8)))
        return np.stack([np.asarray(r["out"], np.float32) for r in res.results], axis=0)
    in_maps = []
    for b in range(8):
        m = dict(shared0)
        m["x"] = np.ascontiguousarray(x[b])
        in_maps.append(m)
    res = run_bass_kernel_spmd(_prog("l0"), in_maps, core_ids=list(range(8)))
    x1 = [np.asarray(r["x1"], np.float32) for r in res.results]
    in_maps = []
    for b in range(8):
        m = dict(p1)
        m["x1"] = np.ascontiguousarray(x1[b])
        in_maps.append(m)
    res = run_bass_kernel_spmd(_prog("l1"), in_maps, core_ids=list(range(8)))
    return np.stack([np.asarray(r["out"], np.float32) for r in res.results], axis=0)
```

```python
import math
import numpy as np
import ml_dtypes
import concourse.bass as bass
import concourse.mybir as mybir
from concourse.bass_utils import run_bass_kernel_spmd

F32 = mybir.dt.float32
BF16 = mybir.dt.bfloat16
AF = mybir.ActivationFunctionType
ALU = mybir.AluOpType

D = 1024
S = 2048
NT = 16
EPS = 1e-6
GROUPS = ((128, 1), (512, 4), (2048, 16))
NEGM = -30000.0
SSM_G = 64
SSM_P = 64
SSM_C = 16


class Res:
    __slots__ = ("name", "w", "rs", "rd")

    def __init__(self, name):
        self.name = name
        self.w = None
        self.rs = {}
        self.rd = []


class DmaSem:
    def __init__(self, sem):
        self.sem = sem
        self.count = 0


class Op:
    __slots__ = ("eng", "fn", "deps", "need", "seq", "dsem", "dval", "idx", "dreq", "group", "gidx")


class Sched:
    ENGS = ("pe", "act", "dve", "pool", "sp")

    def __init__(self, nc):
        self.nc = nc
        self.ops = {e: [] for e in self.ENGS}
        self.allops = []
        self.cur_group = None
        self.ngroups = 0

    def begin_group(self):
        self.ngroups += 1
        self.cur_group = self.ngroups

    def end_group(self):
        self.cur_group = None

    def _mk(self, eng, fn, reads, writes, isdma=False):
        o = Op()
        o.eng = eng
        o.fn = fn
        o.need = False
        o.seq = 0
        o.dsem = None
        o.dval = 0
        deps = []
        for r in reads:
            if r.w is not None:
                deps.append(r.w)
        for r in writes:
            if r.w is not None:
                deps.append(r.w)
            deps.extend(r.rs.values())
            deps.extend(r.rd)
        o.deps = deps
        o.dreq = {}
        for p in deps:
            if p.dsem is not None:
                o.dreq[id(p.dsem)] = (p.dsem.sem, p.dsem.count)
        for r in reads:
            if isdma:
                r.rd.append(o)
            else:
                r.rs[eng] = o
        for r in writes:
            r.w = o
            r.rs = {}
            r.rd = []
        o.idx = len(self.ops[eng])
        o.group = self.cur_group
        o.gidx = len(self.allops)
        self.ops[eng].append(o)
        self.allops.append(o)
        return o

    def op(self, eng, fn, reads=(), writes=()):
        return self._mk(eng, fn, list(reads), list(writes))

    def dma(self, eng, out, in_, dsem, reads=(), writes=()):
        def fn(e):
            return e.dma_start(out=out, in_=in_)
        o = self._mk(eng, fn, list(reads), list(writes), isdma=True)
        dsem.count += 16
        o.dsem = dsem
        o.dval = dsem.count
        return o


def _t5_bucket_np(dist):
    max_exact = 16
    n = np.maximum(dist, 1).astype(np.float32)
    large = max_exact + (np.log(n / np.float32(max_exact)) / np.float32(math.log(2048 / max_exact))
                         * np.float32(32 - max_exact)).astype(np.int32)
    large = np.minimum(large, 31)
    return np.where(dist < max_exact, dist, large)


def _bias_tiles(rel_bias):
    j = np.arange(128)[:, None]
    q = np.arange(256)[None, :]
    delta = q - j
    valid = (delta >= 0) & (delta <= 128)
    out = np.empty((8, 128, 6, 256), np.float32)
    for g, (_, dil) in enumerate(GROUPS):
        bucket = _t5_bucket_np(np.maximum(delta, 0) * dil)
        for h in range(16):
            t = rel_bias[bucket, h]
            t = np.where(valid, t, np.float32(NEGM)).astype(np.float32)
            out[h // 2, :, g * 2 + (h % 2), :] = t
    return out


def _perm_w_in(w):
    out = np.empty((8, D, 1280), np.float32)
    for hp in range(8):
        cs = slice(hp * 128, (hp + 1) * 128)
        blk = []
        for g in range(3):
            blk.append(w[:, (0 * 3 + g) * 1024:(0 * 3 + g + 1) * 1024][:, cs])
            blk.append(w[:, (1 * 3 + g) * 1024:(1 * 3 + g + 1) * 1024][:, cs])
        for g in range(3):
            blk.append(w[:, (2 * 3 + g) * 1024:(2 * 3 + g + 1) * 1024][:, cs])
        blk.append(w[:, 9216:10240][:, cs])
        out[hp] = np.concatenate(blk, axis=1)
    return out


class Builder:
    def __init__(self, nc):
        self.nc = nc
        self.S = Sched(nc)
        self.dsems = []
        self.allres = []
        self.last_barrier = None
        self.scratch = None

    def R(self, name):
        r = Res(name)
        r.w = self.last_barrier
        self.allres.append(r)
        return r

    def barrier(self):
        scr = self.scratch

        def fn(e):
            return e.memset(scr, 0.0)
        o = self.S.op("pool", fn, (), list(self.allres))
        self.last_barrier = o
        return o

    def mm(self, out, lhsT, rhs, start, stop, reads, writes):
        def fn(e):
            return e.matmul(out, lhsT, rhs, start=start, stop=stop)
        return self.S.op("pe", fn, reads, writes)

    def tr(self, out, in_, ident, reads, writes):
        def fn(e):
            return e.transpose(out, in_, ident)
        return self.S.op("pe", fn, reads, writes)

    def act(self, out, in_, func, reads, writes, scale=None, bias=None, accum_out=None):
        kw = {}
        if scale is not None:
            kw["scale"] = scale
        if bias is not None:
            kw["bias"] = bias
        if accum_out is not None:
            kw["accum_out"] = accum_out

        def fn(e):
            return e.activation(out, in_, func, **kw)
        return self.S.op("act", fn, reads, writes)

    def tt(self, eng, out, in0, in1, op, reads, writes):
        def fn(e):
            return e.tensor_tensor(out, in0, in1, op)
        return self.S.op(eng, fn, reads, writes)

    def ts(self, eng, out, in0, s1, op0, reads, writes, s2=None, op1=None):
        def fn(e):
            if op1 is None:
                return e.tensor_scalar(out, in0, s1, None, op0)
            return e.tensor_scalar(out, in0, s1, s2, op0, op1)
        return self.S.op(eng, fn, reads, writes)

    def stt(self, out, in0, scalar, in1, op0, op1, reads, writes):
        def fn(e):
            return e.scalar_tensor_tensor(out, in0, scalar, in1, op0, op1)
        return self.S.op("dve", fn, reads, writes)

    def cp(self, eng, out, in_, reads, writes):
        def fn(e):
            return e.tensor_copy(out, in_)
        return self.S.op(eng, fn, reads, writes)

    def memset(self, eng, ap, val, writes):
        def fn(e):
            return e.memset(ap, val)
        return self.S.op(eng, fn, (), writes)

    def recip(self, out, in_, reads, writes):
        def fn(e):
            return e.reciprocal(out, in_)
        return self.S.op("dve", fn, reads, writes)


def rms_rstd(B, ss, rs, r_ss, r_rs):
    B.ts("dve", rs, ss, 1.0 / D, ALU.mult, [r_ss], [r_rs], s2=EPS, op1=ALU.add)
    B.act(rs, rs, AF.Sqrt, [r_rs], [r_rs])
    B.recip(rs, rs, [r_rs], [r_rs])


def build_program(mode="fused", debug=False, stop=None):
    nc = bass.Bass("TRN2", target_bir_lowering=False)
    B = Builder(nc)
    dt = nc.dram_tensor
    dr = {}
    do_l0 = mode in ("l0", "fused")
    do_l1 = mode in ("l1", "fused")
    if do_l0:
        dr["x"] = dt("x", [S, D], F32, kind="ExternalInput").ap()
        dr["w0"] = dt("w0", [8, D, 1280], F32, kind="ExternalInput").ap()
        dr["wo0"] = dt("wo0", [D, D], F32, kind="ExternalInput").ap()
        dr["gpre0"] = dt("gpre0", [128, 8], F32, kind="ExternalInput").ap()
        dr["gpost0"] = dt("gpost0", [128, D], F32, kind="ExternalInput").ap()
        dr["bias0"] = dt("bias0", [8, 128, 6, 256], F32, kind="ExternalInput").ap()
    dr["ident"] = dt("ident", [128, 128], F32, kind="ExternalInput").ap()
    if mode == "l0":
        dr["x1"] = dt("x1", [S, D], F32, kind="ExternalOutput").ap()
    elif mode == "l1":
        dr["x1"] = dt("x1", [S, D], F32, kind="ExternalInput").ap()
    else:
        dr["x1"] = dt("x1", [S, D], F32, kind="Internal").ap()
    if do_l1:
        build_l1_decl(dt, dr)
        dr["out"] = dt("out", [S, D], F32, kind="ExternalOutput").ap()

    from contextlib import ExitStack
    with ExitStack() as es:
        E = es.enter_context
        sems = {e: E(nc.semaphore("c_" + e)) for e in ("pe", "act", "dve", "pool", "sp")}

        def new_dsem(name):
            return DmaSem(E(nc.semaphore(name)))

        psum = [E(nc.psum_tensor("ps%d" % i, [128, 512], F32)) for i in range(8)]
        rps = [B.R("psum%d" % i) for i in range(8)]
        hT = E(nc.sbuf_tensor("hT", [128, 8, S], BF16))
        GT = E(nc.sbuf_tensor("GT", [128, 8, S], BF16))
        ident = E(nc.sbuf_tensor("ident_sb", [128, 128], F32))
        ones = None
        small = E(nc.sbuf_tensor("small", [128, 64], F32))
        bscr = E(nc.sbuf_tensor("bscr", [128, 8], F32))
        B.scratch = bscr[:, 0:1]
        r_hT = [B.R("hT%d" % i) for i in range(NT)]
        r_GT = B.R("GT")
        r_ident = B.R("ident")
        r_ones = B.R("ones")
        d_const = new_dsem("d_const")
        B.S.dma("sp", ident[:], dr["ident"], d_const, (), [r_ident])
        dumps = {}
        d_dump = new_dsem("d_dump")

        def dump(name, ap, reads):
            if not debug or name in dumps:
                return
            t = dt("dbg_" + name, list(ap.shape), ap.dtype, kind="ExternalOutput").ap()
            o = B.S.dma("pool", t, ap, d_dump, reads, [])
            fr = B.R("dfin_" + name)
            fr.w = o
            dumps[name] = fr
        ctx = dict(nc=nc, B=B, dr=dr, E=E, new_dsem=new_dsem, psum=psum, rps=rps, hT=hT, GT=GT, dump=dump,
                   stop=stop, ident=ident, ones=ones, small=small, r_hT=r_hT, r_GT=r_GT,
                   r_ident=r_ident, r_ones=r_ones, sems=sems)
        if mode == "fused":
            ctx["r_x1"] = [B.R("x1_%d" % i) for i in range(NT)]
        if do_l1:
            l1_prep_stage1(ctx, E)
        else:
            ctx["pump"] = lambda n=None: None
        with nc.Block() as block:
            final = []
            if do_l0:
                final = build_l0(ctx)
            if do_l1:
                final = build_l1(ctx)
            B.S.op("sp", None, list(final) + list(dumps.values()), ())
            emit_all(B, block, sems)
    return nc


def emit_all(B, block, sems):
    S_ = B.S
    nc = B.nc
    for o in S_.allops:
        for p in o.deps:
            if p.dsem is None and not (p.eng == "pe" and o.eng == "pe"):
                p.need = True
    for e in S_.ENGS:
        c = 0
        for o in S_.ops[e]:
            if o.need:
                c += 1
                o.seq = c

    def reqs_of(o, e, before=None):
        req = {}
        for p in o.deps:
            if before is not None and p.gidx >= before:
                continue
            if p.dsem is not None:
                key = ("d", id(p.dsem))
                v = o.dreq[id(p.dsem)]
                if req.get(key, (None, 0))[1] < v[1]:
                    req[key] = v
            else:
                if p.eng == "pe" and e == "pe":
                    continue
                key = ("e", p.eng)
                if req.get(key, (None, 0))[1] < p.seq:
                    req[key] = (sems[p.eng], p.seq)
        return req

    def run(e, eo):
        waited = {}
        ops = S_.ops[e]
        seen_groups = set()
        for k, o in enumerate(ops):
            req = reqs_of(o, e)
            if o.group is not None and o.group not in seen_groups:
                seen_groups.add(o.group)
                j = k + 1
                while j < len(ops) and ops[j].group == o.group:
                    for key, v in reqs_of(ops[j], e, before=o.gidx).items():
                        if req.get(key, (None, 0))[1] < v[1]:
                            req[key] = v
                    j += 1
            for key, (sem, val) in req.items():
                if waited.get(key, 0) >= val:
                    continue
                waited[key] = val
                eo.wait_ge(sem, val)
            if o.fn is None:
                continue
            ins = o.fn(eo)
            if o.dsem is not None:
                ins.then_inc(o.dsem.sem, 16)
            elif o.need:
                ins.then_inc(sems[e], 1)

    @block.tensor
    def _(eo):
        run("pe", eo)

    @block.scalar
    def _(eo):
        run("act", eo)

    @block.vector
    def _(eo):
        run("dve", eo)

    @block.gpsimd
    def _(eo):
        run("pool", eo)

    @block.sync
    def _(eo):
        run("sp", eo)


def make_hT(ctx, src, scope, gp, r_gp, perm8=False, src_res=None, name="a"):
    nc, B, psum, rps, hT, ident = ctx["nc"], ctx["B"], ctx["psum"], ctx["rps"], ctx["hT"], ctx["ident"]
    E = scope
    NS = 3
    xt = E(nc.sbuf_tensor("xt_" + name, [128, NS, D], F32))
    xs = E(nc.sbuf_tensor("xs_" + name, [128, NS, D], F32))
    jk = E(nc.sbuf_tensor("jk_" + name, [128, D], F32))
    st = E(nc.sbuf_tensor("st_" + name, [128, NS, 4], F32))
    r_xt = [B.R("xt%d" % i) for i in range(NS)]
    r_xs = [B.R("xs%d" % i) for i in range(NS)]
    r_st = [B.R("st%d" % i) for i in range(NS)]
    r_jk = B.R("jk")
    d_x = [ctx["new_dsem"]("d_x%s%d" % (name, i)) for i in range(NS)]

    def stage1(tt):
        sl = tt % NS
        B.S.dma("sp", xt[:, sl, :], src[tt * 128:(tt + 1) * 128, :], d_x[sl],
                [src_res[tt]] if src_res else (), [r_xt[sl]])
        B.act(jk[:, :], xt[:, sl, :], AF.Square, [r_xt[sl]], [r_jk, r_st[sl]], accum_out=st[:, sl, 0:1])
        B.ts("dve", st[:, sl, 1:2], st[:, sl, 0:1], 1.0 / D, ALU.mult, [r_st[sl]], [r_st[sl]],
             s2=EPS, op1=ALU.add)

    def stage2(tt):
        sl = tt % NS
        B.act(st[:, sl, 1:2], st[:, sl, 1:2], AF.Sqrt, [r_st[sl]], [r_st[sl]])
        B.recip(st[:, sl, 1:2], st[:, sl, 1:2], [r_st[sl]], [r_st[sl]])
        B.ts("dve", xs[:, sl, :], xt[:, sl, :], st[:, sl, 1:2], ALU.mult,
             [r_xt[sl], r_st[sl]], [r_xs[sl]])

    def stage3(tt):
        sl = tt % NS
        for half in range(2):
            bk = half + 2 * (tt % 2)
            for j in range(4):
                c = half * 4 + j
                B.tr(psum[bk][:, j * 128:(j + 1) * 128], xs[:, sl, c * 128:(c + 1) * 128], ident[:],
                     [r_xs[sl], ctx["r_ident"]], [rps[bk]])
            for j in range(4):
                c = half * 4 + j
                if perm8:
                    dst = hT[:, c, :].rearrange("p (j n) -> p j n", j=8)[:, :, 16 * tt:16 * tt + 16]
                    srcp = psum[bk][:, j * 128:(j + 1) * 128].rearrange("p (n j) -> p j n", j=8)
                else:
                    dst = hT[:, c, tt * 128:(tt + 1) * 128]
                    srcp = psum[bk][:, j * 128:(j + 1) * 128]
                wr = ctx["r_hT"] if perm8 else [ctx["r_hT"][tt]]
                B.ts("dve", dst, srcp, gp[:, c:c + 1], ALU.mult, [r_gp], [rps[bk]] + wr)

    for t in range(NT + 2):
        if t < NT:
            stage1(t)
        if 0 <= t - 1 < NT:
            stage2(t - 1)
        if 0 <= t - 2 < NT:
            stage3(t - 2)


class WStream:
    def __init__(self, ctx, scope, nslot=2, name="ws", cols=256):
        nc, B = ctx["nc"], ctx["B"]
        self.ctx = ctx
        self.n = nslot
        self.wst = scope(nc.sbuf_tensor(name + "_st", [128, nslot, 8, cols], F32))
        self.wbf = scope(nc.sbuf_tensor(name + "_bf", [128, nslot, 8, cols], BF16))
        self.r_st = [B.R(name + "st%d" % i) for i in range(nslot)]
        self.r_bf = [B.R(name + "bf%d" % i) for i in range(nslot)]
        self.ds = [ctx["new_dsem"]("d_" + name + str(i)) for i in range(nslot)]
        self.k = 0

    def dma(self, src):
        B = self.ctx["B"]
        sl = self.k % self.n
        self.k += 1
        B.S.dma("sp", self.wst[:, sl], src.rearrange("(c p) n -> p c n", p=128), self.ds[sl],
                (), [self.r_st[sl]])
        return sl

    def cast(self, sl):
        B = self.ctx["B"]
        B.act(self.wbf[:, sl], self.wst[:, sl], AF.Copy, [self.r_st[sl]], [self.r_bf[sl]])
        return self.wbf[:, sl], self.r_bf[sl]

    def load(self, src, dst=None, r_dst=None):
        B = self.ctx["B"]
        sl = self.k % self.n
        self.k += 1
        B.S.dma("sp", self.wst[:, sl], src.rearrange("(c p) n -> p c n", p=128), self.ds[sl],
                (), [self.r_st[sl]])
        if dst is None:
            dst, r_dst = self.wbf[:, sl], self.r_bf[sl]
        B.act(dst, self.wst[:, sl], AF.Copy, [self.r_st[sl]], [r_dst])
        return dst, r_dst


def grp_tokens(g, bi):
    d = GROUPS[g][1]
    nbr = 16 // d
    r = bi // nbr
    m0 = (bi % nbr) * 128
    return d * m0 + r, d


def build_l0(ctx):
    nc, B, dr, psum, rps = ctx["nc"], ctx["B"], ctx["dr"], ctx["psum"], ctx["rps"]
    hT, GT, ones, small = ctx["hT"], ctx["GT"], ctx["ones"], ctx["small"]
    r_hT, r_GT = ctx["r_hT"], ctx["r_GT"]
    from contextlib import ExitStack
    d_par = ctx["new_dsem"]("d_par0")
    r_gp = B.R("gpre0")
    B.S.dma("sp", small[:, 0:8], dr["gpre0"], d_par, (), [r_gp])
    with ExitStack() as es:
        make_hT(ctx, dr["x"], es.enter_context, small[:, 0:8], r_gp)
        B.barrier()
    allhT = list(r_hT)
    ctx["dump"]("hT", hT[:], allhT)
    with ExitStack() as es:
        E = es.enter_context
        ws = WStream(ctx, E, 2, "w0s", cols=128)
        QK = E(nc.sbuf_tensor("QK", [128, 6, S], BF16))
        VA0 = E(nc.sbuf_tensor("VA0", [128, 48, 65], BF16))
        VA1 = E(nc.sbuf_tensor("VA1", [128, 48, 128], BF16))
        SZ = E(nc.sbuf_tensor("SZ", [128, S], F32))
        BT = E(nc.sbuf_tensor("BT", [128, 2, 6, 256], F32))
        OA = E(nc.sbuf_tensor("OA", [128, 2, S], F32))
        LT = E(nc.sbuf_tensor("LT", [128, 2, 2, 256], F32))
        PT = E(nc.sbuf_tensor("PT", [128, 5, 2, 256], BF16))
        VT = E(nc.sbuf_tensor("VT", [128, 3, S], BF16))
        RHL = E(nc.sbuf_tensor("RHL", [128, 2, S], BF16))
        identb = E(nc.sbuf_tensor("identb", [128, 128], BF16))
        onesb = E(nc.sbuf_tensor("onesb", [128, 128], BF16))
        r_VT = [B.R("VT%d" % i) for i in range(3)]
        r_RHL = B.R("RHL")
        r_cb = B.R("constb")
        B.cp("pool", identb[:], ctx["ident"][:], [ctx["r_ident"]], [r_cb])
        B.memset("pool", onesb[:], 1.0, [r_cb])
        r_QK = [B.R("QK%d" % i) for i in range(6)]
        r_VA = [B.R("VA%d" % i) for i in range(3)]
        r_SZ = B.R("SZ")
        r_BT = [B.R("BT0"), B.R("BT1")]
        r_OA = [B.R("OA0"), B.R("OA1")]
        r_LT = [B.R("LT%d" % i) for i in range(2)]
        r_PT = [B.R("PT%d" % i) for i in range(5)]
        d_bt = [ctx["new_dsem"]("d_bt0"), ctx["new_dsem"]("d_bt1")]
        B.memset("pool", VA0[:], 1.0, r_VA)
        B.memset("pool", VA1[:], 0.0, r_VA)
        B.memset("pool", VA1[:, :, 0:1], 1.0, r_VA)
        pb = [0]

        def proj_bank():
            pb[0] ^= 1
            return pb[0]

        stb = [0]
        fb = [0]
        dq = []
        pre = {}
        wdma = {}

        def wsrc(h, wb):
            return dr["w0"][h][:, wb * 128:(wb + 1) * 128]
        for hp in range(8):
            bsl = hp % 2
            B.S.dma("sp", BT[:, bsl], dr["bias0"][hp], d_bt[bsl], (), [r_BT[bsl]])
            for wb in range(10):
                key = (hp, wb)
                if key not in pre:
                    if key not in wdma:
                        wdma[key] = ws.dma(wsrc(*key))
                    pre[key] = ws.cast(wdma.pop(key))
                wbf, r_w = pre.pop(key)
                nkey = (hp, wb + 1) if wb < 9 else ((hp + 1, 0) if hp + 1 < 8 else None)
                if nkey is not None and nkey not in wdma and nkey not in pre:
                    wdma[nkey] = ws.dma(wsrc(*nkey))

                def at_s2(nkey=nkey):
                    if nkey is not None and nkey in wdma:
                        pre[nkey] = ws.cast(wdma.pop(nkey))
                for half_ in range(1):
                    colo = 0
                    kind = ("q", "k", "q", "k", "q", "k", "v0", "v1", "v2", "z")[wb]
                    half = wb % 2
                    if kind in ("q", "k", "z"):
                        g = wb // 2 if wb < 6 else 0
                        d = GROUPS[g][1] if kind != "z" else 1
                        for s in range(4):
                            if s == 2:
                                at_s2()
                            bk = proj_bank()
                            for c in range(8):
                                B.mm(psum[bk][:, :], wbf[:, c, colo:colo + 128],
                                     hT[:, c, s * 512:(s + 1) * 512], c == 0, c == 7,
                                     [r_w] + allhT, [rps[bk]])
                            if kind == "z":
                                B.act(SZ[:, s * 512:(s + 1) * 512], psum[bk][:, :], AF.Silu,
                                      [], [rps[bk], r_SZ])
                            else:
                                idx = 2 * g + half
                                L = S // d
                                dst = QK[:, idx, :].rearrange("p (r m) -> p r m", r=d)[
                                    :, :, (512 // d) * s:(512 // d) * (s + 1)]
                                srcp = psum[bk][:, :].rearrange("p (m r) -> p r m", r=d)
                                B.act(dst, srcp, AF.Copy, [], [rps[bk], r_QK[idx]],
                                      scale=(0.125 if kind == "q" else 1.0))
                    else:
                        g = int(kind[1])
                        for s in range(4):
                            if s == 2:
                                at_s2()
                            bk = proj_bank()
                            for c in range(8):
                                B.mm(psum[bk][:, :], wbf[:, c, colo:colo + 128],
                                     hT[:, c, s * 512:(s + 1) * 512], c == 0, c == 7,
                                     [r_w] + allhT, [rps[bk]])
                            B.act(VT[:, g, s * 512:(s + 1) * 512], psum[bk][:, :], AF.Copy, [],
                                  [rps[bk], r_VT[g]])
                        for quad in range(4):
                            bk = proj_bank()
                            pb16 = psum[bk][:, :].bitcast(BF16)
                            for j in range(4):
                                bi = quad * 4 + j
                                o, st = grp_tokens(g, bi)
                                B.tr(pb16[:, j * 128:(j + 1) * 128], VT[:, g, o:o + 127 * st + 1:st], identb[:],
                                     [r_VT[g], r_cb], [rps[bk]])
                            pv = pb16[:, 0:512].rearrange("p (j c) -> p j c", j=4)
                            b0 = g * 16 + quad * 4
                            B.cp("dve", VA0[:, b0:b0 + 4, 0:64], pv[:, :, 0:64], [], [rps[bk], r_VA[g]])
                            B.cp("dve", VA1[:, b0:b0 + 4, 64:128], pv[:, :, 64:128], [], [rps[bk], r_VA[g]])
                ctx["pump"](4)
                if wb < 7:
                    while dq:
                        f = dq.pop(0)
                        if f is not None:
                            f()
                            break
                if wb == 8:
                    while dq:
                        f = dq.pop(0)
                        if f is not None:
                            f()
            its = [(hh, g, m) for hh in range(2) for g in range(3) for m in range(8)]
            st_info = {}

            def nq_of(g, kb):
                nbr = 16 // GROUPS[g][1]
                return 256 if (kb + 1) % nbr != 0 else 128

            def issue_st(i):
                hh, g, m = its[i]
                bk = (0, 1, 2, 7)[stb[0] % 4]
                stb[0] += 1
                rows = slice(hh * 64, (hh + 1) * 64)
                for t in range(2):
                    kb = 2 * m + t
                    nq = nq_of(g, kb)
                    B.mm(psum[bk][:, t * 256:t * 256 + nq], QK[rows, 2 * g + 1, kb * 128:(kb + 1) * 128],
                         QK[rows, 2 * g, kb * 128:kb * 128 + nq], True, True,
                         [r_QK[2 * g], r_QK[2 * g + 1]], [rps[bk]])
                ls = i % 2
                ps_ = i % 5
                bias2 = BT[:, bsl, g * 2 + hh, :].unsqueeze(1).to_broadcast([128, 2, 256])
                B.tt("dve", LT[:, ls], psum[bk][:, :].rearrange("p (t q) -> p t q", t=2), bias2, ALU.add,
                     [r_BT[bsl]], [rps[bk], r_LT[ls]])
                B.act(PT[:, ps_], LT[:, ls], AF.Exp, [r_LT[ls]], [r_PT[ps_]])
                st_info[i] = ps_

            def issue_pv(i):
                hh, g, m = its[i]
                d = GROUPS[g][1]
                nbr = 16 // d
                M = 65 if hh == 0 else 128
                VA = VA0 if hh == 0 else VA1
                ps_ = st_info[i]
                for t in range(2):
                    kb = 2 * m + t
                    has_prev = kb % nbr != 0
                    if has_prev:
                        pslot, pt = (ps_, 0) if t == 1 else (st_info[i - 1], 1)
                    if g == 0:
                        outs = [(3 + kb // 4, slice((kb % 4) * 128, (kb % 4) * 128 + 128), slice(0, 128))]
                    elif g == 1:
                        r, nn = kb // 4, kb % 4
                        outs = [(3 + nn, slice(r, 512, 4), slice(0, 128))]
                    else:
                        outs = [(3 + sub, slice(kb, 512, 16), slice(32 * sub, 32 * sub + 32)) for sub in range(4)]
                    for (bo, ocols, qcols) in outs:
                        outp = psum[bo][0:M, ocols]
                        first = (g == 0 and kb % 4 == 0)
                        if has_prev:
                            pq = slice(128 + qcols.start, 128 + qcols.stop)

                            def f1(e, outp=outp, a=VA[:, g * 16 + kb - 1, 0:M], b=PT[:, pslot, pt, pq], first=first):
                                return e.matmul(outp, a, b, start=first, stop=False, skip_group_check=True)
                            B.S.op("pe", f1, [r_VA[g], r_PT[pslot]], [rps[bo]])
                            first = False

                        last = (g == 2 and kb == 15)

                        def f2(e, outp=outp, a=VA[:, g * 16 + kb, 0:M], b=PT[:, ps_, t, qcols], first=first, last=last):
                            return e.matmul(outp, a, b, start=first, stop=last, skip_group_check=True)
                        B.S.op("pe", f2, [r_VA[g], r_PT[ps_]], [rps[bo]])
                if g == 2 and m == 7:
                    for span in range(4):
                        if span % 2 == 0:
                            B.cp("dve", OA[0:M, hh, span * 512:(span + 1) * 512], psum[3 + span][0:M, :],
                                 [], [rps[3 + span], r_OA[hh]])
                        else:
                            B.act(OA[0:M, hh, span * 512:(span + 1) * 512], psum[3 + span][0:M, :], AF.Copy,
                                  [], [rps[3 + span], r_OA[hh]])
                    finalize(hh)

            def finalize(hh, hp=hp):
                drow = 64 if hh == 0 else 0
                R = slice(hh * 64, (hh + 1) * 64)

                def prepA(s):
                    cs = slice(s * 512, (s + 1) * 512)
                    den = OA[drow:drow + 1, hh, cs]
                    B.act(den, den, AF.Ln, [], [r_OA[hh]])
                    B.act(den, den, AF.Exp, [], [r_OA[hh]], scale=-1.0)

                def prepB(s):
                    cs = slice(s * 512, (s + 1) * 512)
                    den = OA[drow:drow + 1, hh, cs]
                    B.cp("dve", RHL[drow:drow + 1, 0, cs], den, [r_OA[hh]], [r_RHL])
                    B.tt("dve", RHL[drow:drow + 1, 1, cs], den, RHL[drow:drow + 1, 0, cs], ALU.subtract,
                         [r_OA[hh]], [r_RHL])

                def span(s):
                    bk = (0, 1, 2, 7)[stb[0] % 4]
                    stb[0] += 1
                    cs = slice(s * 512, (s + 1) * 512)
                    B.mm(psum[bk][:, :], onesb[drow:drow + 1, :], RHL[drow:drow + 1, 0, cs], True, False,
                         [r_cb, r_RHL], [rps[bk]])
                    B.mm(psum[bk][:, :], onesb[drow:drow + 1, :], RHL[drow:drow + 1, 1, cs], False, True,
                         [r_cb, r_RHL], [rps[bk]])
                    B.tt("dve", OA[R, hh, cs], OA[R, hh, cs], psum[bk][R, :], ALU.mult,
                         [], [rps[bk], r_OA[hh]])
                    B.tt("pool", GT[R, hp, cs], OA[R, hh, cs], SZ[R, cs], ALU.mult,
                         [r_OA[hh], r_SZ], [r_GT])
                for s in range(4):
                    dq.append(lambda s=s: prepA(s))
                    dq.append(None)
                    dq.append(lambda s=s: prepB(s))
                for _ in range(4):
                    dq.append(None)
                for s in range(4):
                    dq.append(lambda s=s: span(s))
                    if s < 3:
                        dq.append(None)

            if hp == 0:
                ctx["dump"]("QK", QK[:], r_QK)
                ctx["dump"]("VA0", VA0[:], r_VA)
                ctx["dump"]("VA1", VA1[:], r_VA)
                ctx["dump"]("SZ", SZ[:], [r_SZ])
                ctx["dump"]("BT", BT[:, 0], [r_BT[0]])
            LOOK = 3
            n = len(its)
            for i in range(min(LOOK, n)):
                issue_st(i)
            for i in range(n):
                B.S.begin_group()
                issue_pv(i)
                if i + LOOK < n:
                    issue_st(i + LOOK)
                B.S.end_group()
                if dq:
                    f = dq.pop(0)
                    if f is not None:
                        f()
            if hp == 7:
                while dq:
                    f = dq.pop(0)
                    if f is not None:
                        f()
        B.barrier()
    ctx["dump"]("GT", GT[:], [r_GT])
    finals = []
    with ExitStack() as es:
        E = es.enter_context
        ws = WStream(ctx, E, 2, "wo0s")
        WO = E(nc.sbuf_tensor("WO0", [128, 8, D], BF16))
        gpo = E(nc.sbuf_tensor("gpo0", [128, D], F32))
        r_WO = [B.R("WO%d" % i) for i in range(4)]
        r_gpo = B.R("gpo0")
        B.S.dma("sp", gpo[:], dr["gpost0"], d_par, (), [r_gpo])
        for wb in range(4):
            ws.load(dr["wo0"][:, wb * 256:(wb + 1) * 256], WO[:, :, wb * 256:(wb + 1) * 256], r_WO[wb])
        finals = out_proj_tail(ctx, E, WO, r_WO, gpo, r_gpo, dr["x"], dr["x1"], GT, r_GT, None, "t0",
                               dst_res=ctx.get("r_x1"))
        B.barrier()
    return finals


def out_proj_tail(ctx, E, WO, r_WO, gpo, r_gpo, xsrc, dst, GT, r_GT, tokap, name, src_res=None, dst_res=None):
    nc, B, psum, rps = ctx["nc"], ctx["B"], ctx["psum"], ctx["rps"]
    xt = E(nc.sbuf_tensor("xt_" + name, [128, 2, D], F32))
    yt = E(nc.sbuf_tensor("yt_" + name, [128, 2, D], F32))
    sq = E(nc.sbuf_tensor("sq_" + name, [128, 512], F32))
    st = E(nc.sbuf_tensor("st_" + name, [128, 2, 4], F32))
    r_xt = [B.R("xt0"), B.R("xt1")]
    r_yt = [B.R("yt0"), B.R("yt1")]
    r_sq = B.R("sq")
    r_st = [B.R("st0"), B.R("st1")]
    d_x = [ctx["new_dsem"]("d_x%s0" % name), ctx["new_dsem"]("d_x%s1" % name)]
    d_o = [ctx["new_dsem"]("d_o%s0" % name), ctx["new_dsem"]("d_o%s1" % name)]
    finals = []
    for tt in range(NT):
        sl = tt % 2
        if tokap is None:
            xrows, orows = xsrc[tt * 128:(tt + 1) * 128, :], dst[tt * 128:(tt + 1) * 128, :]
        else:
            jj, nb = tt // 2, tt % 2
            xrows = xsrc.rearrange("(n j) d -> j n d", j=8)[jj, nb * 128:(nb + 1) * 128, :]
            orows = dst.rearrange("(n j) d -> j n d", j=8)[jj, nb * 128:(nb + 1) * 128, :]
        B.S.dma("sp", xt[:, sl, :], xrows, d_x[sl],
                [src_res[tt]] if src_res else (), [r_xt[sl]])
        banks = (0, 1) if sl == 0 else (2, 3)
        for half in range(2):
            bk = banks[half]
            for c in range(8):
                lhs = GT[:, c, tt * 128:(tt + 1) * 128]
                B.mm(psum[bk][:, :], lhs, WO[:, c, half * 512:(half + 1) * 512], c == 0, c == 7,
                     [r_GT] + r_WO[2 * half:2 * half + 2], [rps[bk]])
            B.act(sq[:, :], psum[bk][:, :], AF.Square, [], [rps[bk], r_sq, r_st[sl]],
                  accum_out=st[:, sl, half:half + 1])
        B.tt("dve", st[:, sl, 2:3], st[:, sl, 0:1], st[:, sl, 1:2], ALU.add, [r_st[sl]], [r_st[sl]])
        rms_rstd(B, st[:, sl, 2:3], st[:, sl, 3:4], r_st[sl], r_st[sl])
        for half in range(2):
            bk = banks[half]
            hs = slice(half * 512, (half + 1) * 512)
            B.stt(yt[:, sl, hs], psum[bk][:, :], st[:, sl, 3:4], gpo[:, hs], ALU.mult, ALU.mult,
                  [r_st[sl], r_gpo], [rps[bk], r_yt[sl]])
        B.tt("pool", yt[:, sl, :], yt[:, sl, :], xt[:, sl, :], ALU.add, [r_xt[sl]], [r_yt[sl]])
        o = B.S.dma("pool", orows, yt[:, sl, :], d_o[sl], [r_yt[sl]],
                    [dst_res[tt]] if dst_res else [])
        fr = B.R("fin%d" % tt)
        fr.w = o
        finals.append(fr)
    return finals


def build_l1_decl(dt, dr):
    EI = "ExternalInput"
    dr["w1"] = dt("w1", [D, 2048], F32, kind=EI).ap()
    dr["wg1"] = dt("wg1", [D, D], F32, kind=EI).ap()
    dr["wo1"] = dt("wo1", [D, D], F32, kind=EI).ap()
    dr["gpre1"] = dt("gpre1", [128, 8], F32, kind=EI).ap()
    dr["gpost1"] = dt("gpost1", [128, D], F32, kind=EI).ap()
    dr["bglu"] = dt("bglu", [128, 8], F32, kind=EI).ap()
    dr["pl_a"] = dt("pl_a", [128, 3, 32], F32, kind=EI).ap()
    dr["pl_bc"] = dt("pl_bc", [128, 4, 32, 16], F32, kind=EI).ap()
    dr["dcol"] = dt("dcol", [128, 64], F32, kind=EI).ap()
    dr["mask8"] = dt("mask8", [128, 128], F32, kind=EI).ap()
    dr["esel"] = dt("esel", [128, 64, 128], BF16, kind=EI).ap()


def cmul(B, outr, outi, ar, ai, br, bi, t1, t2, R, neg_i=False, eng="dve", emit=None, Rw=None):
    if emit is None:
        emit = lambda f: f()
    Rw = R if Rw is None else Rw
    rd = [R] if Rw is R else [R, Rw]
    emit(lambda: B.tt(eng, t1, ar, br, ALU.mult, rd, [Rw]))
    emit(lambda: B.tt(eng, t2, ai, bi, ALU.mult, rd, [Rw]))
    emit(lambda: B.tt(eng, outr, t1, t2, ALU.subtract, rd, [Rw]))
    emit(lambda: B.tt(eng, t1, ar, bi, ALU.mult, rd, [Rw]))
    emit(lambda: B.tt(eng, t2, ai, br, ALU.mult, rd, [Rw]))
    if neg_i:
        emit(lambda: B.tt(eng, t1, t1, t2, ALU.add, rd, [Rw]))
        emit(lambda: B.ts(eng, outi, t1, -1.0, ALU.mult, rd, [Rw]))
    else:
        emit(lambda: B.tt(eng, outi, t1, t2, ALU.add, rd, [Rw]))


def l1_prep_stage1(ctx, E):
    nc, B, dr = ctx["nc"], ctx["B"], ctx["dr"]
    PEN = "dve"
    pa = E(nc.sbuf_tensor("pa", [128, 3, 32], F32))
    pbc = E(nc.sbuf_tensor("pbc", [128, 4, 32, 16], F32))
    W = E(nc.sbuf_tensor("Wk", [128, 16, 32], F32))
    PW = E(nc.sbuf_tensor("PW", [128, 9, 2, 32], F32))
    QW = E(nc.sbuf_tensor("QW", [128, 8, 2, 32], F32))
    bb = E(nc.sbuf_tensor("bbar", [128, 2, 32, 16], F32))
    T1 = E(nc.sbuf_tensor("T1", [128, 32, 16], F32))
    T2 = E(nc.sbuf_tensor("T2", [128, 32, 16], F32))
    RP = B.R("prep")
    d_p = ctx["new_dsem"]("d_prep")
    B.S.dma("sp", pa[:], dr["pl_a"], d_p, (), [RP])
    B.S.dma("sp", pbc[:], dr["pl_bc"], d_p, (), [RP])
    q = []
    emit = q.append
    RK = E(nc.sbuf_tensor("RK", [128, 16], F32))
    for k in range(1, 16):
        emit(lambda k=k: B.memset(PEN, RK[:, k:k + 1], float(k), [RP]))
    emit(lambda: B.memset(PEN, RK[:, 0:1], 1.0, [RP]))
    emit(lambda: B.recip(RK[:], RK[:], [RP], [RP]))
    are, aim, ldt = pa[:, 0, :], pa[:, 1, :], pa[:, 2, :]
    w = lambda k: W[:, k, :]
    t1s, t2s = T1[:, :, 0], T2[:, :, 0]
    tt = lambda o, a, b, op: emit(lambda: B.tt(PEN, o, a, b, op, [RP], [RP]))
    ts = lambda o, a, s1, op0, s2=None, op1=None: emit(lambda: B.ts(PEN, o, a, s1, op0, [RP], [RP], s2=s2, op1=op1))
    ms = lambda o, v: emit(lambda: B.memset(PEN, o, v, [RP]))
    cpy = lambda o, a: emit(lambda: B.cp(PEN, o, a, [RP], [RP]))
    cm = lambda *a, **k: cmul(B, *a, eng=PEN, emit=emit, **k)
    ts(w(1), ldt, 0.125, ALU.mult)
    ms(w(0), 1.0)
    for k in range(10, 0, -1):
        tt(w(0), w(0), w(1), ALU.mult)
        ts(w(0), w(0), RK[:, k:k + 1], ALU.mult, 1.0, ALU.add)
    for _ in range(3):
        tt(w(0), w(0), w(0), ALU.mult)
    ts(w(8), w(0), 1.0 / 16, ALU.mult)
    tt(w(1), are, w(8), ALU.mult)
    tt(w(2), aim, w(8), ALU.mult)
    ms(w(3), 1.0)
    ms(w(4), 0.0)
    for k in range(14, 1, -1):
        cm(w(5), w(6), w(3), w(4), w(1), w(2), t1s, t2s, RP)
        ts(w(3), w(5), RK[:, k:k + 1], ALU.mult, 1.0, ALU.add)
        ts(w(4), w(6), RK[:, k:k + 1], ALU.mult)
    cm(w(5), w(6), w(3), w(4), w(1), w(2), t1s, t2s, RP)
    for _ in range(4):
        ts(w(7), w(5), 2.0, ALU.add)
        cm(w(3), w(4), w(5), w(6), w(7), w(6), t1s, t2s, RP)
        cpy(w(5), w(3))
        cpy(w(6), w(4))
    ms(PW[:, 0, 0, :], 1.0)
    ms(PW[:, 0, 1, :], 0.0)
    ms(QW[:, 0, 0, :], 1.0)
    ms(QW[:, 0, 1, :], 0.0)
    abr, abi = PW[:, 1, 0, :], PW[:, 1, 1, :]
    ts(abr, w(5), 1.0, ALU.add)
    cpy(abi, w(6))
    for k in range(2, 9):
        cm(PW[:, k, 0, :], PW[:, k, 1, :], PW[:, k - 1, 0, :], PW[:, k - 1, 1, :], abr, abi, t1s, t2s, RP)
    tt(w(8), abr, abr, ALU.mult)
    tt(w(9), abi, abi, ALU.mult)
    tt(w(8), w(8), w(9), ALU.add)
    emit(lambda: B.recip(w(9), w(8), [RP], [RP]))
    tt(QW[:, 1, 0, :], abr, w(9), ALU.mult)
    tt(w(8), abi, w(9), ALU.mult)
    ts(QW[:, 1, 1, :], w(8), -1.0, ALU.mult)
    for k in range(2, 8):
        cm(QW[:, k, 0, :], QW[:, k, 1, :], QW[:, k - 1, 0, :], QW[:, k - 1, 1, :],
           QW[:, 1, 0, :], QW[:, 1, 1, :], t1s, t2s, RP)
    cpy(w(10), w(5))
    tt(w(11), are, are, ALU.mult)
    tt(w(12), aim, aim, ALU.mult)
    tt(w(11), w(11), w(12), ALU.add)
    emit(lambda: B.recip(w(12), w(11), [RP], [RP]))
    tt(w(13), w(10), are, ALU.mult)
    tt(w(14), abi, aim, ALU.mult)
    tt(w(13), w(13), w(14), ALU.add)
    tt(w(13), w(13), w(12), ALU.mult)
    tt(w(14), abi, are, ALU.mult)
    tt(w(15), w(10), aim, ALU.mult)
    tt(w(14), w(14), w(15), ALU.subtract)
    tt(w(14), w(14), w(12), ALU.mult)
    bc = lambda ap: ap.unsqueeze(2).to_broadcast([128, 32, 16])
    cm(bb[:, 0], bb[:, 1], bc(w(13)), bc(w(14)), pbc[:, 0], pbc[:, 1], T1[:], T2[:], RP)

    def pump(n=None):
        k = len(q) if n is None else min(n, len(q))
        for _ in range(k):
            q.pop(0)()
    ctx["pump"] = pump
    ctx["prep1"] = dict(pbc=pbc, PW=PW, QW=QW, bb=bb, T1=T1, T2=T2, RP=RP, bc=bc)


def build_l1(ctx):
    from contextlib import ExitStack
    nc, B, dr, psum, rps = ctx["nc"], ctx["B"], ctx["dr"], ctx["psum"], ctx["rps"]
    hT, GT, ident, small = ctx["hT"], ctx["GT"], ctx["ident"], ctx["small"]
    r_ident = ctx["r_ident"]
    PI = math.pi
    d_par = ctx["new_dsem"]("d_par1")
    r_gp = B.R("gpre1")
    r_bg = B.R("bglu")
    B.S.dma("sp", small[:, 8:16], dr["gpre1"], d_par, (), [r_gp])
    B.S.dma("sp", small[:, 16:24], dr["bglu"], d_par, (), [r_bg])
    r_hTall = B.R("hTall")
    r_Gb = B.R("Gb")
    Gb = GT[:, :, :].rearrange("p c (a n) -> p (c a) n", n=256)
    with ExitStack() as es1:
        E1 = es1.enter_context
        T0b = E1(nc.sbuf_tensor("T0b", [128, 64, 128], BF16))
        BmT = E1(nc.sbuf_tensor("BmT", [128, 64, 2, 64], BF16))
        Cmb = E1(nc.sbuf_tensor("Cmb", [128, 32, 2, 128], BF16))
        A8 = E1(nc.sbuf_tensor("A8", [128, 2, 2, 32], F32))
        r_T0b, r_BmT, r_Cmb, r_A8 = B.R("T0b"), B.R("BmT"), B.R("Cmb"), B.R("A8")
        with ExitStack() as es:
            E = es.enter_context
            ctx["pump"]()
            p1 = ctx["prep1"]
            pbc, PW, QW, bb, T1, T2, RP, bc = (p1[k] for k in ("pbc", "PW", "QW", "bb", "T1", "T2", "RP", "bc"))
            cre, cim = pbc[:, 2], pbc[:, 3]
            dcol = E(nc.sbuf_tensor("dcol_sb", [128, 64], F32))
            mask8 = E(nc.sbuf_tensor("mask8_sb", [128, 128], F32))
            Lh = E(nc.sbuf_tensor("Lh", [128, 2, 32, 128], BF16))
            Rh = E(nc.sbuf_tensor("Rh", [128, 2, 32, 128], BF16))
            identb1 = E(nc.sbuf_tensor("identb1", [128, 128], BF16))
            r_ib1 = B.R("identb1")
            B.cp("dve", identb1[:], ident[:], [r_ident], [r_ib1])
            TM = E(nc.sbuf_tensor("TM", [128, 2, 128], F32))
            r_dm, r_TM = B.R("dcolmask"), [B.R("TM0"), B.R("TM1")]
            B.S.dma("sp", dcol[:], dr["dcol"], d_par, (), [r_dm])
            B.S.dma("sp", mask8[:], dr["mask8"], d_par, (), [r_dm])
            T3 = E(nc.sbuf_tensor("T3", [128, 32, 16], F32))
            T4 = E(nc.sbuf_tensor("T4", [128, 32, 16], F32))
            R_L, R_R = B.R("prepL"), B.R("prepR")
            for j in range(8):
                js = slice(j * 16, (j + 1) * 16)
                cmul(B, Rh[:, 0, :, js], Rh[:, 1, :, js], bc(QW[:, 7 - j, 0, :]), bc(QW[:, 7 - j, 1, :]),
                     cre, cim, T3[:], T4[:], RP, neg_i=True, eng="dve", Rw=R_R)
            for j in range(8):
                js = slice(j * 16, (j + 1) * 16)
                cmul(B, Lh[:, 0, :, js], Lh[:, 1, :, js], bc(PW[:, 7 - j, 0, :]), bc(PW[:, 7 - j, 1, :]),
                     bb[:, 0], bb[:, 1], T1[:], T2[:], RP, Rw=R_L)
                cmul(B, Cmb[:, :, 0, js], Cmb[:, :, 1, js], bc(PW[:, j + 1, 0, :]), bc(PW[:, j + 1, 1, :]),
                     cre, cim, T1[:], T2[:], RP, neg_i=True, Rw=R_L)
            B.cp("dve", Cmb[:, 0, 0, 0:1], Cmb[:, 0, 0, 0:1], [R_L], [r_Cmb])
            B.cp("dve", A8[:, 0, 0, :], PW[:, 8, 0, :], [RP], [r_A8])
            B.cp("dve", A8[:, 0, 1, :], PW[:, 8, 0, :], [RP], [r_A8])
            B.ts("dve", A8[:, 1, 0, :], PW[:, 8, 1, :], -1.0, ALU.mult, [RP], [r_A8])
            B.cp("dve", A8[:, 1, 1, :], PW[:, 8, 1, :], [RP], [r_A8])
            r_T0g = [B.R("T0g%d" % i) for i in range(4)]
            for g in range(64):
                gh, pr = g // 32, g % 32
                Rr = slice(gh * 64, (gh + 1) * 64)
                bk = g % 2
                B.mm(psum[bk][:, 0:128], Lh[Rr, 0, pr, :], Rh[Rr, 0, pr, :], True, False, [R_L, R_R], [rps[bk]])
                B.mm(psum[bk][:, 0:128], Lh[Rr, 1, pr, :], Rh[Rr, 1, pr, :], False, True, [R_L, R_R], [rps[bk]])
                tm = g % 2
                B.tt("dve", TM[:, tm, :], psum[bk][:, 0:128], mask8[:], ALU.mult, [r_dm], [rps[bk], r_TM[tm]])
                B.stt(T0b[:, g, :], ident[:], dcol[:, g:g + 1], TM[:, tm, :], ALU.mult, ALU.add,
                      [r_dm, r_TM[tm], r_ident], [r_T0g[g % 4]])
            for g4 in range(16):
                bk = 2 + g4 % 2
                for gg in range(4):
                    g = g4 * 4 + gg
                    gh, pr = g // 32, g % 32
                    Rr = slice(gh * 64, (gh + 1) * 64)
                    for part in range(2):
                        cs = (gg * 2 + part) * 64
                        B.tr(psum[bk][:, :].bitcast(BF16)[:, cs:cs + 64], Lh[Rr, part, pr, :], identb1[Rr, Rr],
                             [R_L, r_ib1], [rps[bk]])
                B.cp("dve", BmT[:, g4 * 4:g4 * 4 + 4].rearrange("p g a c -> p (g a c)"),
                     psum[bk][:, :].bitcast(BF16)[:, 0:512], [], [rps[bk], r_BmT])
            ctx["dump"]("T0b", T0b[:], [r_T0b])
            ctx["dump"]("BmT", BmT[:], [r_BmT])
            ctx["dump"]("Cmb", Cmb[:], [r_Cmb])
            ctx["dump"]("A8", A8[:], [r_A8])
            ctx["dump"]("PW", PW[:], [RP])
            ctx["dump"]("QW", QW[:], [RP])
            ctx["dump"]("Lh", Lh[:], [R_L])
            ctx["dump"]("Rh", Rh[:], [R_R])
            B.barrier()
        if ctx.get("stop") == "prep":
            return []
        V = E1(nc.sbuf_tensor("Vssm", [128, 64, 256], BF16))
        r_V = B.R("V")
        with ExitStack() as es:
            ctx2 = dict(ctx)
            ctx2["r_hT"] = [r_hTall]
            make_hT(ctx2, dr["x1"], es.enter_context, small[:, 8:16], r_gp, perm8=True,
                    src_res=ctx.get("r_x1"), name="b")
            B.barrier()
        ctx["dump"]("hT1", hT[:], [r_hTall])
        if ctx.get("stop") == "A":
            return []
        with ExitStack() as es:
            E = es.enter_context
            ws = WStream(ctx, E, 2, "w1s", cols=128)
            uTb = E(nc.sbuf_tensor("uTb", [128, 2, S], BF16))
            Es = E(nc.sbuf_tensor("Esel", [128, 64, 128], BF16))
            r_uT = [B.R("uT0"), B.R("uT1")]
            r_Es = B.R("Esel")
            d_es = ctx["new_dsem"]("d_es")
            B.S.dma("sp", Es[:], dr["esel"], d_es, (), [r_Es])
            pbk = 0
            for fc in range(8):
                wbf, r_w = ws.load(dr["w1"][:, fc * 128:fc * 128 + 128])
                colo = 0
                us = fc % 2
                for s in range(4):
                    bk = pbk
                    pbk ^= 1
                    for c in range(8):
                        B.mm(psum[bk][:, :], wbf[:, c, colo:colo + 128], hT[:, c, s * 512:(s + 1) * 512],
                             c == 0, c == 7, [r_w, r_hTall], [rps[bk]])
                    B.act(uTb[:, us, s * 512:(s + 1) * 512], psum[bk][:, :], AF.Copy, [],
                          [rps[bk], r_uT[us]])
                for g8 in range(8):
                    g = fc * 8 + g8
                    bk = 2 + g % 4
                    for j in range(8):
                        B.mm(psum[bk][:, 0:256], Es[:, g8 * 8 + j, :], uTb[:, us, j * 256:(j + 1) * 256],
                             j == 0, j == 7, [r_Es, r_uT[us]], [rps[bk]])
                    B.cp("dve", V[:, g, :], psum[bk][:, 0:256], [], [rps[bk], r_V])
            ctx["dump"]("V", V[:], [r_V])
            B.barrier()
        if ctx.get("stop") == "B":
            return []
        with ExitStack() as es:
            E = es.enter_context
            Zb = E(nc.sbuf_tensor("Zb", [128, 65, 2, 32], F32))
            Sb = E(nc.sbuf_tensor("Sb", [128, 64, 2, 32], BF16))
            t1 = E(nc.sbuf_tensor("zt1", [128, 2, 32], F32))
            t2 = E(nc.sbuf_tensor("zt2", [128, 2, 32], F32))
            ga = E(nc.sbuf_tensor("ga", [128, 2, 512], F32))
            gb = E(nc.sbuf_tensor("gb", [128, 2, 512], F32))
            r_Z, r_Sb = B.R("Zb"), B.R("Sb")
            r_ga = [B.R("ga0"), B.R("ga1")]
            B.memset("dve", Zb[:, 0], 0.0, [r_Z])
            KG = 2.0 * math.sqrt(2.0 / math.pi)
            for tb in range(4):
                ns = slice(tb * 64, (tb + 1) * 64)
                for q in range(8):
                    bk = q % 2
                    for pp in range(4):
                        pr = q * 4 + pp
                        for gh in range(2):
                            g = gh * 32 + pr
                            for part in range(2):
                                cs = (pp * 2 + part) * 64
                                B.mm(psum[bk][gh * 64:(gh + 1) * 64, cs:cs + 64], BmT[:, g, part, :],
                                     V[:, g, ns], True, True, [r_BmT, r_V], [rps[bk]])
                    dst = Zb[:, 1:65, :, q * 4:q * 4 + 4].rearrange("p n a r -> p r a n")
                    srcp = psum[bk][:, :].rearrange("p (r a n) -> p r a n", r=4, a=2)
                    B.cp("dve", dst, srcp, [], [rps[bk], r_Z])
                for s in range(64):
                    zs = Zb[:, s]
                    zn = Zb[:, s + 1]
                    B.tt("dve", t1[:], A8[:, 0], zs, ALU.mult, [r_A8, r_Z], [r_Z])
                    B.tt("dve", t2[:], A8[:, 1], Zb[:, s, ::-1, :], ALU.mult, [r_A8, r_Z], [r_Z])
                    B.tt("dve", zn, zn, t1[:], ALU.add, [r_Z], [r_Z])
                    B.tt("dve", zn, zn, t2[:], ALU.add, [r_Z], [r_Z])
                B.cp("dve", Sb[:], Zb[:, 0:64], [r_Z], [r_Sb])
                B.cp("dve", Zb[:, 0], Zb[:, 64], [r_Z], [r_Z])
                for fc in range(8):
                    bk = 2 + fc % 4
                    for g8 in range(8):
                        g = fc * 8 + g8
                        gh, pr = g // 32, g % 32
                        Rr = slice(gh * 64, (gh + 1) * 64)
                        outp = psum[bk][:, g8 * 64:(g8 + 1) * 64]
                        B.mm(outp, T0b[:, g, :], V[:, g, ns], True, False, [r_T0b, r_V], [rps[bk]])
                        B.mm(outp, Cmb[Rr, pr, 0, :], Sb[Rr, :, 0, pr], False, False, [r_Cmb, r_Sb], [rps[bk]])
                        B.mm(outp, Cmb[Rr, pr, 1, :], Sb[Rr, :, 1, pr], False, True, [r_Cmb, r_Sb], [rps[bk]])
                    yp = psum[bk][:, :]
                    dst = Gb[:, fc * 8:(fc + 1) * 8, ns]
                    B.act(dst, yp.rearrange("p (g n) -> p g n", g=8), AF.Gelu_apprx_tanh, [],
                          [rps[bk], r_Gb])
            ctx["dump"]("Gb", GT[:], [r_Gb])
            B.barrier()
        B.barrier()
    if ctx.get("stop") == "C":
        return []
    finals = []
    with ExitStack() as es2:
        E2 = es2.enter_context
        gTb = E2(nc.sbuf_tensor("gTb", [128, 8, S], BF16))
        r_gT = B.R("gTb")
        r_G2 = B.R("G2T")
        with ExitStack() as es:
            E = es.enter_context
            Es = E(nc.sbuf_tensor("Esel2", [128, 64, 128], BF16))
            r_Es = B.R("Esel2")
            d_es = ctx["new_dsem"]("d_es2")
            B.S.dma("sp", Es[:], dr["esel"], d_es, (), [r_Es])
            for fc in range(8):
                for ip in range(4):
                    bk = (fc * 4 + ip) % 4
                    for ii in range(2):
                        i0 = ip * 2 + ii
                        for g8 in range(8):
                            B.mm(psum[bk][:, ii * 256:(ii + 1) * 256], Es[:, i0 * 8 + g8, :],
                                 Gb[:, fc * 8 + g8, :], g8 == 0, g8 == 7, [r_Es, r_Gb], [rps[bk]])
                    B.cp("dve" if ip % 2 == 0 else "act_copy", gTb[:, fc, ip * 512:(ip + 1) * 512],
                         psum[bk][:, :], [], [rps[bk], r_gT]) if False else \
                        B.cp("dve", gTb[:, fc, ip * 512:(ip + 1) * 512], psum[bk][:, :], [], [rps[bk], r_gT])
            ctx["dump"]("gTb", gTb[:], [r_gT])
            B.barrier()
        if ctx.get("stop") == "D1":
            return []
        with ExitStack() as es:
            E = es.enter_context
            wsg = WStream(ctx, E, 2, "wg1s")
            wsz = WStream(ctx, E, 2, "wz1s")
            sg = E(nc.sbuf_tensor("sg", [128, 2, 512], F32))
            sz = E(nc.sbuf_tensor("sz", [128, 2, 512], F32))
            r_sg = [B.R("sg0"), B.R("sg1")]
            r_sz = [B.R("sz0"), B.R("sz1")]
            k = 0
            for fo in range(8):
                if fo % 2 == 0:
                    wg, r_wg = wsg.load(dr["wg1"][:, fo * 128:fo * 128 + 256])
                    wz, r_wz = wsz.load(dr["w1"][:, 1024 + fo * 128:1024 + fo * 128 + 256])
                colo = (fo % 2) * 128
                for s in range(4):
                    sl = k % 2
                    k += 1
                    ba, bb_ = (0, 1) if sl == 0 else (2, 3)
                    cs = slice(s * 512, (s + 1) * 512)
                    for c in range(8):
                        B.mm(psum[ba][:, :], wg[:, c, colo:colo + 128], gTb[:, c, cs], c == 0, c == 7,
                             [r_wg, r_gT], [rps[ba]])
                    for c in range(8):
                        B.mm(psum[bb_][:, :], wz[:, c, colo:colo + 128], hT[:, c, cs], c == 0, c == 7,
                             [r_wz, r_hTall], [rps[bb_]])
                    B.act(sg[:, sl, :], psum[ba][:, :], AF.Sigmoid, [r_bg], [rps[ba], r_sg[sl]],
                          bias=small[:, 16 + fo:17 + fo])
                    B.act(sz[:, sl, :], psum[bb_][:, :], AF.Sigmoid, [], [rps[bb_], r_sz[sl]])
                    B.tt("pool", sg[:, sl, :], sg[:, sl, :], sz[:, sl, :], ALU.mult, [r_sz[sl]], [r_sg[sl]])
                    B.tt("dve", sg[:, sl, :], sg[:, sl, :], psum[bb_][:, :], ALU.mult, [], [rps[bb_], r_sg[sl]])
                    dstn = GT[:, fo, :].rearrange("p (n j) -> p j n", j=8)[:, 2 * s:2 * s + 2, :]
                    B.tt("dve", dstn, sg[:, sl, :].rearrange("p (j n) -> p j n", j=2),
                         gTb[:, fo, cs].rearrange("p (j n) -> p j n", j=2), ALU.mult,
                         [r_sg[sl], r_gT], [r_G2])
            ctx["dump"]("G2T", GT[:], [r_G2])
            B.barrier()
        if ctx.get("stop") == "D2":
            return []
        with ExitStack() as es:
            E = es.enter_context
            ws = WStream(ctx, E, 2, "wo1s")
            WO = E(nc.sbuf_tensor("WO1", [128, 8, D], BF16))
            gpo = E(nc.sbuf_tensor("gpo1", [128, D], F32))
            r_WO = [B.R("WO1_%d" % i) for i in range(4)]
            r_gpo = B.R("gpo1")
            B.S.dma("sp", gpo[:], dr["gpost1"], d_par, (), [r_gpo])
            for wb in range(4):
                ws.load(dr["wo1"][:, wb * 256:(wb + 1) * 256], WO[:, :, wb * 256:(wb + 1) * 256], r_WO[wb])

            tokap = None
            finals = out_proj_tail(ctx, E, WO, r_WO, gpo, r_gpo, dr["x1"], dr["out"], GT, r_G2, tokap, "t1",
                                   src_res=ctx.get("r_x1"))
            B.barrier()
    return finals


_PROG = {}


def _prog(mode):
    if mode not in _PROG:
        _PROG[mode] = build_program(mode)
    return _PROG[mode]


def _l0_inputs(inp, b):
    f = np.float32
    return {
        "x": np.ascontiguousarray(inp["x"][b], dtype=f),
        "w0": inp["_w0"], "wo0": inp["_wo0"], "gpre0": inp["_gpre0"], "gpost0": inp["_gpost0"],
        "bias0": inp["_bias0"], "ident": inp["_ident"],
    }


def _prep_l0(inputs):
    f = np.float32
    p = {}
    p["_w0"] = _perm_w_in(np.asarray(inputs["attn_w_in"][0], f))
    p["_wo0"] = np.ascontiguousarray(np.asarray(inputs["attn_w_out"][0], f))
    p["_gpre0"] = np.ascontiguousarray(np.asarray(inputs["attn_pre_norm"][0], f).reshape(8, 128).T)
    p["_gpost0"] = np.ascontiguousarray(np.broadcast_to(np.asarray(inputs["attn_post_norm"][0], f), (128, D)))
    p["_bias0"] = _bias_tiles(np.asarray(inputs["rel_bias"], f))
    p["_ident"] = np.eye(128, dtype=f)
    return p


def run_l0(inputs, debug=False):
    inp = dict(inputs)
    inp.update(_prep_l0(inputs))
    inp["x"] = np.asarray(inputs["x"], np.float32)
    nc = build_program("l0", debug=debug)
    in_maps = [_l0_inputs(inp, b) for b in range(8)]
    res = run_bass_kernel_spmd(nc, in_maps, core_ids=list(range(8)))
    if debug:
        return np.stack([r["x1"] for r in res.results], axis=0), res.results[0]
    return np.stack([r["x1"] for r in res.results], axis=0)


def _prep_l1(inputs):
    f = np.float32
    p = {}
    p["w1"] = np.ascontiguousarray(np.asarray(inputs["ssm_w_in"][0], f))
    p["wg1"] = np.ascontiguousarray(np.asarray(inputs["ssm_w_glu"][0], f))
    p["wo1"] = np.ascontiguousarray(np.asarray(inputs["ssm_w_out"][0], f))
    p["gpre1"] = np.ascontiguousarray(np.asarray(inputs["ssm_pre_norm"][0], f).reshape(8, 128).T)
    p["gpost1"] = np.ascontiguousarray(np.broadcast_to(np.asarray(inputs["ssm_post_norm"][0], f), (128, D)))
    p["bglu"] = np.ascontiguousarray(np.asarray(inputs["ssm_b_glu"][0], f).reshape(8, 128).T)

    def pl(a):
        a = np.asarray(a, f)
        a = a.reshape((2, 32, 64) + a.shape[2:])
        a = np.moveaxis(a, 2, 1)
        return np.ascontiguousarray(a.reshape((128, 32) + a.shape[3:]))
    are = pl(inputs["ssm_a_re"][0])
    aim = pl(inputs["ssm_a_im"][0])
    ldt = pl(np.broadcast_to(np.asarray(inputs["ssm_log_dt"][0], f)[:, None], (64, 64)))
    p["pl_a"] = np.ascontiguousarray(np.stack([are, aim, ldt], axis=1))
    bre = pl(inputs["ssm_b_re"][0])
    bim = pl(inputs["ssm_b_im"][0])
    cre = pl(np.swapaxes(np.asarray(inputs["ssm_c_re"][0], f), 1, 2))
    cim = pl(np.swapaxes(np.asarray(inputs["ssm_c_im"][0], f), 1, 2))
    p["pl_bc"] = np.ascontiguousarray(np.stack([bre, bim, cre, cim], axis=1))
    dvec = np.asarray(inputs["ssm_d"][0], f).reshape(64, 16)
    p["dcol"] = np.ascontiguousarray(np.tile(dvec.T, (8, 1)))
    jj = np.arange(128) // 16
    p["mask8"] = (jj[:, None] <= jj[None, :]).astype(f)
    k = np.arange(128)
    es = np.zeros((128, 64, 128), f)
    for a in range(8):
        for b in range(8):
            es[:, a * 8 + b, :] = ((k[:, None] // 16 == a) & (k[None, :] // 16 == b)
                                   & (k[:, None] % 16 == k[None, :] % 16))
    p["esel"] = es.astype(ml_dtypes.bfloat16)
    p["ident"] = np.eye(128, dtype=f)
    return p


def run_l1(inputs, x1, debug=False, stop=None):
    p = _prep_l1(inputs)
    nc = build_program("l1", debug=debug, stop=stop)
    in_maps = []
    for b in range(8):
        m = dict(p)
        m["x1"] = np.ascontiguousarray(x1[b], dtype=np.float32)
        in_maps.append(m)
    res = run_bass_kernel_spmd(nc, in_maps, core_ids=list(range(8)))
    out = np.stack([r["out"] for r in res.results], axis=0)
    if debug:
        return out, res.results[0]
    return out


FUSED = True


def kernel(**inputs):
    p0 = _prep_l0(inputs)
    p1 = _prep_l1(inputs)
    x = np.asarray(inputs["x"], np.float32)
    shared0 = {"w0": p0["_w0"], "wo0": p0["_wo0"], "gpre0": p0["_gpre0"], "gpost0": p0["_gpost0"],
               "bias0": p0["_bias0"], "ident": p0["_ident"]}
    if FUSED:
        nc = _prog("fused")
        shared = dict(shared0)
        shared.update(p1)
        in_maps = []
        for b in range(8):
            m = dict(shared)
            m["x"] = np.ascontiguousarray(x[b])
            in_maps.append(m)
        res = run_bass_kernel_spmd(nc, in_maps, core_ids=list(range(8)))
        return np.stack([np.asarray(r["out"], np.float32) for r in res.results], axis=0)
    in_maps = []
    for b in range(8):
        m = dict(shared0)
        m["x"] = np.ascontiguousarray(x[b])
        in_maps.append(m)
    res = run_bass_kernel_spmd(_prog("l0"), in_maps, core_ids=list(range(8)))
    x1 = [np.asarray(r["x1"], np.float32) for r in res.results]
    in_maps = []
    for b in range(8):
        m = dict(p1)
        m["x1"] = np.ascontiguousarray(x1[b])
        in_maps.append(m)
    res = run_bass_kernel_spmd(_prog("l1"), in_maps, core_ids=list(range(8)))
    return np.stack([np.asarray(r["out"], np.float32) for r in res.results], axis=0)
```

```python
import math
import numpy as np
import ml_dtypes
import concourse.bass as bass
import concourse.mybir as mybir
from concourse.bass_utils import run_bass_kernel_spmd

F32 = mybir.dt.float32
BF16 = mybir.dt.bfloat16
AF = mybir.ActivationFunctionType
ALU = mybir.AluOpType

D = 1024
S = 2048
NT = 16
EPS = 1e-6
GROUPS = ((128, 1), (512, 4), (2048, 16))
NEGM = -30000.0
SSM_G = 64
SSM_P = 64
SSM_C = 16


class Res:
    __slots__ = ("name", "w", "rs", "rd")

    def __init__(self, name):
        self.name = name
        self.w = None
        self.rs = {}
        self.rd = []


class DmaSem:
    def __init__(self, sem):
        self.sem = sem
        self.count = 0


class Op:
    __slots__ = ("eng", "fn", "deps", "need", "seq", "dsem", "dval", "idx", "dreq", "group", "gidx")


class Sched:
    ENGS = ("pe", "act", "dve", "pool", "sp")

    def __init__(self, nc):
        self.nc = nc
        self.ops = {e: [] for e in self.ENGS}
        self.allops = []
        self.cur_group = None
        self.ngroups = 0

    def begin_group(self):
        self.ngroups += 1
        self.cur_group = self.ngroups

    def end_group(self):
        self.cur_group = None

    def _mk(self, eng, fn, reads, writes, isdma=False):
        o = Op()
        o.eng = eng
        o.fn = fn
        o.need = False
        o.seq = 0
        o.dsem = None
        o.dval = 0
        deps = []
        for r in reads:
            if r.w is not None:
                deps.append(r.w)
        for r in writes:
            if r.w is not None:
                deps.append(r.w)
            deps.extend(r.rs.values())
            deps.extend(r.rd)
        o.deps = deps
        o.dreq = {}
        for p in deps:
            if p.dsem is not None:
                o.dreq[id(p.dsem)] = (p.dsem.sem, p.dsem.count)
        for r in reads:
            if isdma:
                r.rd.append(o)
            else:
                r.rs[eng] = o
        for r in writes:
            r.w = o
            r.rs = {}
            r.rd = []
        o.idx = len(self.ops[eng])
        o.group = self.cur_group
        o.gidx = len(self.allops)
        self.ops[eng].append(o)
        self.allops.append(o)
        return o

    def op(self, eng, fn, reads=(), writes=()):
        return self._mk(eng, fn, list(reads), list(writes))

    def dma(self, eng, out, in_, dsem, reads=(), writes=()):
        def fn(e):
            return e.dma_start(out=out, in_=in_)
        o = self._mk(eng, fn, list(reads), list(writes), isdma=True)
        dsem.count += 16
        o.dsem = dsem
        o.dval = dsem.count
        return o


def _t5_bucket_np(dist):
    max_exact = 16
    n = np.maximum(dist, 1).astype(np.float32)
    large = max_exact + (np.log(n / np.float32(max_exact)) / np.float32(math.log(2048 / max_exact))
                         * np.float32(32 - max_exact)).astype(np.int32)
    large = np.minimum(large, 31)
    return np.where(dist < max_exact, dist, large)


def _bias_tiles(rel_bias):
    j = np.arange(128)[:, None]
    q = np.arange(256)[None, :]
    delta = q - j
    valid = (delta >= 0) & (delta <= 128)
    out = np.empty((8, 128, 6, 256), np.float32)
    for g, (_, dil) in enumerate(GROUPS):
        bucket = _t5_bucket_np(np.maximum(delta, 0) * dil)
        for h in range(16):
            t = rel_bias[bucket, h]
            t = np.where(valid, t, np.float32(NEGM)).astype(np.float32)
            out[h // 2, :, g * 2 + (h % 2), :] = t
    return out


def _perm_w_in(w):
    out = np.empty((8, D, 1280), np.float32)
    for hp in range(8):
        cs = slice(hp * 128, (hp + 1) * 128)
        blk = []
        for g in range(3):
            blk.append(w[:, (0 * 3 + g) * 1024:(0 * 3 + g + 1) * 1024][:, cs])
            blk.append(w[:, (1 * 3 + g) * 1024:(1 * 3 + g + 1) * 1024][:, cs])
        for g in range(3):
            blk.append(w[:, (2 * 3 + g) * 1024:(2 * 3 + g + 1) * 1024][:, cs])
        blk.append(w[:, 9216:10240][:, cs])
        out[hp] = np.concatenate(blk, axis=1)
    return out


class Builder:
    def __init__(self, nc):
        self.nc = nc
        self.S = Sched(nc)
        self.dsems = []
        self.allres = []
        self.last_barrier = None
        self.scratch = None

    def R(self, name):
        r = Res(name)
        r.w = self.last_barrier
        self.allres.append(r)
        return r

    def barrier(self):
        scr = self.scratch

        def fn(e):
            return e.memset(scr, 0.0)
        o = self.S.op("pool", fn, (), list(self.allres))
        self.last_barrier = o
        return o

    def mm(self, out, lhsT, rhs, start, stop, reads, writes):
        def fn(e):
            return e.matmul(out, lhsT, rhs, start=start, stop=stop)
        return self.S.op("pe", fn, reads, writes)

    def tr(self, out, in_, ident, reads, writes):
        def fn(e):
            return e.transpose(out, in_, ident)
        return self.S.op("pe", fn, reads, writes)

    def act(self, out, in_, func, reads, writes, scale=None, bias=None, accum_out=None):
        kw = {}
        if scale is not None:
            kw["scale"] = scale
        if bias is not None:
            kw["bias"] = bias
        if accum_out is not None:
            kw["accum_out"] = accum_out

        def fn(e):
            return e.activation(out, in_, func, **kw)
        return self.S.op("act", fn, reads, writes)

    def tt(self, eng, out, in0, in1, op, reads, writes):
        def fn(e):
            return e.tensor_tensor(out, in0, in1, op)
        return self.S.op(eng, fn, reads, writes)

    def ts(self, eng, out, in0, s1, op0, reads, writes, s2=None, op1=None):
        def fn(e):
            if op1 is None:
                return e.tensor_scalar(out, in0, s1, None, op0)
            return e.tensor_scalar(out, in0, s1, s2, op0, op1)
        return self.S.op(eng, fn, reads, writes)

    def stt(self, out, in0, scalar, in1, op0, op1, reads, writes):
        def fn(e):
            return e.scalar_tensor_tensor(out, in0, scalar, in1, op0, op1)
        return self.S.op("dve", fn, reads, writes)

    def cp(self, eng, out, in_, reads, writes):
        def fn(e):
            return e.tensor_copy(out, in_)
        return self.S.op(eng, fn, reads, writes)

    def memset(self, eng, ap, val, writes):
        def fn(e):
            return e.memset(ap, val)
        return self.S.op(eng, fn, (), writes)

    def recip(self, out, in_, reads, writes):
        def fn(e):
            return e.reciprocal(out, in_)
        return self.S.op("dve", fn, reads, writes)


def rms_rstd(B, ss, rs, r_ss, r_rs):
    B.ts("dve", rs, ss, 1.0 / D, ALU.mult, [r_ss], [r_rs], s2=EPS, op1=ALU.add)
    B.act(rs, rs, AF.Sqrt, [r_rs], [r_rs])
    B.recip(rs, rs, [r_rs], [r_rs])


def build_program(mode="fused", debug=False, stop=None):
    nc = bass.Bass("TRN2", target_bir_lowering=False)
    B = Builder(nc)
    dt = nc.dram_tensor
    dr = {}
    do_l0 = mode in ("l0", "fused")
    do_l1 = mode in ("l1", "fused")
    if do_l0:
        dr["x"] = dt("x", [S, D], F32, kind="ExternalInput").ap()
        dr["w0"] = dt("w0", [8, D, 1280], F32, kind="ExternalInput").ap()
        dr["wo0"] = dt("wo0", [D, D], F32, kind="ExternalInput").ap()
        dr["gpre0"] = dt("gpre0", [128, 8], F32, kind="ExternalInput").ap()
        dr["gpost0"] = dt("gpost0", [128, D], F32, kind="ExternalInput").ap()
        dr["bias0"] = dt("bias0", [8, 128, 6, 256], F32, kind="ExternalInput").ap()
    dr["ident"] = dt("ident", [128, 128], F32, kind="ExternalInput").ap()
    if mode == "l0":
        dr["x1"] = dt("x1", [S, D], F32, kind="ExternalOutput").ap()
    elif mode == "l1":
        dr["x1"] = dt("x1", [S, D], F32, kind="ExternalInput").ap()
    else:
        dr["x1"] = dt("x1", [S, D], F32, kind="Internal").ap()
    if do_l1:
        build_l1_decl(dt, dr)
        dr["out"] = dt("out", [S, D], F32, kind="ExternalOutput").ap()

    from contextlib import ExitStack
    with ExitStack() as es:
        E = es.enter_context
        sems = {e: E(nc.semaphore("c_" + e)) for e in ("pe", "act", "dve", "pool", "sp")}

        def new_dsem(name):
            return DmaSem(E(nc.semaphore(name)))

        psum = [E(nc.psum_tensor("ps%d" % i, [128, 512], F32)) for i in range(8)]
        rps = [B.R("psum%d" % i) for i in range(8)]
        hT = E(nc.sbuf_tensor("hT", [128, 8, S], BF16))
        GT = E(nc.sbuf_tensor("GT", [128, 8, S], BF16))
        ident = E(nc.sbuf_tensor("ident_sb", [128, 128], F32))
        ones = None
        small = E(nc.sbuf_tensor("small", [128, 64], F32))
        bscr = E(nc.sbuf_tensor("bscr", [128, 8], F32))
        B.scratch = bscr[:, 0:1]
        r_hT = [B.R("hT%d" % i) for i in range(NT)]
        r_GT = B.R("GT")
        r_ident = B.R("ident")
        r_ones = B.R("ones")
        d_const = new_dsem("d_const")
        B.S.dma("sp", ident[:], dr["ident"], d_const, (), [r_ident])
        dumps = {}
        d_dump = new_dsem("d_dump")

        def dump(name, ap, reads):
            if not debug or name in dumps:
                return
            t = dt("dbg_" + name, list(ap.shape), ap.dtype, kind="ExternalOutput").ap()
            o = B.S.dma("pool", t, ap, d_dump, reads, [])
            fr = B.R("dfin_" + name)
            fr.w = o
            dumps[name] = fr
        ctx = dict(nc=nc, B=B, dr=dr, E=E, new_dsem=new_dsem, psum=psum, rps=rps, hT=hT, GT=GT, dump=dump,
                   stop=stop, ident=ident, ones=ones, small=small, r_hT=r_hT, r_GT=r_GT,
                   r_ident=r_ident, r_ones=r_ones, sems=sems)
        if mode == "fused":
            ctx["r_x1"] = [B.R("x1_%d" % i) for i in range(NT)]
        if do_l1:
            l1_prep_stage1(ctx, E)
        else:
            ctx["pump"] = lambda n=None: None
        with nc.Block() as block:
            final = []
            if do_l0:
                final = build_l0(ctx)
            if do_l1:
                final = build_l1(ctx)
            B.S.op("sp", None, list(final) + list(dumps.values()), ())
            emit_all(B, block, sems)
    return nc


def emit_all(B, block, sems):
    S_ = B.S
    nc = B.nc
    for o in S_.allops:
        for p in o.deps:
            if p.dsem is None and not (p.eng == "pe" and o.eng == "pe"):
                p.need = True
    for e in S_.ENGS:
        c = 0
        for o in S_.ops[e]:
            if o.need:
                c += 1
                o.seq = c

    def reqs_of(o, e, before=None):
        req = {}
        for p in o.deps:
            if before is not None and p.gidx >= before:
                continue
            if p.dsem is not None:
                key = ("d", id(p.dsem))
                v = o.dreq[id(p.dsem)]
                if req.get(key, (None, 0))[1] < v[1]:
                    req[key] = v
            else:
                if p.eng == "pe" and e == "pe":
                    continue
                key = ("e", p.eng)
                if req.get(key, (None, 0))[1] < p.seq:
                    req[key] = (sems[p.eng], p.seq)
        return req

    def run(e, eo):
        waited = {}
        ops = S_.ops[e]
        seen_groups = set()
        for k, o in enumerate(ops):
            req = reqs_of(o, e)
            if o.group is not None and o.group not in seen_groups:
                seen_groups.add(o.group)
                j = k + 1
                while j < len(ops) and ops[j].group == o.group:
                    for key, v in reqs_of(ops[j], e, before=o.gidx).items():
                        if req.get(key, (None, 0))[1] < v[1]:
                            req[key] = v
                    j += 1
            for key, (sem, val) in req.items():
                if waited.get(key, 0) >= val:
                    continue
                waited[key] = val
                eo.wait_ge(sem, val)
            if o.fn is None:
                continue
            ins = o.fn(eo)
            if o.dsem is not None:
                ins.then_inc(o.dsem.sem, 16)
            elif o.need:
                ins.then_inc(sems[e], 1)

    @block.tensor
    def _(eo):
        run("pe", eo)

    @block.scalar
    def _(eo):
        run("act", eo)

    @block.vector
    def _(eo):
        run("dve", eo)

    @block.gpsimd
    def _(eo):
        run("pool", eo)

    @block.sync
    def _(eo):
        run("sp", eo)


def make_hT(ctx, src, scope, gp, r_gp, perm8=False, src_res=None, name="a"):
    nc, B, psum, rps, hT, ident = ctx["nc"], ctx["B"], ctx["psum"], ctx["rps"], ctx["hT"], ctx["ident"]
    E = scope
    NS = 3
    xt = E(nc.sbuf_tensor("xt_" + name, [128, NS, D], F32))
    xs = E(nc.sbuf_tensor("xs_" + name, [128, NS, D], F32))
    jk = E(nc.sbuf_tensor("jk_" + name, [128, D], F32))
    st = E(nc.sbuf_tensor("st_" + name, [128, NS, 4], F32))
    r_xt = [B.R("xt%d" % i) for i in range(NS)]
    r_xs = [B.R("xs%d" % i) for i in range(NS)]
    r_st = [B.R("st%d" % i) for i in range(NS)]
    r_jk = B.R("jk")
    d_x = [ctx["new_dsem"]("d_x%s%d" % (name, i)) for i in range(NS)]

    def stage1(tt):
        sl = tt % NS
        B.S.dma("sp", xt[:, sl, :], src[tt * 128:(tt + 1) * 128, :], d_x[sl],
                [src_res[tt]] if src_res else (), [r_xt[sl]])
        B.act(jk[:, :], xt[:, sl, :], AF.Square, [r_xt[sl]], [r_jk, r_st[sl]], accum_out=st[:, sl, 0:1])
        B.ts("dve", st[:, sl, 1:2], st[:, sl, 0:1], 1.0 / D, ALU.mult, [r_st[sl]], [r_st[sl]],
             s2=EPS, op1=ALU.add)

    def stage2(tt):
        sl = tt % NS
        B.act(st[:, sl, 1:2], st[:, sl, 1:2], AF.Sqrt, [r_st[sl]], [r_st[sl]])
        B.recip(st[:, sl, 1:2], st[:, sl, 1:2], [r_st[sl]], [r_st[sl]])
        B.ts("dve", xs[:, sl, :], xt[:, sl, :], st[:, sl, 1:2], ALU.mult,
             [r_xt[sl], r_st[sl]], [r_xs[sl]])

    def stage3(tt):
        sl = tt % NS
        for half in range(2):
            bk = half + 2 * (tt % 2)
            for j in range(4):
                c = half * 4 + j
                B.tr(psum[bk][:, j * 128:(j + 1) * 128], xs[:, sl, c * 128:(c + 1) * 128], ident[:],
                     [r_xs[sl], ctx["r_ident"]], [rps[bk]])
            for j in range(4):
                c = half * 4 + j
                if perm8:
                    dst = hT[:, c, :].rearrange("p (j n) -> p j n", j=8)[:, :, 16 * tt:16 * tt + 16]
                    srcp = psum[bk][:, j * 128:(j + 1) * 128].rearrange("p (n j) -> p j n", j=8)
                else:
                    dst = hT[:, c, tt * 128:(tt + 1) * 128]
                    srcp = psum[bk][:, j * 128:(j + 1) * 128]
                wr = ctx["r_hT"] if perm8 else [ctx["r_hT"][tt]]
                B.ts("dve", dst, srcp, gp[:, c:c + 1], ALU.mult, [r_gp], [rps[bk]] + wr)

    for t in range(NT + 2):
        if t < NT:
            stage1(t)
        if 0 <= t - 1 < NT:
            stage2(t - 1)
        if 0 <= t - 2 < NT:
            stage3(t - 2)


class WStream:
    def __init__(self, ctx, scope, nslot=2, name="ws", cols=256):
        nc, B = ctx["nc"], ctx["B"]
        self.ctx = ctx
        self.n = nslot
        self.wst = scope(nc.sbuf_tensor(name + "_st", [128, nslot, 8, cols], F32))
        self.wbf = scope(nc.sbuf_tensor(name + "_bf", [128, nslot, 8, cols], BF16))
        self.r_st = [B.R(name + "st%d" % i) for i in range(nslot)]
        self.r_bf = [B.R(name + "bf%d" % i) for i in range(nslot)]
        self.ds = [ctx["new_dsem"]("d_" + name + str(i)) for i in range(nslot)]
        self.k = 0

    def dma(self, src):
        B = self.ctx["B"]
        sl = self.k % self.n
        self.k += 1
        B.S.dma("sp", self.wst[:, sl], src.rearrange("(c p) n -> p c n", p=128), self.ds[sl],
                (), [self.r_st[sl]])
        return sl

    def cast(self, sl):
        B = self.ctx["B"]
        B.act(self.wbf[:, sl], self.wst[:, sl], AF.Copy, [self.r_st[sl]], [self.r_bf[sl]])
        return self.wbf[:, sl], self.r_bf[sl]

    def load(self, src, dst=None, r_dst=None):
        B = self.ctx["B"]
        sl = self.k % self.n
        self.k += 1
        B.S.dma("sp", self.wst[:, sl], src.rearrange("(c p) n -> p c n", p=128), self.ds[sl],
                (), [self.r_st[sl]])
        if dst is None:
            dst, r_dst = self.wbf[:, sl], self.r_bf[sl]
        B.act(dst, self.wst[:, sl], AF.Copy, [self.r_st[sl]], [r_dst])
        return dst, r_dst


def grp_tokens(g, bi):
    d = GROUPS[g][1]
    nbr = 16 // d
    r = bi // nbr
    m0 = (bi % nbr) * 128
    return d * m0 + r, d


def build_l0(ctx):
    nc, B, dr, psum, rps = ctx["nc"], ctx["B"], ctx["dr"], ctx["psum"], ctx["rps"]
    hT, GT, ones, small = ctx["hT"], ctx["GT"], ctx["ones"], ctx["small"]
    r_hT, r_GT = ctx["r_hT"], ctx["r_GT"]
    from contextlib import ExitStack
    d_par = ctx["new_dsem"]("d_par0")
    r_gp = B.R("gpre0")
    B.S.dma("sp", small[:, 0:8], dr["gpre0"], d_par, (), [r_gp])
    with ExitStack() as es:
        make_hT(ctx, dr["x"], es.enter_context, small[:, 0:8], r_gp)
        B.barrier()
    allhT = list(r_hT)
    ctx["dump"]("hT", hT[:], allhT)
    with ExitStack() as es:
        E = es.enter_context
        ws = WStream(ctx, E, 2, "w0s", cols=128)
        QK = E(nc.sbuf_tensor("QK", [128, 6, S], BF16))
        VA0 = E(nc.sbuf_tensor("VA0", [128, 48, 65], BF16))
        VA1 = E(nc.sbuf_tensor("VA1", [128, 48, 128], BF16))
        SZ = E(nc.sbuf_tensor("SZ", [128, S], F32))
        BT = E(nc.sbuf_tensor("BT", [128, 2, 6, 256], F32))
        OA = E(nc.sbuf_tensor("OA", [128, 2, S], F32))
        LT = E(nc.sbuf_tensor("LT", [128, 2, 2, 256], F32))
        PT = E(nc.sbuf_tensor("PT", [128, 5, 2, 256], BF16))
        VT = E(nc.sbuf_tensor("VT", [128, 3, S], BF16))
        RHL = E(nc.sbuf_tensor("RHL", [128, 2, S], BF16))
        identb = E(nc.sbuf_tensor("identb", [128, 128], BF16))
        onesb = E(nc.sbuf_tensor("onesb", [128, 128], BF16))
        r_VT = [B.R("VT%d" % i) for i in range(3)]
        r_RHL = B.R("RHL")
        r_cb = B.R("constb")
        B.cp("pool", identb[:], ctx["ident"][:], [ctx["r_ident"]], [r_cb])
        B.memset("pool", onesb[:], 1.0, [r_cb])
        r_QK = [B.R("QK%d" % i) for i in range(6)]
        r_VA = [B.R("VA%d" % i) for i in range(3)]
        r_SZ = B.R("SZ")
        r_BT = [B.R("BT0"), B.R("BT1")]
        r_OA = [B.R("OA0"), B.R("OA1")]
        r_LT = [B.R("LT%d" % i) for i in range(2)]
        r_PT = [B.R("PT%d" % i) for i in range(5)]
        d_bt = [ctx["new_dsem"]("d_bt0"), ctx["new_dsem"]("d_bt1")]
        B.memset("pool", VA0[:], 1.0, r_VA)
        B.memset("pool", VA1[:], 0.0, r_VA)
        B.memset("pool", VA1[:, :, 0:1], 1.0, r_VA)
        pb = [0]

        def proj_bank():
            pb[0] ^= 1
            return pb[0]

        stb = [0]
        fb = [0]
        dq = []
        pre = {}
        wdma = {}

        def wsrc(h, wb):
            return dr["w0"][h][:, wb * 128:(wb + 1) * 128]
        for hp in range(8):
            bsl = hp % 2
            B.S.dma("sp", BT[:, bsl], dr["bias0"][hp], d_bt[bsl], (), [r_BT[bsl]])
            for wb in range(10):
                key = (hp, wb)
                if key not in pre:
                    if key not in wdma:
                        wdma[key] = ws.dma(wsrc(*key))
                    pre[key] = ws.cast(wdma.pop(key))
                wbf, r_w = pre.pop(key)
                nkey = (hp, wb + 1) if wb < 9 else ((hp + 1, 0) if hp + 1 < 8 else None)
                if nkey is not None and nkey not in wdma and nkey not in pre:
                    wdma[nkey] = ws.dma(wsrc(*nkey))

                def at_s2(nkey=nkey):
                    if nkey is not None and nkey in wdma:
                        pre[nkey] = ws.cast(wdma.pop(nkey))
                for half_ in range(1):
                    colo = 0
                    kind = ("q", "k", "q", "k", "q", "k", "v0", "v1", "v2", "z")[wb]
                    half = wb % 2
                    if kind in ("q", "k", "z"):
                        g = wb // 2 if wb < 6 else 0
                        d = GROUPS[g][1] if kind != "z" else 1
                        for s in range(4):
                            if s == 2:
                                at_s2()
                            bk = proj_bank()
                            for c in range(8):
                                B.mm(psum[bk][:, :], wbf[:, c, colo:colo + 128],
                                     hT[:, c, s * 512:(s + 1) * 512], c == 0, c == 7,
                                     [r_w] + allhT, [rps[bk]])
                            if kind == "z":
                                B.act(SZ[:, s * 512:(s + 1) * 512], psum[bk][:, :], AF.Silu,
                                      [], [rps[bk], r_SZ])
                            else:
                                idx = 2 * g + half
                                L = S // d
                                dst = QK[:, idx, :].rearrange("p (r m) -> p r m", r=d)[
                                    :, :, (512 // d) * s:(512 // d) * (s + 1)]
                                srcp = psum[bk][:, :].rearrange("p (m r) -> p r m", r=d)
                                B.act(dst, srcp, AF.Copy, [], [rps[bk], r_QK[idx]],
                                      scale=(0.125 if kind == "q" else 1.0))
                    else:
                        g = int(kind[1])
                        for s in range(4):
                            if s == 2:
                                at_s2()
                            bk = proj_bank()
                            for c in range(8):
                                B.mm(psum[bk][:, :], wbf[:, c, colo:colo + 128],
                                     hT[:, c, s * 512:(s + 1) * 512], c == 0, c == 7,
                                     [r_w] + allhT, [rps[bk]])
                            B.act(VT[:, g, s * 512:(s + 1) * 512], psum[bk][:, :], AF.Copy, [],
                                  [rps[bk], r_VT[g]])
                        for quad in range(4):
                            bk = proj_bank()
                            pb16 = psum[bk][:, :].bitcast(BF16)
                            for j in range(4):
                                bi = quad * 4 + j
                                o, st = grp_tokens(g, bi)
                                B.tr(pb16[:, j * 128:(j + 1) * 128], VT[:, g, o:o + 127 * st + 1:st], identb[:],
                                     [r_VT[g], r_cb], [rps[bk]])
                            pv = pb16[:, 0:512].rearrange("p (j c) -> p j c", j=4)
                            b0 = g * 16 + quad * 4
                            B.cp("dve", VA0[:, b0:b0 + 4, 0:64], pv[:, :, 0:64], [], [rps[bk], r_VA[g]])
                            B.cp("dve", VA1[:, b0:b0 + 4, 64:128], pv[:, :, 64:128], [], [rps[bk], r_VA[g]])
                ctx["pump"](4)
                if wb < 7:
                    while dq:
                        f = dq.pop(0)
                        if f is not None:
                            f()
                            break
                if wb == 8:
                    while dq:
                        f = dq.pop(0)
                        if f is not None:
                            f()
            its = [(hh, g, m) for hh in range(2) for g in range(3) for m in range(8)]
            st_info = {}

            def nq_of(g, kb):
                nbr = 16 // GROUPS[g][1]
                return 256 if (kb + 1) % nbr != 0 else 128

            def issue_st(i):
                hh, g, m = its[i]
                bk = (0, 1, 2, 7)[stb[0] % 4]
                stb[0] += 1
                rows = slice(hh * 64, (hh + 1) * 64)
                for t in range(2):
                    kb = 2 * m + t
                    nq = nq_of(g, kb)
                    B.mm(psum[bk][:, t * 256:t * 256 + nq], QK[rows, 2 * g + 1, kb * 128:(kb + 1) * 128],
                         QK[rows, 2 * g, kb * 128:kb * 128 + nq], True, True,
                         [r_QK[2 * g], r_QK[2 * g + 1]], [rps[bk]])
                ls = i % 2
                ps_ = i % 5
                bias2 = BT[:, bsl, g * 2 + hh, :].unsqueeze(1).to_broadcast([128, 2, 256])
                B.tt("dve", LT[:, ls], psum[bk][:, :].rearrange("p (t q) -> p t q", t=2), bias2, ALU.add,
                     [r_BT[bsl]], [rps[bk], r_LT[ls]])
                B.act(PT[:, ps_], LT[:, ls], AF.Exp, [r_LT[ls]], [r_PT[ps_]])
                st_info[i] = ps_

            def issue_pv(i):
                hh, g, m = its[i]
                d = GROUPS[g][1]
                nbr = 16 // d
                M = 65 if hh == 0 else 128
                VA = VA0 if hh == 0 else VA1
                ps_ = st_info[i]
                for t in range(2):
                    kb = 2 * m + t
                    has_prev = kb % nbr != 0
                    if has_prev:
                        pslot, pt = (ps_, 0) if t == 1 else (st_info[i - 1], 1)
                    if g == 0:
                        outs = [(3 + kb // 4, slice((kb % 4) * 128, (kb % 4) * 128 + 128), slice(0, 128))]
                    elif g == 1:
                        r, nn = kb // 4, kb % 4
                        outs = [(3 + nn, slice(r, 512, 4), slice(0, 128))]
                    else:
                        outs = [(3 + sub, slice(kb, 512, 16), slice(32 * sub, 32 * sub + 32)) for sub in range(4)]
                    for (bo, ocols, qcols) in outs:
                        outp = psum[bo][0:M, ocols]
                        first = (g == 0 and kb % 4 == 0)
                        if has_prev:
                            pq = slice(128 + qcols.start, 128 + qcols.stop)

                            def f1(e, outp=outp, a=VA[:, g * 16 + kb - 1, 0:M], b=PT[:, pslot, pt, pq], first=first):
                                return e.matmul(outp, a, b, start=first, stop=False, skip_group_check=True)
                            B.S.op("pe", f1, [r_VA[g], r_PT[pslot]], [rps[bo]])
                            first = False

                        last = (g == 2 and kb == 15)

                        def f2(e, outp=outp, a=VA[:, g * 16 + kb, 0:M], b=PT[:, ps_, t, qcols], first=first, last=last):
                            return e.matmul(outp, a, b, start=first, stop=last, skip_group_check=True)
                        B.S.op("pe", f2, [r_VA[g], r_PT[ps_]], [rps[bo]])
                if g == 2 and m == 7:
                    for span in range(4):
                        if span % 2 == 0:
                            B.cp("dve", OA[0:M, hh, span * 512:(span + 1) * 512], psum[3 + span][0:M, :],
                                 [], [rps[3 + span], r_OA[hh]])
                        else:
                            B.act(OA[0:M, hh, span * 512:(span + 1) * 512], psum[3 + span][0:M, :], AF.Copy,
                                  [], [rps[3 + span], r_OA[hh]])
                    finalize(hh)

            def finalize(hh, hp=hp):
                drow = 64 if hh == 0 else 0
                R = slice(hh * 64, (hh + 1) * 64)

                def prepA(s):
                    cs = slice(s * 512, (s + 1) * 512)
                    den = OA[drow:drow + 1, hh, cs]
                    B.act(den, den, AF.Ln, [], [r_OA[hh]])
                    B.act(den, den, AF.Exp, [], [r_OA[hh]], scale=-1.0)

                def prepB(s):
                    cs = slice(s * 512, (s + 1) * 512)
                    den = OA[drow:drow + 1, hh, cs]
                    B.cp("dve", RHL[drow:drow + 1, 0, cs], den, [r_OA[hh]], [r_RHL])
                    B.tt("dve", RHL[drow:drow + 1, 1, cs], den, RHL[drow:drow + 1, 0, cs], ALU.subtract,
                         [r_OA[hh]], [r_RHL])

                def span(s):
                    bk = (0, 1, 2, 7)[stb[0] % 4]
                    stb[0] += 1
                    cs = slice(s * 512, (s + 1) * 512)
                    B.mm(psum[bk][:, :], onesb[drow:drow + 1, :], RHL[drow:drow + 1, 0, cs], True, False,
                         [r_cb, r_RHL], [rps[bk]])
                    B.mm(psum[bk][:, :], onesb[drow:drow + 1, :], RHL[drow:drow + 1, 1, cs], False, True,
                         [r_cb, r_RHL], [rps[bk]])
                    B.tt("dve", OA[R, hh, cs], OA[R, hh, cs], psum[bk][R, :], ALU.mult,
                         [], [rps[bk], r_OA[hh]])
                    B.tt("pool", GT[R, hp, cs], OA[R, hh, cs], SZ[R, cs], ALU.mult,
                         [r_OA[hh], r_SZ], [r_GT])
                for s in range(4):
                    dq.append(lambda s=s: prepA(s))
                    dq.append(None)
                    dq.append(lambda s=s: prepB(s))
                for _ in range(4):
                    dq.append(None)
                for s in range(4):
                    dq.append(lambda s=s: span(s))
                    if s < 3:
                        dq.append(None)

            if hp == 0:
                ctx["dump"]("QK", QK[:], r_QK)
                ctx["dump"]("VA0", VA0[:], r_VA)
                ctx["dump"]("VA1", VA1[:], r_VA)
                ctx["dump"]("SZ", SZ[:], [r_SZ])
                ctx["dump"]("BT", BT[:, 0], [r_BT[0]])
            LOOK = 3
            n = len(its)
            for i in range(min(LOOK, n)):
                issue_st(i)
            for i in range(n):
                B.S.begin_group()
                issue_pv(i)
                if i + LOOK < n:
                    issue_st(i + LOOK)
                B.S.end_group()
                if dq:
                    f = dq.pop(0)
                    if f is not None:
                        f()
            if hp == 7:
                while dq:
                    f = dq.pop(0)
                    if f is not None:
                        f()
        B.barrier()
    ctx["dump"]("GT", GT[:], [r_GT])
    finals = []
    with ExitStack() as es:
        E = es.enter_context
        ws = WStream(ctx, E, 2, "wo0s")
        WO = E(nc.sbuf_tensor("WO0", [128, 8, D], BF16))
        gpo = E(nc.sbuf_tensor("gpo0", [128, D], F32))
        r_WO = [B.R("WO%d" % i) for i in range(4)]
        r_gpo = B.R("gpo0")
        B.S.dma("sp", gpo[:], dr["gpost0"], d_par, (), [r_gpo])
        for wb in range(4):
            ws.load(dr["wo0"][:, wb * 256:(wb + 1) * 256], WO[:, :, wb * 256:(wb + 1) * 256], r_WO[wb])
        finals = out_proj_tail(ctx, E, WO, r_WO, gpo, r_gpo, dr["x"], dr["x1"], GT, r_GT, None, "t0",
                               dst_res=ctx.get("r_x1"))
        B.barrier()
    return finals


def out_proj_tail(ctx, E, WO, r_WO, gpo, r_gpo, xsrc, dst, GT, r_GT, tokap, name, src_res=None, dst_res=None):
    nc, B, psum, rps = ctx["nc"], ctx["B"], ctx["psum"], ctx["rps"]
    xt = E(nc.sbuf_tensor("xt_" + name, [128, 2, D], F32))
    yt = E(nc.sbuf_tensor("yt_" + name, [128, 2, D], F32))
    sq = E(nc.sbuf_tensor("sq_" + name, [128, 512], F32))
    st = E(nc.sbuf_tensor("st_" + name, [128, 2, 4], F32))
    r_xt = [B.R("xt0"), B.R("xt1")]
    r_yt = [B.R("yt0"), B.R("yt1")]
    r_sq = B.R("sq")
    r_st = [B.R("st0"), B.R("st1")]
    d_x = [ctx["new_dsem"]("d_x%s0" % name), ctx["new_dsem"]("d_x%s1" % name)]
    d_o = [ctx["new_dsem"]("d_o%s0" % name), ctx["new_dsem"]("d_o%s1" % name)]
    finals = []
    for tt in range(NT):
        sl = tt % 2
        if tokap is None:
            xrows, orows = xsrc[tt * 128:(tt + 1) * 128, :], dst[tt * 128:(tt + 1) * 128, :]
        else:
            jj, nb = tt // 2, tt % 2
            xrows = xsrc.rearrange("(n j) d -> j n d", j=8)[jj, nb * 128:(nb + 1) * 128, :]
            orows = dst.rearrange("(n j) d -> j n d", j=8)[jj, nb * 128:(nb + 1) * 128, :]
        B.S.dma("sp", xt[:, sl, :], xrows, d_x[sl],
                [src_res[tt]] if src_res else (), [r_xt[sl]])
        banks = (0, 1) if sl == 0 else (2, 3)
        for half in range(2):
            bk = banks[half]
            for c in range(8):
                lhs = GT[:, c, tt * 128:(tt + 1) * 128]
                B.mm(psum[bk][:, :], lhs, WO[:, c, half * 512:(half + 1) * 512], c == 0, c == 7,
                     [r_GT] + r_WO[2 * half:2 * half + 2], [rps[bk]])
            B.act(sq[:, :], psum[bk][:, :], AF.Square, [], [rps[bk], r_sq, r_st[sl]],
                  accum_out=st[:, sl, half:half + 1])
        B.tt("dve", st[:, sl, 2:3], st[:, sl, 0:1], st[:, sl, 1:2], ALU.add, [r_st[sl]], [r_st[sl]])
        rms_rstd(B, st[:, sl, 2:3], st[:, sl, 3:4], r_st[sl], r_st[sl])
        for half in range(2):
            bk = banks[half]
            hs = slice(half * 512, (half + 1) * 512)
            B.stt(yt[:, sl, hs], psum[bk][:, :], st[:, sl, 3:4], gpo[:, hs], ALU.mult, ALU.mult,
                  [r_st[sl], r_gpo], [rps[bk], r_yt[sl]])
        B.tt("pool", yt[:, sl, :], yt[:, sl, :], xt[:, sl, :], ALU.add, [r_xt[sl]], [r_yt[sl]])
        o = B.S.dma("pool", orows, yt[:, sl, :], d_o[sl], [r_yt[sl]],
                    [dst_res[tt]] if dst_res else [])
        fr = B.R("fin%d" % tt)
        fr.w = o
        finals.append(fr)
    return finals


def build_l1_decl(dt, dr):
    EI = "ExternalInput"
    dr["w1"] = dt("w1", [D, 2048], F32, kind=EI).ap()
    dr["wg1"] = dt("wg1", [D, D], F32, kind=EI).ap()
    dr["wo1"] = dt("wo1", [D, D], F32, kind=EI).ap()
    dr["gpre1"] = dt("gpre1", [128, 8], F32, kind=EI).ap()
    dr["gpost1"] = dt("gpost1", [128, D], F32, kind=EI).ap()
    dr["bglu"] = dt("bglu", [128, 8], F32, kind=EI).ap()
    dr["pl_a"] = dt("pl_a", [128, 3, 32], F32, kind=EI).ap()
    dr["pl_bc"] = dt("pl_bc", [128, 4, 32, 16], F32, kind=EI).ap()
    dr["dcol"] = dt("dcol", [128, 64], F32, kind=EI).ap()
    dr["mask8"] = dt("mask8", [128, 128], F32, kind=EI).ap()
    dr["esel"] = dt("esel", [128, 64, 128], BF16, kind=EI).ap()


def cmul(B, outr, outi, ar, ai, br, bi, t1, t2, R, neg_i=False, eng="dve", emit=None, Rw=None):
    if emit is None:
        emit = lambda f: f()
    Rw = R if Rw is None else Rw
    rd = [R] if Rw is R else [R, Rw]
    emit(lambda: B.tt(eng, t1, ar, br, ALU.mult, rd, [Rw]))
    emit(lambda: B.tt(eng, t2, ai, bi, ALU.mult, rd, [Rw]))
    emit(lambda: B.tt(eng, outr, t1, t2, ALU.subtract, rd, [Rw]))
    emit(lambda: B.tt(eng, t1, ar, bi, ALU.mult, rd, [Rw]))
    emit(lambda: B.tt(eng, t2, ai, br, ALU.mult, rd, [Rw]))
    if neg_i:
        emit(lambda: B.tt(eng, t1, t1, t2, ALU.add, rd, [Rw]))
        emit(lambda: B.ts(eng, outi, t1, -1.0, ALU.mult, rd, [Rw]))
    else:
        emit(lambda: B.tt(eng, outi, t1, t2, ALU.add, rd, [Rw]))


def l1_prep_stage1(ctx, E):
    nc, B, dr = ctx["nc"], ctx["B"], ctx["dr"]
    PEN = "dve"
    pa = E(nc.sbuf_tensor("pa", [128, 3, 32], F32))
    pbc = E(nc.sbuf_tensor("pbc", [128, 4, 32, 16], F32))
    W = E(nc.sbuf_tensor("Wk", [128, 16, 32], F32))
    PW = E(nc.sbuf_tensor("PW", [128, 9, 2, 32], F32))
    QW = E(nc.sbuf_tensor("QW", [128, 8, 2, 32], F32))
    bb = E(nc.sbuf_tensor("bbar", [128, 2, 32, 16], F32))
    T1 = E(nc.sbuf_tensor("T1", [128, 32, 16], F32))
    T2 = E(nc.sbuf_tensor("T2", [128, 32, 16], F32))
    RP = B.R("prep")
    d_p = ctx["new_dsem"]("d_prep")
    B.S.dma("sp", pa[:], dr["pl_a"], d_p, (), [RP])
    B.S.dma("sp", pbc[:], dr["pl_bc"], d_p, (), [RP])
    q = []
    emit = q.append
    RK = E(nc.sbuf_tensor("RK", [128, 16], F32))
    for k in range(1, 16):
        emit(lambda k=k: B.memset(PEN, RK[:, k:k + 1], float(k), [RP]))
    emit(lambda: B.memset(PEN, RK[:, 0:1], 1.0, [RP]))
    emit(lambda: B.recip(RK[:], RK[:], [RP], [RP]))
    are, aim, ldt = pa[:, 0, :], pa[:, 1, :], pa[:, 2, :]
    w = lambda k: W[:, k, :]
    t1s, t2s = T1[:, :, 0], T2[:, :, 0]
    tt = lambda o, a, b, op: emit(lambda: B.tt(PEN, o, a, b, op, [RP], [RP]))
    ts = lambda o, a, s1, op0, s2=None, op1=None: emit(lambda: B.ts(PEN, o, a, s1, op0, [RP], [RP], s2=s2, op1=op1))
    ms = lambda o, v: emit(lambda: B.memset(PEN, o, v, [RP]))
    cpy = lambda o, a: emit(lambda: B.cp(PEN, o, a, [RP], [RP]))
    cm = lambda *a, **k: cmul(B, *a, eng=PEN, emit=emit, **k)
    ts(w(1), ldt, 0.125, ALU.mult)
    ms(w(0), 1.0)
    for k in range(10, 0, -1):
        tt(w(0), w(0), w(1), ALU.mult)
        ts(w(0), w(0), RK[:, k:k + 1], ALU.mult, 1.0, ALU.add)
    for _ in range(3):
        tt(w(0), w(0), w(0), ALU.mult)
    ts(w(8), w(0), 1.0 / 16, ALU.mult)
    tt(w(1), are, w(8), ALU.mult)
    tt(w(2), aim, w(8), ALU.mult)
    ms(w(3), 1.0)
    ms(w(4), 0.0)
    for k in range(14, 1, -1):
        cm(w(5), w(6), w(3), w(4), w(1), w(2), t1s, t2s, RP)
        ts(w(3), w(5), RK[:, k:k + 1], ALU.mult, 1.0, ALU.add)
        ts(w(4), w(6), RK[:, k:k + 1], ALU.mult)
    cm(w(5), w(6), w(3), w(4), w(1), w(2), t1s, t2s, RP)
    for _ in range(4):
        ts(w(7), w(5), 2.0, ALU.add)
        cm(w(3), w(4), w(5), w(6), w(7), w(6), t1s, t2s, RP)
        cpy(w(5), w(3))
        cpy(w(6), w(4))
    ms(PW[:, 0, 0, :], 1.0)
    ms(PW[:, 0, 1, :], 0.0)
    ms(QW[:, 0, 0, :], 1.0)
    ms(QW[:, 0, 1, :], 0.0)
    abr, abi = PW[:, 1, 0, :], PW[:, 1, 1, :]
    ts(abr, w(5), 1.0, ALU.add)
    cpy(abi, w(6))
    for k in range(2, 9):
        cm(PW[:, k, 0, :], PW[:, k, 1, :], PW[:, k - 1, 0, :], PW[:, k - 1, 1, :], abr, abi, t1s, t2s, RP)
    tt(w(8), abr, abr, ALU.mult)
    tt(w(9), abi, abi, ALU.mult)
    tt(w(8), w(8), w(9), ALU.add)
    emit(lambda: B.recip(w(9), w(8), [RP], [RP]))
    tt(QW[:, 1, 0, :], abr, w(9), ALU.mult)
    tt(w(8), abi, w(9), ALU.mult)
    ts(QW[:, 1, 1, :], w(8), -1.0, ALU.mult)
    for k in range(2, 8):
        cm(QW[:, k, 0, :], QW[:, k, 1, :], QW[:, k - 1, 0, :], QW[:, k - 1, 1, :],
           QW[:, 1, 0, :], QW[:, 1, 1, :], t1s, t2s, RP)
    cpy(w(10), w(5))
    tt(w(11), are, are, ALU.mult)
    tt(w(12), aim, aim, ALU.mult)
    tt(w(11), w(11), w(12), ALU.add)
    emit(lambda: B.recip(w(12), w(11), [RP], [RP]))
    tt(w(13), w(10), are, ALU.mult)
    tt(w(14), abi, aim, ALU.mult)
    tt(w(13), w(13), w(14), ALU.add)
    tt(w(13), w(13), w(12), ALU.mult)
    tt(w(14), abi, are, ALU.mult)
    tt(w(15), w(10), aim, ALU.mult)
    tt(w(14), w(14), w(15), ALU.subtract)
    tt(w(14), w(14), w(12), ALU.mult)
    bc = lambda ap: ap.unsqueeze(2).to_broadcast([128, 32, 16])
    cm(bb[:, 0], bb[:, 1], bc(w(13)), bc(w(14)), pbc[:, 0], pbc[:, 1], T1[:], T2[:], RP)

    def pump(n=None):
        k = len(q) if n is None else min(n, len(q))
        for _ in range(k):
            q.pop(0)()
    ctx["pump"] = pump
    ctx["prep1"] = dict(pbc=pbc, PW=PW, QW=QW, bb=bb, T1=T1, T2=T2, RP=RP, bc=bc)


def build_l1(ctx):
    from contextlib import ExitStack
    nc, B, dr, psum, rps = ctx["nc"], ctx["B"], ctx["dr"], ctx["psum"], ctx["rps"]
    hT, GT, ident, small = ctx["hT"], ctx["GT"], ctx["ident"], ctx["small"]
    r_ident = ctx["r_ident"]
    PI = math.pi
    d_par = ctx["new_dsem"]("d_par1")
    r_gp = B.R("gpre1")
    r_bg = B.R("bglu")
    B.S.dma("sp", small[:, 8:16], dr["gpre1"], d_par, (), [r_gp])
    B.S.dma("sp", small[:, 16:24], dr["bglu"], d_par, (), [r_bg])
    r_hTall = B.R("hTall")
    r_Gb = B.R("Gb")
    Gb = GT[:, :, :].rearrange("p c (a n) -> p (c a) n", n=256)
    with ExitStack() as es1:
        E1 = es1.enter_context
        T0b = E1(nc.sbuf_tensor("T0b", [128, 64, 128], BF16))
        BmT = E1(nc.sbuf_tensor("BmT", [128, 64, 2, 64], BF16))
        Cmb = E1(nc.sbuf_tensor("Cmb", [128, 32, 2, 128], BF16))
        A8 = E1(nc.sbuf_tensor("A8", [128, 2, 2, 32], F32))
        r_T0b, r_BmT, r_Cmb, r_A8 = B.R("T0b"), B.R("BmT"), B.R("Cmb"), B.R("A8")
        with ExitStack() as es:
            E = es.enter_context
            ctx["pump"]()
            p1 = ctx["prep1"]
            pbc, PW, QW, bb, T1, T2, RP, bc = (p1[k] for k in ("pbc", "PW", "QW", "bb", "T1", "T2", "RP", "bc"))
            cre, cim = pbc[:, 2], pbc[:, 3]
            dcol = E(nc.sbuf_tensor("dcol_sb", [128, 64], F32))
            mask8 = E(nc.sbuf_tensor("mask8_sb", [128, 128], F32))
            Lh = E(nc.sbuf_tensor("Lh", [128, 2, 32, 128], BF16))
            Rh = E(nc.sbuf_tensor("Rh", [128, 2, 32, 128], BF16))
            identb1 = E(nc.sbuf_tensor("identb1", [128, 128], BF16))
            r_ib1 = B.R("identb1")
            B.cp("dve", identb1[:], ident[:], [r_ident], [r_ib1])
            TM = E(nc.sbuf_tensor("TM", [128, 2, 128], F32))
            r_dm, r_TM = B.R("dcolmask"), [B.R("TM0"), B.R("TM1")]
            B.S.dma("sp", dcol[:], dr["dcol"], d_par, (), [r_dm])
            B.S.dma("sp", mask8[:], dr["mask8"], d_par, (), [r_dm])
            T3 = E(nc.sbuf_tensor("T3", [128, 32, 16], F32))
            T4 = E(nc.sbuf_tensor("T4", [128, 32, 16], F32))
            R_L, R_R = B.R("prepL"), B.R("prepR")
            for j in range(8):
                js = slice(j * 16, (j + 1) * 16)
                cmul(B, Rh[:, 0, :, js], Rh[:, 1, :, js], bc(QW[:, 7 - j, 0, :]), bc(QW[:, 7 - j, 1, :]),
                     cre, cim, T3[:], T4[:], RP, neg_i=True, eng="dve", Rw=R_R)
            for j in range(8):
                js = slice(j * 16, (j + 1) * 16)
                cmul(B, Lh[:, 0, :, js], Lh[:, 1, :, js], bc(PW[:, 7 - j, 0, :]), bc(PW[:, 7 - j, 1, :]),
                     bb[:, 0], bb[:, 1], T1[:], T2[:], RP, Rw=R_L)
                cmul(B, Cmb[:, :, 0, js], Cmb[:, :, 1, js], bc(PW[:, j + 1, 0, :]), bc(PW[:, j + 1, 1, :]),
                     cre, cim, T1[:], T2[:], RP, neg_i=True, Rw=R_L)
            B.cp("dve", Cmb[:, 0, 0, 0:1], Cmb[:, 0, 0, 0:1], [R_L], [r_Cmb])
            B.cp("dve", A8[:, 0, 0, :], PW[:, 8, 0, :], [RP], [r_A8])
            B.cp("dve", A8[:, 0, 1, :], PW[:, 8, 0, :], [RP], [r_A8])
            B.ts("dve", A8[:, 1, 0, :], PW[:, 8, 1, :], -1.0, ALU.mult, [RP], [r_A8])
            B.cp("dve", A8[:, 1, 1, :], PW[:, 8, 1, :], [RP], [r_A8])
            r_T0g = [B.R("T0g%d" % i) for i in range(4)]
            for g in range(64):
                gh, pr = g // 32, g % 32
                Rr = slice(gh * 64, (gh + 1) * 64)
                bk = g % 2
                B.mm(psum[bk][:, 0:128], Lh[Rr, 0, pr, :], Rh[Rr, 0, pr, :], True, False, [R_L, R_R], [rps[bk]])
                B.mm(psum[bk][:, 0:128], Lh[Rr, 1, pr, :], Rh[Rr, 1, pr, :], False, True, [R_L, R_R], [rps[bk]])
                tm = g % 2
                B.tt("dve", TM[:, tm, :], psum[bk][:, 0:128], mask8[:], ALU.mult, [r_dm], [rps[bk], r_TM[tm]])
                B.stt(T0b[:, g, :], ident[:], dcol[:, g:g + 1], TM[:, tm, :], ALU.mult, ALU.add,
                      [r_dm, r_TM[tm], r_ident], [r_T0g[g % 4]])
            for g4 in range(16):
                bk = 2 + g4 % 2
                for gg in range(4):
                    g = g4 * 4 + gg
                    gh, pr = g // 32, g % 32
                    Rr = slice(gh * 64, (gh + 1) * 64)
                    for part in range(2):
                        cs = (gg * 2 + part) * 64
                        B.tr(psum[bk][:, :].bitcast(BF16)[:, cs:cs + 64], Lh[Rr, part, pr, :], identb1[Rr, Rr],
                             [R_L, r_ib1], [rps[bk]])
                B.cp("dve", BmT[:, g4 * 4:g4 * 4 + 4].rearrange("p g a c -> p (g a c)"),
                     psum[bk][:, :].bitcast(BF16)[:, 0:512], [], [rps[bk], r_BmT])
            ctx["dump"]("T0b", T0b[:], [r_T0b])
            ctx["dump"]("BmT", BmT[:], [r_BmT])
            ctx["dump"]("Cmb", Cmb[:], [r_Cmb])
            ctx["dump"]("A8", A8[:], [r_A8])
            ctx["dump"]("PW", PW[:], [RP])
            ctx["dump"]("QW", QW[:], [RP])
            ctx["dump"]("Lh", Lh[:], [R_L])
            ctx["dump"]("Rh", Rh[:], [R_R])
            B.barrier()
        if ctx.get("stop") == "prep":
            return []
        V = E1(nc.sbuf_tensor("Vssm", [128, 64, 256], BF16))
        r_V = B.R("V")
        with ExitStack() as es:
            ctx2 = dict(ctx)
            ctx2["r_hT"] = [r_hTall]
            make_hT(ctx2, dr["x1"], es.enter_context, small[:, 8:16], r_gp, perm8=True,
                    src_res=ctx.get("r_x1"), name="b")
            B.barrier()
        ctx["dump"]("hT1", hT[:], [r_hTall])
        if ctx.get("stop") == "A":
            return []
        with ExitStack() as es:
            E = es.enter_context
            ws = WStream(ctx, E, 2, "w1s", cols=128)
            uTb = E(nc.sbuf_tensor("uTb", [128, 2, S], BF16))
            Es = E(nc.sbuf_tensor("Esel", [128, 64, 128], BF16))
            r_uT = [B.R("uT0"), B.R("uT1")]
            r_Es = B.R("Esel")
            d_es = ctx["new_dsem"]("d_es")
            B.S.dma("sp", Es[:], dr["esel"], d_es, (), [r_Es])
            pbk = 0
            for fc in range(8):
                wbf, r_w = ws.load(dr["w1"][:, fc * 128:fc * 128 + 128])
                colo = 0
                us = fc % 2
                for s in range(4):
                    bk = pbk
                    pbk ^= 1
                    for c in range(8):
                        B.mm(psum[bk][:, :], wbf[:, c, colo:colo + 128], hT[:, c, s * 512:(s + 1) * 512],
                             c == 0, c == 7, [r_w, r_hTall], [rps[bk]])
                    B.act(uTb[:, us, s * 512:(s + 1) * 512], psum[bk][:, :], AF.Copy, [],
                          [rps[bk], r_uT[us]])
                for g8 in range(8):
                    g = fc * 8 + g8
                    bk = 2 + g % 4
                    for j in range(8):
                        B.mm(psum[bk][:, 0:256], Es[:, g8 * 8 + j, :], uTb[:, us, j * 256:(j + 1) * 256],
                             j == 0, j == 7, [r_Es, r_uT[us]], [rps[bk]])
                    B.cp("dve", V[:, g, :], psum[bk][:, 0:256], [], [rps[bk], r_V])
            ctx["dump"]("V", V[:], [r_V])
            B.barrier()
        if ctx.get("stop") == "B":
            return []
        with ExitStack() as es:
            E = es.enter_context
            Zb = E(nc.sbuf_tensor("Zb", [128, 2, 65, 2, 32], F32))
            Sb = ctx["prep1"]["pbc"][:].rearrange("p a b c -> p (a b c)").bitcast(BF16).rearrange(
                "p (n a r) -> p n a r", a=2, r=32)
            t1 = E(nc.sbuf_tensor("zt1", [128, 2, 32], F32))
            t2 = E(nc.sbuf_tensor("zt2", [128, 2, 32], F32))
            r_Z = [B.R("Zb0"), B.R("Zb1")]
            r_Sb = B.R("Sb")
            r_t = B.R("zt")
            B.memset("dve", Zb[:, 0, 0], 0.0, [r_Z[0]])

            def local_states(tb):
                zb = tb % 2
                ns = slice(tb * 64, (tb + 1) * 64)
                for q in range(8):
                    bk = q % 2
                    for pp in range(4):
                        pr = q * 4 + pp
                        for gh in range(2):
                            g = gh * 32 + pr
                            for part in range(2):
                                cs = (pp * 2 + part) * 64
                                B.mm(psum[bk][gh * 64:(gh + 1) * 64, cs:cs + 64], BmT[:, g, part, :],
                                     V[:, g, ns], True, True, [r_BmT, r_V], [rps[bk]])
                    dst = Zb[:, zb, 1:65, :, q * 4:q * 4 + 4].rearrange("p n a r -> p r a n")
                    srcp = psum[bk][:, :].rearrange("p (r a n) -> p r a n", r=4, a=2)
                    B.act(dst, srcp, AF.Copy, [], [rps[bk], r_Z[zb]])

            def recurrence(tb):
                zb = tb % 2
                for s in range(64):
                    zs = Zb[:, zb, s]
                    zn = Zb[:, zb, s + 1]
                    B.tt("dve", t1[:], A8[:, 0], zs, ALU.mult, [r_A8, r_Z[zb]], [r_t])
                    B.tt("dve", t2[:], A8[:, 1], Zb[:, zb, s, ::-1, :], ALU.mult, [r_A8, r_Z[zb]], [r_t])
                    B.tt("dve", zn, zn, t1[:], ALU.add, [r_t], [r_Z[zb]])
                    B.tt("dve", zn, zn, t2[:], ALU.add, [r_t], [r_Z[zb]])
                B.cp("dve", Sb, Zb[:, zb, 0:64], [r_Z[zb]], [r_Sb])
                if tb + 1 < 4:
                    B.cp("dve", Zb[:, 1 - zb, 0], Zb[:, zb, 64], [r_Z[zb]], [r_Z[1 - zb]])

            def outputs(tb):
                ns = slice(tb * 64, (tb + 1) * 64)
                for fc in range(8):
                    bk = 2 + fc % 4
                    for g8 in range(8):
                        g = fc * 8 + g8
                        gh, pr = g // 32, g % 32
                        Rr = slice(gh * 64, (gh + 1) * 64)
                        outp = psum[bk][:, g8 * 64:(g8 + 1) * 64]
                        B.mm(outp, T0b[:, g, :], V[:, g, ns], True, False, [r_T0b, r_V], [rps[bk]])
                        B.mm(outp, Cmb[Rr, pr, 0, :], Sb[Rr, :, 0, pr], False, False, [r_Cmb, r_Sb], [rps[bk]])
                        B.mm(outp, Cmb[Rr, pr, 1, :], Sb[Rr, :, 1, pr], False, True, [r_Cmb, r_Sb], [rps[bk]])
                    yp = psum[bk][:, :]
                    dst = Gb[:, fc * 8:(fc + 1) * 8, ns]
                    B.act(dst, yp.rearrange("p (g n) -> p g n", g=8), AF.Gelu_apprx_tanh, [],
                          [rps[bk], r_Gb])

            local_states(0)
            for tb in range(4):
                if tb + 1 < 4:
                    local_states(tb + 1)
                recurrence(tb)
                outputs(tb)
            ctx["dump"]("Gb", GT[:], [r_Gb])
            B.barrier()
        B.barrier()
    if ctx.get("stop") == "C":
        return []
    finals = []
    with ExitStack() as es2:
        E2 = es2.enter_context
        gTb = E2(nc.sbuf_tensor("gTb", [128, 8, S], BF16))
        r_gT = B.R("gTb")
        r_G2 = B.R("G2T")
        with ExitStack() as es:
            E = es.enter_context
            Es = E(nc.sbuf_tensor("Esel2", [128, 64, 128], BF16))
            r_Es = B.R("Esel2")
            d_es = ctx["new_dsem"]("d_es2")
            B.S.dma("sp", Es[:], dr["esel"], d_es, (), [r_Es])
            for fc in range(8):
                for ip in range(4):
                    bk = (fc * 4 + ip) % 4
                    for ii in range(2):
                        i0 = ip * 2 + ii
                        for g8 in range(8):
                            B.mm(psum[bk][:, ii * 256:(ii + 1) * 256], Es[:, i0 * 8 + g8, :],
                                 Gb[:, fc * 8 + g8, :], g8 == 0, g8 == 7, [r_Es, r_Gb], [rps[bk]])
                    B.cp("dve" if ip % 2 == 0 else "act_copy", gTb[:, fc, ip * 512:(ip + 1) * 512],
                         psum[bk][:, :], [], [rps[bk], r_gT]) if False else \
                        B.cp("dve", gTb[:, fc, ip * 512:(ip + 1) * 512], psum[bk][:, :], [], [rps[bk], r_gT])
            ctx["dump"]("gTb", gTb[:], [r_gT])
            B.barrier()
        if ctx.get("stop") == "D1":
            return []
        with ExitStack() as es:
            E = es.enter_context
            wsg = WStream(ctx, E, 2, "wg1s")
            wsz = WStream(ctx, E, 2, "wz1s")
            sg = E(nc.sbuf_tensor("sg", [128, 2, 512], F32))
            sz = E(nc.sbuf_tensor("sz", [128, 2, 512], F32))
            r_sg = [B.R("sg0"), B.R("sg1")]
            r_sz = [B.R("sz0"), B.R("sz1")]
            k = 0
            for fo in range(8):
                if fo % 2 == 0:
                    wg, r_wg = wsg.load(dr["wg1"][:, fo * 128:fo * 128 + 256])
                    wz, r_wz = wsz.load(dr["w1"][:, 1024 + fo * 128:1024 + fo * 128 + 256])
                colo = (fo % 2) * 128
                for s in range(4):
                    sl = k % 2
                    k += 1
                    ba, bb_ = (0, 1) if sl == 0 else (2, 3)
                    cs = slice(s * 512, (s + 1) * 512)
                    for c in range(8):
                        B.mm(psum[ba][:, :], wg[:, c, colo:colo + 128], gTb[:, c, cs], c == 0, c == 7,
                             [r_wg, r_gT], [rps[ba]])
                    for c in range(8):
                        B.mm(psum[bb_][:, :], wz[:, c, colo:colo + 128], hT[:, c, cs], c == 0, c == 7,
                             [r_wz, r_hTall], [rps[bb_]])
                    B.act(sg[:, sl, :], psum[ba][:, :], AF.Sigmoid, [r_bg], [rps[ba], r_sg[sl]],
                          bias=small[:, 16 + fo:17 + fo])
                    B.act(sz[:, sl, :], psum[bb_][:, :], AF.Sigmoid, [], [rps[bb_], r_sz[sl]])
                    B.tt("pool", sg[:, sl, :], sg[:, sl, :], sz[:, sl, :], ALU.mult, [r_sz[sl]], [r_sg[sl]])
                    B.tt("dve", sg[:, sl, :], sg[:, sl, :], psum[bb_][:, :], ALU.mult, [], [rps[bb_], r_sg[sl]])
                    dstn = GT[:, fo, :].rearrange("p (n j) -> p j n", j=8)[:, 2 * s:2 * s + 2, :]
                    B.tt("dve", dstn, sg[:, sl, :].rearrange("p (j n) -> p j n", j=2),
                         gTb[:, fo, cs].rearrange("p (j n) -> p j n", j=2), ALU.mult,
                         [r_sg[sl], r_gT], [r_G2])
            ctx["dump"]("G2T", GT[:], [r_G2])
            B.barrier()
        if ctx.get("stop") == "D2":
            return []
        with ExitStack() as es:
            E = es.enter_context
            ws = WStream(ctx, E, 2, "wo1s")
            WO = E(nc.sbuf_tensor("WO1", [128, 8, D], BF16))
            gpo = E(nc.sbuf_tensor("gpo1", [128, D], F32))
            r_WO = [B.R("WO1_%d" % i) for i in range(4)]
            r_gpo = B.R("gpo1")
            B.S.dma("sp", gpo[:], dr["gpost1"], d_par, (), [r_gpo])
            for wb in range(4):
                ws.load(dr["wo1"][:, wb * 256:(wb + 1) * 256], WO[:, :, wb * 256:(wb + 1) * 256], r_WO[wb])

            tokap = None
            finals = out_proj_tail(ctx, E, WO, r_WO, gpo, r_gpo, dr["x1"], dr["out"], GT, r_G2, tokap, "t1",
                                   src_res=ctx.get("r_x1"))
            B.barrier()
    return finals


_PROG = {}


def _prog(mode):
    if mode not in _PROG:
        _PROG[mode] = build_program(mode)
    return _PROG[mode]


def _l0_inputs(inp, b):
    f = np.float32
    return {
        "x": np.ascontiguousarray(inp["x"][b], dtype=f),
        "w0": inp["_w0"], "wo0": inp["_wo0"], "gpre0": inp["_gpre0"], "gpost0": inp["_gpost0"],
        "bias0": inp["_bias0"], "ident": inp["_ident"],
    }


def _prep_l0(inputs):
    f = np.float32
    p = {}
    p["_w0"] = _perm_w_in(np.asarray(inputs["attn_w_in"][0], f))
    p["_wo0"] = np.ascontiguousarray(np.asarray(inputs["attn_w_out"][0], f))
    p["_gpre0"] = np.ascontiguousarray(np.asarray(inputs["attn_pre_norm"][0], f).reshape(8, 128).T)
    p["_gpost0"] = np.ascontiguousarray(np.broadcast_to(np.asarray(inputs["attn_post_norm"][0], f), (128, D)))
    p["_bias0"] = _bias_tiles(np.asarray(inputs["rel_bias"], f))
    p["_ident"] = np.eye(128, dtype=f)
    return p


def run_l0(inputs, debug=False):
    inp = dict(inputs)
    inp.update(_prep_l0(inputs))
    inp["x"] = np.asarray(inputs["x"], np.float32)
    nc = build_program("l0", debug=debug)
    in_maps = [_l0_inputs(inp, b) for b in range(8)]
    res = run_bass_kernel_spmd(nc, in_maps, core_ids=list(range(8)))
    if debug:
        return np.stack([r["x1"] for r in res.results], axis=0), res.results[0]
    return np.stack([r["x1"] for r in res.results], axis=0)


def _prep_l1(inputs):
    f = np.float32
    p = {}
    p["w1"] = np.ascontiguousarray(np.asarray(inputs["ssm_w_in"][0], f))
    p["wg1"] = np.ascontiguousarray(np.asarray(inputs["ssm_w_glu"][0], f))
    p["wo1"] = np.ascontiguousarray(np.asarray(inputs["ssm_w_out"][0], f))
    p["gpre1"] = np.ascontiguousarray(np.asarray(inputs["ssm_pre_norm"][0], f).reshape(8, 128).T)
    p["gpost1"] = np.ascontiguousarray(np.broadcast_to(np.asarray(inputs["ssm_post_norm"][0], f), (128, D)))
    p["bglu"] = np.ascontiguousarray(np.asarray(inputs["ssm_b_glu"][0], f).reshape(8, 128).T)

    def pl(a):
        a = np.asarray(a, f)
        a = a.reshape((2, 32, 64) + a.shape[2:])
        a = np.moveaxis(a, 2, 1)
        return np.ascontiguousarray(a.reshape((128, 32) + a.shape[3:]))
    are = pl(inputs["ssm_a_re"][0])
    aim = pl(inputs["ssm_a_im"][0])
    ldt = pl(np.broadcast_to(np.asarray(inputs["ssm_log_dt"][0], f)[:, None], (64, 64)))
    p["pl_a"] = np.ascontiguousarray(np.stack([are, aim, ldt], axis=1))
    bre = pl(inputs["ssm_b_re"][0])
    bim = pl(inputs["ssm_b_im"][0])
    cre = pl(np.swapaxes(np.asarray(inputs["ssm_c_re"][0], f), 1, 2))
    cim = pl(np.swapaxes(np.asarray(inputs["ssm_c_im"][0], f), 1, 2))
    p["pl_bc"] = np.ascontiguousarray(np.stack([bre, bim, cre, cim], axis=1))
    dvec = np.asarray(inputs["ssm_d"][0], f).reshape(64, 16)
    p["dcol"] = np.ascontiguousarray(np.tile(dvec.T, (8, 1)))
    jj = np.arange(128) // 16
    p["mask8"] = (jj[:, None] <= jj[None, :]).astype(f)
    k = np.arange(128)
    es = np.zeros((128, 64, 128), f)
    for a in range(8):
        for b in range(8):
            es[:, a * 8 + b, :] = ((k[:, None] // 16 == a) & (k[None, :] // 16 == b)
                                   & (k[:, None] % 16 == k[None, :] % 16))
    p["esel"] = es.astype(ml_dtypes.bfloat16)
    p["ident"] = np.eye(128, dtype=f)
    return p


def run_l1(inputs, x1, debug=False, stop=None):
    p = _prep_l1(inputs)
    nc = build_program("l1", debug=debug, stop=stop)
    in_maps = []
    for b in range(8):
        m = dict(p)
        m["x1"] = np.ascontiguousarray(x1[b], dtype=np.float32)
        in_maps.append(m)
    res = run_bass_kernel_spmd(nc, in_maps, core_ids=list(range(8)))
    out = np.stack([r["out"] for r in res.results], axis=0)
    if debug:
        return out, res.results[0]
    return out


FUSED = True


def kernel(**inputs):
    p0 = _prep_l0(inputs)
    p1 = _prep_l1(inputs)
    x = np.asarray(inputs["x"], np.float32)
    shared0 = {"w0": p0["_w0"], "wo0": p0["_wo0"], "gpre0": p0["_gpre0"], "gpost0": p0["_gpost0"],
               "bias0": p0["_bias0"], "ident": p0["_ident"]}
    if FUSED:
        nc = _prog("fused")
        shared = dict(shared0)
        shared.update(p1)
        in_maps = []
        for b in range(8):
            m = dict(shared)
            m["x"] = np.ascontiguousarray(x[b])
            in_maps.append(m)
        res = run_bass_kernel_spmd(nc, in_maps, core_ids=list(range(8)))
        return np.stack([np.asarray(r["out"], np.float32) for r in res.results], axis=0)
    in_maps = []
    for b in range(8):
        m = dict(shared0)
        m["x"] = np.ascontiguousarray(x[b])
        in_maps.append(m)
    res = run_bass_kernel_spmd(_prog("l0"), in_maps, core_ids=list(range(8)))
    x1 = [np.asarray(r["x1"], np.float32) for r in res.results]
    in_maps = []
    for b in range(8):
        m = dict(p1)
        m["x1"] = np.ascontiguousarray(x1[b])
        in_maps.append(m)
    res = run_bass_kernel_spmd(_prog("l1"), in_maps, core_ids=list(range(8)))
    return np.stack([np.asarray(r["out"], np.float32) for r in res.results], axis=0)
```

```python
import math
import numpy as np
import ml_dtypes
import concourse.bass as bass
import concourse.mybir as mybir
from concourse.bass_utils import run_bass_kernel_spmd

F32 = mybir.dt.float32
BF16 = mybir.dt.bfloat16
AF = mybir.ActivationFunctionType
ALU = mybir.AluOpType

D = 1024
S = 2048
NT = 16
EPS = 1e-6
GROUPS = ((128, 1), (512, 4), (2048, 16))
NEGM = -30000.0
SSM_G = 64
SSM_P = 64
SSM_C = 16


class Res:
    __slots__ = ("name", "w", "rs", "rd")

    def __init__(self, name):
        self.name = name
        self.w = None
        self.rs = {}
        self.rd = []


class DmaSem:
    def __init__(self, sem):
        self.sem = sem
        self.count = 0


class Op:
    __slots__ = ("eng", "fn", "deps", "need", "seq", "dsem", "dval", "idx", "dreq", "group", "gidx")


class Sched:
    ENGS = ("pe", "act", "dve", "pool", "sp")

    def __init__(self, nc):
        self.nc = nc
        self.ops = {e: [] for e in self.ENGS}
        self.allops = []
        self.cur_group = None
        self.ngroups = 0

    def begin_group(self):
        self.ngroups += 1
        self.cur_group = self.ngroups

    def end_group(self):
        self.cur_group = None

    def _mk(self, eng, fn, reads, writes, isdma=False):
        o = Op()
        o.eng = eng
        o.fn = fn
        o.need = False
        o.seq = 0
        o.dsem = None
        o.dval = 0
        deps = []
        for r in reads:
            if r.w is not None:
                deps.append(r.w)
        for r in writes:
            if r.w is not None:
                deps.append(r.w)
            deps.extend(r.rs.values())
            deps.extend(r.rd)
        o.deps = deps
        o.dreq = {}
        for p in deps:
            if p.dsem is not None:
                o.dreq[id(p.dsem)] = (p.dsem.sem, p.dsem.count)
        for r in reads:
            if isdma:
                r.rd.append(o)
            else:
                r.rs[eng] = o
        for r in writes:
            r.w = o
            r.rs = {}
            r.rd = []
        o.idx = len(self.ops[eng])
        o.group = self.cur_group
        o.gidx = len(self.allops)
        self.ops[eng].append(o)
        self.allops.append(o)
        return o

    def op(self, eng, fn, reads=(), writes=()):
        return self._mk(eng, fn, list(reads), list(writes))

    def dma(self, eng, out, in_, dsem, reads=(), writes=()):
        def fn(e):
            return e.dma_start(out=out, in_=in_)
        o = self._mk(eng, fn, list(reads), list(writes), isdma=True)
        dsem.count += 16
        o.dsem = dsem
        o.dval = dsem.count
        return o


def _t5_bucket_np(dist):
    max_exact = 16
    n = np.maximum(dist, 1).astype(np.float32)
    large = max_exact + (np.log(n / np.float32(max_exact)) / np.float32(math.log(2048 / max_exact))
                         * np.float32(32 - max_exact)).astype(np.int32)
    large = np.minimum(large, 31)
    return np.where(dist < max_exact, dist, large)


def _bias_tiles(rel_bias):
    j = np.arange(128)[:, None]
    q = np.arange(256)[None, :]
    delta = q - j
    valid = (delta >= 0) & (delta <= 128)
    out = np.empty((8, 128, 6, 256), np.float32)
    for g, (_, dil) in enumerate(GROUPS):
        bucket = _t5_bucket_np(np.maximum(delta, 0) * dil)
        for h in range(16):
            t = rel_bias[bucket, h]
            t = np.where(valid, t, np.float32(NEGM)).astype(np.float32)
            out[h // 2, :, g * 2 + (h % 2), :] = t
    return out


def _perm_w_in(w):
    out = np.empty((8, D, 1280), np.float32)
    for hp in range(8):
        cs = slice(hp * 128, (hp + 1) * 128)
        blk = []
        for g in range(3):
            blk.append(w[:, (0 * 3 + g) * 1024:(0 * 3 + g + 1) * 1024][:, cs])
            blk.append(w[:, (1 * 3 + g) * 1024:(1 * 3 + g + 1) * 1024][:, cs])
        for g in range(3):
            blk.append(w[:, (2 * 3 + g) * 1024:(2 * 3 + g + 1) * 1024][:, cs])
        blk.append(w[:, 9216:10240][:, cs])
        out[hp] = np.concatenate(blk, axis=1)
    return out


class Builder:
    def __init__(self, nc):
        self.nc = nc
        self.S = Sched(nc)
        self.dsems = []
        self.allres = []
        self.last_barrier = None
        self.scratch = None

    def R(self, name):
        r = Res(name)
        r.w = self.last_barrier
        self.allres.append(r)
        return r

    def barrier(self):
        scr = self.scratch

        def fn(e):
            return e.memset(scr, 0.0)
        o = self.S.op("pool", fn, (), list(self.allres))
        self.last_barrier = o
        return o

    def mm(self, out, lhsT, rhs, start, stop, reads, writes):
        def fn(e):
            return e.matmul(out, lhsT, rhs, start=start, stop=stop)
        return self.S.op("pe", fn, reads, writes)

    def tr(self, out, in_, ident, reads, writes):
        def fn(e):
            return e.transpose(out, in_, ident)
        return self.S.op("pe", fn, reads, writes)

    def act(self, out, in_, func, reads, writes, scale=None, bias=None, accum_out=None):
        kw = {}
        if scale is not None:
            kw["scale"] = scale
        if bias is not None:
            kw["bias"] = bias
        if accum_out is not None:
            kw["accum_out"] = accum_out

        def fn(e):
            return e.activation(out, in_, func, **kw)
        return self.S.op("act", fn, reads, writes)

    def tt(self, eng, out, in0, in1, op, reads, writes):
        def fn(e):
            return e.tensor_tensor(out, in0, in1, op)
        return self.S.op(eng, fn, reads, writes)

    def ts(self, eng, out, in0, s1, op0, reads, writes, s2=None, op1=None):
        def fn(e):
            if op1 is None:
                return e.tensor_scalar(out, in0, s1, None, op0)
            return e.tensor_scalar(out, in0, s1, s2, op0, op1)
        return self.S.op(eng, fn, reads, writes)

    def stt(self, out, in0, scalar, in1, op0, op1, reads, writes):
        def fn(e):
            return e.scalar_tensor_tensor(out, in0, scalar, in1, op0, op1)
        return self.S.op("dve", fn, reads, writes)

    def cp(self, eng, out, in_, reads, writes):
        def fn(e):
            return e.tensor_copy(out, in_)
        return self.S.op(eng, fn, reads, writes)

    def memset(self, eng, ap, val, writes):
        def fn(e):
            return e.memset(ap, val)
        return self.S.op(eng, fn, (), writes)

    def recip(self, out, in_, reads, writes):
        def fn(e):
            return e.reciprocal(out, in_)
        return self.S.op("dve", fn, reads, writes)


def rms_rstd(B, ss, rs, r_ss, r_rs):
    B.ts("dve", rs, ss, 1.0 / D, ALU.mult, [r_ss], [r_rs], s2=EPS, op1=ALU.add)
    B.act(rs, rs, AF.Sqrt, [r_rs], [r_rs])
    B.recip(rs, rs, [r_rs], [r_rs])


def build_program(mode="fused", debug=False, stop=None):
    nc = bass.Bass("TRN2", target_bir_lowering=False)
    B = Builder(nc)
    dt = nc.dram_tensor
    dr = {}
    do_l0 = mode in ("l0", "fused")
    do_l1 = mode in ("l1", "fused")
    if do_l0:
        dr["x"] = dt("x", [S, D], F32, kind="ExternalInput").ap()
        dr["w0"] = dt("w0", [8, D, 1280], F32, kind="ExternalInput").ap()
        dr["wo0"] = dt("wo0", [D, D], F32, kind="ExternalInput").ap()
        dr["gpre0"] = dt("gpre0", [128, 8], F32, kind="ExternalInput").ap()
        dr["gpost0"] = dt("gpost0", [128, D], F32, kind="ExternalInput").ap()
        dr["bias0"] = dt("bias0", [8, 128, 6, 256], F32, kind="ExternalInput").ap()
    dr["ident"] = dt("ident", [128, 128], F32, kind="ExternalInput").ap()
    if mode == "l0":
        dr["x1"] = dt("x1", [S, D], F32, kind="ExternalOutput").ap()
    elif mode == "l1":
        dr["x1"] = dt("x1", [S, D], F32, kind="ExternalInput").ap()
    else:
        dr["x1"] = dt("x1", [S, D], F32, kind="Internal").ap()
    if do_l1:
        build_l1_decl(dt, dr)
        dr["out"] = dt("out", [S, D], F32, kind="ExternalOutput").ap()

    from contextlib import ExitStack
    with ExitStack() as es:
        E = es.enter_context
        sems = {e: E(nc.semaphore("c_" + e)) for e in ("pe", "act", "dve", "pool", "sp")}

        def new_dsem(name):
            return DmaSem(E(nc.semaphore(name)))

        psum = [E(nc.psum_tensor("ps%d" % i, [128, 512], F32)) for i in range(8)]
        rps = [B.R("psum%d" % i) for i in range(8)]
        hT = E(nc.sbuf_tensor("hT", [128, 8, S], BF16))
        GT = E(nc.sbuf_tensor("GT", [128, 8, S], BF16))
        ident = E(nc.sbuf_tensor("ident_sb", [128, 128], F32))
        ones = None
        small = E(nc.sbuf_tensor("small", [128, 64], F32))
        bscr = E(nc.sbuf_tensor("bscr", [128, 8], F32))
        B.scratch = bscr[:, 0:1]
        r_hT = [B.R("hT%d" % i) for i in range(NT)]
        r_GT = B.R("GT")
        r_ident = B.R("ident")
        r_ones = B.R("ones")
        d_const = new_dsem("d_const")
        B.S.dma("sp", ident[:], dr["ident"], d_const, (), [r_ident])
        dumps = {}
        d_dump = new_dsem("d_dump")

        def dump(name, ap, reads):
            if not debug or name in dumps:
                return
            t = dt("dbg_" + name, list(ap.shape), ap.dtype, kind="ExternalOutput").ap()
            o = B.S.dma("pool", t, ap, d_dump, reads, [])
            fr = B.R("dfin_" + name)
            fr.w = o
            dumps[name] = fr
        ctx = dict(nc=nc, B=B, dr=dr, E=E, new_dsem=new_dsem, psum=psum, rps=rps, hT=hT, GT=GT, dump=dump,
                   stop=stop, ident=ident, ones=ones, small=small, r_hT=r_hT, r_GT=r_GT,
                   r_ident=r_ident, r_ones=r_ones, sems=sems)
        if mode == "fused":
            ctx["r_x1"] = [B.R("x1_%d" % i) for i in range(NT)]
        if do_l1:
            l1_prep_stage1(ctx, E)
        else:
            ctx["pump"] = lambda n=None: None
        with nc.Block() as block:
            final = []
            if do_l0:
                final = build_l0(ctx)
            if do_l1:
                final = build_l1(ctx)
            B.S.op("sp", None, list(final) + list(dumps.values()), ())
            emit_all(B, block, sems)
    return nc


def emit_all(B, block, sems):
    S_ = B.S
    nc = B.nc
    for o in S_.allops:
        for p in o.deps:
            if p.dsem is None and not (p.eng == "pe" and o.eng == "pe"):
                p.need = True
    for e in S_.ENGS:
        c = 0
        for o in S_.ops[e]:
            if o.need:
                c += 1
                o.seq = c

    def reqs_of(o, e, before=None):
        req = {}
        for p in o.deps:
            if before is not None and p.gidx >= before:
                continue
            if p.dsem is not None:
                key = ("d", id(p.dsem))
                v = o.dreq[id(p.dsem)]
                if req.get(key, (None, 0))[1] < v[1]:
                    req[key] = v
            else:
                if p.eng == "pe" and e == "pe":
                    continue
                key = ("e", p.eng)
                if req.get(key, (None, 0))[1] < p.seq:
                    req[key] = (sems[p.eng], p.seq)
        return req

    def run(e, eo):
        waited = {}
        ops = S_.ops[e]
        seen_groups = set()
        for k, o in enumerate(ops):
            req = reqs_of(o, e)
            if o.group is not None and o.group not in seen_groups:
                seen_groups.add(o.group)
                j = k + 1
                while j < len(ops) and ops[j].group == o.group:
                    for key, v in reqs_of(ops[j], e, before=o.gidx).items():
                        if req.get(key, (None, 0))[1] < v[1]:
                            req[key] = v
                    j += 1
            for key, (sem, val) in req.items():
                if waited.get(key, 0) >= val:
                    continue
                waited[key] = val
                eo.wait_ge(sem, val)
            if o.fn is None:
                continue
            ins = o.fn(eo)
            if o.dsem is not None:
                ins.then_inc(o.dsem.sem, 16)
            elif o.need:
                ins.then_inc(sems[e], 1)

    @block.tensor
    def _(eo):
        run("pe", eo)

    @block.scalar
    def _(eo):
        run("act", eo)

    @block.vector
    def _(eo):
        run("dve", eo)

    @block.gpsimd
    def _(eo):
        run("pool", eo)

    @block.sync
    def _(eo):
        run("sp", eo)


def make_hT(ctx, src, scope, gp, r_gp, perm8=False, src_res=None, name="a"):
    nc, B, psum, rps, hT, ident = ctx["nc"], ctx["B"], ctx["psum"], ctx["rps"], ctx["hT"], ctx["ident"]
    E = scope
    NS = 3
    xt = E(nc.sbuf_tensor("xt_" + name, [128, NS, D], F32))
    xs = E(nc.sbuf_tensor("xs_" + name, [128, NS, D], F32))
    jk = E(nc.sbuf_tensor("jk_" + name, [128, D], F32))
    st = E(nc.sbuf_tensor("st_" + name, [128, NS, 4], F32))
    r_xt = [B.R("xt%d" % i) for i in range(NS)]
    r_xs = [B.R("xs%d" % i) for i in range(NS)]
    r_st = [B.R("st%d" % i) for i in range(NS)]
    r_jk = B.R("jk")
    d_x = [ctx["new_dsem"]("d_x%s%d" % (name, i)) for i in range(NS)]

    def stage1(tt):
        sl = tt % NS
        B.S.dma("sp", xt[:, sl, :], src[tt * 128:(tt + 1) * 128, :], d_x[sl],
                [src_res[tt]] if src_res else (), [r_xt[sl]])
        B.act(jk[:, :], xt[:, sl, :], AF.Square, [r_xt[sl]], [r_jk, r_st[sl]], accum_out=st[:, sl, 0:1])
        B.ts("dve", st[:, sl, 1:2], st[:, sl, 0:1], 1.0 / D, ALU.mult, [r_st[sl]], [r_st[sl]],
             s2=EPS, op1=ALU.add)

    def stage2(tt):
        sl = tt % NS
        B.act(st[:, sl, 1:2], st[:, sl, 1:2], AF.Sqrt, [r_st[sl]], [r_st[sl]])
        B.recip(st[:, sl, 1:2], st[:, sl, 1:2], [r_st[sl]], [r_st[sl]])
        B.ts("dve", xs[:, sl, :], xt[:, sl, :], st[:, sl, 1:2], ALU.mult,
             [r_xt[sl], r_st[sl]], [r_xs[sl]])

    def stage3(tt):
        sl = tt % NS
        for half in range(2):
            bk = half + 2 * (tt % 2)
            for j in range(4):
                c = half * 4 + j
                B.tr(psum[bk][:, j * 128:(j + 1) * 128], xs[:, sl, c * 128:(c + 1) * 128], ident[:],
                     [r_xs[sl], ctx["r_ident"]], [rps[bk]])
            for j in range(4):
                c = half * 4 + j
                if perm8:
                    dst = hT[:, c, :].rearrange("p (j n) -> p j n", j=8)[:, :, 16 * tt:16 * tt + 16]
                    srcp = psum[bk][:, j * 128:(j + 1) * 128].rearrange("p (n j) -> p j n", j=8)
                else:
                    dst = hT[:, c, tt * 128:(tt + 1) * 128]
                    srcp = psum[bk][:, j * 128:(j + 1) * 128]
                wr = ctx["r_hT"] if perm8 else [ctx["r_hT"][tt]]
                B.ts("dve", dst, srcp, gp[:, c:c + 1], ALU.mult, [r_gp], [rps[bk]] + wr)

    for t in range(NT + 2):
        if t < NT:
            stage1(t)
        if 0 <= t - 1 < NT:
            stage2(t - 1)
        if 0 <= t - 2 < NT:
            stage3(t - 2)


class WStream:
    def __init__(self, ctx, scope, nslot=2, name="ws", cols=256):
        nc, B = ctx["nc"], ctx["B"]
        self.ctx = ctx
        self.n = nslot
        self.wst = scope(nc.sbuf_tensor(name + "_st", [128, nslot, 8, cols], F32))
        self.wbf = scope(nc.sbuf_tensor(name + "_bf", [128, nslot, 8, cols], BF16))
        self.r_st = [B.R(name + "st%d" % i) for i in range(nslot)]
        self.r_bf = [B.R(name + "bf%d" % i) for i in range(nslot)]
        self.ds = [ctx["new_dsem"]("d_" + name + str(i)) for i in range(nslot)]
        self.k = 0

    def dma(self, src):
        B = self.ctx["B"]
        sl = self.k % self.n
        self.k += 1
        B.S.dma("sp", self.wst[:, sl], src.rearrange("(c p) n -> p c n", p=128), self.ds[sl],
                (), [self.r_st[sl]])
        return sl

    def cast(self, sl):
        B = self.ctx["B"]
        B.act(self.wbf[:, sl], self.wst[:, sl], AF.Copy, [self.r_st[sl]], [self.r_bf[sl]])
        return self.wbf[:, sl], self.r_bf[sl]

    def load(self, src, dst=None, r_dst=None):
        B = self.ctx["B"]
        sl = self.k % self.n
        self.k += 1
        B.S.dma("sp", self.wst[:, sl], src.rearrange("(c p) n -> p c n", p=128), self.ds[sl],
                (), [self.r_st[sl]])
        if dst is None:
            dst, r_dst = self.wbf[:, sl], self.r_bf[sl]
        B.act(dst, self.wst[:, sl], AF.Copy, [self.r_st[sl]], [r_dst])
        return dst, r_dst


def grp_tokens(g, bi):
    d = GROUPS[g][1]
    nbr = 16 // d
    r = bi // nbr
    m0 = (bi % nbr) * 128
    return d * m0 + r, d


def build_l0(ctx):
    nc, B, dr, psum, rps = ctx["nc"], ctx["B"], ctx["dr"], ctx["psum"], ctx["rps"]
    hT, GT, ones, small = ctx["hT"], ctx["GT"], ctx["ones"], ctx["small"]
    r_hT, r_GT = ctx["r_hT"], ctx["r_GT"]
    from contextlib import ExitStack
    d_par = ctx["new_dsem"]("d_par0")
    r_gp = B.R("gpre0")
    B.S.dma("sp", small[:, 0:8], dr["gpre0"], d_par, (), [r_gp])
    with ExitStack() as es:
        make_hT(ctx, dr["x"], es.enter_context, small[:, 0:8], r_gp)
        B.barrier()
    allhT = list(r_hT)
    ctx["dump"]("hT", hT[:], allhT)
    with ExitStack() as es:
        E = es.enter_context
        ws = WStream(ctx, E, 2, "w0s", cols=128)
        QK = E(nc.sbuf_tensor("QK", [128, 6, S], BF16))
        VA0 = E(nc.sbuf_tensor("VA0", [128, 48, 65], BF16))
        VA1 = E(nc.sbuf_tensor("VA1", [128, 48, 128], BF16))
        SZ = E(nc.sbuf_tensor("SZ", [128, S], F32))
        BT = E(nc.sbuf_tensor("BT", [128, 2, 6, 256], F32))
        OA = E(nc.sbuf_tensor("OA", [128, 2, S], F32))
        LT = E(nc.sbuf_tensor("LT", [128, 2, 2, 256], F32))
        PT = E(nc.sbuf_tensor("PT", [128, 5, 2, 256], BF16))
        VT = E(nc.sbuf_tensor("VT", [128, 3, S], BF16))
        RHL = E(nc.sbuf_tensor("RHL", [128, 2, S], BF16))
        identb = E(nc.sbuf_tensor("identb", [128, 128], BF16))
        onesb = E(nc.sbuf_tensor("onesb", [128, 128], BF16))
        r_VT = [B.R("VT%d" % i) for i in range(3)]
        r_RHL = B.R("RHL")
        r_cb = B.R("constb")
        B.cp("pool", identb[:], ctx["ident"][:], [ctx["r_ident"]], [r_cb])
        B.memset("pool", onesb[:], 1.0, [r_cb])
        r_QK = [B.R("QK%d" % i) for i in range(6)]
        r_VA = [B.R("VA%d" % i) for i in range(3)]
        r_SZ = B.R("SZ")
        r_BT = [B.R("BT0"), B.R("BT1")]
        r_OA = [B.R("OA0"), B.R("OA1")]
        r_LT = [B.R("LT%d" % i) for i in range(2)]
        r_PT = [B.R("PT%d" % i) for i in range(5)]
        d_bt = [ctx["new_dsem"]("d_bt0"), ctx["new_dsem"]("d_bt1")]
        B.memset("pool", VA0[:], 1.0, r_VA)
        B.memset("pool", VA1[:], 0.0, r_VA)
        B.memset("pool", VA1[:, :, 0:1], 1.0, r_VA)
        pb = [0]

        def proj_bank():
            pb[0] ^= 1
            return pb[0]

        stb = [0]
        fb = [0]
        dq = []
        pre = {}
        wdma = {}

        def wsrc(h, wb):
            return dr["w0"][h][:, wb * 128:(wb + 1) * 128]
        for hp in range(8):
            bsl = hp % 2
            B.S.dma("sp", BT[:, bsl], dr["bias0"][hp], d_bt[bsl], (), [r_BT[bsl]])
            for wb in range(10):
                key = (hp, wb)
                if key not in pre:
                    if key not in wdma:
                        wdma[key] = ws.dma(wsrc(*key))
                    pre[key] = ws.cast(wdma.pop(key))
                wbf, r_w = pre.pop(key)
                nkey = (hp, wb + 1) if wb < 9 else ((hp + 1, 0) if hp + 1 < 8 else None)
                if nkey is not None and nkey not in wdma and nkey not in pre:
                    wdma[nkey] = ws.dma(wsrc(*nkey))

                def at_s2(nkey=nkey):
                    if nkey is not None and nkey in wdma:
                        pre[nkey] = ws.cast(wdma.pop(nkey))
                for half_ in range(1):
                    colo = 0
                    kind = ("q", "k", "q", "k", "q", "k", "v0", "v1", "v2", "z")[wb]
                    half = wb % 2
                    if kind in ("q", "k", "z"):
                        g = wb // 2 if wb < 6 else 0
                        d = GROUPS[g][1] if kind != "z" else 1
                        for s in range(4):
                            if s == 2:
                                at_s2()
                            bk = proj_bank()
                            for c in range(8):
                                B.mm(psum[bk][:, :], wbf[:, c, colo:colo + 128],
                                     hT[:, c, s * 512:(s + 1) * 512], c == 0, c == 7,
                                     [r_w] + allhT, [rps[bk]])
                            if kind == "z":
                                B.act(SZ[:, s * 512:(s + 1) * 512], psum[bk][:, :], AF.Silu,
                                      [], [rps[bk], r_SZ])
                            else:
                                idx = 2 * g + half
                                L = S // d
                                dst = QK[:, idx, :].rearrange("p (r m) -> p r m", r=d)[
                                    :, :, (512 // d) * s:(512 // d) * (s + 1)]
                                srcp = psum[bk][:, :].rearrange("p (m r) -> p r m", r=d)
                                B.act(dst, srcp, AF.Copy, [], [rps[bk], r_QK[idx]],
                                      scale=(0.125 if kind == "q" else 1.0))
                    else:
                        g = int(kind[1])
                        for s in range(4):
                            if s == 2:
                                at_s2()
                            bk = proj_bank()
                            for c in range(8):
                                B.mm(psum[bk][:, :], wbf[:, c, colo:colo + 128],
                                     hT[:, c, s * 512:(s + 1) * 512], c == 0, c == 7,
                                     [r_w] + allhT, [rps[bk]])
                            B.act(VT[:, g, s * 512:(s + 1) * 512], psum[bk][:, :], AF.Copy, [],
                                  [rps[bk], r_VT[g]])
                        for quad in range(4):
                            bk = proj_bank()
                            pb16 = psum[bk][:, :].bitcast(BF16)
                            for j in range(4):
                                bi = quad * 4 + j
                                o, st = grp_tokens(g, bi)
                                B.tr(pb16[:, j * 128:(j + 1) * 128], VT[:, g, o:o + 127 * st + 1:st], identb[:],
                                     [r_VT[g], r_cb], [rps[bk]])
                            pv = pb16[:, 0:512].rearrange("p (j c) -> p j c", j=4)
                            b0 = g * 16 + quad * 4
                            B.cp("dve", VA0[:, b0:b0 + 4, 0:64], pv[:, :, 0:64], [], [rps[bk], r_VA[g]])
                            B.cp("dve", VA1[:, b0:b0 + 4, 64:128], pv[:, :, 64:128], [], [rps[bk], r_VA[g]])
                ctx["pump"](4)
                if wb < 7:
                    while dq:
                        f = dq.pop(0)
                        if f is not None:
                            f()
                            break
                if wb == 8:
                    while dq:
                        f = dq.pop(0)
                        if f is not None:
                            f()
            its = [(hh, g, m) for hh in range(2) for g in range(3) for m in range(8)]
            st_info = {}

            def nq_of(g, kb):
                nbr = 16 // GROUPS[g][1]
                return 256 if (kb + 1) % nbr != 0 else 128

            def issue_st(i):
                hh, g, m = its[i]
                bk = (0, 1, 2, 7)[stb[0] % 4]
                stb[0] += 1
                rows = slice(hh * 64, (hh + 1) * 64)
                for t in range(2):
                    kb = 2 * m + t
                    nq = nq_of(g, kb)
                    B.mm(psum[bk][:, t * 256:t * 256 + nq], QK[rows, 2 * g + 1, kb * 128:(kb + 1) * 128],
                         QK[rows, 2 * g, kb * 128:kb * 128 + nq], True, True,
                         [r_QK[2 * g], r_QK[2 * g + 1]], [rps[bk]])
                ls = i % 2
                ps_ = i % 5
                bias2 = BT[:, bsl, g * 2 + hh, :].unsqueeze(1).to_broadcast([128, 2, 256])
                B.tt("dve", LT[:, ls], psum[bk][:, :].rearrange("p (t q) -> p t q", t=2), bias2, ALU.add,
                     [r_BT[bsl]], [rps[bk], r_LT[ls]])
                B.act(PT[:, ps_], LT[:, ls], AF.Exp, [r_LT[ls]], [r_PT[ps_]])
                st_info[i] = ps_

            def issue_pv(i):
                hh, g, m = its[i]
                d = GROUPS[g][1]
                nbr = 16 // d
                M = 65 if hh == 0 else 128
                VA = VA0 if hh == 0 else VA1
                ps_ = st_info[i]
                for t in range(2):
                    kb = 2 * m + t
                    has_prev = kb % nbr != 0
                    if has_prev:
                        pslot, pt = (ps_, 0) if t == 1 else (st_info[i - 1], 1)
                    if g == 0:
                        outs = [(3 + kb // 4, slice((kb % 4) * 128, (kb % 4) * 128 + 128), slice(0, 128))]
                    elif g == 1:
                        r, nn = kb // 4, kb % 4
                        outs = [(3 + nn, slice(r, 512, 4), slice(0, 128))]
                    else:
                        outs = [(3 + sub, slice(kb, 512, 16), slice(32 * sub, 32 * sub + 32)) for sub in range(4)]
                    for (bo, ocols, qcols) in outs:
                        outp = psum[bo][0:M, ocols]
                        first = (g == 0 and kb % 4 == 0)
                        if has_prev:
                            pq = slice(128 + qcols.start, 128 + qcols.stop)

                            def f1(e, outp=outp, a=VA[:, g * 16 + kb - 1, 0:M], b=PT[:, pslot, pt, pq], first=first):
                                return e.matmul(outp, a, b, start=first, stop=False, skip_group_check=True)
                            B.S.op("pe", f1, [r_VA[g], r_PT[pslot]], [rps[bo]])
                            first = False

                        last = (g == 2 and kb == 15)

                        def f2(e, outp=outp, a=VA[:, g * 16 + kb, 0:M], b=PT[:, ps_, t, qcols], first=first, last=last):
                            return e.matmul(outp, a, b, start=first, stop=last, skip_group_check=True)
                        B.S.op("pe", f2, [r_VA[g], r_PT[ps_]], [rps[bo]])
                if g == 2 and m == 7:
                    for span in range(4):
                        if span % 2 == 0:
                            B.cp("dve", OA[0:M, hh, span * 512:(span + 1) * 512], psum[3 + span][0:M, :],
                                 [], [rps[3 + span], r_OA[hh]])
                        else:
                            B.act(OA[0:M, hh, span * 512:(span + 1) * 512], psum[3 + span][0:M, :], AF.Copy,
                                  [], [rps[3 + span], r_OA[hh]])
                    finalize(hh)

            def finalize(hh, hp=hp):
                drow = 64 if hh == 0 else 0
                R = slice(hh * 64, (hh + 1) * 64)

                def prepA(s):
                    cs = slice(s * 512, (s + 1) * 512)
                    den = OA[drow:drow + 1, hh, cs]
                    B.act(den, den, AF.Ln, [], [r_OA[hh]])
                    B.act(den, den, AF.Exp, [], [r_OA[hh]], scale=-1.0)

                def prepB(s):
                    cs = slice(s * 512, (s + 1) * 512)
                    den = OA[drow:drow + 1, hh, cs]
                    B.cp("dve", RHL[drow:drow + 1, 0, cs], den, [r_OA[hh]], [r_RHL])
                    B.tt("dve", RHL[drow:drow + 1, 1, cs], den, RHL[drow:drow + 1, 0, cs], ALU.subtract,
                         [r_OA[hh]], [r_RHL])

                def span(s):
                    bk = (0, 1, 2, 7)[stb[0] % 4]
                    stb[0] += 1
                    cs = slice(s * 512, (s + 1) * 512)
                    B.mm(psum[bk][:, :], onesb[drow:drow + 1, :], RHL[drow:drow + 1, 0, cs], True, False,
                         [r_cb, r_RHL], [rps[bk]])
                    B.mm(psum[bk][:, :], onesb[drow:drow + 1, :], RHL[drow:drow + 1, 1, cs], False, True,
                         [r_cb, r_RHL], [rps[bk]])
                    B.tt("dve", OA[R, hh, cs], OA[R, hh, cs], psum[bk][R, :], ALU.mult,
                         [], [rps[bk], r_OA[hh]])
                    B.tt("pool", GT[R, hp, cs], OA[R, hh, cs], SZ[R, cs], ALU.mult,
                         [r_OA[hh], r_SZ], [r_GT])
                for s in range(4):
                    dq.append(lambda s=s: prepA(s))
                    dq.append(None)
                    dq.append(lambda s=s: prepB(s))
                for _ in range(4):
                    dq.append(None)
                for s in range(4):
                    dq.append(lambda s=s: span(s))
                    if s < 3:
                        dq.append(None)

            if hp == 0:
                ctx["dump"]("QK", QK[:], r_QK)
                ctx["dump"]("VA0", VA0[:], r_VA)
                ctx["dump"]("VA1", VA1[:], r_VA)
                ctx["dump"]("SZ", SZ[:], [r_SZ])
                ctx["dump"]("BT", BT[:, 0], [r_BT[0]])
            LOOK = 3
            n = len(its)
            for i in range(min(LOOK, n)):
                issue_st(i)
            for i in range(n):
                B.S.begin_group()
                issue_pv(i)
                if i + LOOK < n:
                    issue_st(i + LOOK)
                B.S.end_group()
                if dq:
                    f = dq.pop(0)
                    if f is not None:
                        f()
            if hp == 7:
                while dq:
                    f = dq.pop(0)
                    if f is not None:
                        f()
        B.barrier()
    ctx["dump"]("GT", GT[:], [r_GT])
    finals = []
    with ExitStack() as es:
        E = es.enter_context
        ws = WStream(ctx, E, 2, "wo0s")
        WO = E(nc.sbuf_tensor("WO0", [128, 8, D], BF16))
        gpo = E(nc.sbuf_tensor("gpo0", [128, D], F32))
        r_WO = [B.R("WO%d" % i) for i in range(4)]
        r_gpo = B.R("gpo0")
        B.S.dma("sp", gpo[:], dr["gpost0"], d_par, (), [r_gpo])
        for wb in range(4):
            ws.load(dr["wo0"][:, wb * 256:(wb + 1) * 256], WO[:, :, wb * 256:(wb + 1) * 256], r_WO[wb])
        finals = out_proj_tail(ctx, E, WO, r_WO, gpo, r_gpo, dr["x"], dr["x1"], GT, r_GT, None, "t0",
                               dst_res=ctx.get("r_x1"))
        B.barrier()
    return finals


def out_proj_tail(ctx, E, WO, r_WO, gpo, r_gpo, xsrc, dst, GT, r_GT, tokap, name, src_res=None, dst_res=None):
    nc, B, psum, rps = ctx["nc"], ctx["B"], ctx["psum"], ctx["rps"]
    xt = E(nc.sbuf_tensor("xt_" + name, [128, 2, D], F32))
    yt = E(nc.sbuf_tensor("yt_" + name, [128, 2, D], F32))
    sq = E(nc.sbuf_tensor("sq_" + name, [128, 512], F32))
    st = E(nc.sbuf_tensor("st_" + name, [128, 2, 4], F32))
    r_xt = [B.R("xt0"), B.R("xt1")]
    r_yt = [B.R("yt0"), B.R("yt1")]
    r_sq = B.R("sq")
    r_st = [B.R("st0"), B.R("st1")]
    d_x = [ctx["new_dsem"]("d_x%s0" % name), ctx["new_dsem"]("d_x%s1" % name)]
    d_o = [ctx["new_dsem"]("d_o%s0" % name), ctx["new_dsem"]("d_o%s1" % name)]
    finals = []
    for tt in range(NT):
        sl = tt % 2
        if tokap is None:
            xrows, orows = xsrc[tt * 128:(tt + 1) * 128, :], dst[tt * 128:(tt + 1) * 128, :]
        else:
            jj, nb = tt // 2, tt % 2
            xrows = xsrc.rearrange("(n j) d -> j n d", j=8)[jj, nb * 128:(nb + 1) * 128, :]
            orows = dst.rearrange("(n j) d -> j n d", j=8)[jj, nb * 128:(nb + 1) * 128, :]
        B.S.dma("sp", xt[:, sl, :], xrows, d_x[sl],
                [src_res[tt]] if src_res else (), [r_xt[sl]])
        banks = (0, 1) if sl == 0 else (2, 3)
        for half in range(2):
            bk = banks[half]
            for c in range(8):
                lhs = GT[:, c, tt * 128:(tt + 1) * 128]
                B.mm(psum[bk][:, :], lhs, WO[:, c, half * 512:(half + 1) * 512], c == 0, c == 7,
                     [r_GT] + r_WO[2 * half:2 * half + 2], [rps[bk]])
            B.act(sq[:, :], psum[bk][:, :], AF.Square, [], [rps[bk], r_sq, r_st[sl]],
                  accum_out=st[:, sl, half:half + 1])
        B.tt("dve", st[:, sl, 2:3], st[:, sl, 0:1], st[:, sl, 1:2], ALU.add, [r_st[sl]], [r_st[sl]])
        rms_rstd(B, st[:, sl, 2:3], st[:, sl, 3:4], r_st[sl], r_st[sl])
        for half in range(2):
            bk = banks[half]
            hs = slice(half * 512, (half + 1) * 512)
            B.stt(yt[:, sl, hs], psum[bk][:, :], st[:, sl, 3:4], gpo[:, hs], ALU.mult, ALU.mult,
                  [r_st[sl], r_gpo], [rps[bk], r_yt[sl]])
        B.tt("pool", yt[:, sl, :], yt[:, sl, :], xt[:, sl, :], ALU.add, [r_xt[sl]], [r_yt[sl]])
        o = B.S.dma("pool", orows, yt[:, sl, :], d_o[sl], [r_yt[sl]],
                    [dst_res[tt]] if dst_res else [])
        fr = B.R("fin%d" % tt)
        fr.w = o
        finals.append(fr)
    return finals


def build_l1_decl(dt, dr):
    EI = "ExternalInput"
    dr["w1"] = dt("w1", [D, 2048], F32, kind=EI).ap()
    dr["wg1"] = dt("wg1", [D, D], F32, kind=EI).ap()
    dr["wo1"] = dt("wo1", [D, D], F32, kind=EI).ap()
    dr["gpre1"] = dt("gpre1", [128, 8], F32, kind=EI).ap()
    dr["gpost1"] = dt("gpost1", [128, D], F32, kind=EI).ap()
    dr["bglu"] = dt("bglu", [128, 8], F32, kind=EI).ap()
    dr["pl_a"] = dt("pl_a", [128, 3, 32], F32, kind=EI).ap()
    dr["pl_bc"] = dt("pl_bc", [128, 4, 32, 16], F32, kind=EI).ap()
    dr["dcol"] = dt("dcol", [128, 64], F32, kind=EI).ap()
    dr["mask8"] = dt("mask8", [128, 128], F32, kind=EI).ap()
    dr["esel"] = dt("esel", [128, 64, 128], BF16, kind=EI).ap()


def cmul(B, outr, outi, ar, ai, br, bi, t1, t2, R, neg_i=False, eng="dve", emit=None, Rw=None):
    if emit is None:
        emit = lambda f: f()
    Rw = R if Rw is None else Rw
    rd = [R] if Rw is R else [R, Rw]
    emit(lambda: B.tt(eng, t1, ar, br, ALU.mult, rd, [Rw]))
    emit(lambda: B.tt(eng, t2, ai, bi, ALU.mult, rd, [Rw]))
    emit(lambda: B.tt(eng, outr, t1, t2, ALU.subtract, rd, [Rw]))
    emit(lambda: B.tt(eng, t1, ar, bi, ALU.mult, rd, [Rw]))
    emit(lambda: B.tt(eng, t2, ai, br, ALU.mult, rd, [Rw]))
    if neg_i:
        emit(lambda: B.tt(eng, t1, t1, t2, ALU.add, rd, [Rw]))
        emit(lambda: B.ts(eng, outi, t1, -1.0, ALU.mult, rd, [Rw]))
    else:
        emit(lambda: B.tt(eng, outi, t1, t2, ALU.add, rd, [Rw]))


def l1_prep_stage1(ctx, E):
    nc, B, dr = ctx["nc"], ctx["B"], ctx["dr"]
    PEN = "dve"
    pa = E(nc.sbuf_tensor("pa", [128, 3, 32], F32))
    pbc = E(nc.sbuf_tensor("pbc", [128, 4, 32, 16], F32))
    W = E(nc.sbuf_tensor("Wk", [128, 16, 32], F32))
    PW = E(nc.sbuf_tensor("PW", [128, 9, 2, 32], F32))
    QW = E(nc.sbuf_tensor("QW", [128, 8, 2, 32], F32))
    bb = E(nc.sbuf_tensor("bbar", [128, 2, 32, 16], F32))
    T1 = E(nc.sbuf_tensor("T1", [128, 32, 16], F32))
    T2 = E(nc.sbuf_tensor("T2", [128, 32, 16], F32))
    RP = B.R("prep")
    d_p = ctx["new_dsem"]("d_prep")
    B.S.dma("sp", pa[:], dr["pl_a"], d_p, (), [RP])
    B.S.dma("sp", pbc[:], dr["pl_bc"], d_p, (), [RP])
    q = []
    emit = q.append
    RK = E(nc.sbuf_tensor("RK", [128, 16], F32))
    for k in range(1, 16):
        emit(lambda k=k: B.memset(PEN, RK[:, k:k + 1], float(k), [RP]))
    emit(lambda: B.memset(PEN, RK[:, 0:1], 1.0, [RP]))
    emit(lambda: B.recip(RK[:], RK[:], [RP], [RP]))
    are, aim, ldt = pa[:, 0, :], pa[:, 1, :], pa[:, 2, :]
    w = lambda k: W[:, k, :]
    t1s, t2s = T1[:, :, 0], T2[:, :, 0]
    tt = lambda o, a, b, op: emit(lambda: B.tt(PEN, o, a, b, op, [RP], [RP]))
    ts = lambda o, a, s1, op0, s2=None, op1=None: emit(lambda: B.ts(PEN, o, a, s1, op0, [RP], [RP], s2=s2, op1=op1))
    ms = lambda o, v: emit(lambda: B.memset(PEN, o, v, [RP]))
    cpy = lambda o, a: emit(lambda: B.cp(PEN, o, a, [RP], [RP]))
    cm = lambda *a, **k: cmul(B, *a, eng=PEN, emit=emit, **k)
    ts(w(1), ldt, 0.125, ALU.mult)
    ms(w(0), 1.0)
    for k in range(10, 0, -1):
        tt(w(0), w(0), w(1), ALU.mult)
        ts(w(0), w(0), RK[:, k:k + 1], ALU.mult, 1.0, ALU.add)
    for _ in range(3):
        tt(w(0), w(0), w(0), ALU.mult)
    ts(w(8), w(0), 1.0 / 16, ALU.mult)
    tt(w(1), are, w(8), ALU.mult)
    tt(w(2), aim, w(8), ALU.mult)
    ms(w(3), 1.0)
    ms(w(4), 0.0)
    for k in range(14, 1, -1):
        cm(w(5), w(6), w(3), w(4), w(1), w(2), t1s, t2s, RP)
        ts(w(3), w(5), RK[:, k:k + 1], ALU.mult, 1.0, ALU.add)
        ts(w(4), w(6), RK[:, k:k + 1], ALU.mult)
    cm(w(5), w(6), w(3), w(4), w(1), w(2), t1s, t2s, RP)
    for _ in range(4):
        ts(w(7), w(5), 2.0, ALU.add)
        cm(w(3), w(4), w(5), w(6), w(7), w(6), t1s, t2s, RP)
        cpy(w(5), w(3))
        cpy(w(6), w(4))
    ms(PW[:, 0, 0, :], 1.0)
    ms(PW[:, 0, 1, :], 0.0)
    ms(QW[:, 0, 0, :], 1.0)
    ms(QW[:, 0, 1, :], 0.0)
    abr, abi = PW[:, 1, 0, :], PW[:, 1, 1, :]
    ts(abr, w(5), 1.0, ALU.add)
    cpy(abi, w(6))
    for k in range(2, 9):
        cm(PW[:, k, 0, :], PW[:, k, 1, :], PW[:, k - 1, 0, :], PW[:, k - 1, 1, :], abr, abi, t1s, t2s, RP)
    tt(w(8), abr, abr, ALU.mult)
    tt(w(9), abi, abi, ALU.mult)
    tt(w(8), w(8), w(9), ALU.add)
    emit(lambda: B.recip(w(9), w(8), [RP], [RP]))
    tt(QW[:, 1, 0, :], abr, w(9), ALU.mult)
    tt(w(8), abi, w(9), ALU.mult)
    ts(QW[:, 1, 1, :], w(8), -1.0, ALU.mult)
    for k in range(2, 8):
        cm(QW[:, k, 0, :], QW[:, k, 1, :], QW[:, k - 1, 0, :], QW[:, k - 1, 1, :],
           QW[:, 1, 0, :], QW[:, 1, 1, :], t1s, t2s, RP)
    cpy(w(10), w(5))
    tt(w(11), are, are, ALU.mult)
    tt(w(12), aim, aim, ALU.mult)
    tt(w(11), w(11), w(12), ALU.add)
    emit(lambda: B.recip(w(12), w(11), [RP], [RP]))
    tt(w(13), w(10), are, ALU.mult)
    tt(w(14), abi, aim, ALU.mult)
    tt(w(13), w(13), w(14), ALU.add)
    tt(w(13), w(13), w(12), ALU.mult)
    tt(w(14), abi, are, ALU.mult)
    tt(w(15), w(10), aim, ALU.mult)
    tt(w(14), w(14), w(15), ALU.subtract)
    tt(w(14), w(14), w(12), ALU.mult)
    bc = lambda ap: ap.unsqueeze(2).to_broadcast([128, 32, 16])
    cm(bb[:, 0], bb[:, 1], bc(w(13)), bc(w(14)), pbc[:, 0], pbc[:, 1], T1[:], T2[:], RP)

    def pump(n=None):
        k = len(q) if n is None else min(n, len(q))
        for _ in range(k):
            q.pop(0)()
    ctx["pump"] = pump
    ctx["prep1"] = dict(pbc=pbc, PW=PW, QW=QW, bb=bb, T1=T1, T2=T2, RP=RP, bc=bc)


def build_l1(ctx):
    from contextlib import ExitStack
    nc, B, dr, psum, rps = ctx["nc"], ctx["B"], ctx["dr"], ctx["psum"], ctx["rps"]
    hT, GT, ident, small = ctx["hT"], ctx["GT"], ctx["ident"], ctx["small"]
    r_ident = ctx["r_ident"]
    PI = math.pi
    d_par = ctx["new_dsem"]("d_par1")
    r_gp = B.R("gpre1")
    r_bg = B.R("bglu")
    B.S.dma("sp", small[:, 8:16], dr["gpre1"], d_par, (), [r_gp])
    B.S.dma("sp", small[:, 16:24], dr["bglu"], d_par, (), [r_bg])
    r_hTall = B.R("hTall")
    r_Gb = B.R("Gb")
    Gb = GT[:, :, :].rearrange("p c (a n) -> p (c a) n", n=256)
    with ExitStack() as es1:
        E1 = es1.enter_context
        T0b = E1(nc.sbuf_tensor("T0b", [128, 64, 128], BF16))
        BmT = E1(nc.sbuf_tensor("BmT", [128, 64, 2, 64], BF16))
        Cmb = E1(nc.sbuf_tensor("Cmb", [128, 32, 2, 128], BF16))
        A8 = E1(nc.sbuf_tensor("A8", [128, 2, 2, 32], F32))
        r_T0b, r_BmT, r_Cmb, r_A8 = B.R("T0b"), B.R("BmT"), B.R("Cmb"), B.R("A8")
        with ExitStack() as es:
            E = es.enter_context
            ctx["pump"]()
            p1 = ctx["prep1"]
            pbc, PW, QW, bb, T1, T2, RP, bc = (p1[k] for k in ("pbc", "PW", "QW", "bb", "T1", "T2", "RP", "bc"))
            cre, cim = pbc[:, 2], pbc[:, 3]
            dcol = E(nc.sbuf_tensor("dcol_sb", [128, 64], F32))
            mask8 = E(nc.sbuf_tensor("mask8_sb", [128, 128], F32))
            Lh = E(nc.sbuf_tensor("Lh", [128, 2, 32, 128], BF16))
            Rh = E(nc.sbuf_tensor("Rh", [128, 2, 32, 128], BF16))
            identb1 = E(nc.sbuf_tensor("identb1", [128, 128], BF16))
            r_ib1 = B.R("identb1")
            B.cp("dve", identb1[:], ident[:], [r_ident], [r_ib1])
            TM = E(nc.sbuf_tensor("TM", [128, 2, 128], F32))
            r_dm, r_TM = B.R("dcolmask"), [B.R("TM0"), B.R("TM1")]
            B.S.dma("sp", dcol[:], dr["dcol"], d_par, (), [r_dm])
            B.S.dma("sp", mask8[:], dr["mask8"], d_par, (), [r_dm])
            T3 = E(nc.sbuf_tensor("T3", [128, 32, 16], F32))
            T4 = E(nc.sbuf_tensor("T4", [128, 32, 16], F32))
            R_L, R_R = B.R("prepL"), B.R("prepR")
            for j in range(8):
                js = slice(j * 16, (j + 1) * 16)
                cmul(B, Rh[:, 0, :, js], Rh[:, 1, :, js], bc(QW[:, 7 - j, 0, :]), bc(QW[:, 7 - j, 1, :]),
                     cre, cim, T3[:], T4[:], RP, neg_i=True, eng="dve", Rw=R_R)
            for j in range(8):
                js = slice(j * 16, (j + 1) * 16)
                cmul(B, Lh[:, 0, :, js], Lh[:, 1, :, js], bc(PW[:, 7 - j, 0, :]), bc(PW[:, 7 - j, 1, :]),
                     bb[:, 0], bb[:, 1], T1[:], T2[:], RP, Rw=R_L)
                cmul(B, Cmb[:, :, 0, js], Cmb[:, :, 1, js], bc(PW[:, j + 1, 0, :]), bc(PW[:, j + 1, 1, :]),
                     cre, cim, T1[:], T2[:], RP, neg_i=True, Rw=R_L)
            B.cp("dve", Cmb[:, 0, 0, 0:1], Cmb[:, 0, 0, 0:1], [R_L], [r_Cmb])
            B.cp("dve", A8[:, 0, 0, :], PW[:, 8, 0, :], [RP], [r_A8])
            B.cp("dve", A8[:, 0, 1, :], PW[:, 8, 0, :], [RP], [r_A8])
            B.ts("dve", A8[:, 1, 0, :], PW[:, 8, 1, :], -1.0, ALU.mult, [RP], [r_A8])
            B.cp("dve", A8[:, 1, 1, :], PW[:, 8, 1, :], [RP], [r_A8])
            r_T0g = [B.R("T0g%d" % i) for i in range(4)]
            for g in range(64):
                gh, pr = g // 32, g % 32
                Rr = slice(gh * 64, (gh + 1) * 64)
                bk = g % 2
                B.mm(psum[bk][:, 0:128], Lh[Rr, 0, pr, :], Rh[Rr, 0, pr, :], True, False, [R_L, R_R], [rps[bk]])
                B.mm(psum[bk][:, 0:128], Lh[Rr, 1, pr, :], Rh[Rr, 1, pr, :], False, True, [R_L, R_R], [rps[bk]])
                tm = g % 2
                B.tt("dve", TM[:, tm, :], psum[bk][:, 0:128], mask8[:], ALU.mult, [r_dm], [rps[bk], r_TM[tm]])
                B.stt(T0b[:, g, :], ident[:], dcol[:, g:g + 1], TM[:, tm, :], ALU.mult, ALU.add,
                      [r_dm, r_TM[tm], r_ident], [r_T0g[g % 4]])
            for g4 in range(16):
                bk = 2 + g4 % 2
                for gg in range(4):
                    g = g4 * 4 + gg
                    gh, pr = g // 32, g % 32
                    Rr = slice(gh * 64, (gh + 1) * 64)
                    for part in range(2):
                        cs = (gg * 2 + part) * 64
                        B.tr(psum[bk][:, :].bitcast(BF16)[:, cs:cs + 64], Lh[Rr, part, pr, :], identb1[Rr, Rr],
                             [R_L, r_ib1], [rps[bk]])
                B.cp("dve", BmT[:, g4 * 4:g4 * 4 + 4].rearrange("p g a c -> p (g a c)"),
                     psum[bk][:, :].bitcast(BF16)[:, 0:512], [], [rps[bk], r_BmT])
            ctx["dump"]("T0b", T0b[:], [r_T0b])
            ctx["dump"]("BmT", BmT[:], [r_BmT])
            ctx["dump"]("Cmb", Cmb[:], [r_Cmb])
            ctx["dump"]("A8", A8[:], [r_A8])
            ctx["dump"]("PW", PW[:], [RP])
            ctx["dump"]("QW", QW[:], [RP])
            ctx["dump"]("Lh", Lh[:], [R_L])
            ctx["dump"]("Rh", Rh[:], [R_R])
            B.barrier()
        if ctx.get("stop") == "prep":
            return []
        V = E1(nc.sbuf_tensor("Vssm", [128, 64, 256], BF16))
        r_V = B.R("V")
        with ExitStack() as es:
            ctx2 = dict(ctx)
            ctx2["r_hT"] = [r_hTall]
            make_hT(ctx2, dr["x1"], es.enter_context, small[:, 8:16], r_gp, perm8=True,
                    src_res=ctx.get("r_x1"), name="b")
            B.barrier()
        ctx["dump"]("hT1", hT[:], [r_hTall])
        if ctx.get("stop") == "A":
            return []
        with ExitStack() as es:
            E = es.enter_context
            ws = WStream(ctx, E, 2, "w1s", cols=128)
            uTb = E(nc.sbuf_tensor("uTb", [128, 2, S], BF16))
            Es = E(nc.sbuf_tensor("Esel", [128, 64, 128], BF16))
            r_uT = [B.R("uT0"), B.R("uT1")]
            r_Es = B.R("Esel")
            d_es = ctx["new_dsem"]("d_es")
            B.S.dma("sp", Es[:], dr["esel"], d_es, (), [r_Es])
            pbk = 0
            for fc in range(8):
                wbf, r_w = ws.load(dr["w1"][:, fc * 128:fc * 128 + 128])
                colo = 0
                us = fc % 2
                for s in range(4):
                    bk = pbk
                    pbk ^= 1
                    for c in range(8):
                        B.mm(psum[bk][:, :], wbf[:, c, colo:colo + 128], hT[:, c, s * 512:(s + 1) * 512],
                             c == 0, c == 7, [r_w, r_hTall], [rps[bk]])
                    B.act(uTb[:, us, s * 512:(s + 1) * 512], psum[bk][:, :], AF.Copy, [],
                          [rps[bk], r_uT[us]])
                for g8 in range(8):
                    g = fc * 8 + g8
                    bk = 2 + g % 4
                    for j in range(8):
                        B.mm(psum[bk][:, 0:256], Es[:, g8 * 8 + j, :], uTb[:, us, j * 256:(j + 1) * 256],
                             j == 0, j == 7, [r_Es, r_uT[us]], [rps[bk]])
                    B.cp("dve", V[:, g, :], psum[bk][:, 0:256], [], [rps[bk], r_V])
            ctx["dump"]("V", V[:], [r_V])
            B.barrier()
        if ctx.get("stop") == "B":
            return []
        with ExitStack() as es:
            E = es.enter_context
            Zb = E(nc.sbuf_tensor("Zb", [128, 2, 65, 2, 32], F32))
            Sb = ctx["prep1"]["pbc"][:].rearrange("p a b c -> p (a b c)").bitcast(BF16).rearrange(
                "p (n a r) -> p n a r", a=2, r=32)
            t1 = E(nc.sbuf_tensor("zt1", [128, 2, 32], F32))
            t2 = E(nc.sbuf_tensor("zt2", [128, 2, 32], F32))
            r_Z = [B.R("Zb0"), B.R("Zb1")]
            r_Sb = B.R("Sb")
            r_t = B.R("zt")
            B.memset("dve", Zb[:, 0, 0], 0.0, [r_Z[0]])

            def local_states(tb):
                zb = tb % 2
                ns = slice(tb * 64, (tb + 1) * 64)
                for q in range(8):
                    bk = q % 2
                    for pp in range(4):
                        pr = q * 4 + pp
                        for gh in range(2):
                            g = gh * 32 + pr
                            for part in range(2):
                                cs = (pp * 2 + part) * 64
                                B.mm(psum[bk][gh * 64:(gh + 1) * 64, cs:cs + 64], BmT[:, g, part, :],
                                     V[:, g, ns], True, True, [r_BmT, r_V], [rps[bk]])
                    dst = Zb[:, zb, 1:65, :, q * 4:q * 4 + 4].rearrange("p n a r -> p r a n")
                    srcp = psum[bk][:, :].rearrange("p (r a n) -> p r a n", r=4, a=2)
                    B.act(dst, srcp, AF.Copy, [], [rps[bk], r_Z[zb]])

            def recurrence(tb):
                zb = tb % 2
                for s in range(64):
                    zs = Zb[:, zb, s]
                    zn = Zb[:, zb, s + 1]
                    B.tt("dve", t1[:], A8[:, 0], zs, ALU.mult, [r_A8, r_Z[zb]], [r_t])
                    B.tt("dve", t2[:], A8[:, 1], Zb[:, zb, s, ::-1, :], ALU.mult, [r_A8, r_Z[zb]], [r_t])
                    B.tt("dve", zn, zn, t1[:], ALU.add, [r_t], [r_Z[zb]])
                    B.tt("dve", zn, zn, t2[:], ALU.add, [r_t], [r_Z[zb]])
                if tb + 1 < 4:
                    B.cp("dve", Zb[:, 1 - zb, 0], Zb[:, zb, 64], [r_Z[zb]], [r_Z[1 - zb]])
                B.act(Sb, Zb[:, zb, 0:64], AF.Copy, [r_Z[zb]], [r_Sb])

            def outputs(tb):
                ns = slice(tb * 64, (tb + 1) * 64)
                for fc in range(8):
                    bk = 2 + fc % 4
                    for g8 in range(8):
                        g = fc * 8 + g8
                        gh, pr = g // 32, g % 32
                        Rr = slice(gh * 64, (gh + 1) * 64)
                        outp = psum[bk][:, g8 * 64:(g8 + 1) * 64]
                        B.mm(outp, T0b[:, g, :], V[:, g, ns], True, False, [r_T0b, r_V], [rps[bk]])
                        B.mm(outp, Cmb[Rr, pr, 0, :], Sb[Rr, :, 0, pr], False, False, [r_Cmb, r_Sb], [rps[bk]])
                        B.mm(outp, Cmb[Rr, pr, 1, :], Sb[Rr, :, 1, pr], False, True, [r_Cmb, r_Sb], [rps[bk]])
                    yp = psum[bk][:, :]
                    dst = Gb[:, fc * 8:(fc + 1) * 8, ns]
                    B.act(dst, yp.rearrange("p (g n) -> p g n", g=8), AF.Gelu_apprx_tanh, [],
                          [rps[bk], r_Gb])

            local_states(0)
            for tb in range(4):
                if tb + 1 < 4:
                    local_states(tb + 1)
                recurrence(tb)
                outputs(tb)
            ctx["dump"]("Gb", GT[:], [r_Gb])
            B.barrier()
        B.barrier()
    if ctx.get("stop") == "C":
        return []
    finals = []
    with ExitStack() as es2:
        E2 = es2.enter_context
        gTb = E2(nc.sbuf_tensor("gTb", [128, 8, S], BF16))
        r_gT = B.R("gTb")
        r_G2 = B.R("G2T")
        with ExitStack() as es:
            E = es.enter_context
            Es = E(nc.sbuf_tensor("Esel2", [128, 64, 128], BF16))
            r_Es = B.R("Esel2")
            d_es = ctx["new_dsem"]("d_es2")
            B.S.dma("sp", Es[:], dr["esel"], d_es, (), [r_Es])
            for fc in range(8):
                for ip in range(4):
                    bk = (fc * 4 + ip) % 4
                    for ii in range(2):
                        i0 = ip * 2 + ii
                        for g8 in range(8):
                            B.mm(psum[bk][:, ii * 256:(ii + 1) * 256], Es[:, i0 * 8 + g8, :],
                                 Gb[:, fc * 8 + g8, :], g8 == 0, g8 == 7, [r_Es, r_Gb], [rps[bk]])
                    B.cp("dve" if ip % 2 == 0 else "act_copy", gTb[:, fc, ip * 512:(ip + 1) * 512],
                         psum[bk][:, :], [], [rps[bk], r_gT]) if False else \
                        B.cp("dve", gTb[:, fc, ip * 512:(ip + 1) * 512], psum[bk][:, :], [], [rps[bk], r_gT])
            ctx["dump"]("gTb", gTb[:], [r_gT])
            B.barrier()
        if ctx.get("stop") == "D1":
            return []
        with ExitStack() as es:
            E = es.enter_context
            wsg = WStream(ctx, E, 2, "wg1s")
            wsz = WStream(ctx, E, 2, "wz1s")
            sg = E(nc.sbuf_tensor("sg", [128, 2, 512], F32))
            sz = E(nc.sbuf_tensor("sz", [128, 2, 512], F32))
            r_sg = [B.R("sg0"), B.R("sg1")]
            r_sz = [B.R("sz0"), B.R("sz1")]
            k = 0
            for fo in range(8):
                if fo % 2 == 0:
                    wg, r_wg = wsg.load(dr["wg1"][:, fo * 128:fo * 128 + 256])
                    wz, r_wz = wsz.load(dr["w1"][:, 1024 + fo * 128:1024 + fo * 128 + 256])
                colo = (fo % 2) * 128
                for s in range(4):
                    sl = k % 2
                    k += 1
                    ba, bb_ = (0, 1) if sl == 0 else (2, 3)
                    cs = slice(s * 512, (s + 1) * 512)
                    for c in range(8):
                        B.mm(psum[ba][:, :], wg[:, c, colo:colo + 128], gTb[:, c, cs], c == 0, c == 7,
                             [r_wg, r_gT], [rps[ba]])
                    for c in range(8):
                        B.mm(psum[bb_][:, :], wz[:, c, colo:colo + 128], hT[:, c, cs], c == 0, c == 7,
                             [r_wz, r_hTall], [rps[bb_]])
                    B.act(sg[:, sl, :], psum[ba][:, :], AF.Sigmoid, [r_bg], [rps[ba], r_sg[sl]],
                          bias=small[:, 16 + fo:17 + fo])
                    B.act(sz[:, sl, :], psum[bb_][:, :], AF.Sigmoid, [], [rps[bb_], r_sz[sl]])
                    B.tt("pool", sg[:, sl, :], sg[:, sl, :], sz[:, sl, :], ALU.mult, [r_sz[sl]], [r_sg[sl]])
                    B.tt("dve", sg[:, sl, :], sg[:, sl, :], psum[bb_][:, :], ALU.mult, [], [rps[bb_], r_sg[sl]])
                    dstn = GT[:, fo, :].rearrange("p (n j) -> p j n", j=8)[:, 2 * s:2 * s + 2, :]
                    B.tt("dve", dstn, sg[:, sl, :].rearrange("p (j n) -> p j n", j=2),
                         gTb[:, fo, cs].rearrange("p (j n) -> p j n", j=2), ALU.mult,
                         [r_sg[sl], r_gT], [r_G2])
            ctx["dump"]("G2T", GT[:], [r_G2])
            B.barrier()
        if ctx.get("stop") == "D2":
            return []
        with ExitStack() as es:
            E = es.enter_context
            ws = WStream(ctx, E, 2, "wo1s")
            WO = E(nc.sbuf_tensor("WO1", [128, 8, D], BF16))
            gpo = E(nc.sbuf_tensor("gpo1", [128, D], F32))
            r_WO = [B.R("WO1_%d" % i) for i in range(4)]
            r_gpo = B.R("gpo1")
            B.S.dma("sp", gpo[:], dr["gpost1"], d_par, (), [r_gpo])
            for wb in range(4):
                ws.load(dr["wo1"][:, wb * 256:(wb + 1) * 256], WO[:, :, wb * 256:(wb + 1) * 256], r_WO[wb])

            tokap = None
            finals = out_proj_tail(ctx, E, WO, r_WO, gpo, r_gpo, dr["x1"], dr["out"], GT, r_G2, tokap, "t1",
                                   src_res=ctx.get("r_x1"))
            B.barrier()
    return finals


_PROG = {}


def _prog(mode):
    if mode not in _PROG:
        _PROG[mode] = build_program(mode)
    return _PROG[mode]


def _l0_inputs(inp, b):
    f = np.float32
    return {
        "x": np.ascontiguousarray(inp["x"][b], dtype=f),
        "w0": inp["_w0"], "wo0": inp["_wo0"], "gpre0": inp["_gpre0"], "gpost0": inp["_gpost0"],
        "bias0": inp["_bias0"], "ident": inp["_ident"],
    }


def _prep_l0(inputs):
    f = np.float32
    p = {}
    p["_w0"] = _perm_w_in(np.asarray(inputs["attn_w_in"][0], f))
    p["_wo0"] = np.ascontiguousarray(np.asarray(inputs["attn_w_out"][0], f))
    p["_gpre0"] = np.ascontiguousarray(np.asarray(inputs["attn_pre_norm"][0], f).reshape(8, 128).T)
    p["_gpost0"] = np.ascontiguousarray(np.broadcast_to(np.asarray(inputs["attn_post_norm"][0], f), (128, D)))
    p["_bias0"] = _bias_tiles(np.asarray(inputs["rel_bias"], f))
    p["_ident"] = np.eye(128, dtype=f)
    return p


def run_l0(inputs, debug=False):
    inp = dict(inputs)
    inp.update(_prep_l0(inputs))
    inp["x"] = np.asarray(inputs["x"], np.float32)
    nc = build_program("l0", debug=debug)
    in_maps = [_l0_inputs(inp, b) for b in range(8)]
    res = run_bass_kernel_spmd(nc, in_maps, core_ids=list(range(8)))
    if debug:
        return np.stack([r["x1"] for r in res.results], axis=0), res.results[0]
    return np.stack([r["x1"] for r in res.results], axis=0)


def _prep_l1(inputs):
    f = np.float32
    p = {}
    p["w1"] = np.ascontiguousarray(np.asarray(inputs["ssm_w_in"][0], f))
    p["wg1"] = np.ascontiguousarray(np.asarray(inputs["ssm_w_glu"][0], f))
    p["wo1"] = np.ascontiguousarray(np.asarray(inputs["ssm_w_out"][0], f))
    p["gpre1"] = np.ascontiguousarray(np.asarray(inputs["ssm_pre_norm"][0], f).reshape(8, 128).T)
    p["gpost1"] = np.ascontiguousarray(np.broadcast_to(np.asarray(inputs["ssm_post_norm"][0], f), (128, D)))
    p["bglu"] = np.ascontiguousarray(np.asarray(inputs["ssm_b_glu"][0], f).reshape(8, 128).T)

    def pl(a):
        a = np.asarray(a, f)
        a = a.reshape((2, 32, 64) + a.shape[2:])
        a = np.moveaxis(a, 2, 1)
        return np.ascontiguousarray(a.reshape((128, 32) + a.shape[3:]))
    are = pl(inputs["ssm_a_re"][0])
    aim = pl(inputs["ssm_a_im"][0])
    ldt = pl(np.broadcast_to(np.asarray(inputs["ssm_log_dt"][0], f)[:, None], (64, 64)))
    p["pl_a"] = np.ascontiguousarray(np.stack([are, aim, ldt], axis=1))
    bre = pl(inputs["ssm_b_re"][0])
    bim = pl(inputs["ssm_b_im"][0])
    cre = pl(np.swapaxes(np.asarray(inputs["ssm_c_re"][0], f), 1, 2))
    cim = pl(np.swapaxes(np.asarray(inputs["ssm_c_im"][0], f), 1, 2))
    p["pl_bc"] = np.ascontiguousarray(np.stack([bre, bim, cre, cim], axis=1))
    dvec = np.asarray(inputs["ssm_d"][0], f).reshape(64, 16)
    p["dcol"] = np.ascontiguousarray(np.tile(dvec.T, (8, 1)))
    jj = np.arange(128) // 16
    p["mask8"] = (jj[:, None] <= jj[None, :]).astype(f)
    k = np.arange(128)
    es = np.zeros((128, 64, 128), f)
    for a in range(8):
        for b in range(8):
            es[:, a * 8 + b, :] = ((k[:, None] // 16 == a) & (k[None, :] // 16 == b)
                                   & (k[:, None] % 16 == k[None, :] % 16))
    p["esel"] = es.astype(ml_dtypes.bfloat16)
    p["ident"] = np.eye(128, dtype=f)
    return p


def run_l1(inputs, x1, debug=False, stop=None):
    p = _prep_l1(inputs)
    nc = build_program("l1", debug=debug, stop=stop)
    in_maps = []
    for b in range(8):
        m = dict(p)
        m["x1"] = np.ascontiguousarray(x1[b], dtype=np.float32)
        in_maps.append(m)
    res = run_bass_kernel_spmd(nc, in_maps, core_ids=list(range(8)))
    out = np.stack([r["out"] for r in res.results], axis=0)
    if debug:
        return out, res.results[0]
    return out


FUSED = True


def kernel(**inputs):
    p0 = _prep_l0(inputs)
    p1 = _prep_l1(inputs)
    x = np.asarray(inputs["x"], np.float32)
    shared0 = {"w0": p0["_w0"], "wo0": p0["_wo0"], "gpre0": p0["_gpre0"], "gpost0": p0["_gpost0"],
               "bias0": p0["_bias0"], "ident": p0["_ident"]}
    if FUSED:
        nc = _prog("fused")
        shared = dict(shared0)
        shared.update(p1)
        in_maps = []
        for b in range(8):
            m = dict(shared)
            m["x"] = np.ascontiguousarray(x[b])
            in_maps.append(m)
        res = run_bass_kernel_spmd(nc, in_maps, core_ids=list(range(8)))
        return np.stack([np.asarray(r["out"], np.float32) for r in res.results], axis=0)
    in_maps = []
    for b in range(8):
        m = dict(shared0)
        m["x"] = np.ascontiguousarray(x[b])
        in_maps.append(m)
    res = run_bass_kernel_spmd(_prog("l0"), in_maps, core_ids=list(range(8)))
    x1 = [np.asarray(r["x1"], np.float32) for r in res.results]
    in_maps = []
    for b in range(8):
        m = dict(p1)
        m["x1"] = np.ascontiguousarray(x1[b])
        in_maps.append(m)
    res = run_bass_kernel_spmd(_prog("l1"), in_maps, core_ids=list(range(8)))
    return np.stack([np.asarray(r["out"], np.float32) for r in res.results], axis=0)
```

```python
import math
import numpy as np
import ml_dtypes
import concourse.bass as bass
import concourse.mybir as mybir
from concourse.bass_utils import run_bass_kernel_spmd

F32 = mybir.dt.float32
BF16 = mybir.dt.bfloat16
AF = mybir.ActivationFunctionType
ALU = mybir.AluOpType

D = 1024
S = 2048
NT = 16
EPS = 1e-6
GROUPS = ((128, 1), (512, 4), (2048, 16))
NEGM = -30000.0
SSM_G = 64
SSM_P = 64
SSM_C = 16


class Res:
    __slots__ = ("name", "w", "rs", "rd")

    def __init__(self, name):
        self.name = name
        self.w = None
        self.rs = {}
        self.rd = []


class DmaSem:
    def __init__(self, sem):
        self.sem = sem
        self.count = 0


class Op:
    __slots__ = ("eng", "fn", "deps", "need", "seq", "dsem", "dval", "idx", "dreq", "group", "gidx")


class Sched:
    ENGS = ("pe", "act", "dve", "pool", "sp")

    def __init__(self, nc):
        self.nc = nc
        self.ops = {e: [] for e in self.ENGS}
        self.allops = []
        self.cur_group = None
        self.ngroups = 0

    def begin_group(self):
        self.ngroups += 1
        self.cur_group = self.ngroups

    def end_group(self):
        self.cur_group = None

    def _mk(self, eng, fn, reads, writes, isdma=False):
        o = Op()
        o.eng = eng
        o.fn = fn
        o.need = False
        o.seq = 0
        o.dsem = None
        o.dval = 0
        deps = []
        for r in reads:
            if r.w is not None:
                deps.append(r.w)
        for r in writes:
            if r.w is not None:
                deps.append(r.w)
            deps.extend(r.rs.values())
            deps.extend(r.rd)
        o.deps = deps
        o.dreq = {}
        for p in deps:
            if p.dsem is not None:
                o.dreq[id(p.dsem)] = (p.dsem.sem, p.dsem.count)
        for r in reads:
            if isdma:
                r.rd.append(o)
            else:
                r.rs[eng] = o
        for r in writes:
            r.w = o
            r.rs = {}
            r.rd = []
        o.idx = len(self.ops[eng])
        o.group = self.cur_group
        o.gidx = len(self.allops)
        self.ops[eng].append(o)
        self.allops.append(o)
        return o

    def op(self, eng, fn, reads=(), writes=()):
        return self._mk(eng, fn, list(reads), list(writes))

    def dma(self, eng, out, in_, dsem, reads=(), writes=()):
        def fn(e):
            return e.dma_start(out=out, in_=in_)
        o = self._mk(eng, fn, list(reads), list(writes), isdma=True)
        dsem.count += 16
        o.dsem = dsem
        o.dval = dsem.count
        return o


def _t5_bucket_np(dist):
    max_exact = 16
    n = np.maximum(dist, 1).astype(np.float32)
    large = max_exact + (np.log(n / np.float32(max_exact)) / np.float32(math.log(2048 / max_exact))
                         * np.float32(32 - max_exact)).astype(np.int32)
    large = np.minimum(large, 31)
    return np.where(dist < max_exact, dist, large)


def _bias_tiles(rel_bias):
    j = np.arange(128)[:, None]
    q = np.arange(256)[None, :]
    delta = q - j
    valid = (delta >= 0) & (delta <= 128)
    out = np.empty((8, 128, 6, 256), np.float32)
    for g, (_, dil) in enumerate(GROUPS):
        bucket = _t5_bucket_np(np.maximum(delta, 0) * dil)
        for h in range(16):
            t = rel_bias[bucket, h]
            t = np.where(valid, t, np.float32(NEGM)).astype(np.float32)
            out[h // 2, :, g * 2 + (h % 2), :] = t
    return out


def _perm_w_in(w):
    out = np.empty((8, D, 1280), np.float32)
    for hp in range(8):
        cs = slice(hp * 128, (hp + 1) * 128)
        blk = []
        for g in range(3):
            blk.append(w[:, (0 * 3 + g) * 1024:(0 * 3 + g + 1) * 1024][:, cs])
            blk.append(w[:, (1 * 3 + g) * 1024:(1 * 3 + g + 1) * 1024][:, cs])
        for g in range(3):
            blk.append(w[:, (2 * 3 + g) * 1024:(2 * 3 + g + 1) * 1024][:, cs])
        blk.append(w[:, 9216:10240][:, cs])
        out[hp] = np.concatenate(blk, axis=1)
    return out


class Builder:
    def __init__(self, nc):
        self.nc = nc
        self.S = Sched(nc)
        self.dsems = []
        self.allres = []
        self.last_barrier = None
        self.scratch = None

    def R(self, name):
        r = Res(name)
        r.w = self.last_barrier
        self.allres.append(r)
        return r

    def barrier(self):
        scr = self.scratch

        def fn(e):
            return e.memset(scr, 0.0)
        o = self.S.op("pool", fn, (), list(self.allres))
        self.last_barrier = o
        return o

    def mm(self, out, lhsT, rhs, start, stop, reads, writes):
        def fn(e):
            return e.matmul(out, lhsT, rhs, start=start, stop=stop)
        return self.S.op("pe", fn, reads, writes)

    def tr(self, out, in_, ident, reads, writes):
        def fn(e):
            return e.transpose(out, in_, ident)
        return self.S.op("pe", fn, reads, writes)

    def act(self, out, in_, func, reads, writes, scale=None, bias=None, accum_out=None):
        kw = {}
        if scale is not None:
            kw["scale"] = scale
        if bias is not None:
            kw["bias"] = bias
        if accum_out is not None:
            kw["accum_out"] = accum_out

        def fn(e):
            return e.activation(out, in_, func, **kw)
        return self.S.op("act", fn, reads, writes)

    def tt(self, eng, out, in0, in1, op, reads, writes):
        def fn(e):
            return e.tensor_tensor(out, in0, in1, op)
        return self.S.op(eng, fn, reads, writes)

    def ts(self, eng, out, in0, s1, op0, reads, writes, s2=None, op1=None):
        def fn(e):
            if op1 is None:
                return e.tensor_scalar(out, in0, s1, None, op0)
            return e.tensor_scalar(out, in0, s1, s2, op0, op1)
        return self.S.op(eng, fn, reads, writes)

    def stt(self, out, in0, scalar, in1, op0, op1, reads, writes):
        def fn(e):
            return e.scalar_tensor_tensor(out, in0, scalar, in1, op0, op1)
        return self.S.op("dve", fn, reads, writes)

    def cp(self, eng, out, in_, reads, writes):
        def fn(e):
            return e.tensor_copy(out, in_)
        return self.S.op(eng, fn, reads, writes)

    def memset(self, eng, ap, val, writes):
        def fn(e):
            return e.memset(ap, val)
        return self.S.op(eng, fn, (), writes)

    def recip(self, out, in_, reads, writes):
        def fn(e):
            return e.reciprocal(out, in_)
        return self.S.op("dve", fn, reads, writes)


def rms_rstd(B, ss, rs, r_ss, r_rs):
    B.ts("dve", rs, ss, 1.0 / D, ALU.mult, [r_ss], [r_rs], s2=EPS, op1=ALU.add)
    B.act(rs, rs, AF.Sqrt, [r_rs], [r_rs])
    B.recip(rs, rs, [r_rs], [r_rs])


def build_program(mode="fused", debug=False, stop=None):
    nc = bass.Bass("TRN2", target_bir_lowering=False)
    B = Builder(nc)
    dt = nc.dram_tensor
    dr = {}
    do_l0 = mode in ("l0", "fused")
    do_l1 = mode in ("l1", "fused")
    if do_l0:
        dr["x"] = dt("x", [S, D], F32, kind="ExternalInput").ap()
        dr["w0"] = dt("w0", [8, D, 1280], F32, kind="ExternalInput").ap()
        dr["wo0"] = dt("wo0", [D, D], F32, kind="ExternalInput").ap()
        dr["gpre0"] = dt("gpre0", [128, 8], F32, kind="ExternalInput").ap()
        dr["gpost0"] = dt("gpost0", [128, D], F32, kind="ExternalInput").ap()
        dr["bias0"] = dt("bias0", [8, 128, 6, 256], F32, kind="ExternalInput").ap()
    dr["ident"] = dt("ident", [128, 128], F32, kind="ExternalInput").ap()
    if mode == "l0":
        dr["x1"] = dt("x1", [S, D], F32, kind="ExternalOutput").ap()
    elif mode == "l1":
        dr["x1"] = dt("x1", [S, D], F32, kind="ExternalInput").ap()
    else:
        dr["x1"] = dt("x1", [S, D], F32, kind="Internal").ap()
    if do_l1:
        build_l1_decl(dt, dr)
        dr["out"] = dt("out", [S, D], F32, kind="ExternalOutput").ap()

    from contextlib import ExitStack
    with ExitStack() as es:
        E = es.enter_context
        sems = {e: E(nc.semaphore("c_" + e)) for e in ("pe", "act", "dve", "pool", "sp")}

        def new_dsem(name):
            return DmaSem(E(nc.semaphore(name)))

        psum = [E(nc.psum_tensor("ps%d" % i, [128, 512], F32)) for i in range(8)]
        rps = [B.R("psum%d" % i) for i in range(8)]
        hT = E(nc.sbuf_tensor("hT", [128, 8, S], BF16))
        GT = E(nc.sbuf_tensor("GT", [128, 8, S], BF16))
        ident = E(nc.sbuf_tensor("ident_sb", [128, 128], F32))
        ones = None
        small = E(nc.sbuf_tensor("small", [128, 64], F32))
        bscr = E(nc.sbuf_tensor("bscr", [128, 8], F32))
        B.scratch = bscr[:, 0:1]
        r_hT = [B.R("hT%d" % i) for i in range(NT)]
        r_GT = B.R("GT")
        r_ident = B.R("ident")
        r_ones = B.R("ones")
        d_const = new_dsem("d_const")
        B.S.dma("sp", ident[:], dr["ident"], d_const, (), [r_ident])
        dumps = {}
        d_dump = new_dsem("d_dump")

        def dump(name, ap, reads):
            if not debug or name in dumps:
                return
            t = dt("dbg_" + name, list(ap.shape), ap.dtype, kind="ExternalOutput").ap()
            o = B.S.dma("pool", t, ap, d_dump, reads, [])
            fr = B.R("dfin_" + name)
            fr.w = o
            dumps[name] = fr
        ctx = dict(nc=nc, B=B, dr=dr, E=E, new_dsem=new_dsem, psum=psum, rps=rps, hT=hT, GT=GT, dump=dump,
                   stop=stop, ident=ident, ones=ones, small=small, r_hT=r_hT, r_GT=r_GT,
                   r_ident=r_ident, r_ones=r_ones, sems=sems)
        if mode == "fused":
            ctx["r_x1"] = [B.R("x1_%d" % i) for i in range(NT)]
        if do_l1:
            l1_prep_stage1(ctx, E)
        else:
            ctx["pump"] = lambda n=None: None
        with nc.Block() as block:
            final = []
            if do_l0:
                final = build_l0(ctx)
            if do_l1:
                final = build_l1(ctx)
            B.S.op("sp", None, list(final) + list(dumps.values()), ())
            emit_all(B, block, sems)
    return nc


def emit_all(B, block, sems):
    S_ = B.S
    nc = B.nc
    for o in S_.allops:
        for p in o.deps:
            if p.dsem is None and not (p.eng == "pe" and o.eng == "pe"):
                p.need = True
    for e in S_.ENGS:
        c = 0
        for o in S_.ops[e]:
            if o.need:
                c += 1
                o.seq = c

    def reqs_of(o, e, before=None):
        req = {}
        for p in o.deps:
            if before is not None and p.gidx >= before:
                continue
            if p.dsem is not None:
                key = ("d", id(p.dsem))
                v = o.dreq[id(p.dsem)]
                if req.get(key, (None, 0))[1] < v[1]:
                    req[key] = v
            else:
                if p.eng == "pe" and e == "pe":
                    continue
                key = ("e", p.eng)
                if req.get(key, (None, 0))[1] < p.seq:
                    req[key] = (sems[p.eng], p.seq)
        return req

    def run(e, eo):
        waited = {}
        ops = S_.ops[e]
        seen_groups = set()
        for k, o in enumerate(ops):
            req = reqs_of(o, e)
            if o.group is not None and o.group not in seen_groups:
                seen_groups.add(o.group)
                j = k + 1
                while j < len(ops) and ops[j].group == o.group:
                    for key, v in reqs_of(ops[j], e, before=o.gidx).items():
                        if req.get(key, (None, 0))[1] < v[1]:
                            req[key] = v
                    j += 1
            for key, (sem, val) in req.items():
                if waited.get(key, 0) >= val:
                    continue
                waited[key] = val
                eo.wait_ge(sem, val)
            if o.fn is None:
                continue
            ins = o.fn(eo)
            if o.dsem is not None:
                ins.then_inc(o.dsem.sem, 16)
            elif o.need:
                ins.then_inc(sems[e], 1)

    @block.tensor
    def _(eo):
        run("pe", eo)

    @block.scalar
    def _(eo):
        run("act", eo)

    @block.vector
    def _(eo):
        run("dve", eo)

    @block.gpsimd
    def _(eo):
        run("pool", eo)

    @block.sync
    def _(eo):
        run("sp", eo)


def make_hT(ctx, src, scope, gp, r_gp, perm8=False, src_res=None, name="a"):
    nc, B, psum, rps, hT, ident = ctx["nc"], ctx["B"], ctx["psum"], ctx["rps"], ctx["hT"], ctx["ident"]
    E = scope
    NS = 3
    xt = E(nc.sbuf_tensor("xt_" + name, [128, NS, D], F32))
    xs = E(nc.sbuf_tensor("xs_" + name, [128, NS, D], F32))
    jk = E(nc.sbuf_tensor("jk_" + name, [128, D], F32))
    st = E(nc.sbuf_tensor("st_" + name, [128, NS, 4], F32))
    r_xt = [B.R("xt%d" % i) for i in range(NS)]
    r_xs = [B.R("xs%d" % i) for i in range(NS)]
    r_st = [B.R("st%d" % i) for i in range(NS)]
    r_jk = B.R("jk")
    d_x = [ctx["new_dsem"]("d_x%s%d" % (name, i)) for i in range(NS)]

    def stage1(tt):
        sl = tt % NS
        B.S.dma("sp", xt[:, sl, :], src[tt * 128:(tt + 1) * 128, :], d_x[sl],
                [src_res[tt]] if src_res else (), [r_xt[sl]])
        B.act(jk[:, :], xt[:, sl, :], AF.Square, [r_xt[sl]], [r_jk, r_st[sl]], accum_out=st[:, sl, 0:1])
        B.ts("dve", st[:, sl, 1:2], st[:, sl, 0:1], 1.0 / D, ALU.mult, [r_st[sl]], [r_st[sl]],
             s2=EPS, op1=ALU.add)

    def stage2(tt):
        sl = tt % NS
        B.act(st[:, sl, 1:2], st[:, sl, 1:2], AF.Sqrt, [r_st[sl]], [r_st[sl]])
        B.recip(st[:, sl, 1:2], st[:, sl, 1:2], [r_st[sl]], [r_st[sl]])
        B.ts("dve", xs[:, sl, :], xt[:, sl, :], st[:, sl, 1:2], ALU.mult,
             [r_xt[sl], r_st[sl]], [r_xs[sl]])

    def stage3(tt):
        sl = tt % NS
        for half in range(2):
            bk = half + 2 * (tt % 2)
            for j in range(4):
                c = half * 4 + j
                B.tr(psum[bk][:, j * 128:(j + 1) * 128], xs[:, sl, c * 128:(c + 1) * 128], ident[:],
                     [r_xs[sl], ctx["r_ident"]], [rps[bk]])
            c0 = half * 4
            if perm8:
                dst = hT[:, c0:c0 + 4, :].rearrange("p c (j n) -> p c j n", j=8)[:, :, :, 16 * tt:16 * tt + 16]
                srcp = psum[bk][:, :].rearrange("p (c n j) -> p c j n", c=4, j=8)
                gain = gp[:, c0:c0 + 4].unsqueeze(2).unsqueeze(3).to_broadcast([128, 4, 8, 16])
            else:
                dst = hT[:, c0:c0 + 4, tt * 128:(tt + 1) * 128]
                srcp = psum[bk][:, :].rearrange("p (c t) -> p c t", c=4)
                gain = gp[:, c0:c0 + 4].unsqueeze(2).to_broadcast([128, 4, 128])
            wr = ctx["r_hT"] if perm8 else [ctx["r_hT"][tt]]
            B.tt("dve", dst, srcp, gain, ALU.mult, [r_gp], [rps[bk]] + wr)

    for t in range(NT + 2):
        if t < NT:
            stage1(t)
        if 0 <= t - 1 < NT:
            stage2(t - 1)
        if 0 <= t - 2 < NT:
            stage3(t - 2)


class WStream:
    def __init__(self, ctx, scope, nslot=2, name="ws", cols=256):
        nc, B = ctx["nc"], ctx["B"]
        self.ctx = ctx
        self.n = nslot
        self.wst = scope(nc.sbuf_tensor(name + "_st", [128, nslot, 8, cols], F32))
        self.wbf = scope(nc.sbuf_tensor(name + "_bf", [128, nslot, 8, cols], BF16))
        self.r_st = [B.R(name + "st%d" % i) for i in range(nslot)]
        self.r_bf = [B.R(name + "bf%d" % i) for i in range(nslot)]
        self.ds = [ctx["new_dsem"]("d_" + name + str(i)) for i in range(nslot)]
        self.k = 0

    def dma(self, src):
        B = self.ctx["B"]
        sl = self.k % self.n
        self.k += 1
        B.S.dma("sp", self.wst[:, sl], src.rearrange("(c p) n -> p c n", p=128), self.ds[sl],
                (), [self.r_st[sl]])
        return sl

    def cast(self, sl):
        B = self.ctx["B"]
        B.act(self.wbf[:, sl], self.wst[:, sl], AF.Copy, [self.r_st[sl]], [self.r_bf[sl]])
        return self.wbf[:, sl], self.r_bf[sl]

    def load(self, src, dst=None, r_dst=None):
        B = self.ctx["B"]
        sl = self.k % self.n
        self.k += 1
        B.S.dma("sp", self.wst[:, sl], src.rearrange("(c p) n -> p c n", p=128), self.ds[sl],
                (), [self.r_st[sl]])
        if dst is None:
            dst, r_dst = self.wbf[:, sl], self.r_bf[sl]
        B.act(dst, self.wst[:, sl], AF.Copy, [self.r_st[sl]], [r_dst])
        return dst, r_dst


def grp_tokens(g, bi):
    d = GROUPS[g][1]
    nbr = 16 // d
    r = bi // nbr
    m0 = (bi % nbr) * 128
    return d * m0 + r, d


def build_l0(ctx):
    nc, B, dr, psum, rps = ctx["nc"], ctx["B"], ctx["dr"], ctx["psum"], ctx["rps"]
    hT, GT, ones, small = ctx["hT"], ctx["GT"], ctx["ones"], ctx["small"]
    r_hT, r_GT = ctx["r_hT"], ctx["r_GT"]
    from contextlib import ExitStack
    d_par = ctx["new_dsem"]("d_par0")
    r_gp = B.R("gpre0")
    B.S.dma("sp", small[:, 0:8], dr["gpre0"], d_par, (), [r_gp])
    with ExitStack() as es:
        make_hT(ctx, dr["x"], es.enter_context, small[:, 0:8], r_gp)
        B.barrier()
    allhT = list(r_hT)
    ctx["dump"]("hT", hT[:], allhT)
    with ExitStack() as es:
        E = es.enter_context
        ws = WStream(ctx, E, 2, "w0s", cols=128)
        QK = E(nc.sbuf_tensor("QK", [128, 6, S], BF16))
        VA0 = E(nc.sbuf_tensor("VA0", [128, 48, 65], BF16))
        VA1 = E(nc.sbuf_tensor("VA1", [128, 48, 128], BF16))
        SZ = E(nc.sbuf_tensor("SZ", [128, S], F32))
        BT = E(nc.sbuf_tensor("BT", [128, 2, 6, 256], F32))
        OA = E(nc.sbuf_tensor("OA", [128, 2, S], F32))
        LT = E(nc.sbuf_tensor("LT", [128, 2, 2, 256], F32))
        PT = E(nc.sbuf_tensor("PT", [128, 5, 2, 256], BF16))
        VT = E(nc.sbuf_tensor("VT", [128, 3, S], BF16))
        RHL = E(nc.sbuf_tensor("RHL", [128, 2, S], BF16))
        identb = E(nc.sbuf_tensor("identb", [128, 128], BF16))
        onesb = E(nc.sbuf_tensor("onesb", [128, 128], BF16))
        r_VT = [B.R("VT%d" % i) for i in range(3)]
        r_RHL = B.R("RHL")
        r_cb = B.R("constb")
        B.cp("pool", identb[:], ctx["ident"][:], [ctx["r_ident"]], [r_cb])
        B.memset("pool", onesb[:], 1.0, [r_cb])
        r_QK = [B.R("QK%d" % i) for i in range(6)]
        r_VA = [B.R("VA%d" % i) for i in range(3)]
        r_SZ = B.R("SZ")
        r_BT = [B.R("BT0"), B.R("BT1")]
        r_OA = [B.R("OA0"), B.R("OA1")]
        r_LT = [B.R("LT%d" % i) for i in range(2)]
        r_PT = [B.R("PT%d" % i) for i in range(5)]
        d_bt = [ctx["new_dsem"]("d_bt0"), ctx["new_dsem"]("d_bt1")]
        B.memset("pool", VA0[:], 1.0, r_VA)
        B.memset("pool", VA1[:], 0.0, r_VA)
        B.memset("pool", VA1[:, :, 0:1], 1.0, r_VA)
        pb = [0]

        def proj_bank():
            pb[0] ^= 1
            return pb[0]

        stb = [0]
        fb = [0]
        dq = []
        pre = {}
        wdma = {}

        def wsrc(h, wb):
            return dr["w0"][h][:, wb * 128:(wb + 1) * 128]
        for hp in range(8):
            bsl = hp % 2
            B.S.dma("sp", BT[:, bsl], dr["bias0"][hp], d_bt[bsl], (), [r_BT[bsl]])
            for wb in range(10):
                key = (hp, wb)
                if key not in pre:
                    if key not in wdma:
                        wdma[key] = ws.dma(wsrc(*key))
                    pre[key] = ws.cast(wdma.pop(key))
                wbf, r_w = pre.pop(key)
                nkey = (hp, wb + 1) if wb < 9 else ((hp + 1, 0) if hp + 1 < 8 else None)
                if nkey is not None and nkey not in wdma and nkey not in pre:
                    wdma[nkey] = ws.dma(wsrc(*nkey))

                def at_s2(nkey=nkey):
                    if nkey is not None and nkey in wdma:
                        pre[nkey] = ws.cast(wdma.pop(nkey))
                for half_ in range(1):
                    colo = 0
                    kind = ("q", "k", "q", "k", "q", "k", "v0", "v1", "v2", "z")[wb]
                    half = wb % 2
                    if kind in ("q", "k", "z"):
                        g = wb // 2 if wb < 6 else 0
                        d = GROUPS[g][1] if kind != "z" else 1
                        for s in range(4):
                            if s == 2:
                                at_s2()
                            bk = proj_bank()
                            for c in range(8):
                                B.mm(psum[bk][:, :], wbf[:, c, colo:colo + 128],
                                     hT[:, c, s * 512:(s + 1) * 512], c == 0, c == 7,
                                     [r_w] + allhT, [rps[bk]])
                            if kind == "z":
                                B.act(SZ[:, s * 512:(s + 1) * 512], psum[bk][:, :], AF.Silu,
                                      [], [rps[bk], r_SZ])
                            else:
                                idx = 2 * g + half
                                L = S // d
                                dst = QK[:, idx, :].rearrange("p (r m) -> p r m", r=d)[
                                    :, :, (512 // d) * s:(512 // d) * (s + 1)]
                                srcp = psum[bk][:, :].rearrange("p (m r) -> p r m", r=d)
                                B.act(dst, srcp, AF.Copy, [], [rps[bk], r_QK[idx]],
                                      scale=(0.125 if kind == "q" else 1.0))
                    else:
                        g = int(kind[1])
                        for s in range(4):
                            if s == 2:
                                at_s2()
                            bk = proj_bank()
                            for c in range(8):
                                B.mm(psum[bk][:, :], wbf[:, c, colo:colo + 128],
                                     hT[:, c, s * 512:(s + 1) * 512], c == 0, c == 7,
                                     [r_w] + allhT, [rps[bk]])
                            B.act(VT[:, g, s * 512:(s + 1) * 512], psum[bk][:, :], AF.Copy, [],
                                  [rps[bk], r_VT[g]])
                        for quad in range(4):
                            bk = proj_bank()
                            pb16 = psum[bk][:, :].bitcast(BF16)
                            for j in range(4):
                                bi = quad * 4 + j
                                o, st = grp_tokens(g, bi)
                                B.tr(pb16[:, j * 128:(j + 1) * 128], VT[:, g, o:o + 127 * st + 1:st], identb[:],
                                     [r_VT[g], r_cb], [rps[bk]])
                            pv = pb16[:, 0:512].rearrange("p (j c) -> p j c", j=4)
                            b0 = g * 16 + quad * 4
                            B.cp("dve", VA0[:, b0:b0 + 4, 0:64], pv[:, :, 0:64], [], [rps[bk], r_VA[g]])
                            B.cp("dve", VA1[:, b0:b0 + 4, 64:128], pv[:, :, 64:128], [], [rps[bk], r_VA[g]])
                ctx["pump"](4)
                if wb < 7:
                    while dq:
                        f = dq.pop(0)
                        if f is not None:
                            f()
                            break
                if wb == 8:
                    while dq:
                        f = dq.pop(0)
                        if f is not None:
                            f()
            its = [(hh, g, m) for hh in range(2) for g in range(3) for m in range(8)]
            st_info = {}

            def nq_of(g, kb):
                nbr = 16 // GROUPS[g][1]
                return 256 if (kb + 1) % nbr != 0 else 128

            def issue_st(i):
                hh, g, m = its[i]
                bk = (0, 1, 2, 7)[stb[0] % 4]
                stb[0] += 1
                rows = slice(hh * 64, (hh + 1) * 64)
                for t in range(2):
                    kb = 2 * m + t
                    nq = nq_of(g, kb)
                    B.mm(psum[bk][:, t * 256:t * 256 + nq], QK[rows, 2 * g + 1, kb * 128:(kb + 1) * 128],
                         QK[rows, 2 * g, kb * 128:kb * 128 + nq], True, True,
                         [r_QK[2 * g], r_QK[2 * g + 1]], [rps[bk]])
                ls = i % 2
                ps_ = i % 5
                bias2 = BT[:, bsl, g * 2 + hh, :].unsqueeze(1).to_broadcast([128, 2, 256])
                B.tt("dve", LT[:, ls], psum[bk][:, :].rearrange("p (t q) -> p t q", t=2), bias2, ALU.add,
                     [r_BT[bsl]], [rps[bk], r_LT[ls]])
                B.act(PT[:, ps_], LT[:, ls], AF.Exp, [r_LT[ls]], [r_PT[ps_]])
                st_info[i] = ps_

            def issue_pv(i):
                hh, g, m = its[i]
                d = GROUPS[g][1]
                nbr = 16 // d
                M = 65 if hh == 0 else 128
                VA = VA0 if hh == 0 else VA1
                ps_ = st_info[i]
                for t in range(2):
                    kb = 2 * m + t
                    has_prev = kb % nbr != 0
                    if has_prev:
                        pslot, pt = (ps_, 0) if t == 1 else (st_info[i - 1], 1)
                    if g == 0:
                        outs = [(3 + kb // 4, slice((kb % 4) * 128, (kb % 4) * 128 + 128), slice(0, 128))]
                    elif g == 1:
                        r, nn = kb // 4, kb % 4
                        outs = [(3 + nn, slice(r, 512, 4), slice(0, 128))]
                    else:
                        outs = [(3 + sub, slice(kb, 512, 16), slice(32 * sub, 32 * sub + 32)) for sub in range(4)]
                    for (bo, ocols, qcols) in outs:
                        outp = psum[bo][0:M, ocols]
                        first = (g == 0 and kb % 4 == 0)
                        if has_prev:
                            pq = slice(128 + qcols.start, 128 + qcols.stop)

                            def f1(e, outp=outp, a=VA[:, g * 16 + kb - 1, 0:M], b=PT[:, pslot, pt, pq], first=first):
                                return e.matmul(outp, a, b, start=first, stop=False, skip_group_check=True)
                            B.S.op("pe", f1, [r_VA[g], r_PT[pslot]], [rps[bo]])
                            first = False

                        last = (g == 2 and kb == 15)

                        def f2(e, outp=outp, a=VA[:, g * 16 + kb, 0:M], b=PT[:, ps_, t, qcols], first=first, last=last):
                            return e.matmul(outp, a, b, start=first, stop=last, skip_group_check=True)
                        B.S.op("pe", f2, [r_VA[g], r_PT[ps_]], [rps[bo]])
                if g == 2 and m == 7:
                    for span in range(4):
                        if span % 2 == 0:
                            B.cp("dve", OA[0:M, hh, span * 512:(span + 1) * 512], psum[3 + span][0:M, :],
                                 [], [rps[3 + span], r_OA[hh]])
                        else:
                            B.act(OA[0:M, hh, span * 512:(span + 1) * 512], psum[3 + span][0:M, :], AF.Copy,
                                  [], [rps[3 + span], r_OA[hh]])
                    finalize(hh)

            def finalize(hh, hp=hp):
                drow = 64 if hh == 0 else 0
                R = slice(hh * 64, (hh + 1) * 64)

                def prepA(s):
                    cs = slice(s * 512, (s + 1) * 512)
                    den = OA[drow:drow + 1, hh, cs]
                    B.act(den, den, AF.Ln, [], [r_OA[hh]])
                    B.act(den, den, AF.Exp, [], [r_OA[hh]], scale=-1.0)

                def prepB(s):
                    cs = slice(s * 512, (s + 1) * 512)
                    den = OA[drow:drow + 1, hh, cs]
                    B.cp("dve", RHL[drow:drow + 1, 0, cs], den, [r_OA[hh]], [r_RHL])
                    B.tt("dve", RHL[drow:drow + 1, 1, cs], den, RHL[drow:drow + 1, 0, cs], ALU.subtract,
                         [r_OA[hh]], [r_RHL])

                def span(s):
                    bk = (0, 1, 2, 7)[stb[0] % 4]
                    stb[0] += 1
                    cs = slice(s * 512, (s + 1) * 512)
                    B.mm(psum[bk][:, :], onesb[drow:drow + 1, :], RHL[drow:drow + 1, 0, cs], True, False,
                         [r_cb, r_RHL], [rps[bk]])
                    B.mm(psum[bk][:, :], onesb[drow:drow + 1, :], RHL[drow:drow + 1, 1, cs], False, True,
                         [r_cb, r_RHL], [rps[bk]])
                    B.tt("dve", OA[R, hh, cs], OA[R, hh, cs], psum[bk][R, :], ALU.mult,
                         [], [rps[bk], r_OA[hh]])
                    B.tt("pool", GT[R, hp, cs], OA[R, hh, cs], SZ[R, cs], ALU.mult,
                         [r_OA[hh], r_SZ], [r_GT])
                for s in range(4):
                    dq.append(lambda s=s: prepA(s))
                    dq.append(None)
                    dq.append(lambda s=s: prepB(s))
                for _ in range(4):
                    dq.append(None)
                for s in range(4):
                    dq.append(lambda s=s: span(s))
                    if s < 3:
                        dq.append(None)

            if hp == 0:
                ctx["dump"]("QK", QK[:], r_QK)
                ctx["dump"]("VA0", VA0[:], r_VA)
                ctx["dump"]("VA1", VA1[:], r_VA)
                ctx["dump"]("SZ", SZ[:], [r_SZ])
                ctx["dump"]("BT", BT[:, 0], [r_BT[0]])
            LOOK = 3
            n = len(its)
            for i in range(min(LOOK, n)):
                issue_st(i)
            for i in range(n):
                B.S.begin_group()
                issue_pv(i)
                if i + LOOK < n:
                    issue_st(i + LOOK)
                B.S.end_group()
                if dq:
                    f = dq.pop(0)
                    if f is not None:
                        f()
            if hp == 7:
                while dq:
                    f = dq.pop(0)
                    if f is not None:
                        f()
        B.barrier()
    ctx["dump"]("GT", GT[:], [r_GT])
    finals = []
    with ExitStack() as es:
        E = es.enter_context
        ws = WStream(ctx, E, 2, "wo0s")
        WO = E(nc.sbuf_tensor("WO0", [128, 8, D], BF16))
        gpo = E(nc.sbuf_tensor("gpo0", [128, D], F32))
        r_WO = [B.R("WO%d" % i) for i in range(4)]
        r_gpo = B.R("gpo0")
        B.S.dma("sp", gpo[:], dr["gpost0"], d_par, (), [r_gpo])
        for wb in range(4):
            ws.load(dr["wo0"][:, wb * 256:(wb + 1) * 256], WO[:, :, wb * 256:(wb + 1) * 256], r_WO[wb])
        finals = out_proj_tail(ctx, E, WO, r_WO, gpo, r_gpo, dr["x"], dr["x1"], GT, r_GT, None, "t0",
                               dst_res=ctx.get("r_x1"))
        B.barrier()
    return finals


def out_proj_tail(ctx, E, WO, r_WO, gpo, r_gpo, xsrc, dst, GT, r_GT, tokap, name, src_res=None, dst_res=None):
    nc, B, psum, rps = ctx["nc"], ctx["B"], ctx["psum"], ctx["rps"]
    xt = E(nc.sbuf_tensor("xt_" + name, [128, 2, D], F32))
    yt = E(nc.sbuf_tensor("yt_" + name, [128, 2, D], F32))
    sq = E(nc.sbuf_tensor("sq_" + name, [128, 512], F32))
    st = E(nc.sbuf_tensor("st_" + name, [128, 2, 4], F32))
    r_xt = [B.R("xt0"), B.R("xt1")]
    r_yt = [B.R("yt0"), B.R("yt1")]
    r_sq = B.R("sq")
    r_st = [B.R("st0"), B.R("st1")]
    d_x = [ctx["new_dsem"]("d_x%s0" % name), ctx["new_dsem"]("d_x%s1" % name)]
    d_o = [ctx["new_dsem"]("d_o%s0" % name), ctx["new_dsem"]("d_o%s1" % name)]
    finals = []
    for tt in range(NT):
        sl = tt % 2
        if tokap is None:
            xrows, orows = xsrc[tt * 128:(tt + 1) * 128, :], dst[tt * 128:(tt + 1) * 128, :]
        else:
            jj, nb = tt // 2, tt % 2
            xrows = xsrc.rearrange("(n j) d -> j n d", j=8)[jj, nb * 128:(nb + 1) * 128, :]
            orows = dst.rearrange("(n j) d -> j n d", j=8)[jj, nb * 128:(nb + 1) * 128, :]
        B.S.dma("sp", xt[:, sl, :], xrows, d_x[sl],
                [src_res[tt]] if src_res else (), [r_xt[sl]])
        banks = (0, 1) if sl == 0 else (2, 3)
        for half in range(2):
            bk = banks[half]
            for c in range(8):
                lhs = GT[:, c, tt * 128:(tt + 1) * 128]
                B.mm(psum[bk][:, :], lhs, WO[:, c, half * 512:(half + 1) * 512], c == 0, c == 7,
                     [r_GT] + r_WO[2 * half:2 * half + 2], [rps[bk]])
            B.act(sq[:, :], psum[bk][:, :], AF.Square, [], [rps[bk], r_sq, r_st[sl]],
                  accum_out=st[:, sl, half:half + 1])
        B.tt("dve", st[:, sl, 2:3], st[:, sl, 0:1], st[:, sl, 1:2], ALU.add, [r_st[sl]], [r_st[sl]])
        rms_rstd(B, st[:, sl, 2:3], st[:, sl, 3:4], r_st[sl], r_st[sl])
        for half in range(2):
            bk = banks[half]
            hs = slice(half * 512, (half + 1) * 512)
            B.stt(yt[:, sl, hs], psum[bk][:, :], st[:, sl, 3:4], gpo[:, hs], ALU.mult, ALU.mult,
                  [r_st[sl], r_gpo], [rps[bk], r_yt[sl]])
        B.tt("pool", yt[:, sl, :], yt[:, sl, :], xt[:, sl, :], ALU.add, [r_xt[sl]], [r_yt[sl]])
        o = B.S.dma("pool", orows, yt[:, sl, :], d_o[sl], [r_yt[sl]],
                    [dst_res[tt]] if dst_res else [])
        fr = B.R("fin%d" % tt)
        fr.w = o
        finals.append(fr)
    return finals


def build_l1_decl(dt, dr):
    EI = "ExternalInput"
    dr["w1"] = dt("w1", [D, 2048], F32, kind=EI).ap()
    dr["wg1"] = dt("wg1", [D, D], F32, kind=EI).ap()
    dr["wo1"] = dt("wo1", [D, D], F32, kind=EI).ap()
    dr["gpre1"] = dt("gpre1", [128, 8], F32, kind=EI).ap()
    dr["gpost1"] = dt("gpost1", [128, D], F32, kind=EI).ap()
    dr["bglu"] = dt("bglu", [128, 8], F32, kind=EI).ap()
    dr["pl_a"] = dt("pl_a", [128, 3, 32], F32, kind=EI).ap()
    dr["pl_bc"] = dt("pl_bc", [128, 4, 32, 16], F32, kind=EI).ap()
    dr["dcol"] = dt("dcol", [128, 64], F32, kind=EI).ap()
    dr["mask8"] = dt("mask8", [128, 128], F32, kind=EI).ap()
    dr["esel"] = dt("esel", [128, 64, 128], BF16, kind=EI).ap()


def cmul(B, outr, outi, ar, ai, br, bi, t1, t2, R, neg_i=False, eng="dve", emit=None, Rw=None):
    if emit is None:
        emit = lambda f: f()
    Rw = R if Rw is None else Rw
    rd = [R] if Rw is R else [R, Rw]
    emit(lambda: B.tt(eng, t1, ar, br, ALU.mult, rd, [Rw]))
    emit(lambda: B.tt(eng, t2, ai, bi, ALU.mult, rd, [Rw]))
    emit(lambda: B.tt(eng, outr, t1, t2, ALU.subtract, rd, [Rw]))
    emit(lambda: B.tt(eng, t1, ar, bi, ALU.mult, rd, [Rw]))
    emit(lambda: B.tt(eng, t2, ai, br, ALU.mult, rd, [Rw]))
    if neg_i:
        emit(lambda: B.tt(eng, t1, t1, t2, ALU.add, rd, [Rw]))
        emit(lambda: B.ts(eng, outi, t1, -1.0, ALU.mult, rd, [Rw]))
    else:
        emit(lambda: B.tt(eng, outi, t1, t2, ALU.add, rd, [Rw]))


def l1_prep_stage1(ctx, E):
    nc, B, dr = ctx["nc"], ctx["B"], ctx["dr"]
    PEN = "dve"
    pa = E(nc.sbuf_tensor("pa", [128, 3, 32], F32))
    pbc = E(nc.sbuf_tensor("pbc", [128, 4, 32, 16], F32))
    W = E(nc.sbuf_tensor("Wk", [128, 16, 32], F32))
    PW = E(nc.sbuf_tensor("PW", [128, 9, 2, 32], F32))
    QW = E(nc.sbuf_tensor("QW", [128, 8, 2, 32], F32))
    bb = E(nc.sbuf_tensor("bbar", [128, 2, 32, 16], F32))
    T1 = E(nc.sbuf_tensor("T1", [128, 32, 16], F32))
    T2 = E(nc.sbuf_tensor("T2", [128, 32, 16], F32))
    RP = B.R("prep")
    d_p = ctx["new_dsem"]("d_prep")
    B.S.dma("sp", pa[:], dr["pl_a"], d_p, (), [RP])
    B.S.dma("sp", pbc[:], dr["pl_bc"], d_p, (), [RP])
    q = []
    emit = q.append
    RK = E(nc.sbuf_tensor("RK", [128, 16], F32))
    for k in range(1, 16):
        emit(lambda k=k: B.memset(PEN, RK[:, k:k + 1], float(k), [RP]))
    emit(lambda: B.memset(PEN, RK[:, 0:1], 1.0, [RP]))
    emit(lambda: B.recip(RK[:], RK[:], [RP], [RP]))
    are, aim, ldt = pa[:, 0, :], pa[:, 1, :], pa[:, 2, :]
    w = lambda k: W[:, k, :]
    t1s, t2s = T1[:, :, 0], T2[:, :, 0]
    tt = lambda o, a, b, op: emit(lambda: B.tt(PEN, o, a, b, op, [RP], [RP]))
    ts = lambda o, a, s1, op0, s2=None, op1=None: emit(lambda: B.ts(PEN, o, a, s1, op0, [RP], [RP], s2=s2, op1=op1))
    ms = lambda o, v: emit(lambda: B.memset(PEN, o, v, [RP]))
    cpy = lambda o, a: emit(lambda: B.cp(PEN, o, a, [RP], [RP]))
    cm = lambda *a, **k: cmul(B, *a, eng=PEN, emit=emit, **k)
    ts(w(1), ldt, 0.125, ALU.mult)
    ms(w(0), 1.0)
    for k in range(10, 0, -1):
        tt(w(0), w(0), w(1), ALU.mult)
        ts(w(0), w(0), RK[:, k:k + 1], ALU.mult, 1.0, ALU.add)
    for _ in range(3):
        tt(w(0), w(0), w(0), ALU.mult)
    ts(w(8), w(0), 1.0 / 16, ALU.mult)
    tt(w(1), are, w(8), ALU.mult)
    tt(w(2), aim, w(8), ALU.mult)
    ms(w(3), 1.0)
    ms(w(4), 0.0)
    for k in range(14, 1, -1):
        cm(w(5), w(6), w(3), w(4), w(1), w(2), t1s, t2s, RP)
        ts(w(3), w(5), RK[:, k:k + 1], ALU.mult, 1.0, ALU.add)
        ts(w(4), w(6), RK[:, k:k + 1], ALU.mult)
    cm(w(5), w(6), w(3), w(4), w(1), w(2), t1s, t2s, RP)
    for _ in range(4):
        ts(w(7), w(5), 2.0, ALU.add)
        cm(w(3), w(4), w(5), w(6), w(7), w(6), t1s, t2s, RP)
        cpy(w(5), w(3))
        cpy(w(6), w(4))
    ms(PW[:, 0, 0, :], 1.0)
    ms(PW[:, 0, 1, :], 0.0)
    ms(QW[:, 0, 0, :], 1.0)
    ms(QW[:, 0, 1, :], 0.0)
    abr, abi = PW[:, 1, 0, :], PW[:, 1, 1, :]
    ts(abr, w(5), 1.0, ALU.add)
    cpy(abi, w(6))
    for k in range(2, 9):
        cm(PW[:, k, 0, :], PW[:, k, 1, :], PW[:, k - 1, 0, :], PW[:, k - 1, 1, :], abr, abi, t1s, t2s, RP)
    tt(w(8), abr, abr, ALU.mult)
    tt(w(9), abi, abi, ALU.mult)
    tt(w(8), w(8), w(9), ALU.add)
    emit(lambda: B.recip(w(9), w(8), [RP], [RP]))
    tt(QW[:, 1, 0, :], abr, w(9), ALU.mult)
    tt(w(8), abi, w(9), ALU.mult)
    ts(QW[:, 1, 1, :], w(8), -1.0, ALU.mult)
    for k in range(2, 8):
        cm(QW[:, k, 0, :], QW[:, k, 1, :], QW[:, k - 1, 0, :], QW[:, k - 1, 1, :],
           QW[:, 1, 0, :], QW[:, 1, 1, :], t1s, t2s, RP)
    cpy(w(10), w(5))
    tt(w(11), are, are, ALU.mult)
    tt(w(12), aim, aim, ALU.mult)
    tt(w(11), w(11), w(12), ALU.add)
    emit(lambda: B.recip(w(12), w(11), [RP], [RP]))
    tt(w(13), w(10), are, ALU.mult)
    tt(w(14), abi, aim, ALU.mult)
    tt(w(13), w(13), w(14), ALU.add)
    tt(w(13), w(13), w(12), ALU.mult)
    tt(w(14), abi, are, ALU.mult)
    tt(w(15), w(10), aim, ALU.mult)
    tt(w(14), w(14), w(15), ALU.subtract)
    tt(w(14), w(14), w(12), ALU.mult)
    bc = lambda ap: ap.unsqueeze(2).to_broadcast([128, 32, 16])
    cm(bb[:, 0], bb[:, 1], bc(w(13)), bc(w(14)), pbc[:, 0], pbc[:, 1], T1[:], T2[:], RP)

    def pump(n=None):
        k = len(q) if n is None else min(n, len(q))
        for _ in range(k):
            q.pop(0)()
    ctx["pump"] = pump
    ctx["prep1"] = dict(pbc=pbc, PW=PW, QW=QW, bb=bb, T1=T1, T2=T2, RP=RP, bc=bc)


def build_l1(ctx):
    from contextlib import ExitStack
    nc, B, dr, psum, rps = ctx["nc"], ctx["B"], ctx["dr"], ctx["psum"], ctx["rps"]
    hT, GT, ident, small = ctx["hT"], ctx["GT"], ctx["ident"], ctx["small"]
    r_ident = ctx["r_ident"]
    PI = math.pi
    d_par = ctx["new_dsem"]("d_par1")
    r_gp = B.R("gpre1")
    r_bg = B.R("bglu")
    B.S.dma("sp", small[:, 8:16], dr["gpre1"], d_par, (), [r_gp])
    B.S.dma("sp", small[:, 16:24], dr["bglu"], d_par, (), [r_bg])
    r_hTall = B.R("hTall")
    r_Gb = B.R("Gb")
    Gb = GT[:, :, :].rearrange("p c (a n) -> p (c a) n", n=256)
    with ExitStack() as es1:
        E1 = es1.enter_context
        T0b = E1(nc.sbuf_tensor("T0b", [128, 64, 128], BF16))
        BmT = E1(nc.sbuf_tensor("BmT", [128, 64, 2, 64], BF16))
        Cmb = E1(nc.sbuf_tensor("Cmb", [128, 32, 2, 128], BF16))
        A8 = E1(nc.sbuf_tensor("A8", [128, 2, 2, 32], F32))
        r_T0b, r_BmT, r_Cmb, r_A8 = B.R("T0b"), B.R("BmT"), B.R("Cmb"), B.R("A8")
        with ExitStack() as es:
            E = es.enter_context
            ctx["pump"]()
            p1 = ctx["prep1"]
            pbc, PW, QW, bb, T1, T2, RP, bc = (p1[k] for k in ("pbc", "PW", "QW", "bb", "T1", "T2", "RP", "bc"))
            cre, cim = pbc[:, 2], pbc[:, 3]
            dcol = E(nc.sbuf_tensor("dcol_sb", [128, 64], F32))
            mask8 = E(nc.sbuf_tensor("mask8_sb", [128, 128], F32))
            Lh = E(nc.sbuf_tensor("Lh", [128, 2, 32, 128], BF16))
            Rh = E(nc.sbuf_tensor("Rh", [128, 2, 32, 128], BF16))
            identb1 = E(nc.sbuf_tensor("identb1", [128, 128], BF16))
            r_ib1 = B.R("identb1")
            B.cp("dve", identb1[:], ident[:], [r_ident], [r_ib1])
            TM = E(nc.sbuf_tensor("TM", [128, 2, 128], F32))
            r_dm, r_TM = B.R("dcolmask"), [B.R("TM0"), B.R("TM1")]
            B.S.dma("sp", dcol[:], dr["dcol"], d_par, (), [r_dm])
            B.S.dma("sp", mask8[:], dr["mask8"], d_par, (), [r_dm])
            T3 = E(nc.sbuf_tensor("T3", [128, 32, 16], F32))
            T4 = E(nc.sbuf_tensor("T4", [128, 32, 16], F32))
            R_L, R_R = B.R("prepL"), B.R("prepR")
            for j in range(8):
                js = slice(j * 16, (j + 1) * 16)
                cmul(B, Rh[:, 0, :, js], Rh[:, 1, :, js], bc(QW[:, 7 - j, 0, :]), bc(QW[:, 7 - j, 1, :]),
                     cre, cim, T3[:], T4[:], RP, neg_i=True, eng="dve", Rw=R_R)
            for j in range(8):
                js = slice(j * 16, (j + 1) * 16)
                cmul(B, Lh[:, 0, :, js], Lh[:, 1, :, js], bc(PW[:, 7 - j, 0, :]), bc(PW[:, 7 - j, 1, :]),
                     bb[:, 0], bb[:, 1], T1[:], T2[:], RP, Rw=R_L)
                cmul(B, Cmb[:, :, 0, js], Cmb[:, :, 1, js], bc(PW[:, j + 1, 0, :]), bc(PW[:, j + 1, 1, :]),
                     cre, cim, T1[:], T2[:], RP, neg_i=True, Rw=R_L)
            B.cp("dve", Cmb[:, 0, 0, 0:1], Cmb[:, 0, 0, 0:1], [R_L], [r_Cmb])
            B.cp("dve", A8[:, 0, 0, :], PW[:, 8, 0, :], [RP], [r_A8])
            B.cp("dve", A8[:, 0, 1, :], PW[:, 8, 0, :], [RP], [r_A8])
            B.ts("dve", A8[:, 1, 0, :], PW[:, 8, 1, :], -1.0, ALU.mult, [RP], [r_A8])
            B.cp("dve", A8[:, 1, 1, :], PW[:, 8, 1, :], [RP], [r_A8])
            r_T0g = [B.R("T0g%d" % i) for i in range(4)]
            for g in range(64):
                gh, pr = g // 32, g % 32
                Rr = slice(gh * 64, (gh + 1) * 64)
                bk = g % 2
                B.mm(psum[bk][:, 0:128], Lh[Rr, 0, pr, :], Rh[Rr, 0, pr, :], True, False, [R_L, R_R], [rps[bk]])
                B.mm(psum[bk][:, 0:128], Lh[Rr, 1, pr, :], Rh[Rr, 1, pr, :], False, True, [R_L, R_R], [rps[bk]])
                tm = g % 2
                B.tt("dve", TM[:, tm, :], psum[bk][:, 0:128], mask8[:], ALU.mult, [r_dm], [rps[bk], r_TM[tm]])
                B.stt(T0b[:, g, :], ident[:], dcol[:, g:g + 1], TM[:, tm, :], ALU.mult, ALU.add,
                      [r_dm, r_TM[tm], r_ident], [r_T0g[g % 4]])
            for g4 in range(16):
                bk = 2 + g4 % 2
                for gg in range(4):
                    g = g4 * 4 + gg
                    gh, pr = g // 32, g % 32
                    Rr = slice(gh * 64, (gh + 1) * 64)
                    for part in range(2):
                        cs = (gg * 2 + part) * 64
                        B.tr(psum[bk][:, :].bitcast(BF16)[:, cs:cs + 64], Lh[Rr, part, pr, :], identb1[Rr, Rr],
                             [R_L, r_ib1], [rps[bk]])
                B.cp("dve", BmT[:, g4 * 4:g4 * 4 + 4].rearrange("p g a c -> p (g a c)"),
                     psum[bk][:, :].bitcast(BF16)[:, 0:512], [], [rps[bk], r_BmT])
            ctx["dump"]("T0b", T0b[:], [r_T0b])
            ctx["dump"]("BmT", BmT[:], [r_BmT])
            ctx["dump"]("Cmb", Cmb[:], [r_Cmb])
            ctx["dump"]("A8", A8[:], [r_A8])
            ctx["dump"]("PW", PW[:], [RP])
            ctx["dump"]("QW", QW[:], [RP])
            ctx["dump"]("Lh", Lh[:], [R_L])
            ctx["dump"]("Rh", Rh[:], [R_R])
            B.barrier()
        if ctx.get("stop") == "prep":
            return []
        V = E1(nc.sbuf_tensor("Vssm", [128, 64, 256], BF16))
        r_V = B.R("V")
        with ExitStack() as es:
            ctx2 = dict(ctx)
            ctx2["r_hT"] = [r_hTall]
            make_hT(ctx2, dr["x1"], es.enter_context, small[:, 8:16], r_gp, perm8=True,
                    src_res=ctx.get("r_x1"), name="b")
            B.barrier()
        ctx["dump"]("hT1", hT[:], [r_hTall])
        if ctx.get("stop") == "A":
            return []
        with ExitStack() as es:
            E = es.enter_context
            ws = WStream(ctx, E, 2, "w1s", cols=128)
            uTb = E(nc.sbuf_tensor("uTb", [128, 2, S], BF16))
            Es = E(nc.sbuf_tensor("Esel", [128, 64, 128], BF16))
            r_uT = [B.R("uT0"), B.R("uT1")]
            r_Es = B.R("Esel")
            d_es = ctx["new_dsem"]("d_es")
            B.S.dma("sp", Es[:], dr["esel"], d_es, (), [r_Es])
            pbk = 0
            for fc in range(8):
                wbf, r_w = ws.load(dr["w1"][:, fc * 128:fc * 128 + 128])
                colo = 0
                us = fc % 2
                for s in range(4):
                    bk = pbk
                    pbk ^= 1
                    for c in range(8):
                        B.mm(psum[bk][:, :], wbf[:, c, colo:colo + 128], hT[:, c, s * 512:(s + 1) * 512],
                             c == 0, c == 7, [r_w, r_hTall], [rps[bk]])
                    B.act(uTb[:, us, s * 512:(s + 1) * 512], psum[bk][:, :], AF.Copy, [],
                          [rps[bk], r_uT[us]])
                for g8 in range(8):
                    g = fc * 8 + g8
                    bk = 2 + g % 4
                    for j in range(8):
                        B.mm(psum[bk][:, 0:256], Es[:, g8 * 8 + j, :], uTb[:, us, j * 256:(j + 1) * 256],
                             j == 0, j == 7, [r_Es, r_uT[us]], [rps[bk]])
                    B.cp("dve", V[:, g, :], psum[bk][:, 0:256], [], [rps[bk], r_V])
            ctx["dump"]("V", V[:], [r_V])
            B.barrier()
        if ctx.get("stop") == "B":
            return []
        with ExitStack() as es:
            E = es.enter_context
            Zb = E(nc.sbuf_tensor("Zb", [128, 2, 65, 2, 32], F32))
            Sb = ctx["prep1"]["pbc"][:].rearrange("p a b c -> p (a b c)").bitcast(BF16).rearrange(
                "p (n a r) -> p n a r", a=2, r=32)
            t1 = E(nc.sbuf_tensor("zt1", [128, 2, 32], F32))
            t2 = E(nc.sbuf_tensor("zt2", [128, 2, 32], F32))
            r_Z = [B.R("Zb0"), B.R("Zb1")]
            r_Sb = B.R("Sb")
            r_t = B.R("zt")
            B.memset("dve", Zb[:, 0, 0], 0.0, [r_Z[0]])

            def local_states(tb):
                zb = tb % 2
                ns = slice(tb * 64, (tb + 1) * 64)
                for q in range(8):
                    bk = q % 2
                    for pp in range(4):
                        pr = q * 4 + pp
                        for gh in range(2):
                            g = gh * 32 + pr
                            for part in range(2):
                                cs = (pp * 2 + part) * 64
                                B.mm(psum[bk][gh * 64:(gh + 1) * 64, cs:cs + 64], BmT[:, g, part, :],
                                     V[:, g, ns], True, True, [r_BmT, r_V], [rps[bk]])
                    dst = Zb[:, zb, 1:65, :, q * 4:q * 4 + 4].rearrange("p n a r -> p r a n")
                    srcp = psum[bk][:, :].rearrange("p (r a n) -> p r a n", r=4, a=2)
                    B.act(dst, srcp, AF.Copy, [], [rps[bk], r_Z[zb]])

            def recurrence(tb):
                zb = tb % 2
                for s in range(64):
                    zs = Zb[:, zb, s]
                    zn = Zb[:, zb, s + 1]
                    B.tt("dve", t1[:], A8[:, 0], zs, ALU.mult, [r_A8, r_Z[zb]], [r_t])
                    B.tt("dve", t2[:], A8[:, 1], Zb[:, zb, s, ::-1, :], ALU.mult, [r_A8, r_Z[zb]], [r_t])
                    B.tt("dve", zn, zn, t1[:], ALU.add, [r_t], [r_Z[zb]])
                    B.tt("dve", zn, zn, t2[:], ALU.add, [r_t], [r_Z[zb]])
                if tb + 1 < 4:
                    B.cp("dve", Zb[:, 1 - zb, 0], Zb[:, zb, 64], [r_Z[zb]], [r_Z[1 - zb]])
                B.act(Sb, Zb[:, zb, 0:64], AF.Copy, [r_Z[zb]], [r_Sb])

            def outputs(tb):
                ns = slice(tb * 64, (tb + 1) * 64)
                for fc in range(8):
                    bk = 2 + fc % 4
                    for g8 in range(8):
                        g = fc * 8 + g8
                        gh, pr = g // 32, g % 32
                        Rr = slice(gh * 64, (gh + 1) * 64)
                        outp = psum[bk][:, g8 * 64:(g8 + 1) * 64]
                        B.mm(outp, T0b[:, g, :], V[:, g, ns], True, False, [r_T0b, r_V], [rps[bk]])
                        B.mm(outp, Cmb[Rr, pr, 0, :], Sb[Rr, :, 0, pr], False, False, [r_Cmb, r_Sb], [rps[bk]])
                        B.mm(outp, Cmb[Rr, pr, 1, :], Sb[Rr, :, 1, pr], False, True, [r_Cmb, r_Sb], [rps[bk]])
                    yp = psum[bk][:, :]
                    dst = Gb[:, fc * 8:(fc + 1) * 8, ns]
                    B.act(dst, yp.rearrange("p (g n) -> p g n", g=8), AF.Gelu_apprx_tanh, [],
                          [rps[bk], r_Gb])

            local_states(0)
            for tb in range(4):
                if tb + 1 < 4:
                    local_states(tb + 1)
                recurrence(tb)
                outputs(tb)
            ctx["dump"]("Gb", GT[:], [r_Gb])
            B.barrier()
        B.barrier()
    if ctx.get("stop") == "C":
        return []
    finals = []
    with ExitStack() as es2:
        E2 = es2.enter_context
        gTb = E2(nc.sbuf_tensor("gTb", [128, 8, S], BF16))
        r_gT = B.R("gTb")
        r_G2 = B.R("G2T")
        with ExitStack() as es:
            E = es.enter_context
            Es = E(nc.sbuf_tensor("Esel2", [128, 64, 128], BF16))
            r_Es = B.R("Esel2")
            d_es = ctx["new_dsem"]("d_es2")
            B.S.dma("sp", Es[:], dr["esel"], d_es, (), [r_Es])
            for fc in range(8):
                for ip in range(4):
                    bk = (fc * 4 + ip) % 4
                    for ii in range(2):
                        i0 = ip * 2 + ii
                        for g8 in range(8):
                            B.mm(psum[bk][:, ii * 256:(ii + 1) * 256], Es[:, i0 * 8 + g8, :],
                                 Gb[:, fc * 8 + g8, :], g8 == 0, g8 == 7, [r_Es, r_Gb], [rps[bk]])
                    B.cp("dve" if ip % 2 == 0 else "act_copy", gTb[:, fc, ip * 512:(ip + 1) * 512],
                         psum[bk][:, :], [], [rps[bk], r_gT]) if False else \
                        B.cp("dve", gTb[:, fc, ip * 512:(ip + 1) * 512], psum[bk][:, :], [], [rps[bk], r_gT])
            ctx["dump"]("gTb", gTb[:], [r_gT])
            B.barrier()
        if ctx.get("stop") == "D1":
            return []
        with ExitStack() as es:
            E = es.enter_context
            wsg = WStream(ctx, E, 2, "wg1s")
            wsz = WStream(ctx, E, 2, "wz1s")
            sg = E(nc.sbuf_tensor("sg", [128, 2, 512], F32))
            sz = E(nc.sbuf_tensor("sz", [128, 2, 512], F32))
            r_sg = [B.R("sg0"), B.R("sg1")]
            r_sz = [B.R("sz0"), B.R("sz1")]
            k = 0
            for fo in range(8):
                if fo % 2 == 0:
                    wg, r_wg = wsg.load(dr["wg1"][:, fo * 128:fo * 128 + 256])
                    wz, r_wz = wsz.load(dr["w1"][:, 1024 + fo * 128:1024 + fo * 128 + 256])
                colo = (fo % 2) * 128
                for s in range(4):
                    sl = k % 2
                    k += 1
                    ba, bb_ = (0, 1) if sl == 0 else (2, 3)
                    cs = slice(s * 512, (s + 1) * 512)
                    for c in range(8):
                        B.mm(psum[ba][:, :], wg[:, c, colo:colo + 128], gTb[:, c, cs], c == 0, c == 7,
                             [r_wg, r_gT], [rps[ba]])
                    for c in range(8):
                        B.mm(psum[bb_][:, :], wz[:, c, colo:colo + 128], hT[:, c, cs], c == 0, c == 7,
                             [r_wz, r_hTall], [rps[bb_]])
                    B.act(sg[:, sl, :], psum[ba][:, :], AF.Sigmoid, [r_bg], [rps[ba], r_sg[sl]],
                          bias=small[:, 16 + fo:17 + fo])
                    B.act(sz[:, sl, :], psum[bb_][:, :], AF.Sigmoid, [], [rps[bb_], r_sz[sl]])
                    B.tt("pool", sg[:, sl, :], sg[:, sl, :], sz[:, sl, :], ALU.mult, [r_sz[sl]], [r_sg[sl]])
                    B.tt("dve", sg[:, sl, :], sg[:, sl, :], psum[bb_][:, :], ALU.mult, [], [rps[bb_], r_sg[sl]])
                    dstn = GT[:, fo, :].rearrange("p (n j) -> p j n", j=8)[:, 2 * s:2 * s + 2, :]
                    B.tt("dve", dstn, sg[:, sl, :].rearrange("p (j n) -> p j n", j=2),
                         gTb[:, fo, cs].rearrange("p (j n) -> p j n", j=2), ALU.mult,
                         [r_sg[sl], r_gT], [r_G2])
            ctx["dump"]("G2T", GT[:], [r_G2])
            B.barrier()
        if ctx.get("stop") == "D2":
            return []
        with ExitStack() as es:
            E = es.enter_context
            ws = WStream(ctx, E, 2, "wo1s")
            WO = E(nc.sbuf_tensor("WO1", [128, 8, D], BF16))
            gpo = E(nc.sbuf_tensor("gpo1", [128, D], F32))
            r_WO = [B.R("WO1_%d" % i) for i in range(4)]
            r_gpo = B.R("gpo1")
            B.S.dma("sp", gpo[:], dr["gpost1"], d_par, (), [r_gpo])
            for wb in range(4):
                ws.load(dr["wo1"][:, wb * 256:(wb + 1) * 256], WO[:, :, wb * 256:(wb + 1) * 256], r_WO[wb])

            tokap = None
            finals = out_proj_tail(ctx, E, WO, r_WO, gpo, r_gpo, dr["x1"], dr["out"], GT, r_G2, tokap, "t1",
                                   src_res=ctx.get("r_x1"))
            B.barrier()
    return finals


_PROG = {}


def _prog(mode):
    if mode not in _PROG:
        _PROG[mode] = build_program(mode)
    return _PROG[mode]


def _l0_inputs(inp, b):
    f = np.float32
    return {
        "x": np.ascontiguousarray(inp["x"][b], dtype=f),
        "w0": inp["_w0"], "wo0": inp["_wo0"], "gpre0": inp["_gpre0"], "gpost0": inp["_gpost0"],
        "bias0": inp["_bias0"], "ident": inp["_ident"],
    }


def _prep_l0(inputs):
    f = np.float32
    p = {}
    p["_w0"] = _perm_w_in(np.asarray(inputs["attn_w_in"][0], f))
    p["_wo0"] = np.ascontiguousarray(np.asarray(inputs["attn_w_out"][0], f))
    p["_gpre0"] = np.ascontiguousarray(np.asarray(inputs["attn_pre_norm"][0], f).reshape(8, 128).T)
    p["_gpost0"] = np.ascontiguousarray(np.broadcast_to(np.asarray(inputs["attn_post_norm"][0], f), (128, D)))
    p["_bias0"] = _bias_tiles(np.asarray(inputs["rel_bias"], f))
    p["_ident"] = np.eye(128, dtype=f)
    return p


def run_l0(inputs, debug=False):
    inp = dict(inputs)
    inp.update(_prep_l0(inputs))
    inp["x"] = np.asarray(inputs["x"], np.float32)
    nc = build_program("l0", debug=debug)
    in_maps = [_l0_inputs(inp, b) for b in range(8)]
    res = run_bass_kernel_spmd(nc, in_maps, core_ids=list(range(8)))
    if debug:
        return np.stack([r["x1"] for r in res.results], axis=0), res.results[0]
    return np.stack([r["x1"] for r in res.results], axis=0)


def _prep_l1(inputs):
    f = np.float32
    p = {}
    p["w1"] = np.ascontiguousarray(np.asarray(inputs["ssm_w_in"][0], f))
    p["wg1"] = np.ascontiguousarray(np.asarray(inputs["ssm_w_glu"][0], f))
    p["wo1"] = np.ascontiguousarray(np.asarray(inputs["ssm_w_out"][0], f))
    p["gpre1"] = np.ascontiguousarray(np.asarray(inputs["ssm_pre_norm"][0], f).reshape(8, 128).T)
    p["gpost1"] = np.ascontiguousarray(np.broadcast_to(np.asarray(inputs["ssm_post_norm"][0], f), (128, D)))
    p["bglu"] = np.ascontiguousarray(np.asarray(inputs["ssm_b_glu"][0], f).reshape(8, 128).T)

    def pl(a):
        a = np.asarray(a, f)
        a = a.reshape((2, 32, 64) + a.shape[2:])
        a = np.moveaxis(a, 2, 1)
        return np.ascontiguousarray(a.reshape((128, 32) + a.shape[3:]))
    are = pl(inputs["ssm_a_re"][0])
    aim = pl(inputs["ssm_a_im"][0])
    ldt = pl(np.broadcast_to(np.asarray(inputs["ssm_log_dt"][0], f)[:, None], (64, 64)))
    p["pl_a"] = np.ascontiguousarray(np.stack([are, aim, ldt], axis=1))
    bre = pl(inputs["ssm_b_re"][0])
    bim = pl(inputs["ssm_b_im"][0])
    cre = pl(np.swapaxes(np.asarray(inputs["ssm_c_re"][0], f), 1, 2))
    cim = pl(np.swapaxes(np.asarray(inputs["ssm_c_im"][0], f), 1, 2))
    p["pl_bc"] = np.ascontiguousarray(np.stack([bre, bim, cre, cim], axis=1))
    dvec = np.asarray(inputs["ssm_d"][0], f).reshape(64, 16)
    p["dcol"] = np.ascontiguousarray(np.tile(dvec.T, (8, 1)))
    jj = np.arange(128) // 16
    p["mask8"] = (jj[:, None] <= jj[None, :]).astype(f)
    k = np.arange(128)
    es = np.zeros((128, 64, 128), f)
    for a in range(8):
        for b in range(8):
            es[:, a * 8 + b, :] = ((k[:, None] // 16 == a) & (k[None, :] // 16 == b)
                                   & (k[:, None] % 16 == k[None, :] % 16))
    p["esel"] = es.astype(ml_dtypes.bfloat16)
    p["ident"] = np.eye(128, dtype=f)
    return p


def run_l1(inputs, x1, debug=False, stop=None):
    p = _prep_l1(inputs)
    nc = build_program("l1", debug=debug, stop=stop)
    in_maps = []
    for b in range(8):
        m = dict(p)
        m["x1"] = np.ascontiguousarray(x1[b], dtype=np.float32)
        in_maps.append(m)
    res = run_bass_kernel_spmd(nc, in_maps, core_ids=list(range(8)))
    out = np.stack([r["out"] for r in res.results], axis=0)
    if debug:
        return out, res.results[0]
    return out


FUSED = True


def kernel(**inputs):
    p0 = _prep_l0(inputs)
    p1 = _prep_l1(inputs)
    x = np.asarray(inputs["x"], np.float32)
    shared0 = {"w0": p0["_w0"], "wo0": p0["_wo0"], "gpre0": p0["_gpre0"], "gpost0": p0["_gpost0"],
               "bias0": p0["_bias0"], "ident": p0["_ident"]}
    if FUSED:
        nc = _prog("fused")
        shared = dict(shared0)
        shared.update(p1)
        in_maps = []
        for b in range(8):
            m = dict(shared)
            m["x"] = np.ascontiguousarray(x[b])
            in_maps.append(m)
        res = run_bass_kernel_spmd(nc, in_maps, core_ids=list(range(8)))
        return np.stack([np.asarray(r["out"], np.float32) for r in res.results], axis=0)
    in_maps = []
    for b in range(8):
        m = dict(shared0)
        m["x"] = np.ascontiguousarray(x[b])
        in_maps.append(m)
    res = run_bass_kernel_spmd(_prog("l0"), in_maps, core_ids=list(range(8)))
    x1 = [np.asarray(r["x1"], np.float32) for r in res.results]
    in_maps = []
    for b in range(8):
        m = dict(p1)
        m["x1"] = np.ascontiguousarray(x1[b])
        in_maps.append(m)
    res = run_bass_kernel_spmd(_prog("l1"), in_maps, core_ids=list(range(8)))
    return np.stack([np.asarray(r["out"], np.float32) for r in res.results], axis=0)
```

```python
import math
import numpy as np
import ml_dtypes
import concourse.bass as bass
import concourse.mybir as mybir
from concourse.bass_utils import run_bass_kernel_spmd

F32 = mybir.dt.float32
BF16 = mybir.dt.bfloat16
AF = mybir.ActivationFunctionType
ALU = mybir.AluOpType

D = 1024
S = 2048
NT = 16
EPS = 1e-6
GROUPS = ((128, 1), (512, 4), (2048, 16))
NEGM = -30000.0
SSM_G = 64
SSM_P = 64
SSM_C = 16


class Res:
    __slots__ = ("name", "w", "rs", "rd")

    def __init__(self, name):
        self.name = name
        self.w = None
        self.rs = {}
        self.rd = []


class DmaSem:
    def __init__(self, sem):
        self.sem = sem
        self.count = 0


class Op:
    __slots__ = ("eng", "fn", "deps", "need", "seq", "dsem", "dval", "idx", "dreq", "group", "gidx")


class Sched:
    ENGS = ("pe", "act", "dve", "pool", "sp")

    def __init__(self, nc):
        self.nc = nc
        self.ops = {e: [] for e in self.ENGS}
        self.allops = []
        self.cur_group = None
        self.ngroups = 0

    def begin_group(self):
        self.ngroups += 1
        self.cur_group = self.ngroups

    def end_group(self):
        self.cur_group = None

    def _mk(self, eng, fn, reads, writes, isdma=False):
        o = Op()
        o.eng = eng
        o.fn = fn
        o.need = False
        o.seq = 0
        o.dsem = None
        o.dval = 0
        deps = []
        for r in reads:
            if r.w is not None:
                deps.append(r.w)
        for r in writes:
            if r.w is not None:
                deps.append(r.w)
            deps.extend(r.rs.values())
            deps.extend(r.rd)
        o.deps = deps
        o.dreq = {}
        for p in deps:
            if p.dsem is not None:
                o.dreq[id(p.dsem)] = (p.dsem.sem, p.dsem.count)
        for r in reads:
            if isdma:
                r.rd.append(o)
            else:
                r.rs[eng] = o
        for r in writes:
            r.w = o
            r.rs = {}
            r.rd = []
        o.idx = len(self.ops[eng])
        o.group = self.cur_group
        o.gidx = len(self.allops)
        self.ops[eng].append(o)
        self.allops.append(o)
        return o

    def op(self, eng, fn, reads=(), writes=()):
        return self._mk(eng, fn, list(reads), list(writes))

    def dma(self, eng, out, in_, dsem, reads=(), writes=()):
        def fn(e):
            return e.dma_start(out=out, in_=in_)
        o = self._mk(eng, fn, list(reads), list(writes), isdma=True)
        dsem.count += 16
        o.dsem = dsem
        o.dval = dsem.count
        return o


def _t5_bucket_np(dist):
    max_exact = 16
    n = np.maximum(dist, 1).astype(np.float32)
    large = max_exact + (np.log(n / np.float32(max_exact)) / np.float32(math.log(2048 / max_exact))
                         * np.float32(32 - max_exact)).astype(np.int32)
    large = np.minimum(large, 31)
    return np.where(dist < max_exact, dist, large)


def _bias_tiles(rel_bias):
    j = np.arange(128)[:, None]
    q = np.arange(256)[None, :]
    delta = q - j
    valid = (delta >= 0) & (delta <= 128)
    out = np.empty((8, 128, 6, 256), np.float32)
    for g, (_, dil) in enumerate(GROUPS):
        bucket = _t5_bucket_np(np.maximum(delta, 0) * dil)
        for h in range(16):
            t = rel_bias[bucket, h]
            t = np.where(valid, t, np.float32(NEGM)).astype(np.float32)
            out[h // 2, :, g * 2 + (h % 2), :] = t
    return out


def _perm_w_in(w):
    out = np.empty((8, D, 1280), np.float32)
    for hp in range(8):
        cs = slice(hp * 128, (hp + 1) * 128)
        blk = []
        for g in range(3):
            blk.append(w[:, (0 * 3 + g) * 1024:(0 * 3 + g + 1) * 1024][:, cs])
            blk.append(w[:, (1 * 3 + g) * 1024:(1 * 3 + g + 1) * 1024][:, cs])
        for g in range(3):
            blk.append(w[:, (2 * 3 + g) * 1024:(2 * 3 + g + 1) * 1024][:, cs])
        blk.append(w[:, 9216:10240][:, cs])
        out[hp] = np.concatenate(blk, axis=1)
    return out


class Builder:
    def __init__(self, nc):
        self.nc = nc
        self.S = Sched(nc)
        self.dsems = []
        self.allres = []
        self.last_barrier = None
        self.scratch = None

    def R(self, name):
        r = Res(name)
        r.w = self.last_barrier
        self.allres.append(r)
        return r

    def barrier(self):
        scr = self.scratch

        def fn(e):
            return e.memset(scr, 0.0)
        o = self.S.op("pool", fn, (), list(self.allres))
        self.last_barrier = o
        return o

    def mm(self, out, lhsT, rhs, start, stop, reads, writes):
        def fn(e):
            return e.matmul(out, lhsT, rhs, start=start, stop=stop)
        return self.S.op("pe", fn, reads, writes)

    def tr(self, out, in_, ident, reads, writes):
        def fn(e):
            return e.transpose(out, in_, ident)
        return self.S.op("pe", fn, reads, writes)

    def act(self, out, in_, func, reads, writes, scale=None, bias=None, accum_out=None):
        kw = {}
        if scale is not None:
            kw["scale"] = scale
        if bias is not None:
            kw["bias"] = bias
        if accum_out is not None:
            kw["accum_out"] = accum_out

        def fn(e):
            return e.activation(out, in_, func, **kw)
        return self.S.op("act", fn, reads, writes)

    def tt(self, eng, out, in0, in1, op, reads, writes):
        def fn(e):
            return e.tensor_tensor(out, in0, in1, op)
        return self.S.op(eng, fn, reads, writes)

    def ts(self, eng, out, in0, s1, op0, reads, writes, s2=None, op1=None):
        def fn(e):
            if op1 is None:
                return e.tensor_scalar(out, in0, s1, None, op0)
            return e.tensor_scalar(out, in0, s1, s2, op0, op1)
        return self.S.op(eng, fn, reads, writes)

    def stt(self, out, in0, scalar, in1, op0, op1, reads, writes):
        def fn(e):
            return e.scalar_tensor_tensor(out, in0, scalar, in1, op0, op1)
        return self.S.op("dve", fn, reads, writes)

    def cp(self, eng, out, in_, reads, writes):
        def fn(e):
            return e.tensor_copy(out, in_)
        return self.S.op(eng, fn, reads, writes)

    def memset(self, eng, ap, val, writes):
        def fn(e):
            return e.memset(ap, val)
        return self.S.op(eng, fn, (), writes)

    def recip(self, out, in_, reads, writes):
        def fn(e):
            return e.reciprocal(out, in_)
        return self.S.op("dve", fn, reads, writes)


def rms_rstd(B, ss, rs, r_ss, r_rs):
    B.ts("dve", rs, ss, 1.0 / D, ALU.mult, [r_ss], [r_rs], s2=EPS, op1=ALU.add)
    B.act(rs, rs, AF.Sqrt, [r_rs], [r_rs])
    B.recip(rs, rs, [r_rs], [r_rs])


def build_program(mode="fused", debug=False, stop=None):
    nc = bass.Bass("TRN2", target_bir_lowering=False)
    B = Builder(nc)
    dt = nc.dram_tensor
    dr = {}
    do_l0 = mode in ("l0", "fused")
    do_l1 = mode in ("l1", "fused")
    if do_l0:
        dr["x"] = dt("x", [S, D], F32, kind="ExternalInput").ap()
        dr["w0"] = dt("w0", [8, D, 1280], F32, kind="ExternalInput").ap()
        dr["wo0"] = dt("wo0", [D, D], F32, kind="ExternalInput").ap()
        dr["gpre0"] = dt("gpre0", [128, 8], F32, kind="ExternalInput").ap()
        dr["gpost0"] = dt("gpost0", [128, D], F32, kind="ExternalInput").ap()
        dr["bias0"] = dt("bias0", [8, 128, 6, 256], F32, kind="ExternalInput").ap()
    dr["ident"] = dt("ident", [128, 128], F32, kind="ExternalInput").ap()
    if mode == "l0":
        dr["x1"] = dt("x1", [S, D], F32, kind="ExternalOutput").ap()
    elif mode == "l1":
        dr["x1"] = dt("x1", [S, D], F32, kind="ExternalInput").ap()
    else:
        dr["x1"] = dt("x1", [S, D], F32, kind="Internal").ap()
    if do_l1:
        build_l1_decl(dt, dr)
        dr["out"] = dt("out", [S, D], F32, kind="ExternalOutput").ap()

    from contextlib import ExitStack
    with ExitStack() as es:
        E = es.enter_context
        sems = {e: E(nc.semaphore("c_" + e)) for e in ("pe", "act", "dve", "pool", "sp")}

        def new_dsem(name):
            return DmaSem(E(nc.semaphore(name)))

        psum = [E(nc.psum_tensor("ps%d" % i, [128, 512], F32)) for i in range(8)]
        rps = [B.R("psum%d" % i) for i in range(8)]
        hT = E(nc.sbuf_tensor("hT", [128, 8, S], BF16))
        GT = E(nc.sbuf_tensor("GT", [128, 8, S], BF16))
        ident = E(nc.sbuf_tensor("ident_sb", [128, 128], F32))
        ones = None
        small = E(nc.sbuf_tensor("small", [128, 64], F32))
        bscr = E(nc.sbuf_tensor("bscr", [128, 8], F32))
        B.scratch = bscr[:, 0:1]
        r_hT = [B.R("hT%d" % i) for i in range(NT)]
        r_GT = B.R("GT")
        r_ident = B.R("ident")
        r_ones = B.R("ones")
        d_const = new_dsem("d_const")
        B.S.dma("sp", ident[:], dr["ident"], d_const, (), [r_ident])
        dumps = {}
        d_dump = new_dsem("d_dump")

        def dump(name, ap, reads):
            if not debug or name in dumps:
                return
            t = dt("dbg_" + name, list(ap.shape), ap.dtype, kind="ExternalOutput").ap()
            o = B.S.dma("pool", t, ap, d_dump, reads, [])
            fr = B.R("dfin_" + name)
            fr.w = o
            dumps[name] = fr
        ctx = dict(nc=nc, B=B, dr=dr, E=E, new_dsem=new_dsem, psum=psum, rps=rps, hT=hT, GT=GT, dump=dump,
                   stop=stop, ident=ident, ones=ones, small=small, r_hT=r_hT, r_GT=r_GT,
                   r_ident=r_ident, r_ones=r_ones, sems=sems)
        if mode == "fused":
            ctx["r_x1"] = [B.R("x1_%d" % i) for i in range(NT)]
        if do_l1:
            l1_prep_stage1(ctx, E)
        else:
            ctx["pump"] = lambda n=None: None
        with nc.Block() as block:
            final = []
            if do_l0:
                final = build_l0(ctx)
            if do_l1:
                final = build_l1(ctx)
            B.S.op("sp", None, list(final) + list(dumps.values()), ())
            emit_all(B, block, sems)
    return nc


def emit_all(B, block, sems):
    S_ = B.S
    nc = B.nc
    for o in S_.allops:
        for p in o.deps:
            if p.dsem is None and not (p.eng == "pe" and o.eng == "pe"):
                p.need = True
    for e in S_.ENGS:
        c = 0
        for o in S_.ops[e]:
            if o.need:
                c += 1
                o.seq = c

    def reqs_of(o, e, before=None):
        req = {}
        for p in o.deps:
            if before is not None and p.gidx >= before:
                continue
            if p.dsem is not None:
                key = ("d", id(p.dsem))
                v = o.dreq[id(p.dsem)]
                if req.get(key, (None, 0))[1] < v[1]:
                    req[key] = v
            else:
                if p.eng == "pe" and e == "pe":
                    continue
                key = ("e", p.eng)
                if req.get(key, (None, 0))[1] < p.seq:
                    req[key] = (sems[p.eng], p.seq)
        return req

    def run(e, eo):
        waited = {}
        ops = S_.ops[e]
        seen_groups = set()
        for k, o in enumerate(ops):
            req = reqs_of(o, e)
            if o.group is not None and o.group not in seen_groups:
                seen_groups.add(o.group)
                j = k + 1
                while j < len(ops) and ops[j].group == o.group:
                    for key, v in reqs_of(ops[j], e, before=o.gidx).items():
                        if req.get(key, (None, 0))[1] < v[1]:
                            req[key] = v
                    j += 1
            for key, (sem, val) in req.items():
                if waited.get(key, 0) >= val:
                    continue
                waited[key] = val
                eo.wait_ge(sem, val)
            if o.fn is None:
                continue
            ins = o.fn(eo)
            if o.dsem is not None:
                ins.then_inc(o.dsem.sem, 16)
            elif o.need:
                ins.then_inc(sems[e], 1)

    @block.tensor
    def _(eo):
        run("pe", eo)

    @block.scalar
    def _(eo):
        run("act", eo)

    @block.vector
    def _(eo):
        run("dve", eo)

    @block.gpsimd
    def _(eo):
        run("pool", eo)

    @block.sync
    def _(eo):
        run("sp", eo)


def make_hT(ctx, src, scope, gp, r_gp, perm8=False, src_res=None, name="a"):
    nc, B, psum, rps, hT, ident = ctx["nc"], ctx["B"], ctx["psum"], ctx["rps"], ctx["hT"], ctx["ident"]
    E = scope
    NS = 3
    xt = E(nc.sbuf_tensor("xt_" + name, [128, NS, D], F32))
    xs = E(nc.sbuf_tensor("xs_" + name, [128, NS, D], F32))
    jk = E(nc.sbuf_tensor("jk_" + name, [128, D], F32))
    st = E(nc.sbuf_tensor("st_" + name, [128, NS, 4], F32))
    r_xt = [B.R("xt%d" % i) for i in range(NS)]
    r_xs = [B.R("xs%d" % i) for i in range(NS)]
    r_st = [B.R("st%d" % i) for i in range(NS)]
    r_jk = B.R("jk")
    d_x = [ctx["new_dsem"]("d_x%s%d" % (name, i)) for i in range(NS)]

    def stage1(tt):
        sl = tt % NS
        B.S.dma("sp", xt[:, sl, :], src[tt * 128:(tt + 1) * 128, :], d_x[sl],
                [src_res[tt]] if src_res else (), [r_xt[sl]])
        B.act(jk[:, :], xt[:, sl, :], AF.Square, [r_xt[sl]], [r_jk, r_st[sl]], accum_out=st[:, sl, 0:1])
        B.ts("dve", st[:, sl, 1:2], st[:, sl, 0:1], 1.0 / D, ALU.mult, [r_st[sl]], [r_st[sl]],
             s2=EPS, op1=ALU.add)

    def stage2(tt):
        sl = tt % NS
        B.act(st[:, sl, 1:2], st[:, sl, 1:2], AF.Sqrt, [r_st[sl]], [r_st[sl]])
        B.recip(st[:, sl, 1:2], st[:, sl, 1:2], [r_st[sl]], [r_st[sl]])
        B.ts("dve", xs[:, sl, :], xt[:, sl, :], st[:, sl, 1:2], ALU.mult,
             [r_xt[sl], r_st[sl]], [r_xs[sl]])

    def stage3(tt):
        sl = tt % NS
        for half in range(2):
            bk = half + 2 * (tt % 2)
            for j in range(4):
                c = half * 4 + j
                B.tr(psum[bk][:, j * 128:(j + 1) * 128], xs[:, sl, c * 128:(c + 1) * 128], ident[:],
                     [r_xs[sl], ctx["r_ident"]], [rps[bk]])
            c0 = half * 4
            if perm8:
                dst = hT[:, c0:c0 + 4, :].rearrange("p c (j n) -> p c j n", j=8)[:, :, :, 16 * tt:16 * tt + 16]
                srcp = psum[bk][:, :].rearrange("p (c n j) -> p c j n", c=4, j=8)
                gain = gp[:, c0:c0 + 4].unsqueeze(2).unsqueeze(3).to_broadcast([128, 4, 8, 16])
            else:
                dst = hT[:, c0:c0 + 4, tt * 128:(tt + 1) * 128]
                srcp = psum[bk][:, :].rearrange("p (c t) -> p c t", c=4)
                gain = gp[:, c0:c0 + 4].unsqueeze(2).to_broadcast([128, 4, 128])
            wr = ctx["r_hT"] if perm8 else [ctx["r_hT"][tt]]
            B.tt("dve", dst, srcp, gain, ALU.mult, [r_gp], [rps[bk]] + wr)

    for t in range(NT + 2):
        if t < NT:
            stage1(t)
        if 0 <= t - 1 < NT:
            stage2(t - 1)
        if 0 <= t - 2 < NT:
            stage3(t - 2)


class WStream:
    def __init__(self, ctx, scope, nslot=2, name="ws", cols=256):
        nc, B = ctx["nc"], ctx["B"]
        self.ctx = ctx
        self.n = nslot
        self.wst = scope(nc.sbuf_tensor(name + "_st", [128, nslot, 8, cols], F32))
        self.wbf = scope(nc.sbuf_tensor(name + "_bf", [128, nslot, 8, cols], BF16))
        self.r_st = [B.R(name + "st%d" % i) for i in range(nslot)]
        self.r_bf = [B.R(name + "bf%d" % i) for i in range(nslot)]
        self.ds = [ctx["new_dsem"]("d_" + name + str(i)) for i in range(nslot)]
        self.k = 0

    def dma(self, src):
        B = self.ctx["B"]
        sl = self.k % self.n
        self.k += 1
        B.S.dma("sp", self.wst[:, sl], src.rearrange("(c p) n -> p c n", p=128), self.ds[sl],
                (), [self.r_st[sl]])
        return sl

    def cast(self, sl):
        B = self.ctx["B"]
        B.act(self.wbf[:, sl], self.wst[:, sl], AF.Copy, [self.r_st[sl]], [self.r_bf[sl]])
        return self.wbf[:, sl], self.r_bf[sl]

    def load(self, src, dst=None, r_dst=None):
        B = self.ctx["B"]
        sl = self.k % self.n
        self.k += 1
        B.S.dma("sp", self.wst[:, sl], src.rearrange("(c p) n -> p c n", p=128), self.ds[sl],
                (), [self.r_st[sl]])
        if dst is None:
            dst, r_dst = self.wbf[:, sl], self.r_bf[sl]
        B.act(dst, self.wst[:, sl], AF.Copy, [self.r_st[sl]], [r_dst])
        return dst, r_dst


def grp_tokens(g, bi):
    d = GROUPS[g][1]
    nbr = 16 // d
    r = bi // nbr
    m0 = (bi % nbr) * 128
    return d * m0 + r, d


def build_l0(ctx):
    nc, B, dr, psum, rps = ctx["nc"], ctx["B"], ctx["dr"], ctx["psum"], ctx["rps"]
    hT, GT, ones, small = ctx["hT"], ctx["GT"], ctx["ones"], ctx["small"]
    r_hT, r_GT = ctx["r_hT"], ctx["r_GT"]
    from contextlib import ExitStack
    d_par = ctx["new_dsem"]("d_par0")
    r_gp = B.R("gpre0")
    B.S.dma("sp", small[:, 0:8], dr["gpre0"], d_par, (), [r_gp])
    with ExitStack() as es:
        make_hT(ctx, dr["x"], es.enter_context, small[:, 0:8], r_gp)
        B.barrier()
    allhT = list(r_hT)
    ctx["dump"]("hT", hT[:], allhT)
    with ExitStack() as es:
        E = es.enter_context
        ws = WStream(ctx, E, 2, "w0s", cols=128)
        QK = E(nc.sbuf_tensor("QK", [128, 6, S], BF16))
        VA0 = E(nc.sbuf_tensor("VA0", [128, 48, 65], BF16))
        VA1 = E(nc.sbuf_tensor("VA1", [128, 48, 128], BF16))
        SZ = E(nc.sbuf_tensor("SZ", [128, S], F32))
        BT = E(nc.sbuf_tensor("BT", [128, 2, 6, 256], F32))
        OA = E(nc.sbuf_tensor("OA", [128, 2, S], F32))
        LT = E(nc.sbuf_tensor("LT", [128, 2, 2, 256], F32))
        PT = E(nc.sbuf_tensor("PT", [128, 5, 2, 256], BF16))
        VT = E(nc.sbuf_tensor("VT", [128, 3, S], BF16))
        RHL = E(nc.sbuf_tensor("RHL", [128, 2, S], BF16))
        identb = E(nc.sbuf_tensor("identb", [128, 128], BF16))
        onesb = E(nc.sbuf_tensor("onesb", [128, 128], BF16))
        r_VT = [B.R("VT%d" % i) for i in range(3)]
        r_RHL = B.R("RHL")
        r_cb = B.R("constb")
        B.cp("pool", identb[:], ctx["ident"][:], [ctx["r_ident"]], [r_cb])
        B.memset("pool", onesb[:], 1.0, [r_cb])
        r_QK = [B.R("QK%d" % i) for i in range(6)]
        r_VA = [B.R("VA%d" % i) for i in range(3)]
        r_SZ = B.R("SZ")
        r_BT = [B.R("BT0"), B.R("BT1")]
        r_OA = [B.R("OA0"), B.R("OA1")]
        r_LT = [B.R("LT%d" % i) for i in range(2)]
        r_PT = [B.R("PT%d" % i) for i in range(5)]
        d_bt = [ctx["new_dsem"]("d_bt0"), ctx["new_dsem"]("d_bt1")]
        B.memset("pool", VA0[:], 1.0, r_VA)
        B.memset("pool", VA1[:], 0.0, r_VA)
        B.memset("pool", VA1[:, :, 0:1], 1.0, r_VA)
        pb = [0]

        def proj_bank():
            pb[0] ^= 1
            return pb[0]

        stb = [0]
        fb = [0]
        dq = []
        pre = {}
        wdma = {}

        def wsrc(h, wb):
            return dr["w0"][h][:, wb * 128:(wb + 1) * 128]
        for hp in range(8):
            bsl = hp % 2
            B.S.dma("sp", BT[:, bsl], dr["bias0"][hp], d_bt[bsl], (), [r_BT[bsl]])
            for wb in range(10):
                key = (hp, wb)
                if key not in pre:
                    if key not in wdma:
                        wdma[key] = ws.dma(wsrc(*key))
                    pre[key] = ws.cast(wdma.pop(key))
                wbf, r_w = pre.pop(key)
                nkey = (hp, wb + 1) if wb < 9 else ((hp + 1, 0) if hp + 1 < 8 else None)
                if nkey is not None and nkey not in wdma and nkey not in pre:
                    wdma[nkey] = ws.dma(wsrc(*nkey))

                def at_s2(nkey=nkey):
                    if nkey is not None and nkey in wdma:
                        pre[nkey] = ws.cast(wdma.pop(nkey))
                for half_ in range(1):
                    colo = 0
                    kind = ("q", "k", "q", "k", "q", "k", "v0", "v1", "v2", "z")[wb]
                    half = wb % 2
                    if kind in ("q", "k", "z"):
                        g = wb // 2 if wb < 6 else 0
                        d = GROUPS[g][1] if kind != "z" else 1
                        for s in range(4):
                            if s == 2:
                                at_s2()
                            bk = proj_bank()
                            for c in range(8):
                                B.mm(psum[bk][:, :], wbf[:, c, colo:colo + 128],
                                     hT[:, c, s * 512:(s + 1) * 512], c == 0, c == 7,
                                     [r_w] + allhT, [rps[bk]])
                            if kind == "z":
                                B.act(SZ[:, s * 512:(s + 1) * 512], psum[bk][:, :], AF.Silu,
                                      [], [rps[bk], r_SZ])
                            else:
                                idx = 2 * g + half
                                L = S // d
                                dst = QK[:, idx, :].rearrange("p (r m) -> p r m", r=d)[
                                    :, :, (512 // d) * s:(512 // d) * (s + 1)]
                                srcp = psum[bk][:, :].rearrange("p (m r) -> p r m", r=d)
                                B.act(dst, srcp, AF.Copy, [], [rps[bk], r_QK[idx]],
                                      scale=(0.125 if kind == "q" else 1.0))
                    else:
                        g = int(kind[1])
                        for s in range(4):
                            if s == 2:
                                at_s2()
                            bk = proj_bank()
                            for c in range(8):
                                B.mm(psum[bk][:, :], wbf[:, c, colo:colo + 128],
                                     hT[:, c, s * 512:(s + 1) * 512], c == 0, c == 7,
                                     [r_w] + allhT, [rps[bk]])
                            B.act(VT[:, g, s * 512:(s + 1) * 512], psum[bk][:, :], AF.Copy, [],
                                  [rps[bk], r_VT[g]])
                        for quad in range(4):
                            bk = proj_bank()
                            pb16 = psum[bk][:, :].bitcast(BF16)
                            for j in range(4):
                                bi = quad * 4 + j
                                o, st = grp_tokens(g, bi)
                                B.tr(pb16[:, j * 128:(j + 1) * 128], VT[:, g, o:o + 127 * st + 1:st], identb[:],
                                     [r_VT[g], r_cb], [rps[bk]])
                            pv = pb16[:, 0:512].rearrange("p (j c) -> p j c", j=4)
                            b0 = g * 16 + quad * 4
                            B.cp("dve", VA0[:, b0:b0 + 4, 0:64], pv[:, :, 0:64], [], [rps[bk], r_VA[g]])
                            B.cp("dve", VA1[:, b0:b0 + 4, 64:128], pv[:, :, 64:128], [], [rps[bk], r_VA[g]])
                ctx["pump"](4)
                if wb < 7:
                    while dq:
                        f = dq.pop(0)
                        if f is not None:
                            f()
                            break
                if wb == 8:
                    while dq:
                        f = dq.pop(0)
                        if f is not None:
                            f()
            its = [(hh, g, m) for hh in range(2) for g in range(3) for m in range(8)]
            st_info = {}

            def nq_of(g, kb):
                nbr = 16 // GROUPS[g][1]
                return 256 if (kb + 1) % nbr != 0 else 128

            def issue_st(i):
                hh, g, m = its[i]
                bk = (0, 1, 2, 7)[stb[0] % 4]
                stb[0] += 1
                rows = slice(hh * 64, (hh + 1) * 64)
                for t in range(2):
                    kb = 2 * m + t
                    nq = nq_of(g, kb)
                    B.mm(psum[bk][:, t * 256:t * 256 + nq], QK[rows, 2 * g + 1, kb * 128:(kb + 1) * 128],
                         QK[rows, 2 * g, kb * 128:kb * 128 + nq], True, True,
                         [r_QK[2 * g], r_QK[2 * g + 1]], [rps[bk]])
                ls = i % 2
                ps_ = i % 5
                bias2 = BT[:, bsl, g * 2 + hh, :].unsqueeze(1).to_broadcast([128, 2, 256])
                B.tt("dve", LT[:, ls], psum[bk][:, :].rearrange("p (t q) -> p t q", t=2), bias2, ALU.add,
                     [r_BT[bsl]], [rps[bk], r_LT[ls]])
                B.act(PT[:, ps_], LT[:, ls], AF.Exp, [r_LT[ls]], [r_PT[ps_]])
                st_info[i] = ps_

            def issue_pv(i):
                hh, g, m = its[i]
                d = GROUPS[g][1]
                nbr = 16 // d
                M = 65 if hh == 0 else 128
                VA = VA0 if hh == 0 else VA1
                ps_ = st_info[i]
                for t in range(2):
                    kb = 2 * m + t
                    has_prev = kb % nbr != 0
                    if has_prev:
                        pslot, pt = (ps_, 0) if t == 1 else (st_info[i - 1], 1)
                    if g == 0:
                        outs = [(3 + kb // 4, slice((kb % 4) * 128, (kb % 4) * 128 + 128), slice(0, 128))]
                    elif g == 1:
                        r, nn = kb // 4, kb % 4
                        outs = [(3 + nn, slice(r, 512, 4), slice(0, 128))]
                    else:
                        outs = [(3 + sub, slice(kb, 512, 16), slice(32 * sub, 32 * sub + 32)) for sub in range(4)]
                    for (bo, ocols, qcols) in outs:
                        outp = psum[bo][0:M, ocols]
                        first = (g == 0 and kb % 4 == 0)
                        if has_prev:
                            pq = slice(128 + qcols.start, 128 + qcols.stop)

                            def f1(e, outp=outp, a=VA[:, g * 16 + kb - 1, 0:M], b=PT[:, pslot, pt, pq], first=first):
                                return e.matmul(outp, a, b, start=first, stop=False, skip_group_check=True)
                            B.S.op("pe", f1, [r_VA[g], r_PT[pslot]], [rps[bo]])
                            first = False

                        last = (g == 2 and kb == 15)

                        def f2(e, outp=outp, a=VA[:, g * 16 + kb, 0:M], b=PT[:, ps_, t, qcols], first=first, last=last):
                            return e.matmul(outp, a, b, start=first, stop=last, skip_group_check=True)
                        B.S.op("pe", f2, [r_VA[g], r_PT[ps_]], [rps[bo]])
                if g == 2 and m == 7:
                    for span in range(4):
                        if span % 2 == 0:
                            B.cp("dve", OA[0:M, hh, span * 512:(span + 1) * 512], psum[3 + span][0:M, :],
                                 [], [rps[3 + span], r_OA[hh]])
                        else:
                            B.act(OA[0:M, hh, span * 512:(span + 1) * 512], psum[3 + span][0:M, :], AF.Copy,
                                  [], [rps[3 + span], r_OA[hh]])
                    finalize(hh)

            def finalize(hh, hp=hp):
                drow = 64 if hh == 0 else 0
                R = slice(hh * 64, (hh + 1) * 64)

                def prepA(s):
                    cs = slice(s * 512, (s + 1) * 512)
                    den = OA[drow:drow + 1, hh, cs]
                    B.act(den, den, AF.Ln, [], [r_OA[hh]])
                    B.act(den, den, AF.Exp, [], [r_OA[hh]], scale=-1.0)

                def prepB(s):
                    cs = slice(s * 512, (s + 1) * 512)
                    den = OA[drow:drow + 1, hh, cs]
                    B.cp("dve", RHL[drow:drow + 1, 0, cs], den, [r_OA[hh]], [r_RHL])
                    B.tt("dve", RHL[drow:drow + 1, 1, cs], den, RHL[drow:drow + 1, 0, cs], ALU.subtract,
                         [r_OA[hh]], [r_RHL])

                def span(s):
                    bk = (0, 1, 2, 7)[stb[0] % 4]
                    stb[0] += 1
                    cs = slice(s * 512, (s + 1) * 512)
                    B.mm(psum[bk][:, :], onesb[drow:drow + 1, :], RHL[drow:drow + 1, 0, cs], True, False,
                         [r_cb, r_RHL], [rps[bk]])
                    B.mm(psum[bk][:, :], onesb[drow:drow + 1, :], RHL[drow:drow + 1, 1, cs], False, True,
                         [r_cb, r_RHL], [rps[bk]])
                    B.tt("dve", OA[R, hh, cs], OA[R, hh, cs], psum[bk][R, :], ALU.mult,
                         [], [rps[bk], r_OA[hh]])
                    B.tt("pool", GT[R, hp, cs], OA[R, hh, cs], SZ[R, cs], ALU.mult,
                         [r_OA[hh], r_SZ], [r_GT])
                for s in range(4):
                    dq.append(lambda s=s: prepA(s))
                    dq.append(None)
                    dq.append(lambda s=s: prepB(s))
                for _ in range(4):
                    dq.append(None)
                for s in range(4):
                    dq.append(lambda s=s: span(s))
                    if s < 3:
                        dq.append(None)

            if hp == 0:
                ctx["dump"]("QK", QK[:], r_QK)
                ctx["dump"]("VA0", VA0[:], r_VA)
                ctx["dump"]("VA1", VA1[:], r_VA)
                ctx["dump"]("SZ", SZ[:], [r_SZ])
                ctx["dump"]("BT", BT[:, 0], [r_BT[0]])
            LOOK = 3
            n = len(its)
            for i in range(min(LOOK, n)):
                issue_st(i)
            for i in range(n):
                B.S.begin_group()
                issue_pv(i)
                if i + LOOK < n:
                    issue_st(i + LOOK)
                B.S.end_group()
                if dq:
                    f = dq.pop(0)
                    if f is not None:
                        f()
            if hp == 7:
                while dq:
                    f = dq.pop(0)
                    if f is not None:
                        f()
        B.barrier()
    ctx["dump"]("GT", GT[:], [r_GT])
    finals = []
    with ExitStack() as es:
        E = es.enter_context
        ws = WStream(ctx, E, 2, "wo0s")
        WO = E(nc.sbuf_tensor("WO0", [128, 8, D], BF16))
        gpo = E(nc.sbuf_tensor("gpo0", [128, D], F32))
        r_WO = [B.R("WO%d" % i) for i in range(4)]
        r_gpo = B.R("gpo0")
        B.S.dma("sp", gpo[:], dr["gpost0"], d_par, (), [r_gpo])
        for wb in range(4):
            ws.load(dr["wo0"][:, wb * 256:(wb + 1) * 256], WO[:, :, wb * 256:(wb + 1) * 256], r_WO[wb])
        finals = out_proj_tail(ctx, E, WO, r_WO, gpo, r_gpo, dr["x"], dr["x1"], GT, r_GT, None, "t0",
                               dst_res=ctx.get("r_x1"))
        B.barrier()
    return finals


def out_proj_tail(ctx, E, WO, r_WO, gpo, r_gpo, xsrc, dst, GT, r_GT, tokap, name, src_res=None, dst_res=None):
    nc, B, psum, rps = ctx["nc"], ctx["B"], ctx["psum"], ctx["rps"]
    xt = E(nc.sbuf_tensor("xt_" + name, [128, 2, D], F32))
    yt = E(nc.sbuf_tensor("yt_" + name, [128, 2, D], F32))
    sq = E(nc.sbuf_tensor("sq_" + name, [128, 512], F32))
    st = E(nc.sbuf_tensor("st_" + name, [128, 2, 4], F32))
    r_xt = [B.R("xt0"), B.R("xt1")]
    r_yt = [B.R("yt0"), B.R("yt1")]
    r_sq = B.R("sq")
    r_st = [B.R("st0"), B.R("st1")]
    d_x = [ctx["new_dsem"]("d_x%s0" % name), ctx["new_dsem"]("d_x%s1" % name)]
    d_o = [ctx["new_dsem"]("d_o%s0" % name), ctx["new_dsem"]("d_o%s1" % name)]
    finals = []
    for tt in range(NT):
        sl = tt % 2
        if tokap is None:
            xrows, orows = xsrc[tt * 128:(tt + 1) * 128, :], dst[tt * 128:(tt + 1) * 128, :]
        else:
            jj, nb = tt // 2, tt % 2
            xrows = xsrc.rearrange("(n j) d -> j n d", j=8)[jj, nb * 128:(nb + 1) * 128, :]
            orows = dst.rearrange("(n j) d -> j n d", j=8)[jj, nb * 128:(nb + 1) * 128, :]
        B.S.dma("sp", xt[:, sl, :], xrows, d_x[sl],
                [src_res[tt]] if src_res else (), [r_xt[sl]])
        banks = (0, 1) if sl == 0 else (2, 3)
        for half in range(2):
            bk = banks[half]
            for c in range(8):
                lhs = GT[:, c, tt * 128:(tt + 1) * 128]
                B.mm(psum[bk][:, :], lhs, WO[:, c, half * 512:(half + 1) * 512], c == 0, c == 7,
                     [r_GT] + r_WO[2 * half:2 * half + 2], [rps[bk]])
            B.act(sq[:, :], psum[bk][:, :], AF.Square, [], [rps[bk], r_sq, r_st[sl]],
                  accum_out=st[:, sl, half:half + 1])
        B.tt("dve", st[:, sl, 2:3], st[:, sl, 0:1], st[:, sl, 1:2], ALU.add, [r_st[sl]], [r_st[sl]])
        rms_rstd(B, st[:, sl, 2:3], st[:, sl, 3:4], r_st[sl], r_st[sl])
        for half in range(2):
            bk = banks[half]
            hs = slice(half * 512, (half + 1) * 512)
            B.stt(yt[:, sl, hs], psum[bk][:, :], st[:, sl, 3:4], gpo[:, hs], ALU.mult, ALU.mult,
                  [r_st[sl], r_gpo], [rps[bk], r_yt[sl]])
        B.tt("pool", yt[:, sl, :], yt[:, sl, :], xt[:, sl, :], ALU.add, [r_xt[sl]], [r_yt[sl]])
        o = B.S.dma("pool", orows, yt[:, sl, :], d_o[sl], [r_yt[sl]],
                    [dst_res[tt]] if dst_res else [])
        fr = B.R("fin%d" % tt)
        fr.w = o
        finals.append(fr)
    return finals


def build_l1_decl(dt, dr):
    EI = "ExternalInput"
    dr["w1"] = dt("w1", [D, 2048], F32, kind=EI).ap()
    dr["wg1"] = dt("wg1", [D, D], F32, kind=EI).ap()
    dr["wo1"] = dt("wo1", [D, D], F32, kind=EI).ap()
    dr["gpre1"] = dt("gpre1", [128, 8], F32, kind=EI).ap()
    dr["gpost1"] = dt("gpost1", [128, D], F32, kind=EI).ap()
    dr["bglu"] = dt("bglu", [128, 8], F32, kind=EI).ap()
    dr["pl_a"] = dt("pl_a", [128, 3, 32], F32, kind=EI).ap()
    dr["pl_bc"] = dt("pl_bc", [128, 4, 32, 16], F32, kind=EI).ap()
    dr["dcol"] = dt("dcol", [128, 64], F32, kind=EI).ap()
    dr["mask8"] = dt("mask8", [128, 128], F32, kind=EI).ap()
    dr["esel"] = dt("esel", [128, 64, 128], BF16, kind=EI).ap()


def cmul(B, outr, outi, ar, ai, br, bi, t1, t2, R, neg_i=False, eng="dve", emit=None, Rw=None):
    if emit is None:
        emit = lambda f: f()
    Rw = R if Rw is None else Rw
    rd = [R] if Rw is R else [R, Rw]
    emit(lambda: B.tt(eng, t1, ar, br, ALU.mult, rd, [Rw]))
    emit(lambda: B.tt(eng, t2, ai, bi, ALU.mult, rd, [Rw]))
    emit(lambda: B.tt(eng, outr, t1, t2, ALU.subtract, rd, [Rw]))
    emit(lambda: B.tt(eng, t1, ar, bi, ALU.mult, rd, [Rw]))
    emit(lambda: B.tt(eng, t2, ai, br, ALU.mult, rd, [Rw]))
    if neg_i:
        emit(lambda: B.tt(eng, t1, t1, t2, ALU.add, rd, [Rw]))
        emit(lambda: B.ts(eng, outi, t1, -1.0, ALU.mult, rd, [Rw]))
    else:
        emit(lambda: B.tt(eng, outi, t1, t2, ALU.add, rd, [Rw]))


def l1_prep_stage1(ctx, E):
    nc, B, dr = ctx["nc"], ctx["B"], ctx["dr"]
    PEN = "dve"
    pa = E(nc.sbuf_tensor("pa", [128, 3, 32], F32))
    pbc = E(nc.sbuf_tensor("pbc", [128, 4, 32, 16], F32))
    W = E(nc.sbuf_tensor("Wk", [128, 16, 32], F32))
    PW = E(nc.sbuf_tensor("PW", [128, 9, 2, 32], F32))
    QW = E(nc.sbuf_tensor("QW", [128, 8, 2, 32], F32))
    bb = E(nc.sbuf_tensor("bbar", [128, 2, 32, 16], F32))
    T1 = E(nc.sbuf_tensor("T1", [128, 32, 16], F32))
    T2 = E(nc.sbuf_tensor("T2", [128, 32, 16], F32))
    RP = B.R("prep")
    d_p = ctx["new_dsem"]("d_prep")
    B.S.dma("sp", pa[:], dr["pl_a"], d_p, (), [RP])
    B.S.dma("sp", pbc[:], dr["pl_bc"], d_p, (), [RP])
    q = []
    emit = q.append
    RK = E(nc.sbuf_tensor("RK", [128, 16], F32))
    for k in range(1, 16):
        emit(lambda k=k: B.memset(PEN, RK[:, k:k + 1], float(k), [RP]))
    emit(lambda: B.memset(PEN, RK[:, 0:1], 1.0, [RP]))
    emit(lambda: B.recip(RK[:], RK[:], [RP], [RP]))
    are, aim, ldt = pa[:, 0, :], pa[:, 1, :], pa[:, 2, :]
    w = lambda k: W[:, k, :]
    t1s, t2s = T1[:, :, 0], T2[:, :, 0]
    tt = lambda o, a, b, op: emit(lambda: B.tt(PEN, o, a, b, op, [RP], [RP]))
    ts = lambda o, a, s1, op0, s2=None, op1=None: emit(lambda: B.ts(PEN, o, a, s1, op0, [RP], [RP], s2=s2, op1=op1))
    ms = lambda o, v: emit(lambda: B.memset(PEN, o, v, [RP]))
    cpy = lambda o, a: emit(lambda: B.cp(PEN, o, a, [RP], [RP]))
    cm = lambda *a, **k: cmul(B, *a, eng=PEN, emit=emit, **k)
    ts(w(1), ldt, 0.125, ALU.mult)
    ms(w(0), 1.0)
    for k in range(10, 0, -1):
        tt(w(0), w(0), w(1), ALU.mult)
        ts(w(0), w(0), RK[:, k:k + 1], ALU.mult, 1.0, ALU.add)
    for _ in range(3):
        tt(w(0), w(0), w(0), ALU.mult)
    ts(w(8), w(0), 1.0 / 16, ALU.mult)
    tt(w(1), are, w(8), ALU.mult)
    tt(w(2), aim, w(8), ALU.mult)
    ms(w(3), 1.0)
    ms(w(4), 0.0)
    for k in range(14, 1, -1):
        cm(w(5), w(6), w(3), w(4), w(1), w(2), t1s, t2s, RP)
        ts(w(3), w(5), RK[:, k:k + 1], ALU.mult, 1.0, ALU.add)
        ts(w(4), w(6), RK[:, k:k + 1], ALU.mult)
    cm(w(5), w(6), w(3), w(4), w(1), w(2), t1s, t2s, RP)
    for _ in range(4):
        ts(w(7), w(5), 2.0, ALU.add)
        cm(w(3), w(4), w(5), w(6), w(7), w(6), t1s, t2s, RP)
        cpy(w(5), w(3))
        cpy(w(6), w(4))
    ms(PW[:, 0, 0, :], 1.0)
    ms(PW[:, 0, 1, :], 0.0)
    ms(QW[:, 0, 0, :], 1.0)
    ms(QW[:, 0, 1, :], 0.0)
    abr, abi = PW[:, 1, 0, :], PW[:, 1, 1, :]
    ts(abr, w(5), 1.0, ALU.add)
    cpy(abi, w(6))
    for k in range(2, 9):
        cm(PW[:, k, 0, :], PW[:, k, 1, :], PW[:, k - 1, 0, :], PW[:, k - 1, 1, :], abr, abi, t1s, t2s, RP)
    tt(w(8), abr, abr, ALU.mult)
    tt(w(9), abi, abi, ALU.mult)
    tt(w(8), w(8), w(9), ALU.add)
    emit(lambda: B.recip(w(9), w(8), [RP], [RP]))
    tt(QW[:, 1, 0, :], abr, w(9), ALU.mult)
    tt(w(8), abi, w(9), ALU.mult)
    ts(QW[:, 1, 1, :], w(8), -1.0, ALU.mult)
    for k in range(2, 8):
        cm(QW[:, k, 0, :], QW[:, k, 1, :], QW[:, k - 1, 0, :], QW[:, k - 1, 1, :],
           QW[:, 1, 0, :], QW[:, 1, 1, :], t1s, t2s, RP)
    cpy(w(10), w(5))
    tt(w(11), are, are, ALU.mult)
    tt(w(12), aim, aim, ALU.mult)
    tt(w(11), w(11), w(12), ALU.add)
    emit(lambda: B.recip(w(12), w(11), [RP], [RP]))
    tt(w(13), w(10), are, ALU.mult)
    tt(w(14), abi, aim, ALU.mult)
    tt(w(13), w(13), w(14), ALU.add)
    tt(w(13), w(13), w(12), ALU.mult)
    tt(w(14), abi, are, ALU.mult)
    tt(w(15), w(10), aim, ALU.mult)
    tt(w(14), w(14), w(15), ALU.subtract)
    tt(w(14), w(14), w(12), ALU.mult)
    bc = lambda ap: ap.unsqueeze(2).to_broadcast([128, 32, 16])
    cm(bb[:, 0], bb[:, 1], bc(w(13)), bc(w(14)), pbc[:, 0], pbc[:, 1], T1[:], T2[:], RP)

    def pump(n=None):
        k = len(q) if n is None else min(n, len(q))
        for _ in range(k):
            q.pop(0)()
    ctx["pump"] = pump
    ctx["prep1"] = dict(pbc=pbc, PW=PW, QW=QW, bb=bb, T1=T1, T2=T2, RP=RP, bc=bc)


def build_l1(ctx):
    from contextlib import ExitStack
    nc, B, dr, psum, rps = ctx["nc"], ctx["B"], ctx["dr"], ctx["psum"], ctx["rps"]
    hT, GT, ident, small = ctx["hT"], ctx["GT"], ctx["ident"], ctx["small"]
    r_ident = ctx["r_ident"]
    PI = math.pi
    d_par = ctx["new_dsem"]("d_par1")
    r_gp = B.R("gpre1")
    r_bg = B.R("bglu")
    B.S.dma("sp", small[:, 8:16], dr["gpre1"], d_par, (), [r_gp])
    B.S.dma("sp", small[:, 16:24], dr["bglu"], d_par, (), [r_bg])
    r_hTall = B.R("hTall")
    r_Gb = B.R("Gb")
    Gb = GT[:, :, :].rearrange("p c (a n) -> p (c a) n", n=256)
    with ExitStack() as es1:
        E1 = es1.enter_context
        T0b = E1(nc.sbuf_tensor("T0b", [128, 64, 128], BF16))
        BmT = E1(nc.sbuf_tensor("BmT", [128, 64, 2, 64], BF16))
        Cmb = E1(nc.sbuf_tensor("Cmb", [128, 32, 2, 128], BF16))
        A8 = E1(nc.sbuf_tensor("A8", [128, 2, 2, 32], F32))
        r_T0b, r_BmT, r_Cmb, r_A8 = B.R("T0b"), B.R("BmT"), B.R("Cmb"), B.R("A8")
        with ExitStack() as es:
            E = es.enter_context
            ctx["pump"]()
            p1 = ctx["prep1"]
            pbc, PW, QW, bb, T1, T2, RP, bc = (p1[k] for k in ("pbc", "PW", "QW", "bb", "T1", "T2", "RP", "bc"))
            cre, cim = pbc[:, 2], pbc[:, 3]
            dcol = E(nc.sbuf_tensor("dcol_sb", [128, 64], F32))
            mask8 = E(nc.sbuf_tensor("mask8_sb", [128, 128], F32))
            Lh = E(nc.sbuf_tensor("Lh", [128, 2, 32, 128], BF16))
            Rh = E(nc.sbuf_tensor("Rh", [128, 2, 32, 128], BF16))
            identb1 = E(nc.sbuf_tensor("identb1", [128, 128], BF16))
            r_ib1 = B.R("identb1")
            B.cp("dve", identb1[:], ident[:], [r_ident], [r_ib1])
            r_dm = B.R("dcolmask")
            B.S.dma("sp", dcol[:], dr["dcol"], d_par, (), [r_dm])
            B.S.dma("sp", mask8[:], dr["mask8"], d_par, (), [r_dm])
            T3 = E(nc.sbuf_tensor("T3", [128, 32, 16], F32))
            T4 = E(nc.sbuf_tensor("T4", [128, 32, 16], F32))
            R_L, R_R = B.R("prepL"), B.R("prepR")
            qa, qb = [], []
            for j in range(8):
                js = slice(j * 16, (j + 1) * 16)
                cmul(B, Rh[:, 0, :, js], Rh[:, 1, :, js], bc(QW[:, 7 - j, 0, :]), bc(QW[:, 7 - j, 1, :]),
                     cre, cim, T3[:], T4[:], RP, neg_i=True, eng="dve", Rw=R_R, emit=qa.append)
            for j in range(8):
                js = slice(j * 16, (j + 1) * 16)
                cmul(B, Lh[:, 0, :, js], Lh[:, 1, :, js], bc(PW[:, 7 - j, 0, :]), bc(PW[:, 7 - j, 1, :]),
                     bb[:, 0], bb[:, 1], T1[:], T2[:], RP, Rw=R_L, emit=qb.append)
                cmul(B, Cmb[:, :, 0, js], Cmb[:, :, 1, js], bc(PW[:, j + 1, 0, :]), bc(PW[:, j + 1, 1, :]),
                     cre, cim, T1[:], T2[:], RP, neg_i=True, Rw=R_L, emit=qb.append)
            while qa or qb:
                for _ in range(2):
                    if qb:
                        qb.pop(0)()
                if qa:
                    qa.pop(0)()
            B.cp("dve", Cmb[:, 0, 0, 0:1], Cmb[:, 0, 0, 0:1], [R_L], [r_Cmb])
            B.cp("dve", A8[:, 0, 0, :], PW[:, 8, 0, :], [RP], [r_A8])
            B.cp("dve", A8[:, 0, 1, :], PW[:, 8, 0, :], [RP], [r_A8])
            B.ts("dve", A8[:, 1, 0, :], PW[:, 8, 1, :], -1.0, ALU.mult, [RP], [r_A8])
            B.cp("dve", A8[:, 1, 1, :], PW[:, 8, 1, :], [RP], [r_A8])
            DA = E(nc.sbuf_tensor("DA", [128, 64, 128], F32))
            TM4 = E(nc.sbuf_tensor("TM4", [128, 1, 4, 128], F32))
            r_DA = B.R("DA")
            r_TM4 = [B.R("TM4a"), B.R("TM4b")]
            r_T0g = [B.R("T0g%d" % i) for i in range(2)]
            B.tt("dve", DA[:], ident[:].unsqueeze(1).to_broadcast([128, 64, 128]),
                 dcol[:].unsqueeze(2).to_broadcast([128, 64, 128]), ALU.mult, [r_dm, r_ident], [r_DA])
            for g4 in range(16):
                bk = g4 % 2
                for gg in range(4):
                    g = g4 * 4 + gg
                    gh, pr = g // 32, g % 32
                    Rr = slice(gh * 64, (gh + 1) * 64)
                    outp = psum[bk][:, gg * 128:(gg + 1) * 128]
                    B.mm(outp, Lh[Rr, 0, pr, :], Rh[Rr, 0, pr, :], True, False, [R_L, R_R], [rps[bk]])
                    B.mm(outp, Lh[Rr, 1, pr, :], Rh[Rr, 1, pr, :], False, True, [R_L, R_R], [rps[bk]])
                tm = 0
                B.tt("dve", TM4[:, tm], psum[bk][:, :].rearrange("p (g c) -> p g c", g=4),
                     mask8[:].unsqueeze(1).to_broadcast([128, 4, 128]), ALU.mult,
                     [r_dm], [rps[bk], r_TM4[tm]])
                B.tt("dve", T0b[:, g4 * 4:g4 * 4 + 4, :], TM4[:, tm], DA[:, g4 * 4:g4 * 4 + 4, :], ALU.add,
                     [r_TM4[tm], r_DA], [r_T0g[tm]])
            for g4 in range(16):
                bk = 2 + g4 % 2
                for gg in range(4):
                    g = g4 * 4 + gg
                    gh, pr = g // 32, g % 32
                    Rr = slice(gh * 64, (gh + 1) * 64)
                    for part in range(2):
                        cs = (gg * 2 + part) * 64
                        B.tr(psum[bk][:, :].bitcast(BF16)[:, cs:cs + 64], Lh[Rr, part, pr, :], identb1[Rr, Rr],
                             [R_L, r_ib1], [rps[bk]])
                B.cp("dve", BmT[:, g4 * 4:g4 * 4 + 4].rearrange("p g a c -> p (g a c)"),
                     psum[bk][:, :].bitcast(BF16)[:, 0:512], [], [rps[bk], r_BmT])
            ctx["dump"]("T0b", T0b[:], [r_T0b])
            ctx["dump"]("BmT", BmT[:], [r_BmT])
            ctx["dump"]("Cmb", Cmb[:], [r_Cmb])
            ctx["dump"]("A8", A8[:], [r_A8])
            ctx["dump"]("PW", PW[:], [RP])
            ctx["dump"]("QW", QW[:], [RP])
            ctx["dump"]("Lh", Lh[:], [R_L])
            ctx["dump"]("Rh", Rh[:], [R_R])
            B.barrier()
        if ctx.get("stop") == "prep":
            return []
        V = E1(nc.sbuf_tensor("Vssm", [128, 64, 256], BF16))
        r_V = B.R("V")
        with ExitStack() as es:
            ctx2 = dict(ctx)
            ctx2["r_hT"] = [r_hTall]
            make_hT(ctx2, dr["x1"], es.enter_context, small[:, 8:16], r_gp, perm8=True,
                    src_res=ctx.get("r_x1"), name="b")
            B.barrier()
        ctx["dump"]("hT1", hT[:], [r_hTall])
        if ctx.get("stop") == "A":
            return []
        with ExitStack() as es:
            E = es.enter_context
            ws = WStream(ctx, E, 2, "w1s", cols=128)
            uTb = E(nc.sbuf_tensor("uTb", [128, 2, S], BF16))
            Es = E(nc.sbuf_tensor("Esel", [128, 64, 128], BF16))
            r_uT = [B.R("uT0"), B.R("uT1")]
            r_Es = B.R("Esel")
            d_es = ctx["new_dsem"]("d_es")
            B.S.dma("sp", Es[:], dr["esel"], d_es, (), [r_Es])
            pbk = 0
            for fc in range(8):
                wbf, r_w = ws.load(dr["w1"][:, fc * 128:fc * 128 + 128])
                colo = 0
                us = fc % 2
                for s in range(4):
                    bk = pbk
                    pbk ^= 1
                    for c in range(8):
                        B.mm(psum[bk][:, :], wbf[:, c, colo:colo + 128], hT[:, c, s * 512:(s + 1) * 512],
                             c == 0, c == 7, [r_w, r_hTall], [rps[bk]])
                    B.act(uTb[:, us, s * 512:(s + 1) * 512], psum[bk][:, :], AF.Copy, [],
                          [rps[bk], r_uT[us]])
                for g8 in range(8):
                    g = fc * 8 + g8
                    bk = 2 + g % 4
                    for j in range(8):
                        B.mm(psum[bk][:, 0:256], Es[:, g8 * 8 + j, :], uTb[:, us, j * 256:(j + 1) * 256],
                             j == 0, j == 7, [r_Es, r_uT[us]], [rps[bk]])
                    B.cp("dve", V[:, g, :], psum[bk][:, 0:256], [], [rps[bk], r_V])
            ctx["dump"]("V", V[:], [r_V])
            B.barrier()
        if ctx.get("stop") == "B":
            return []
        with ExitStack() as es:
            E = es.enter_context
            Zb = E(nc.sbuf_tensor("Zb", [128, 2, 65, 2, 32], F32))
            Sb = ctx["prep1"]["pbc"][:].rearrange("p a b c -> p (a b c)").bitcast(BF16).rearrange(
                "p (n a r) -> p n a r", a=2, r=32)
            t1 = E(nc.sbuf_tensor("zt1", [128, 2, 32], F32))
            t2 = E(nc.sbuf_tensor("zt2", [128, 2, 32], F32))
            r_Z = [B.R("Zb0"), B.R("Zb1")]
            r_Sb = B.R("Sb")
            r_t = B.R("zt")
            B.memset("dve", Zb[:, 0, 0], 0.0, [r_Z[0]])

            def local_states(tb):
                zb = tb % 2
                ns = slice(tb * 64, (tb + 1) * 64)
                for q in range(8):
                    bk = q % 2
                    for pp in range(4):
                        pr = q * 4 + pp
                        for gh in range(2):
                            g = gh * 32 + pr
                            for part in range(2):
                                cs = (pp * 2 + part) * 64
                                B.mm(psum[bk][gh * 64:(gh + 1) * 64, cs:cs + 64], BmT[:, g, part, :],
                                     V[:, g, ns], True, True, [r_BmT, r_V], [rps[bk]])
                    dst = Zb[:, zb, 1:65, :, q * 4:q * 4 + 4].rearrange("p n a r -> p r a n")
                    srcp = psum[bk][:, :].rearrange("p (r a n) -> p r a n", r=4, a=2)
                    B.act(dst, srcp, AF.Copy, [], [rps[bk], r_Z[zb]])

            def recurrence(tb):
                zb = tb % 2
                for s in range(64):
                    zs = Zb[:, zb, s]
                    zn = Zb[:, zb, s + 1]
                    B.tt("dve", t1[:], A8[:, 0], zs, ALU.mult, [r_A8, r_Z[zb]], [r_t])
                    B.tt("dve", t2[:], A8[:, 1], Zb[:, zb, s, ::-1, :], ALU.mult, [r_A8, r_Z[zb]], [r_t])
                    B.tt("dve", zn, zn, t1[:], ALU.add, [r_t], [r_Z[zb]])
                    B.tt("dve", zn, zn, t2[:], ALU.add, [r_t], [r_Z[zb]])
                if tb + 1 < 4:
                    B.cp("dve", Zb[:, 1 - zb, 0], Zb[:, zb, 64], [r_Z[zb]], [r_Z[1 - zb]])
                B.act(Sb, Zb[:, zb, 0:64], AF.Copy, [r_Z[zb]], [r_Sb])

            def outputs(tb):
                ns = slice(tb * 64, (tb + 1) * 64)
                for fc in range(8):
                    bk = 2 + fc % 4
                    for g8 in range(8):
                        g = fc * 8 + g8
                        gh, pr = g // 32, g % 32
                        Rr = slice(gh * 64, (gh + 1) * 64)
                        outp = psum[bk][:, g8 * 64:(g8 + 1) * 64]
                        B.mm(outp, T0b[:, g, :], V[:, g, ns], True, False, [r_T0b, r_V], [rps[bk]])
                        B.mm(outp, Cmb[Rr, pr, 0, :], Sb[Rr, :, 0, pr], False, False, [r_Cmb, r_Sb], [rps[bk]])
                        B.mm(outp, Cmb[Rr, pr, 1, :], Sb[Rr, :, 1, pr], False, True, [r_Cmb, r_Sb], [rps[bk]])
                    yp = psum[bk][:, :]
                    dst = Gb[:, fc * 8:(fc + 1) * 8, ns]
                    B.act(dst, yp.rearrange("p (g n) -> p g n", g=8), AF.Gelu_apprx_tanh, [],
                          [rps[bk], r_Gb])

            local_states(0)
            for tb in range(4):
                if tb + 1 < 4:
                    local_states(tb + 1)
                recurrence(tb)
                outputs(tb)
            ctx["dump"]("Gb", GT[:], [r_Gb])
            B.barrier()
        B.barrier()
    if ctx.get("stop") == "C":
        return []
    finals = []
    with ExitStack() as es2:
        E2 = es2.enter_context
        gTb = E2(nc.sbuf_tensor("gTb", [128, 8, S], BF16))
        r_gT = B.R("gTb")
        r_G2 = B.R("G2T")
        with ExitStack() as es:
            E = es.enter_context
            Es = E(nc.sbuf_tensor("Esel2", [128, 64, 128], BF16))
            r_Es = B.R("Esel2")
            d_es = ctx["new_dsem"]("d_es2")
            B.S.dma("sp", Es[:], dr["esel"], d_es, (), [r_Es])
            for fc in range(8):
                for ip in range(4):
                    bk = (fc * 4 + ip) % 4
                    for ii in range(2):
                        i0 = ip * 2 + ii
                        for g8 in range(8):
                            B.mm(psum[bk][:, ii * 256:(ii + 1) * 256], Es[:, i0 * 8 + g8, :],
                                 Gb[:, fc * 8 + g8, :], g8 == 0, g8 == 7, [r_Es, r_Gb], [rps[bk]])
                    B.cp("dve" if ip % 2 == 0 else "act_copy", gTb[:, fc, ip * 512:(ip + 1) * 512],
                         psum[bk][:, :], [], [rps[bk], r_gT]) if False else \
                        B.cp("dve", gTb[:, fc, ip * 512:(ip + 1) * 512], psum[bk][:, :], [], [rps[bk], r_gT])
            ctx["dump"]("gTb", gTb[:], [r_gT])
            B.barrier()
        if ctx.get("stop") == "D1":
            return []
        with ExitStack() as es:
            E = es.enter_context
            wsg = WStream(ctx, E, 2, "wg1s")
            wsz = WStream(ctx, E, 2, "wz1s")
            sg = E(nc.sbuf_tensor("sg", [128, 2, 512], F32))
            sz = E(nc.sbuf_tensor("sz", [128, 2, 512], F32))
            r_sg = [B.R("sg0"), B.R("sg1")]
            r_sz = [B.R("sz0"), B.R("sz1")]
            k = 0
            for fo in range(8):
                if fo % 2 == 0:
                    wg, r_wg = wsg.load(dr["wg1"][:, fo * 128:fo * 128 + 256])
                    wz, r_wz = wsz.load(dr["w1"][:, 1024 + fo * 128:1024 + fo * 128 + 256])
                colo = (fo % 2) * 128
                for s in range(4):
                    sl = k % 2
                    k += 1
                    ba, bb_ = (0, 1) if sl == 0 else (2, 3)
                    cs = slice(s * 512, (s + 1) * 512)
                    for c in range(8):
                        B.mm(psum[ba][:, :], wg[:, c, colo:colo + 128], gTb[:, c, cs], c == 0, c == 7,
                             [r_wg, r_gT], [rps[ba]])
                    for c in range(8):
                        B.mm(psum[bb_][:, :], wz[:, c, colo:colo + 128], hT[:, c, cs], c == 0, c == 7,
                             [r_wz, r_hTall], [rps[bb_]])
                    B.act(sg[:, sl, :], psum[ba][:, :], AF.Sigmoid, [r_bg], [rps[ba], r_sg[sl]],
                          bias=small[:, 16 + fo:17 + fo])
                    B.act(sz[:, sl, :], psum[bb_][:, :], AF.Sigmoid, [], [rps[bb_], r_sz[sl]])
                    B.tt("pool", sg[:, sl, :], sg[:, sl, :], sz[:, sl, :], ALU.mult, [r_sz[sl]], [r_sg[sl]])
                    B.tt("dve", sg[:, sl, :], sg[:, sl, :], psum[bb_][:, :], ALU.mult, [], [rps[bb_], r_sg[sl]])
                    dstn = GT[:, fo, :].rearrange("p (n j) -> p j n", j=8)[:, 2 * s:2 * s + 2, :]
                    B.tt("dve", dstn, sg[:, sl, :].rearrange("p (j n) -> p j n", j=2),
                         gTb[:, fo, cs].rearrange("p (j n) -> p j n", j=2), ALU.mult,
                         [r_sg[sl], r_gT], [r_G2])
            ctx["dump"]("G2T", GT[:], [r_G2])
            B.barrier()
        if ctx.get("stop") == "D2":
            return []
        with ExitStack() as es:
            E = es.enter_context
            ws = WStream(ctx, E, 2, "wo1s")
            WO = E(nc.sbuf_tensor("WO1", [128, 8, D], BF16))
            gpo = E(nc.sbuf_tensor("gpo1", [128, D], F32))
            r_WO = [B.R("WO1_%d" % i) for i in range(4)]
            r_gpo = B.R("gpo1")
            B.S.dma("sp", gpo[:], dr["gpost1"], d_par, (), [r_gpo])
            for wb in range(4):
                ws.load(dr["wo1"][:, wb * 256:(wb + 1) * 256], WO[:, :, wb * 256:(wb + 1) * 256], r_WO[wb])

            tokap = None
            finals = out_proj_tail(ctx, E, WO, r_WO, gpo, r_gpo, dr["x1"], dr["out"], GT, r_G2, tokap, "t1",
                                   src_res=ctx.get("r_x1"))
            B.barrier()
    return finals


_PROG = {}


def _prog(mode):
    if mode not in _PROG:
        _PROG[mode] = build_program(mode)
    return _PROG[mode]


def _l0_inputs(inp, b):
    f = np.float32
    return {
        "x": np.ascontiguousarray(inp["x"][b], dtype=f),
        "w0": inp["_w0"], "wo0": inp["_wo0"], "gpre0": inp["_gpre0"], "gpost0": inp["_gpost0"],
        "bias0": inp["_bias0"], "ident": inp["_ident"],
    }


def _prep_l0(inputs):
    f = np.float32
    p = {}
    p["_w0"] = _perm_w_in(np.asarray(inputs["attn_w_in"][0], f))
    p["_wo0"] = np.ascontiguousarray(np.asarray(inputs["attn_w_out"][0], f))
    p["_gpre0"] = np.ascontiguousarray(np.asarray(inputs["attn_pre_norm"][0], f).reshape(8, 128).T)
    p["_gpost0"] = np.ascontiguousarray(np.broadcast_to(np.asarray(inputs["attn_post_norm"][0], f), (128, D)))
    p["_bias0"] = _bias_tiles(np.asarray(inputs["rel_bias"], f))
    p["_ident"] = np.eye(128, dtype=f)
    return p


def run_l0(inputs, debug=False):
    inp = dict(inputs)
    inp.update(_prep_l0(inputs))
    inp["x"] = np.asarray(inputs["x"], np.float32)
    nc = build_program("l0", debug=debug)
    in_maps = [_l0_inputs(inp, b) for b in range(8)]
    res = run_bass_kernel_spmd(nc, in_maps, core_ids=list(range(8)))
    if debug:
        return np.stack([r["x1"] for r in res.results], axis=0), res.results[0]
    return np.stack([r["x1"] for r in res.results], axis=0)


def _prep_l1(inputs):
    f = np.float32
    p = {}
    p["w1"] = np.ascontiguousarray(np.asarray(inputs["ssm_w_in"][0], f))
    p["wg1"] = np.ascontiguousarray(np.asarray(inputs["ssm_w_glu"][0], f))
    p["wo1"] = np.ascontiguousarray(np.asarray(inputs["ssm_w_out"][0], f))
    p["gpre1"] = np.ascontiguousarray(np.asarray(inputs["ssm_pre_norm"][0], f).reshape(8, 128).T)
    p["gpost1"] = np.ascontiguousarray(np.broadcast_to(np.asarray(inputs["ssm_post_norm"][0], f), (128, D)))
    p["bglu"] = np.ascontiguousarray(np.asarray(inputs["ssm_b_glu"][0], f).reshape(8, 128).T)

    def pl(a):
        a = np.asarray(a, f)
        a = a.reshape((2, 32, 64) + a.shape[2:])
        a = np.moveaxis(a, 2, 1)
        return np.ascontiguousarray(a.reshape((128, 32) + a.shape[3:]))
    are = pl(inputs["ssm_a_re"][0])
    aim = pl(inputs["ssm_a_im"][0])
    ldt = pl(np.broadcast_to(np.asarray(inputs["ssm_log_dt"][0], f)[:, None], (64, 64)))
    p["pl_a"] = np.ascontiguousarray(np.stack([are, aim, ldt], axis=1))
    bre = pl(inputs["ssm_b_re"][0])
    bim = pl(inputs["ssm_b_im"][0])
    cre = pl(np.swapaxes(np.asarray(inputs["ssm_c_re"][0], f), 1, 2))
    cim = pl(np.swapaxes(np.asarray(inputs["ssm_c_im"][0], f), 1, 2))
    p["pl_bc"] = np.ascontiguousarray(np.stack([bre, bim, cre, cim], axis=1))
    dvec = np.asarray(inputs["ssm_d"][0], f).reshape(64, 16)
    p["dcol"] = np.ascontiguousarray(np.tile(dvec.T, (8, 1)))
    jj = np.arange(128) // 16
    p["mask8"] = (jj[:, None] <= jj[None, :]).astype(f)
    k = np.arange(128)
    es = np.zeros((128, 64, 128), f)
    for a in range(8):
        for b in range(8):
            es[:, a * 8 + b, :] = ((k[:, None] // 16 == a) & (k[None, :] // 16 == b)
                                   & (k[:, None] % 16 == k[None, :] % 16))
    p["esel"] = es.astype(ml_dtypes.bfloat16)
    p["ident"] = np.eye(128, dtype=f)
    return p


def run_l1(inputs, x1, debug=False, stop=None):
    p = _prep_l1(inputs)
    nc = build_program("l1", debug=debug, stop=stop)
    in_maps = []
    for b in range(8):
        m = dict(p)
        m["x1"] = np.ascontiguousarray(x1[b], dtype=np.float32)
        in_maps.append(m)
    res = run_bass_kernel_spmd(nc, in_maps, core_ids=list(range(8)))
    out = np.stack([r["out"] for r in res.results], axis=0)
    if debug:
        return out, res.results[0]
    return out


FUSED = True


def kernel(**inputs):
    p0 = _prep_l0(inputs)
    p1 = _prep_l1(inputs)
    x = np.asarray(inputs["x"], np.float32)
    shared0 = {"w0": p0["_w0"], "wo0": p0["_wo0"], "gpre0": p0["_gpre0"], "gpost0": p0["_gpost0"],
               "bias0": p0["_bias0"], "ident": p0["_ident"]}
    if FUSED:
        nc = _prog("fused")
        shared = dict(shared0)
        shared.update(p1)
        in_maps = []
        for b in range(8):
            m = dict(shared)
            m["x"] = np.ascontiguousarray(x[b])
            in_maps.append(m)
        res = run_bass_kernel_spmd(nc, in_maps, core_ids=list(range(8)))
        return np.stack([np.asarray(r["out"], np.float32) for r in res.results], axis=0)
    in_maps = []
    for b in range(8):
        m = dict(shared0)
        m["x"] = np.ascontiguousarray(x[b])
        in_maps.append(m)
    res = run_bass_kernel_spmd(_prog("l0"), in_maps, core_ids=list(range(8)))
    x1 = [np.asarray(r["x1"], np.float32) for r in res.results]
    in_maps = []
    for b in range(8):
        m = dict(p1)
        m["x1"] = np.ascontiguousarray(x1[b])
        in_maps.append(m)
    res = run_bass_kernel_spmd(_prog("l1"), in_maps, core_ids=list(range(8)))
    return np.stack([np.asarray(r["out"], np.float32) for r in res.results], axis=0)
```

```python
import math
import numpy as np
import ml_dtypes
import concourse.bass as bass
import concourse.mybir as mybir
from concourse.bass_utils import run_bass_kernel_spmd

F32 = mybir.dt.float32
BF16 = mybir.dt.bfloat16
AF = mybir.ActivationFunctionType
ALU = mybir.AluOpType

D = 1024
S = 2048
NT = 16
EPS = 1e-6
GROUPS = ((128, 1), (512, 4), (2048, 16))
NEGM = -30000.0
SSM_G = 64
SSM_P = 64
SSM_C = 16


class Res:
    __slots__ = ("name", "w", "rs", "rd")

    def __init__(self, name):
        self.name = name
        self.w = None
        self.rs = {}
        self.rd = []


class DmaSem:
    def __init__(self, sem):
        self.sem = sem
        self.count = 0


class Op:
    __slots__ = ("eng", "fn", "deps", "need", "seq", "dsem", "dval", "idx", "dreq", "group", "gidx")


class Sched:
    ENGS = ("pe", "act", "dve", "pool", "sp")

    def __init__(self, nc):
        self.nc = nc
        self.ops = {e: [] for e in self.ENGS}
        self.allops = []
        self.cur_group = None
        self.ngroups = 0

    def begin_group(self):
        self.ngroups += 1
        self.cur_group = self.ngroups

    def end_group(self):
        self.cur_group = None

    def _mk(self, eng, fn, reads, writes, isdma=False):
        o = Op()
        o.eng = eng
        o.fn = fn
        o.need = False
        o.seq = 0
        o.dsem = None
        o.dval = 0
        deps = []
        for r in reads:
            if r.w is not None:
                deps.append(r.w)
        for r in writes:
            if r.w is not None:
                deps.append(r.w)
            deps.extend(r.rs.values())
            deps.extend(r.rd)
        o.deps = deps
        o.dreq = {}
        for p in deps:
            if p.dsem is not None:
                o.dreq[id(p.dsem)] = (p.dsem.sem, p.dsem.count)
        for r in reads:
            if isdma:
                r.rd.append(o)
            else:
                r.rs[eng] = o
        for r in writes:
            r.w = o
            r.rs = {}
            r.rd = []
        o.idx = len(self.ops[eng])
        o.group = self.cur_group
        o.gidx = len(self.allops)
        self.ops[eng].append(o)
        self.allops.append(o)
        return o

    def op(self, eng, fn, reads=(), writes=()):
        return self._mk(eng, fn, list(reads), list(writes))

    def dma(self, eng, out, in_, dsem, reads=(), writes=()):
        def fn(e):
            return e.dma_start(out=out, in_=in_)
        o = self._mk(eng, fn, list(reads), list(writes), isdma=True)
        dsem.count += 16
        o.dsem = dsem
        o.dval = dsem.count
        return o


def _t5_bucket_np(dist):
    max_exact = 16
    n = np.maximum(dist, 1).astype(np.float32)
    large = max_exact + (np.log(n / np.float32(max_exact)) / np.float32(math.log(2048 / max_exact))
                         * np.float32(32 - max_exact)).astype(np.int32)
    large = np.minimum(large, 31)
    return np.where(dist < max_exact, dist, large)


def _bias_tiles(rel_bias):
    j = np.arange(128)[:, None]
    q = np.arange(256)[None, :]
    delta = q - j
    valid = (delta >= 0) & (delta <= 128)
    out = np.empty((8, 128, 6, 256), np.float32)
    for g, (_, dil) in enumerate(GROUPS):
        bucket = _t5_bucket_np(np.maximum(delta, 0) * dil)
        for h in range(16):
            t = rel_bias[bucket, h]
            t = np.where(valid, t, np.float32(NEGM)).astype(np.float32)
            out[h // 2, :, g * 2 + (h % 2), :] = t
    return out


def _perm_w_in(w):
    out = np.empty((8, D, 1280), np.float32)
    for hp in range(8):
        cs = slice(hp * 128, (hp + 1) * 128)
        blk = []
        for g in range(3):
            blk.append(w[:, (0 * 3 + g) * 1024:(0 * 3 + g + 1) * 1024][:, cs])
            blk.append(w[:, (1 * 3 + g) * 1024:(1 * 3 + g + 1) * 1024][:, cs])
        for g in range(3):
            blk.append(w[:, (2 * 3 + g) * 1024:(2 * 3 + g + 1) * 1024][:, cs])
        blk.append(w[:, 9216:10240][:, cs])
        out[hp] = np.concatenate(blk, axis=1)
    return out


class Builder:
    def __init__(self, nc):
        self.nc = nc
        self.S = Sched(nc)
        self.dsems = []
        self.allres = []
        self.last_barrier = None
        self.scratch = None

    def R(self, name):
        r = Res(name)
        r.w = self.last_barrier
        self.allres.append(r)
        return r

    def barrier(self):
        scr = self.scratch

        def fn(e):
            return e.memset(scr, 0.0)
        o = self.S.op("pool", fn, (), list(self.allres))
        self.last_barrier = o
        return o

    def mm(self, out, lhsT, rhs, start, stop, reads, writes):
        def fn(e):
            return e.matmul(out, lhsT, rhs, start=start, stop=stop)
        return self.S.op("pe", fn, reads, writes)

    def tr(self, out, in_, ident, reads, writes):
        def fn(e):
            return e.transpose(out, in_, ident)
        return self.S.op("pe", fn, reads, writes)

    def act(self, out, in_, func, reads, writes, scale=None, bias=None, accum_out=None):
        kw = {}
        if scale is not None:
            kw["scale"] = scale
        if bias is not None:
            kw["bias"] = bias
        if accum_out is not None:
            kw["accum_out"] = accum_out

        def fn(e):
            return e.activation(out, in_, func, **kw)
        return self.S.op("act", fn, reads, writes)

    def tt(self, eng, out, in0, in1, op, reads, writes):
        def fn(e):
            return e.tensor_tensor(out, in0, in1, op)
        return self.S.op(eng, fn, reads, writes)

    def ts(self, eng, out, in0, s1, op0, reads, writes, s2=None, op1=None):
        def fn(e):
            if op1 is None:
                return e.tensor_scalar(out, in0, s1, None, op0)
            return e.tensor_scalar(out, in0, s1, s2, op0, op1)
        return self.S.op(eng, fn, reads, writes)

    def stt(self, out, in0, scalar, in1, op0, op1, reads, writes):
        def fn(e):
            return e.scalar_tensor_tensor(out, in0, scalar, in1, op0, op1)
        return self.S.op("dve", fn, reads, writes)

    def cp(self, eng, out, in_, reads, writes):
        def fn(e):
            return e.tensor_copy(out, in_)
        return self.S.op(eng, fn, reads, writes)

    def memset(self, eng, ap, val, writes):
        def fn(e):
            return e.memset(ap, val)
        return self.S.op(eng, fn, (), writes)

    def recip(self, out, in_, reads, writes):
        def fn(e):
            return e.reciprocal(out, in_)
        return self.S.op("dve", fn, reads, writes)


def rms_rstd(B, ss, rs, r_ss, r_rs):
    B.ts("dve", rs, ss, 1.0 / D, ALU.mult, [r_ss], [r_rs], s2=EPS, op1=ALU.add)
    B.act(rs, rs, AF.Sqrt, [r_rs], [r_rs])
    B.recip(rs, rs, [r_rs], [r_rs])


def build_program(mode="fused", debug=False, stop=None):
    nc = bass.Bass("TRN2", target_bir_lowering=False)
    B = Builder(nc)
    dt = nc.dram_tensor
    dr = {}
    do_l0 = mode in ("l0", "fused")
    do_l1 = mode in ("l1", "fused")
    if do_l0:
        dr["x"] = dt("x", [S, D], F32, kind="ExternalInput").ap()
        dr["w0"] = dt("w0", [8, D, 1280], F32, kind="ExternalInput").ap()
        dr["wo0"] = dt("wo0", [D, D], F32, kind="ExternalInput").ap()
        dr["gpre0"] = dt("gpre0", [128, 8], F32, kind="ExternalInput").ap()
        dr["gpost0"] = dt("gpost0", [128, D], F32, kind="ExternalInput").ap()
        dr["bias0"] = dt("bias0", [8, 128, 6, 256], F32, kind="ExternalInput").ap()
    dr["ident"] = dt("ident", [128, 128], F32, kind="ExternalInput").ap()
    if mode == "l0":
        dr["x1"] = dt("x1", [S, D], F32, kind="ExternalOutput").ap()
    elif mode == "l1":
        dr["x1"] = dt("x1", [S, D], F32, kind="ExternalInput").ap()
    else:
        dr["x1"] = dt("x1", [S, D], F32, kind="Internal").ap()
    if do_l1:
        build_l1_decl(dt, dr)
        dr["out"] = dt("out", [S, D], F32, kind="ExternalOutput").ap()

    from contextlib import ExitStack
    with ExitStack() as es:
        E = es.enter_context
        sems = {e: E(nc.semaphore("c_" + e)) for e in ("pe", "act", "dve", "pool", "sp")}

        def new_dsem(name):
            return DmaSem(E(nc.semaphore(name)))

        psum = [E(nc.psum_tensor("ps%d" % i, [128, 512], F32)) for i in range(8)]
        rps = [B.R("psum%d" % i) for i in range(8)]
        hT = E(nc.sbuf_tensor("hT", [128, 8, S], BF16))
        GT = E(nc.sbuf_tensor("GT", [128, 8, S], BF16))
        ident = E(nc.sbuf_tensor("ident_sb", [128, 128], F32))
        ones = None
        small = E(nc.sbuf_tensor("small", [128, 64], F32))
        bscr = E(nc.sbuf_tensor("bscr", [128, 8], F32))
        B.scratch = bscr[:, 0:1]
        r_hT = [B.R("hT%d" % i) for i in range(NT)]
        r_GT = B.R("GT")
        r_ident = B.R("ident")
        r_ones = B.R("ones")
        d_const = new_dsem("d_const")
        B.S.dma("sp", ident[:], dr["ident"], d_const, (), [r_ident])
        dumps = {}
        d_dump = new_dsem("d_dump")

        def dump(name, ap, reads):
            if not debug or name in dumps:
                return
            t = dt("dbg_" + name, list(ap.shape), ap.dtype, kind="ExternalOutput").ap()
            o = B.S.dma("pool", t, ap, d_dump, reads, [])
            fr = B.R("dfin_" + name)
            fr.w = o
            dumps[name] = fr
        ctx = dict(nc=nc, B=B, dr=dr, E=E, new_dsem=new_dsem, psum=psum, rps=rps, hT=hT, GT=GT, dump=dump,
                   stop=stop, ident=ident, ones=ones, small=small, r_hT=r_hT, r_GT=r_GT,
                   r_ident=r_ident, r_ones=r_ones, sems=sems)
        if mode == "fused":
            ctx["r_x1"] = [B.R("x1_%d" % i) for i in range(NT)]
        if do_l1:
            l1_prep_stage1(ctx, E)
        else:
            ctx["pump"] = lambda n=None: None
        with nc.Block() as block:
            final = []
            if do_l0:
                final = build_l0(ctx)
            if do_l1:
                final = build_l1(ctx)
            B.S.op("sp", None, list(final) + list(dumps.values()), ())
            emit_all(B, block, sems)
    return nc


def emit_all(B, block, sems):
    S_ = B.S
    nc = B.nc
    for o in S_.allops:
        for p in o.deps:
            if p.dsem is None and not (p.eng == "pe" and o.eng == "pe"):
                p.need = True
    for e in S_.ENGS:
        c = 0
        for o in S_.ops[e]:
            if o.need:
                c += 1
                o.seq = c

    def reqs_of(o, e, before=None):
        req = {}
        for p in o.deps:
            if before is not None and p.gidx >= before:
                continue
            if p.dsem is not None:
                key = ("d", id(p.dsem))
                v = o.dreq[id(p.dsem)]
                if req.get(key, (None, 0))[1] < v[1]:
                    req[key] = v
            else:
                if p.eng == "pe" and e == "pe":
                    continue
                key = ("e", p.eng)
                if req.get(key, (None, 0))[1] < p.seq:
                    req[key] = (sems[p.eng], p.seq)
        return req

    def run(e, eo):
        waited = {}
        ops = S_.ops[e]
        seen_groups = set()
        for k, o in enumerate(ops):
            req = reqs_of(o, e)
            if o.group is not None and o.group not in seen_groups:
                seen_groups.add(o.group)
                j = k + 1
                while j < len(ops) and ops[j].group == o.group:
                    for key, v in reqs_of(ops[j], e, before=o.gidx).items():
                        if req.get(key, (None, 0))[1] < v[1]:
                            req[key] = v
                    j += 1
            for key, (sem, val) in req.items():
                if waited.get(key, 0) >= val:
                    continue
                waited[key] = val
                eo.wait_ge(sem, val)
            if o.fn is None:
                continue
            ins = o.fn(eo)
            if o.dsem is not None:
                ins.then_inc(o.dsem.sem, 16)
            elif o.need:
                ins.then_inc(sems[e], 1)

    @block.tensor
    def _(eo):
        run("pe", eo)

    @block.scalar
    def _(eo):
        run("act", eo)

    @block.vector
    def _(eo):
        run("dve", eo)

    @block.gpsimd
    def _(eo):
        run("pool", eo)

    @block.sync
    def _(eo):
        run("sp", eo)


def make_hT(ctx, src, scope, gp, r_gp, perm8=False, src_res=None, name="a"):
    nc, B, psum, rps, hT, ident = ctx["nc"], ctx["B"], ctx["psum"], ctx["rps"], ctx["hT"], ctx["ident"]
    E = scope
    NS = 3
    xt = E(nc.sbuf_tensor("xt_" + name, [128, NS, D], F32))
    xs = E(nc.sbuf_tensor("xs_" + name, [128, NS, D], F32))
    jk = E(nc.sbuf_tensor("jk_" + name, [128, D], F32))
    st = E(nc.sbuf_tensor("st_" + name, [128, NS, 4], F32))
    r_xt = [B.R("xt%d" % i) for i in range(NS)]
    r_xs = [B.R("xs%d" % i) for i in range(NS)]
    r_st = [B.R("st%d" % i) for i in range(NS)]
    r_jk = B.R("jk")
    d_x = [ctx["new_dsem"]("d_x%s%d" % (name, i)) for i in range(NS)]

    def stage1(tt):
        sl = tt % NS
        B.S.dma("sp", xt[:, sl, :], src[tt * 128:(tt + 1) * 128, :], d_x[sl],
                [src_res[tt]] if src_res else (), [r_xt[sl]])
        B.act(jk[:, :], xt[:, sl, :], AF.Square, [r_xt[sl]], [r_jk, r_st[sl]], accum_out=st[:, sl, 0:1])
        B.ts("dve", st[:, sl, 1:2], st[:, sl, 0:1], 1.0 / D, ALU.mult, [r_st[sl]], [r_st[sl]],
             s2=EPS, op1=ALU.add)

    def stage2(tt):
        sl = tt % NS
        B.act(st[:, sl, 1:2], st[:, sl, 1:2], AF.Sqrt, [r_st[sl]], [r_st[sl]])
        B.recip(st[:, sl, 1:2], st[:, sl, 1:2], [r_st[sl]], [r_st[sl]])
        B.ts("dve", xs[:, sl, :], xt[:, sl, :], st[:, sl, 1:2], ALU.mult,
             [r_xt[sl], r_st[sl]], [r_xs[sl]])

    def stage3(tt):
        sl = tt % NS
        for half in range(2):
            bk = half + 2 * (tt % 2)
            for j in range(4):
                c = half * 4 + j
                B.tr(psum[bk][:, j * 128:(j + 1) * 128], xs[:, sl, c * 128:(c + 1) * 128], ident[:],
                     [r_xs[sl], ctx["r_ident"]], [rps[bk]])
            c0 = half * 4
            if perm8:
                dst = hT[:, c0:c0 + 4, :].rearrange("p c (j n) -> p c j n", j=8)[:, :, :, 16 * tt:16 * tt + 16]
                srcp = psum[bk][:, :].rearrange("p (c n j) -> p c j n", c=4, j=8)
                gain = gp[:, c0:c0 + 4].unsqueeze(2).unsqueeze(3).to_broadcast([128, 4, 8, 16])
            else:
                dst = hT[:, c0:c0 + 4, tt * 128:(tt + 1) * 128]
                srcp = psum[bk][:, :].rearrange("p (c t) -> p c t", c=4)
                gain = gp[:, c0:c0 + 4].unsqueeze(2).to_broadcast([128, 4, 128])
            wr = ctx["r_hT"] if perm8 else [ctx["r_hT"][tt]]
            B.tt("dve", dst, srcp, gain, ALU.mult, [r_gp], [rps[bk]] + wr)

    for t in range(NT + 2):
        if t < NT:
            stage1(t)
        if 0 <= t - 1 < NT:
            stage2(t - 1)
        if 0 <= t - 2 < NT:
            stage3(t - 2)


class WStream:
    def __init__(self, ctx, scope, nslot=2, name="ws", cols=256):
        nc, B = ctx["nc"], ctx["B"]
        self.ctx = ctx
        self.n = nslot
        self.wst = scope(nc.sbuf_tensor(name + "_st", [128, nslot, 8, cols], F32))
        self.wbf = scope(nc.sbuf_tensor(name + "_bf", [128, nslot, 8, cols], BF16))
        self.r_st = [B.R(name + "st%d" % i) for i in range(nslot)]
        self.r_bf = [B.R(name + "bf%d" % i) for i in range(nslot)]
        self.ds = [ctx["new_dsem"]("d_" + name + str(i)) for i in range(nslot)]
        self.k = 0

    def dma(self, src):
        B = self.ctx["B"]
        sl = self.k % self.n
        self.k += 1
        B.S.dma("sp", self.wst[:, sl], src.rearrange("(c p) n -> p c n", p=128), self.ds[sl],
                (), [self.r_st[sl]])
        return sl

    def cast(self, sl):
        B = self.ctx["B"]
        B.act(self.wbf[:, sl], self.wst[:, sl], AF.Copy, [self.r_st[sl]], [self.r_bf[sl]])
        return self.wbf[:, sl], self.r_bf[sl]

    def load(self, src, dst=None, r_dst=None):
        B = self.ctx["B"]
        sl = self.k % self.n
        self.k += 1
        B.S.dma("sp", self.wst[:, sl], src.rearrange("(c p) n -> p c n", p=128), self.ds[sl],
                (), [self.r_st[sl]])
        if dst is None:
            dst, r_dst = self.wbf[:, sl], self.r_bf[sl]
        B.act(dst, self.wst[:, sl], AF.Copy, [self.r_st[sl]], [r_dst])
        return dst, r_dst


def grp_tokens(g, bi):
    d = GROUPS[g][1]
    nbr = 16 // d
    r = bi // nbr
    m0 = (bi % nbr) * 128
    return d * m0 + r, d


def build_l0(ctx):
    nc, B, dr, psum, rps = ctx["nc"], ctx["B"], ctx["dr"], ctx["psum"], ctx["rps"]
    hT, GT, ones, small = ctx["hT"], ctx["GT"], ctx["ones"], ctx["small"]
    r_hT, r_GT = ctx["r_hT"], ctx["r_GT"]
    from contextlib import ExitStack
    d_par = ctx["new_dsem"]("d_par0")
    r_gp = B.R("gpre0")
    B.S.dma("sp", small[:, 0:8], dr["gpre0"], d_par, (), [r_gp])
    with ExitStack() as es:
        make_hT(ctx, dr["x"], es.enter_context, small[:, 0:8], r_gp)
        B.barrier()
    allhT = list(r_hT)
    ctx["dump"]("hT", hT[:], allhT)
    with ExitStack() as es:
        E = es.enter_context
        ws = WStream(ctx, E, 2, "w0s", cols=128)
        QK = E(nc.sbuf_tensor("QK", [128, 6, S], BF16))
        VA0 = E(nc.sbuf_tensor("VA0", [128, 48, 65], BF16))
        VA1 = E(nc.sbuf_tensor("VA1", [128, 48, 128], BF16))
        SZ = E(nc.sbuf_tensor("SZ", [128, S], F32))
        BT = E(nc.sbuf_tensor("BT", [128, 2, 6, 256], F32))
        OA = E(nc.sbuf_tensor("OA", [128, 2, S], F32))
        LT = E(nc.sbuf_tensor("LT", [128, 2, 2, 256], F32))
        PT = E(nc.sbuf_tensor("PT", [128, 5, 2, 256], BF16))
        VT = E(nc.sbuf_tensor("VT", [128, 3, S], BF16))
        RHL = E(nc.sbuf_tensor("RHL", [128, 2, S], BF16))
        identb = E(nc.sbuf_tensor("identb", [128, 128], BF16))
        onesb = E(nc.sbuf_tensor("onesb", [128, 128], BF16))
        r_VT = [B.R("VT%d" % i) for i in range(3)]
        r_RHL = B.R("RHL")
        r_cb = B.R("constb")
        B.cp("pool", identb[:], ctx["ident"][:], [ctx["r_ident"]], [r_cb])
        B.memset("pool", onesb[:], 1.0, [r_cb])
        r_QK = [B.R("QK%d" % i) for i in range(6)]
        r_VA = [B.R("VA%d" % i) for i in range(3)]
        r_SZ = B.R("SZ")
        r_BT = [B.R("BT0"), B.R("BT1")]
        r_OA = [B.R("OA0"), B.R("OA1")]
        r_LT = [B.R("LT%d" % i) for i in range(2)]
        r_PT = [B.R("PT%d" % i) for i in range(5)]
        d_bt = [ctx["new_dsem"]("d_bt0"), ctx["new_dsem"]("d_bt1")]
        B.memset("pool", VA0[:], 1.0, r_VA)
        B.memset("pool", VA1[:], 0.0, r_VA)
        B.memset("pool", VA1[:, :, 0:1], 1.0, r_VA)
        pb = [0]

        def proj_bank():
            pb[0] ^= 1
            return pb[0]

        stb = [0]
        fb = [0]
        dq = []
        pre = {}
        wdma = {}

        def wsrc(h, wb):
            return dr["w0"][h][:, wb * 128:(wb + 1) * 128]
        for hp in range(8):
            bsl = hp % 2
            B.S.dma("sp", BT[:, bsl], dr["bias0"][hp], d_bt[bsl], (), [r_BT[bsl]])
            for wb in range(10):
                key = (hp, wb)
                if key not in pre:
                    if key not in wdma:
                        wdma[key] = ws.dma(wsrc(*key))
                    pre[key] = ws.cast(wdma.pop(key))
                wbf, r_w = pre.pop(key)
                nkey = (hp, wb + 1) if wb < 9 else ((hp + 1, 0) if hp + 1 < 8 else None)
                if nkey is not None and nkey not in wdma and nkey not in pre:
                    wdma[nkey] = ws.dma(wsrc(*nkey))

                def at_s2(nkey=nkey):
                    if nkey is not None and nkey in wdma:
                        pre[nkey] = ws.cast(wdma.pop(nkey))
                for half_ in range(1):
                    colo = 0
                    kind = ("q", "k", "q", "k", "q", "k", "v0", "v1", "v2", "z")[wb]
                    half = wb % 2
                    if kind in ("q", "k", "z"):
                        g = wb // 2 if wb < 6 else 0
                        d = GROUPS[g][1] if kind != "z" else 1
                        for s in range(4):
                            if s == 2:
                                at_s2()
                            bk = proj_bank()
                            for c in range(8):
                                B.mm(psum[bk][:, :], wbf[:, c, colo:colo + 128],
                                     hT[:, c, s * 512:(s + 1) * 512], c == 0, c == 7,
                                     [r_w] + allhT, [rps[bk]])
                            if kind == "z":
                                B.act(SZ[:, s * 512:(s + 1) * 512], psum[bk][:, :], AF.Silu,
                                      [], [rps[bk], r_SZ])
                            else:
                                idx = 2 * g + half
                                L = S // d
                                dst = QK[:, idx, :].rearrange("p (r m) -> p r m", r=d)[
                                    :, :, (512 // d) * s:(512 // d) * (s + 1)]
                                srcp = psum[bk][:, :].rearrange("p (m r) -> p r m", r=d)
                                B.act(dst, srcp, AF.Copy, [], [rps[bk], r_QK[idx]],
                                      scale=(0.125 if kind == "q" else 1.0))
                    else:
                        g = int(kind[1])
                        for s in range(4):
                            if s == 2:
                                at_s2()
                            bk = proj_bank()
                            for c in range(8):
                                B.mm(psum[bk][:, :], wbf[:, c, colo:colo + 128],
                                     hT[:, c, s * 512:(s + 1) * 512], c == 0, c == 7,
                                     [r_w] + allhT, [rps[bk]])
                            B.act(VT[:, g, s * 512:(s + 1) * 512], psum[bk][:, :], AF.Copy, [],
                                  [rps[bk], r_VT[g]])
                        for quad in range(4):
                            bk = proj_bank()
                            pb16 = psum[bk][:, :].bitcast(BF16)
                            for j in range(4):
                                bi = quad * 4 + j
                                o, st = grp_tokens(g, bi)
                                B.tr(pb16[:, j * 128:(j + 1) * 128], VT[:, g, o:o + 127 * st + 1:st], identb[:],
                                     [r_VT[g], r_cb], [rps[bk]])
                            pv = pb16[:, 0:512].rearrange("p (j c) -> p j c", j=4)
                            b0 = g * 16 + quad * 4
                            B.cp("dve", VA0[:, b0:b0 + 4, 0:64], pv[:, :, 0:64], [], [rps[bk], r_VA[g]])
                            B.cp("dve", VA1[:, b0:b0 + 4, 64:128], pv[:, :, 64:128], [], [rps[bk], r_VA[g]])
                ctx["pump"](4)
                if wb < 7:
                    while dq:
                        f = dq.pop(0)
                        if f is not None:
                            f()
                            break
                if wb == 8:
                    while dq:
                        f = dq.pop(0)
                        if f is not None:
                            f()
            its = [(hh, g, m) for hh in range(2) for g in range(3) for m in range(8)]
            st_info = {}

            def nq_of(g, kb):
                nbr = 16 // GROUPS[g][1]
                return 256 if (kb + 1) % nbr != 0 else 128

            def issue_st(i):
                hh, g, m = its[i]
                bk = (0, 1, 2, 7)[stb[0] % 4]
                stb[0] += 1
                rows = slice(hh * 64, (hh + 1) * 64)
                for t in range(2):
                    kb = 2 * m + t
                    nq = nq_of(g, kb)
                    B.mm(psum[bk][:, t * 256:t * 256 + nq], QK[rows, 2 * g + 1, kb * 128:(kb + 1) * 128],
                         QK[rows, 2 * g, kb * 128:kb * 128 + nq], True, True,
                         [r_QK[2 * g], r_QK[2 * g + 1]], [rps[bk]])
                ls = i % 2
                ps_ = i % 5
                bias2 = BT[:, bsl, g * 2 + hh, :].unsqueeze(1).to_broadcast([128, 2, 256])
                B.tt("dve", LT[:, ls], psum[bk][:, :].rearrange("p (t q) -> p t q", t=2), bias2, ALU.add,
                     [r_BT[bsl]], [rps[bk], r_LT[ls]])
                B.act(PT[:, ps_], LT[:, ls], AF.Exp, [r_LT[ls]], [r_PT[ps_]])
                st_info[i] = ps_

            def issue_pv(i):
                hh, g, m = its[i]
                d = GROUPS[g][1]
                nbr = 16 // d
                M = 65 if hh == 0 else 128
                VA = VA0 if hh == 0 else VA1
                ps_ = st_info[i]
                for t in range(2):
                    kb = 2 * m + t
                    has_prev = kb % nbr != 0
                    if has_prev:
                        pslot, pt = (ps_, 0) if t == 1 else (st_info[i - 1], 1)
                    if g == 0:
                        outs = [(3 + kb // 4, slice((kb % 4) * 128, (kb % 4) * 128 + 128), slice(0, 128))]
                    elif g == 1:
                        r, nn = kb // 4, kb % 4
                        outs = [(3 + nn, slice(r, 512, 4), slice(0, 128))]
                    else:
                        outs = [(3 + sub, slice(kb, 512, 16), slice(32 * sub, 32 * sub + 32)) for sub in range(4)]
                    for (bo, ocols, qcols) in outs:
                        outp = psum[bo][0:M, ocols]
                        first = (g == 0 and kb % 4 == 0)
                        if has_prev:
                            pq = slice(128 + qcols.start, 128 + qcols.stop)

                            def f1(e, outp=outp, a=VA[:, g * 16 + kb - 1, 0:M], b=PT[:, pslot, pt, pq], first=first):
                                return e.matmul(outp, a, b, start=first, stop=False, skip_group_check=True)
                            B.S.op("pe", f1, [r_VA[g], r_PT[pslot]], [rps[bo]])
                            first = False

                        last = (g == 2 and kb == 15)

                        def f2(e, outp=outp, a=VA[:, g * 16 + kb, 0:M], b=PT[:, ps_, t, qcols], first=first, last=last):
                            return e.matmul(outp, a, b, start=first, stop=last, skip_group_check=True)
                        B.S.op("pe", f2, [r_VA[g], r_PT[ps_]], [rps[bo]])
                if g == 2 and m == 7:
                    for span in range(4):
                        if span % 2 == 0:
                            B.cp("dve", OA[0:M, hh, span * 512:(span + 1) * 512], psum[3 + span][0:M, :],
                                 [], [rps[3 + span], r_OA[hh]])
                        else:
                            B.act(OA[0:M, hh, span * 512:(span + 1) * 512], psum[3 + span][0:M, :], AF.Copy,
                                  [], [rps[3 + span], r_OA[hh]])
                    finalize(hh)

            def finalize(hh, hp=hp):
                drow = 64 if hh == 0 else 0
                R = slice(hh * 64, (hh + 1) * 64)

                def prepA(s):
                    cs = slice(s * 512, (s + 1) * 512)
                    den = OA[drow:drow + 1, hh, cs]
                    B.act(den, den, AF.Ln, [], [r_OA[hh]])
                    B.act(den, den, AF.Exp, [], [r_OA[hh]], scale=-1.0)

                def prepB(s):
                    cs = slice(s * 512, (s + 1) * 512)
                    den = OA[drow:drow + 1, hh, cs]
                    B.cp("dve", RHL[drow:drow + 1, 0, cs], den, [r_OA[hh]], [r_RHL])
                    B.tt("dve", RHL[drow:drow + 1, 1, cs], den, RHL[drow:drow + 1, 0, cs], ALU.subtract,
                         [r_OA[hh]], [r_RHL])

                def span(s):
                    bk = (0, 1, 2, 7)[stb[0] % 4]
                    stb[0] += 1
                    cs = slice(s * 512, (s + 1) * 512)
                    B.mm(psum[bk][:, :], onesb[drow:drow + 1, :], RHL[drow:drow + 1, 0, cs], True, False,
                         [r_cb, r_RHL], [rps[bk]])
                    B.mm(psum[bk][:, :], onesb[drow:drow + 1, :], RHL[drow:drow + 1, 1, cs], False, True,
                         [r_cb, r_RHL], [rps[bk]])
                    B.tt("dve", OA[R, hh, cs], OA[R, hh, cs], psum[bk][R, :], ALU.mult,
                         [], [rps[bk], r_OA[hh]])
                    B.tt("pool", GT[R, hp, cs], OA[R, hh, cs], SZ[R, cs], ALU.mult,
                         [r_OA[hh], r_SZ], [r_GT])
                for s in range(4):
                    dq.append(lambda s=s: prepA(s))
                    dq.append(None)
                    dq.append(lambda s=s: prepB(s))
                for _ in range(4):
                    dq.append(None)
                for s in range(4):
                    dq.append(lambda s=s: span(s))
                    if s < 3:
                        dq.append(None)

            if hp == 0:
                ctx["dump"]("QK", QK[:], r_QK)
                ctx["dump"]("VA0", VA0[:], r_VA)
                ctx["dump"]("VA1", VA1[:], r_VA)
                ctx["dump"]("SZ", SZ[:], [r_SZ])
                ctx["dump"]("BT", BT[:, 0], [r_BT[0]])
            LOOK = 3
            n = len(its)
            for i in range(min(LOOK, n)):
                issue_st(i)
            for i in range(n):
                B.S.begin_group()
                issue_pv(i)
                if i + LOOK < n:
                    issue_st(i + LOOK)
                B.S.end_group()
                if dq:
                    f = dq.pop(0)
                    if f is not None:
                        f()
            if hp == 7:
                while dq:
                    f = dq.pop(0)
                    if f is not None:
                        f()
        B.barrier()
    ctx["dump"]("GT", GT[:], [r_GT])
    finals = []
    with ExitStack() as es:
        E = es.enter_context
        ws = WStream(ctx, E, 2, "wo0s")
        WO = E(nc.sbuf_tensor("WO0", [128, 8, D], BF16))
        gpo = E(nc.sbuf_tensor("gpo0", [128, D], F32))
        r_WO = [B.R("WO%d" % i) for i in range(4)]
        r_gpo = B.R("gpo0")
        B.S.dma("sp", gpo[:], dr["gpost0"], d_par, (), [r_gpo])
        for wb in range(4):
            ws.load(dr["wo0"][:, wb * 256:(wb + 1) * 256], WO[:, :, wb * 256:(wb + 1) * 256], r_WO[wb])
        finals = out_proj_tail(ctx, E, WO, r_WO, gpo, r_gpo, dr["x"], dr["x1"], GT, r_GT, None, "t0",
                               dst_res=ctx.get("r_x1"))
        B.barrier()
    return finals


def out_proj_tail(ctx, E, WO, r_WO, gpo, r_gpo, xsrc, dst, GT, r_GT, tokap, name, src_res=None, dst_res=None):
    nc, B, psum, rps = ctx["nc"], ctx["B"], ctx["psum"], ctx["rps"]
    xt = E(nc.sbuf_tensor("xt_" + name, [128, 2, D], F32))
    yt = E(nc.sbuf_tensor("yt_" + name, [128, 2, D], F32))
    sq = E(nc.sbuf_tensor("sq_" + name, [128, 512], F32))
    st = E(nc.sbuf_tensor("st_" + name, [128, 2, 4], F32))
    r_xt = [B.R("xt0"), B.R("xt1")]
    r_yt = [B.R("yt0"), B.R("yt1")]
    r_sq = B.R("sq")
    r_st = [B.R("st0"), B.R("st1")]
    d_x = [ctx["new_dsem"]("d_x%s0" % name), ctx["new_dsem"]("d_x%s1" % name)]
    d_o = [ctx["new_dsem"]("d_o%s0" % name), ctx["new_dsem"]("d_o%s1" % name)]
    finals = []
    for tt in range(NT):
        sl = tt % 2
        if tokap is None:
            xrows, orows = xsrc[tt * 128:(tt + 1) * 128, :], dst[tt * 128:(tt + 1) * 128, :]
        else:
            jj, nb = tt // 2, tt % 2
            xrows = xsrc.rearrange("(n j) d -> j n d", j=8)[jj, nb * 128:(nb + 1) * 128, :]
            orows = dst.rearrange("(n j) d -> j n d", j=8)[jj, nb * 128:(nb + 1) * 128, :]
        B.S.dma("sp", xt[:, sl, :], xrows, d_x[sl],
                [src_res[tt]] if src_res else (), [r_xt[sl]])
        banks = (0, 1) if sl == 0 else (2, 3)
        for half in range(2):
            bk = banks[half]
            for c in range(8):
                lhs = GT[:, c, tt * 128:(tt + 1) * 128]
                B.mm(psum[bk][:, :], lhs, WO[:, c, half * 512:(half + 1) * 512], c == 0, c == 7,
                     [r_GT] + r_WO[2 * half:2 * half + 2], [rps[bk]])
            B.act(sq[:, :], psum[bk][:, :], AF.Square, [], [rps[bk], r_sq, r_st[sl]],
                  accum_out=st[:, sl, half:half + 1])
        B.tt("dve", st[:, sl, 2:3], st[:, sl, 0:1], st[:, sl, 1:2], ALU.add, [r_st[sl]], [r_st[sl]])
        rms_rstd(B, st[:, sl, 2:3], st[:, sl, 3:4], r_st[sl], r_st[sl])
        for half in range(2):
            bk = banks[half]
            hs = slice(half * 512, (half + 1) * 512)
            B.stt(yt[:, sl, hs], psum[bk][:, :], st[:, sl, 3:4], gpo[:, hs], ALU.mult, ALU.mult,
                  [r_st[sl], r_gpo], [rps[bk], r_yt[sl]])
        B.tt("pool", yt[:, sl, :], yt[:, sl, :], xt[:, sl, :], ALU.add, [r_xt[sl]], [r_yt[sl]])
        o = B.S.dma("pool", orows, yt[:, sl, :], d_o[sl], [r_yt[sl]],
                    [dst_res[tt]] if dst_res else [])
        fr = B.R("fin%d" % tt)
        fr.w = o
        finals.append(fr)
    return finals


def build_l1_decl(dt, dr):
    EI = "ExternalInput"
    dr["w1"] = dt("w1", [D, 2048], F32, kind=EI).ap()
    dr["wg1"] = dt("wg1", [D, D], F32, kind=EI).ap()
    dr["wo1"] = dt("wo1", [D, D], F32, kind=EI).ap()
    dr["gpre1"] = dt("gpre1", [128, 8], F32, kind=EI).ap()
    dr["gpost1"] = dt("gpost1", [128, D], F32, kind=EI).ap()
    dr["bglu"] = dt("bglu", [128, 8], F32, kind=EI).ap()
    dr["pl_a"] = dt("pl_a", [128, 3, 32], F32, kind=EI).ap()
    dr["pl_bc"] = dt("pl_bc", [128, 4, 32, 16], F32, kind=EI).ap()
    dr["dcol"] = dt("dcol", [128, 64], F32, kind=EI).ap()
    dr["mask8"] = dt("mask8", [128, 128], F32, kind=EI).ap()
    dr["esel"] = dt("esel", [128, 64, 128], BF16, kind=EI).ap()


def cmul(B, outr, outi, ar, ai, br, bi, t1, t2, R, neg_i=False, eng="dve", emit=None, Rw=None):
    if emit is None:
        emit = lambda f: f()
    Rw = R if Rw is None else Rw
    rd = [R] if Rw is R else [R, Rw]
    emit(lambda: B.tt(eng, t1, ar, br, ALU.mult, rd, [Rw]))
    emit(lambda: B.tt(eng, t2, ai, bi, ALU.mult, rd, [Rw]))
    emit(lambda: B.tt(eng, outr, t1, t2, ALU.subtract, rd, [Rw]))
    emit(lambda: B.tt(eng, t1, ar, bi, ALU.mult, rd, [Rw]))
    emit(lambda: B.tt(eng, t2, ai, br, ALU.mult, rd, [Rw]))
    if neg_i:
        emit(lambda: B.tt(eng, t1, t1, t2, ALU.add, rd, [Rw]))
        emit(lambda: B.ts(eng, outi, t1, -1.0, ALU.mult, rd, [Rw]))
    else:
        emit(lambda: B.tt(eng, outi, t1, t2, ALU.add, rd, [Rw]))


def l1_prep_stage1(ctx, E):
    nc, B, dr = ctx["nc"], ctx["B"], ctx["dr"]
    PEN = "dve"
    pa = E(nc.sbuf_tensor("pa", [128, 3, 32], F32))
    pbc = E(nc.sbuf_tensor("pbc", [128, 4, 32, 16], F32))
    W = E(nc.sbuf_tensor("Wk", [128, 16, 32], F32))
    PW = E(nc.sbuf_tensor("PW", [128, 9, 2, 32], F32))
    QW = E(nc.sbuf_tensor("QW", [128, 8, 2, 32], F32))
    bb = E(nc.sbuf_tensor("bbar", [128, 2, 32, 16], F32))
    T1 = E(nc.sbuf_tensor("T1", [128, 32, 16], F32))
    T2 = E(nc.sbuf_tensor("T2", [128, 32, 16], F32))
    RP = B.R("prep")
    d_p = ctx["new_dsem"]("d_prep")
    B.S.dma("sp", pa[:], dr["pl_a"], d_p, (), [RP])
    B.S.dma("sp", pbc[:], dr["pl_bc"], d_p, (), [RP])
    q = []
    emit = q.append
    RK = E(nc.sbuf_tensor("RK", [128, 16], F32))
    for k in range(1, 16):
        emit(lambda k=k: B.memset(PEN, RK[:, k:k + 1], float(k), [RP]))
    emit(lambda: B.memset(PEN, RK[:, 0:1], 1.0, [RP]))
    emit(lambda: B.recip(RK[:], RK[:], [RP], [RP]))
    are, aim, ldt = pa[:, 0, :], pa[:, 1, :], pa[:, 2, :]
    w = lambda k: W[:, k, :]
    t1s, t2s = T1[:, :, 0], T2[:, :, 0]
    tt = lambda o, a, b, op: emit(lambda: B.tt(PEN, o, a, b, op, [RP], [RP]))
    ts = lambda o, a, s1, op0, s2=None, op1=None: emit(lambda: B.ts(PEN, o, a, s1, op0, [RP], [RP], s2=s2, op1=op1))
    ms = lambda o, v: emit(lambda: B.memset(PEN, o, v, [RP]))
    cpy = lambda o, a: emit(lambda: B.cp(PEN, o, a, [RP], [RP]))
    cm = lambda *a, **k: cmul(B, *a, eng=PEN, emit=emit, **k)
    ts(w(1), ldt, 0.125, ALU.mult)
    ms(w(0), 1.0)
    for k in range(10, 0, -1):
        tt(w(0), w(0), w(1), ALU.mult)
        ts(w(0), w(0), RK[:, k:k + 1], ALU.mult, 1.0, ALU.add)
    for _ in range(3):
        tt(w(0), w(0), w(0), ALU.mult)
    ts(w(8), w(0), 1.0 / 16, ALU.mult)
    tt(w(1), are, w(8), ALU.mult)
    tt(w(2), aim, w(8), ALU.mult)
    ms(w(3), 1.0)
    ms(w(4), 0.0)
    for k in range(14, 1, -1):
        cm(w(5), w(6), w(3), w(4), w(1), w(2), t1s, t2s, RP)
        ts(w(3), w(5), RK[:, k:k + 1], ALU.mult, 1.0, ALU.add)
        ts(w(4), w(6), RK[:, k:k + 1], ALU.mult)
    cm(w(5), w(6), w(3), w(4), w(1), w(2), t1s, t2s, RP)
    for _ in range(4):
        ts(w(7), w(5), 2.0, ALU.add)
        cm(w(3), w(4), w(5), w(6), w(7), w(6), t1s, t2s, RP)
        cpy(w(5), w(3))
        cpy(w(6), w(4))
    ms(PW[:, 0, 0, :], 1.0)
    ms(PW[:, 0, 1, :], 0.0)
    ms(QW[:, 0, 0, :], 1.0)
    ms(QW[:, 0, 1, :], 0.0)
    abr, abi = PW[:, 1, 0, :], PW[:, 1, 1, :]
    ts(abr, w(5), 1.0, ALU.add)
    cpy(abi, w(6))
    for k in range(2, 9):
        cm(PW[:, k, 0, :], PW[:, k, 1, :], PW[:, k - 1, 0, :], PW[:, k - 1, 1, :], abr, abi, t1s, t2s, RP)
    tt(w(8), abr, abr, ALU.mult)
    tt(w(9), abi, abi, ALU.mult)
    tt(w(8), w(8), w(9), ALU.add)
    emit(lambda: B.recip(w(9), w(8), [RP], [RP]))
    tt(QW[:, 1, 0, :], abr, w(9), ALU.mult)
    tt(w(8), abi, w(9), ALU.mult)
    ts(QW[:, 1, 1, :], w(8), -1.0, ALU.mult)
    for k in range(2, 8):
        cm(QW[:, k, 0, :], QW[:, k, 1, :], QW[:, k - 1, 0, :], QW[:, k - 1, 1, :],
           QW[:, 1, 0, :], QW[:, 1, 1, :], t1s, t2s, RP)
    cpy(w(10), w(5))
    tt(w(11), are, are, ALU.mult)
    tt(w(12), aim, aim, ALU.mult)
    tt(w(11), w(11), w(12), ALU.add)
    emit(lambda: B.recip(w(12), w(11), [RP], [RP]))
    tt(w(13), w(10), are, ALU.mult)
    tt(w(14), abi, aim, ALU.mult)
    tt(w(13), w(13), w(14), ALU.add)
    tt(w(13), w(13), w(12), ALU.mult)
    tt(w(14), abi, are, ALU.mult)
    tt(w(15), w(10), aim, ALU.mult)
    tt(w(14), w(14), w(15), ALU.subtract)
    tt(w(14), w(14), w(12), ALU.mult)
    bc = lambda ap: ap.unsqueeze(2).to_broadcast([128, 32, 16])
    cm(bb[:, 0], bb[:, 1], bc(w(13)), bc(w(14)), pbc[:, 0], pbc[:, 1], T1[:], T2[:], RP)

    def pump(n=None):
        k = len(q) if n is None else min(n, len(q))
        for _ in range(k):
            q.pop(0)()
    ctx["pump"] = pump
    ctx["prep1"] = dict(pbc=pbc, PW=PW, QW=QW, bb=bb, T1=T1, T2=T2, RP=RP, bc=bc)


def build_l1(ctx):
    from contextlib import ExitStack
    nc, B, dr, psum, rps = ctx["nc"], ctx["B"], ctx["dr"], ctx["psum"], ctx["rps"]
    hT, GT, ident, small = ctx["hT"], ctx["GT"], ctx["ident"], ctx["small"]
    r_ident = ctx["r_ident"]
    PI = math.pi
    d_par = ctx["new_dsem"]("d_par1")
    r_gp = B.R("gpre1")
    r_bg = B.R("bglu")
    B.S.dma("sp", small[:, 8:16], dr["gpre1"], d_par, (), [r_gp])
    B.S.dma("sp", small[:, 16:24], dr["bglu"], d_par, (), [r_bg])
    r_hTall = B.R("hTall")
    r_Gb = B.R("Gb")
    Gb = GT[:, :, :].rearrange("p c (a n) -> p (c a) n", n=256)
    with ExitStack() as es1:
        E1 = es1.enter_context
        T0b = E1(nc.sbuf_tensor("T0b", [128, 64, 128], BF16))
        BmT = E1(nc.sbuf_tensor("BmT", [128, 64, 2, 64], BF16))
        Cmb = E1(nc.sbuf_tensor("Cmb", [128, 32, 2, 128], BF16))
        A8 = E1(nc.sbuf_tensor("A8", [128, 2, 2, 32], F32))
        r_T0b, r_BmT, r_Cmb, r_A8 = B.R("T0b"), B.R("BmT"), B.R("Cmb"), B.R("A8")
        with ExitStack() as es:
            E = es.enter_context
            ctx["pump"]()
            p1 = ctx["prep1"]
            pbc, PW, QW, bb, T1, T2, RP, bc = (p1[k] for k in ("pbc", "PW", "QW", "bb", "T1", "T2", "RP", "bc"))
            cre, cim = pbc[:, 2], pbc[:, 3]
            dcol = E(nc.sbuf_tensor("dcol_sb", [128, 64], F32))
            mask8 = E(nc.sbuf_tensor("mask8_sb", [128, 128], F32))
            Lh = E(nc.sbuf_tensor("Lh", [128, 2, 32, 128], BF16))
            Rh = E(nc.sbuf_tensor("Rh", [128, 2, 32, 128], BF16))
            identb1 = E(nc.sbuf_tensor("identb1", [128, 128], BF16))
            r_ib1 = B.R("identb1")
            B.cp("dve", identb1[:], ident[:], [r_ident], [r_ib1])
            r_dm = B.R("dcolmask")
            B.S.dma("sp", dcol[:], dr["dcol"], d_par, (), [r_dm])
            B.S.dma("sp", mask8[:], dr["mask8"], d_par, (), [r_dm])
            T3 = E(nc.sbuf_tensor("T3", [128, 32, 16], F32))
            T4 = E(nc.sbuf_tensor("T4", [128, 32, 16], F32))
            R_L, R_R = B.R("prepL"), B.R("prepR")
            for j in range(8):
                js = slice(j * 16, (j + 1) * 16)
                cmul(B, Rh[:, 0, :, js], Rh[:, 1, :, js], bc(QW[:, 7 - j, 0, :]), bc(QW[:, 7 - j, 1, :]),
                     cre, cim, T3[:], T4[:], RP, neg_i=True, eng="dve", Rw=R_R)
            for j in range(8):
                js = slice(j * 16, (j + 1) * 16)
                cmul(B, Lh[:, 0, :, js], Lh[:, 1, :, js], bc(PW[:, 7 - j, 0, :]), bc(PW[:, 7 - j, 1, :]),
                     bb[:, 0], bb[:, 1], T1[:], T2[:], RP, Rw=R_L)
                cmul(B, Cmb[:, :, 0, js], Cmb[:, :, 1, js], bc(PW[:, j + 1, 0, :]), bc(PW[:, j + 1, 1, :]),
                     cre, cim, T1[:], T2[:], RP, neg_i=True, Rw=R_L)
            B.cp("dve", Cmb[:, 0, 0, 0:1], Cmb[:, 0, 0, 0:1], [R_L], [r_Cmb])
            B.cp("dve", A8[:, 0, 0, :], PW[:, 8, 0, :], [RP], [r_A8])
            B.cp("dve", A8[:, 0, 1, :], PW[:, 8, 0, :], [RP], [r_A8])
            B.ts("dve", A8[:, 1, 0, :], PW[:, 8, 1, :], -1.0, ALU.mult, [RP], [r_A8])
            B.cp("dve", A8[:, 1, 1, :], PW[:, 8, 1, :], [RP], [r_A8])
            DA = E(nc.sbuf_tensor("DA", [128, 64, 128], F32))
            TM4 = E(nc.sbuf_tensor("TM4", [128, 1, 4, 128], F32))
            r_DA = B.R("DA")
            r_TM4 = [B.R("TM4a"), B.R("TM4b")]
            r_T0g = [B.R("T0g%d" % i) for i in range(2)]
            B.tt("dve", DA[:], ident[:].unsqueeze(1).to_broadcast([128, 64, 128]),
                 dcol[:].unsqueeze(2).to_broadcast([128, 64, 128]), ALU.mult, [r_dm, r_ident], [r_DA])
            for g4 in range(16):
                bk = g4 % 2
                for gg in range(4):
                    g = g4 * 4 + gg
                    gh, pr = g // 32, g % 32
                    Rr = slice(gh * 64, (gh + 1) * 64)
                    outp = psum[bk][:, gg * 128:(gg + 1) * 128]
                    B.mm(outp, Lh[Rr, 0, pr, :], Rh[Rr, 0, pr, :], True, False, [R_L, R_R], [rps[bk]])
                    B.mm(outp, Lh[Rr, 1, pr, :], Rh[Rr, 1, pr, :], False, True, [R_L, R_R], [rps[bk]])
                tm = 0
                B.tt("dve", TM4[:, tm], psum[bk][:, :].rearrange("p (g c) -> p g c", g=4),
                     mask8[:].unsqueeze(1).to_broadcast([128, 4, 128]), ALU.mult,
                     [r_dm], [rps[bk], r_TM4[tm]])
                B.tt("dve", T0b[:, g4 * 4:g4 * 4 + 4, :], TM4[:, tm], DA[:, g4 * 4:g4 * 4 + 4, :], ALU.add,
                     [r_TM4[tm], r_DA], [r_T0g[tm]])
            for g4 in range(16):
                bk = 2 + g4 % 2
                for gg in range(4):
                    g = g4 * 4 + gg
                    gh, pr = g // 32, g % 32
                    Rr = slice(gh * 64, (gh + 1) * 64)
                    for part in range(2):
                        cs = (gg * 2 + part) * 64
                        B.tr(psum[bk][:, :].bitcast(BF16)[:, cs:cs + 64], Lh[Rr, part, pr, :], identb1[Rr, Rr],
                             [R_L, r_ib1], [rps[bk]])
                B.cp("dve", BmT[:, g4 * 4:g4 * 4 + 4].rearrange("p g a c -> p (g a c)"),
                     psum[bk][:, :].bitcast(BF16)[:, 0:512], [], [rps[bk], r_BmT])
            ctx["dump"]("T0b", T0b[:], [r_T0b])
            ctx["dump"]("BmT", BmT[:], [r_BmT])
            ctx["dump"]("Cmb", Cmb[:], [r_Cmb])
            ctx["dump"]("A8", A8[:], [r_A8])
            ctx["dump"]("PW", PW[:], [RP])
            ctx["dump"]("QW", QW[:], [RP])
            ctx["dump"]("Lh", Lh[:], [R_L])
            ctx["dump"]("Rh", Rh[:], [R_R])
            B.barrier()
        if ctx.get("stop") == "prep":
            return []
        V = E1(nc.sbuf_tensor("Vssm", [128, 64, 256], BF16))
        r_V = B.R("V")
        with ExitStack() as es:
            ctx2 = dict(ctx)
            ctx2["r_hT"] = [r_hTall]
            make_hT(ctx2, dr["x1"], es.enter_context, small[:, 8:16], r_gp, perm8=True,
                    src_res=ctx.get("r_x1"), name="b")
            B.barrier()
        ctx["dump"]("hT1", hT[:], [r_hTall])
        if ctx.get("stop") == "A":
            return []
        with ExitStack() as es:
            E = es.enter_context
            ws = WStream(ctx, E, 2, "w1s", cols=128)
            uTb = E(nc.sbuf_tensor("uTb", [128, 2, S], BF16))
            Es = E(nc.sbuf_tensor("Esel", [128, 64, 128], BF16))
            r_uT = [B.R("uT0"), B.R("uT1")]
            r_Es = B.R("Esel")
            d_es = ctx["new_dsem"]("d_es")
            B.S.dma("sp", Es[:], dr["esel"], d_es, (), [r_Es])
            pbk = 0
            for fc in range(8):
                wbf, r_w = ws.load(dr["w1"][:, fc * 128:fc * 128 + 128])
                colo = 0
                us = fc % 2
                for s in range(4):
                    bk = pbk
                    pbk ^= 1
                    for c in range(8):
                        B.mm(psum[bk][:, :], wbf[:, c, colo:colo + 128], hT[:, c, s * 512:(s + 1) * 512],
                             c == 0, c == 7, [r_w, r_hTall], [rps[bk]])
                    B.act(uTb[:, us, s * 512:(s + 1) * 512], psum[bk][:, :], AF.Copy, [],
                          [rps[bk], r_uT[us]])
                for g8 in range(8):
                    g = fc * 8 + g8
                    bk = 2 + g % 4
                    for j in range(8):
                        B.mm(psum[bk][:, 0:256], Es[:, g8 * 8 + j, :], uTb[:, us, j * 256:(j + 1) * 256],
                             j == 0, j == 7, [r_Es, r_uT[us]], [rps[bk]])
                    B.cp("dve", V[:, g, :], psum[bk][:, 0:256], [], [rps[bk], r_V])
            ctx["dump"]("V", V[:], [r_V])
            B.barrier()
        if ctx.get("stop") == "B":
            return []
        with ExitStack() as es:
            E = es.enter_context
            Zb = E(nc.sbuf_tensor("Zb", [128, 2, 65, 2, 32], F32))
            Sb = ctx["prep1"]["pbc"][:].rearrange("p a b c -> p (a b c)").bitcast(BF16).rearrange(
                "p (n a r) -> p n a r", a=2, r=32)
            t1 = E(nc.sbuf_tensor("zt1", [128, 2, 32], F32))
            t2 = E(nc.sbuf_tensor("zt2", [128, 2, 32], F32))
            r_Z = [B.R("Zb0"), B.R("Zb1")]
            r_Sb = B.R("Sb")
            r_t = B.R("zt")
            r_t2 = B.R("zt2")
            B.memset("dve", Zb[:, 0, 0], 0.0, [r_Z[0]])

            def local_states(tb):
                zb = tb % 2
                ns = slice(tb * 64, (tb + 1) * 64)
                for q in range(8):
                    bk = q % 2
                    for pp in range(4):
                        pr = q * 4 + pp
                        for gh in range(2):
                            g = gh * 32 + pr
                            for part in range(2):
                                cs = (pp * 2 + part) * 64
                                B.mm(psum[bk][gh * 64:(gh + 1) * 64, cs:cs + 64], BmT[:, g, part, :],
                                     V[:, g, ns], True, True, [r_BmT, r_V], [rps[bk]])
                    dst = Zb[:, zb, 1:65, :, q * 4:q * 4 + 4].rearrange("p n a r -> p r a n")
                    srcp = psum[bk][:, :].rearrange("p (r a n) -> p r a n", r=4, a=2)
                    B.act(dst, srcp, AF.Copy, [], [rps[bk], r_Z[zb]])

            def recurrence(tb):
                zb = tb % 2
                for s in range(64):
                    zs = Zb[:, zb, s]
                    zn = Zb[:, zb, s + 1]
                    B.tt("dve", t1[:], A8[:, 0], zs, ALU.mult, [r_A8, r_Z[zb]], [r_t])
                    B.tt("dve", t2[:], A8[:, 1], Zb[:, zb, s, ::-1, :], ALU.mult, [r_A8, r_Z[zb]], [r_t2])
                    B.tt("dve", zn, zn, t1[:], ALU.add, [r_t], [r_Z[zb]])
                    B.tt("dve", zn, zn, t2[:], ALU.add, [r_t2], [r_Z[zb]])
                if tb + 1 < 4:
                    B.cp("dve", Zb[:, 1 - zb, 0], Zb[:, zb, 64], [r_Z[zb]], [r_Z[1 - zb]])
                B.act(Sb, Zb[:, zb, 0:64], AF.Copy, [r_Z[zb]], [r_Sb])

            def outputs(tb):
                ns = slice(tb * 64, (tb + 1) * 64)
                for fc in range(8):
                    bk = 2 + fc % 4
                    for g8 in range(8):
                        g = fc * 8 + g8
                        gh, pr = g // 32, g % 32
                        Rr = slice(gh * 64, (gh + 1) * 64)
                        outp = psum[bk][:, g8 * 64:(g8 + 1) * 64]
                        B.mm(outp, T0b[:, g, :], V[:, g, ns], True, False, [r_T0b, r_V], [rps[bk]])
                        B.mm(outp, Cmb[Rr, pr, 0, :], Sb[Rr, :, 0, pr], False, False, [r_Cmb, r_Sb], [rps[bk]])
                        B.mm(outp, Cmb[Rr, pr, 1, :], Sb[Rr, :, 1, pr], False, True, [r_Cmb, r_Sb], [rps[bk]])
                    yp = psum[bk][:, :]
                    dst = Gb[:, fc * 8:(fc + 1) * 8, ns]
                    B.act(dst, yp.rearrange("p (g n) -> p g n", g=8), AF.Gelu_apprx_tanh, [],
                          [rps[bk], r_Gb])

            local_states(0)
            for tb in range(4):
                if tb + 1 < 4:
                    local_states(tb + 1)
                recurrence(tb)
                outputs(tb)
            ctx["dump"]("Gb", GT[:], [r_Gb])
            B.barrier()
        B.barrier()
    if ctx.get("stop") == "C":
        return []
    finals = []
    with ExitStack() as es2:
        E2 = es2.enter_context
        gTb = E2(nc.sbuf_tensor("gTb", [128, 8, S], BF16))
        r_gT = B.R("gTb")
        r_G2 = B.R("G2T")
        with ExitStack() as es:
            E = es.enter_context
            Es = E(nc.sbuf_tensor("Esel2", [128, 64, 128], BF16))
            r_Es = B.R("Esel2")
            d_es = ctx["new_dsem"]("d_es2")
            B.S.dma("sp", Es[:], dr["esel"], d_es, (), [r_Es])
            for fc in range(8):
                for ip in range(4):
                    bk = (fc * 4 + ip) % 4
                    for ii in range(2):
                        i0 = ip * 2 + ii
                        for g8 in range(8):
                            B.mm(psum[bk][:, ii * 256:(ii + 1) * 256], Es[:, i0 * 8 + g8, :],
                                 Gb[:, fc * 8 + g8, :], g8 == 0, g8 == 7, [r_Es, r_Gb], [rps[bk]])
                    B.cp("dve" if ip % 2 == 0 else "act_copy", gTb[:, fc, ip * 512:(ip + 1) * 512],
                         psum[bk][:, :], [], [rps[bk], r_gT]) if False else \
                        B.cp("dve", gTb[:, fc, ip * 512:(ip + 1) * 512], psum[bk][:, :], [], [rps[bk], r_gT])
            ctx["dump"]("gTb", gTb[:], [r_gT])
            B.barrier()
        if ctx.get("stop") == "D1":
            return []
        with ExitStack() as es:
            E = es.enter_context
            wsg = WStream(ctx, E, 2, "wg1s")
            wsz = WStream(ctx, E, 2, "wz1s")
            sg = E(nc.sbuf_tensor("sg", [128, 2, 512], F32))
            sz = E(nc.sbuf_tensor("sz", [128, 2, 512], F32))
            r_sg = [B.R("sg0"), B.R("sg1")]
            r_sz = [B.R("sz0"), B.R("sz1")]
            k = 0
            for fo in range(8):
                if fo % 2 == 0:
                    wg, r_wg = wsg.load(dr["wg1"][:, fo * 128:fo * 128 + 256])
                    wz, r_wz = wsz.load(dr["w1"][:, 1024 + fo * 128:1024 + fo * 128 + 256])
                colo = (fo % 2) * 128
                for s in range(4):
                    sl = k % 2
                    k += 1
                    ba, bb_ = (0, 1) if sl == 0 else (2, 3)
                    cs = slice(s * 512, (s + 1) * 512)
                    for c in range(8):
                        B.mm(psum[ba][:, :], wg[:, c, colo:colo + 128], gTb[:, c, cs], c == 0, c == 7,
                             [r_wg, r_gT], [rps[ba]])
                    for c in range(8):
                        B.mm(psum[bb_][:, :], wz[:, c, colo:colo + 128], hT[:, c, cs], c == 0, c == 7,
                             [r_wz, r_hTall], [rps[bb_]])
                    B.act(sg[:, sl, :], psum[ba][:, :], AF.Sigmoid, [r_bg], [rps[ba], r_sg[sl]],
                          bias=small[:, 16 + fo:17 + fo])
                    B.act(sz[:, sl, :], psum[bb_][:, :], AF.Sigmoid, [], [rps[bb_], r_sz[sl]])
                    B.tt("pool", sg[:, sl, :], sg[:, sl, :], sz[:, sl, :], ALU.mult, [r_sz[sl]], [r_sg[sl]])
                    B.tt("dve", sg[:, sl, :], sg[:, sl, :], psum[bb_][:, :], ALU.mult, [], [rps[bb_], r_sg[sl]])
                    dstn = GT[:, fo, :].rearrange("p (n j) -> p j n", j=8)[:, 2 * s:2 * s + 2, :]
                    B.tt("dve", dstn, sg[:, sl, :].rearrange("p (j n) -> p j n", j=2),
                         gTb[:, fo, cs].rearrange("p (j n) -> p j n", j=2), ALU.mult,
                         [r_sg[sl], r_gT], [r_G2])
            ctx["dump"]("G2T", GT[:], [r_G2])
            B.barrier()
        if ctx.get("stop") == "D2":
            return []
        with ExitStack() as es:
            E = es.enter_context
            ws = WStream(ctx, E, 2, "wo1s")
            WO = E(nc.sbuf_tensor("WO1", [128, 8, D], BF16))
            gpo = E(nc.sbuf_tensor("gpo1", [128, D], F32))
            r_WO = [B.R("WO1_%d" % i) for i in range(4)]
            r_gpo = B.R("gpo1")
            B.S.dma("sp", gpo[:], dr["gpost1"], d_par, (), [r_gpo])
            for wb in range(4):
                ws.load(dr["wo1"][:, wb * 256:(wb + 1) * 256], WO[:, :, wb * 256:(wb + 1) * 256], r_WO[wb])

            tokap = None
            finals = out_proj_tail(ctx, E, WO, r_WO, gpo, r_gpo, dr["x1"], dr["out"], GT, r_G2, tokap, "t1",
                                   src_res=ctx.get("r_x1"))
            B.barrier()
    return finals


_PROG = {}


def _prog(mode):
    if mode not in _PROG:
        _PROG[mode] = build_program(mode)
    return _PROG[mode]


def _l0_inputs(inp, b):
    f = np.float32
    return {
        "x": np.ascontiguousarray(inp["x"][b], dtype=f),
        "w0": inp["_w0"], "wo0": inp["_wo0"], "gpre0": inp["_gpre0"], "gpost0": inp["_gpost0"],
        "bias0": inp["_bias0"], "ident": inp["_ident"],
    }


def _prep_l0(inputs):
    f = np.float32
    p = {}
    p["_w0"] = _perm_w_in(np.asarray(inputs["attn_w_in"][0], f))
    p["_wo0"] = np.ascontiguousarray(np.asarray(inputs["attn_w_out"][0], f))
    p["_gpre0"] = np.ascontiguousarray(np.asarray(inputs["attn_pre_norm"][0], f).reshape(8, 128).T)
    p["_gpost0"] = np.ascontiguousarray(np.broadcast_to(np.asarray(inputs["attn_post_norm"][0], f), (128, D)))
    p["_bias0"] = _bias_tiles(np.asarray(inputs["rel_bias"], f))
    p["_ident"] = np.eye(128, dtype=f)
    return p


def run_l0(inputs, debug=False):
    inp = dict(inputs)
    inp.update(_prep_l0(inputs))
    inp["x"] = np.asarray(inputs["x"], np.float32)
    nc = build_program("l0", debug=debug)
    in_maps = [_l0_inputs(inp, b) for b in range(8)]
    res = run_bass_kernel_spmd(nc, in_maps, core_ids=list(range(8)))
    if debug:
        return np.stack([r["x1"] for r in res.results], axis=0), res.results[0]
    return np.stack([r["x1"] for r in res.results], axis=0)


def _prep_l1(inputs):
    f = np.float32
    p = {}
    p["w1"] = np.ascontiguousarray(np.asarray(inputs["ssm_w_in"][0], f))
    p["wg1"] = np.ascontiguousarray(np.asarray(inputs["ssm_w_glu"][0], f))
    p["wo1"] = np.ascontiguousarray(np.asarray(inputs["ssm_w_out"][0], f))
    p["gpre1"] = np.ascontiguousarray(np.asarray(inputs["ssm_pre_norm"][0], f).reshape(8, 128).T)
    p["gpost1"] = np.ascontiguousarray(np.broadcast_to(np.asarray(inputs["ssm_post_norm"][0], f), (128, D)))
    p["bglu"] = np.ascontiguousarray(np.asarray(inputs["ssm_b_glu"][0], f).reshape(8, 128).T)

    def pl(a):
        a = np.asarray(a, f)
        a = a.reshape((2, 32, 64) + a.shape[2:])
        a = np.moveaxis(a, 2, 1)
        return np.ascontiguousarray(a.reshape((128, 32) + a.shape[3:]))
    are = pl(inputs["ssm_a_re"][0])
    aim = pl(inputs["ssm_a_im"][0])
    ldt = pl(np.broadcast_to(np.asarray(inputs["ssm_log_dt"][0], f)[:, None], (64, 64)))
    p["pl_a"] = np.ascontiguousarray(np.stack([are, aim, ldt], axis=1))
    bre = pl(inputs["ssm_b_re"][0])
    bim = pl(inputs["ssm_b_im"][0])
    cre = pl(np.swapaxes(np.asarray(inputs["ssm_c_re"][0], f), 1, 2))
    cim = pl(np.swapaxes(np.asarray(inputs["ssm_c_im"][0], f), 1, 2))
    p["pl_bc"] = np.ascontiguousarray(np.stack([bre, bim, cre, cim], axis=1))
    dvec = np.asarray(inputs["ssm_d"][0], f).reshape(64, 16)
    p["dcol"] = np.ascontiguousarray(np.tile(dvec.T, (8, 1)))
    jj = np.arange(128) // 16
    p["mask8"] = (jj[:, None] <= jj[None, :]).astype(f)
    k = np.arange(128)
    es = np.zeros((128, 64, 128), f)
    for a in range(8):
        for b in range(8):
            es[:, a * 8 + b, :] = ((k[:, None] // 16 == a) & (k[None, :] // 16 == b)
                                   & (k[:, None] % 16 == k[None, :] % 16))
    p["esel"] = es.astype(ml_dtypes.bfloat16)
    p["ident"] = np.eye(128, dtype=f)
    return p


def run_l1(inputs, x1, debug=False, stop=None):
    p = _prep_l1(inputs)
    nc = build_program("l1", debug=debug, stop=stop)
    in_maps = []
    for b in range(8):
        m = dict(p)
        m["x1"] = np.ascontiguousarray(x1[b], dtype=np.float32)
        in_maps.append(m)
    res = run_bass_kernel_spmd(nc, in_maps, core_ids=list(range(8)))
    out = np.stack([r["out"] for r in res.results], axis=0)
    if debug:
        return out, res.results[0]
    return out


FUSED = True


def kernel(**inputs):
    p0 = _prep_l0(inputs)
    p1 = _prep_l1(inputs)
    x = np.asarray(inputs["x"], np.float32)
    shared0 = {"w0": p0["_w0"], "wo0": p0["_wo0"], "gpre0": p0["_gpre0"], "gpost0": p0["_gpost0"],
               "bias0": p0["_bias0"], "ident": p0["_ident"]}
    if FUSED:
        nc = _prog("fused")
        shared = dict(shared0)
        shared.update(p1)
        in_maps = []
        for b in range(8):
            m = dict(shared)
            m["x"] = np.ascontiguousarray(x[b])
            in_maps.append(m)
        res = run_bass_kernel_spmd(nc, in_maps, core_ids=list(range(8)))
        return np.stack([np.asarray(r["out"], np.float32) for r in res.results], axis=0)
    in_maps = []
    for b in range(8):
        m = dict(shared0)
        m["x"] = np.ascontiguousarray(x[b])
        in_maps.append(m)
    res = run_bass_kernel_spmd(_prog("l0"), in_maps, core_ids=list(range(8)))
    x1 = [np.asarray(r["x1"], np.float32) for r in res.results]
    in_maps = []
    for b in range(8):
        m = dict(p1)
        m["x1"] = np.ascontiguousarray(x1[b])
        in_maps.append(m)
    res = run_bass_kernel_spmd(_prog("l1"), in_maps, core_ids=list(range(8)))
    return np.stack([np.asarray(r["out"], np.float32) for r in res.results], axis=0)
```
